# Optimizing a Trainium2 kernel written in Bass

```python
import jax, jax.numpy as jnp
from jax import lax
import numpy as np

D_MODEL = 1024
BATCH = 2
SEQ = 8192
DEPTH = 1
DEC_BATCH = 128
DEC_SEQ = 1
PAST_LEN = 2048
PAGE_SIZE = 128

D_MIX = D_MODEL
D_ATT = D_MIX // 2
D_CHK = D_MIX - D_ATT
HEAD_DIM = 64
N_HEADS = D_ATT // HEAD_DIM
CHUNK = 128
CH_GROUP = 128
N_GROUPS = D_CHK // CH_GROUP
Q_BLOCK = 128
EPS = 1e-6
SCALE = HEAD_DIM ** -0.5
SPLIT_IDX = (D_ATT, 2 * D_ATT, 3 * D_ATT, 3 * D_ATT + N_HEADS, 4 * D_ATT + N_HEADS,
             4 * D_ATT + N_HEADS + D_CHK, 4 * D_ATT + N_HEADS + 2 * D_CHK)
D_IN = 4 * D_ATT + N_HEADS + 3 * D_CHK

kernel_name = 'hymba_fox_chunk_gmlp_decode_step'


def rms_norm(x, g):
    xf = x.astype(jnp.float32)
    y = xf * lax.rsqrt(jnp.mean(xf * xf, axis=-1, keepdims=True) + EPS)
    return (y * g.astype(jnp.float32)).astype(x.dtype)


def branch_inputs(x, g_norm, w_in, b_f, g_q, g_k, g_v):
    B, S, _ = x.shape
    h = rms_norm(x, g_norm)
    p = h @ w_in
    q, k, v, fz, za, u, gv, zc = jnp.split(p, SPLIT_IDX, axis=-1)
    q = rms_norm(q.reshape(B, S, N_HEADS, HEAD_DIM), g_q)
    k = rms_norm(k.reshape(B, S, N_HEADS, HEAD_DIM), g_k)
    v = v.reshape(B, S, N_HEADS, HEAD_DIM)
    logf = jax.nn.log_sigmoid((fz + b_f).astype(jnp.float32))
    u = u.reshape(B, S, N_GROUPS, CH_GROUP)
    gv = rms_norm(gv.reshape(B, S, N_GROUPS, CH_GROUP), g_v.reshape(N_GROUPS, CH_GROUP))
    return q, k, v, logf, za, u, gv, zc


def fox_prompt(q, k, v, logf):
    f32 = jnp.float32
    B, S, H, Dh = q.shape
    c = jnp.cumsum(logf, axis=1).transpose(0, 2, 1)
    kf = k.astype(f32)
    vf = v.astype(f32)
    qf = q.astype(f32)
    pos_k = jnp.arange(S)

    def block(i):
        start = i * Q_BLOCK
        qb = lax.dynamic_slice_in_dim(qf, start, Q_BLOCK, axis=1)
        cb = lax.dynamic_slice_in_dim(c, start, Q_BLOCK, axis=2)
        s = jnp.einsum('bqhd,bkhd->bhqk', qb, kf) * SCALE
        s = s + cb[:, :, :, None] - c[:, :, None, :]
        pos_q = start + jnp.arange(Q_BLOCK)
        s = jnp.where(pos_k[None, :] <= pos_q[:, None], s, -jnp.inf)
        pr = jax.nn.softmax(s, axis=-1)
        return jnp.einsum('bhqk,bkhd->bqhd', pr, vf)

    o = lax.map(block, jnp.arange(S // Q_BLOCK))
    return o.transpose(1, 0, 2, 3, 4).reshape(B, S, H, Dh)


def fox_sample(q, k, v, logf, k_past, v_past, logf_past):
    f32 = jnp.float32
    P = k_past.shape[1]
    T = q.shape[1]
    kk = jnp.concatenate([k_past.astype(f32), k.astype(f32)], axis=1)
    vv = jnp.concatenate([v_past.astype(f32), v.astype(f32)], axis=1)
    c = jnp.cumsum(jnp.concatenate([logf_past.astype(f32), logf], axis=1), axis=1).transpose(0, 2, 1)
    s = jnp.einsum('bqhd,bkhd->bhqk', q.astype(f32), kk) * SCALE
    s = s + c[:, :, P:, None] - c[:, :, None, :]
    mask = jnp.arange(P + T)[None, :] <= (P + jnp.arange(T))[:, None]
    s = jnp.where(mask, s, -jnp.inf)
    pr = jax.nn.softmax(s, axis=-1)
    return jnp.einsum('bhqk,bkhd->bqhd', pr, vv)


def chunk_mix(gv, w_s, b_s):
    B, S, G, C = gv.shape
    n = -(-S // CHUNK)
    vp = jnp.pad(gv, ((0, 0), (0, n * CHUNK - S), (0, 0), (0, 0))).reshape(B, n, CHUNK, G, C)
    w = w_s * jnp.tril(jnp.ones((CHUNK, CHUNK), w_s.dtype))
    s = jnp.einsum('gts,bnsgc->bntgc', w, vp) + b_s.T[None, None, :, :, None]
    return s.reshape(B, n * CHUNK, G, C)[:, :S]


def merge(x, attn, za, mix, u, zc, w_out):
    B, S, _ = x.shape
    a = attn.astype(x.dtype).reshape(B, S, D_ATT) * jax.nn.silu(za)
    m = (u * mix.astype(x.dtype)).reshape(B, S, D_CHK) * jax.nn.silu(zc)
    return x + jnp.concatenate([a, m], axis=-1) @ w_out


def setup_inputs(seed: int = 0) -> dict:
    key = jax.random.key(seed)
    ks = jax.random.split(key, 16)
    f32 = jnp.float32
    n_pages = PAST_LEN // PAGE_SIZE
    n_used = DEC_BATCH * n_pages
    n_phys = n_used + (n_used + 3) // 4
    nrm = jax.random.normal
    x_prompt = nrm(ks[0], (BATCH, SEQ, D_MODEL), f32)
    x_sample = nrm(ks[1], (DEC_BATCH, DEC_SEQ, D_MODEL), f32)
    cache_k = nrm(ks[2], (DEPTH, n_phys, PAGE_SIZE, N_HEADS, HEAD_DIM), f32)
    cache_v = nrm(ks[3], (DEPTH, n_phys, PAGE_SIZE, N_HEADS, HEAD_DIM), f32)
    cache_logf = jax.nn.log_sigmoid(3.0 + nrm(ks[4], (DEPTH, n_phys, PAGE_SIZE, N_HEADS), f32))
    page_table = jax.random.permutation(ks[5], n_phys)[:n_used].reshape(DEC_BATCH, n_pages).astype(jnp.int32)
    g_norm = 1.0 + 0.02 * nrm(ks[6], (DEPTH, D_MODEL), f32)
    w_in = nrm(ks[7], (DEPTH, D_MODEL, D_IN), f32) * D_MODEL ** -0.5
    b_f = 3.0 + 0.5 * nrm(ks[8], (DEPTH, N_HEADS), f32)
    g_q = 1.0 + 0.02 * nrm(ks[9], (DEPTH, HEAD_DIM), f32)
    g_k = 1.0 + 0.02 * nrm(ks[10], (DEPTH, HEAD_DIM), f32)
    g_v = 1.0 + 0.02 * nrm(ks[11], (DEPTH, D_CHK), f32)
    w_s = nrm(ks[12], (DEPTH, N_GROUPS, CHUNK, CHUNK), f32) * CHUNK ** -0.5
    b_s = 1.0 + 0.1 * nrm(ks[13], (DEPTH, N_GROUPS, CHUNK), f32)
    w_out = nrm(ks[14], (DEPTH, D_MIX, D_MODEL), f32) * D_MIX ** -0.5
    return {'x_prompt': x_prompt, 'x_sample': x_sample, 'cache_k': cache_k, 'cache_v': cache_v,
            'cache_logf': cache_logf, 'page_table': page_table, 'g_norm': g_norm, 'w_in': w_in,
            'b_f': b_f, 'g_q': g_q, 'g_k': g_k, 'g_v': g_v, 'w_s': w_s, 'b_s': b_s, 'w_out': w_out}


def reference(x_prompt, x_sample, cache_k, cache_v, cache_logf, page_table, g_norm, w_in,
              b_f, g_q, g_k, g_v, w_s, b_s, w_out):
    DB = x_sample.shape[0]
    P = page_table.shape[1] * PAGE_SIZE
    xp, xs = x_prompt, x_sample
    kp_l, vp_l, lp_l, ks_l, vs_l, ls_l, gs_l = [], [], [], [], [], [], []
    for l in range(DEPTH):
        q, k, v, logf, za, u, gv, zc = branch_inputs(xp, g_norm[l], w_in[l], b_f[l], g_q[l], g_k[l], g_v[l])
        attn = fox_prompt(q, k, v, logf)
        mix = chunk_mix(gv, w_s[l], b_s[l])
        xp = merge(xp, attn, za, mix, u, zc, w_out[l])
        kp_l.append(k); vp_l.append(v); lp_l.append(logf)
        q2, k2, v2, logf2, za2, u2, gv2, zc2 = branch_inputs(xs, g_norm[l], w_in[l], b_f[l], g_q[l], g_k[l], g_v[l])
        k_past = cache_k[l][page_table].reshape(DB, P, N_HEADS, HEAD_DIM)
        v_past = cache_v[l][page_table].reshape(DB, P, N_HEADS, HEAD_DIM)
        logf_past = cache_logf[l][page_table].reshape(DB, P, N_HEADS)
        attn2 = fox_sample(q2, k2, v2, logf2, k_past, v_past, logf_past)
        mix2 = chunk_mix(gv2, w_s[l], b_s[l])
        xs = merge(xs, attn2, za2, mix2, u2, zc2, w_out[l])
        ks_l.append(k2); vs_l.append(v2); ls_l.append(logf2); gs_l.append(gv2)
    k_prompt = jnp.stack(kp_l); v_prompt = jnp.stack(vp_l); logf_prompt = jnp.stack(lp_l)
    k_sample = jnp.stack(ks_l); v_sample = jnp.stack(vs_l); logf_sample = jnp.stack(ls_l)
    gv_sample = jnp.stack(gs_l)
    return (xp, xs, k_prompt, v_prompt, logf_prompt, k_sample, v_sample, logf_sample, gv_sample)
```

```python
import contextlib
import numpy as np
import concourse.bass as bass
import concourse.mybir as mybir
from concourse.bass_utils import run_bass_kernel_spmd

F32 = mybir.dt.float32
BF16 = mybir.dt.bfloat16
I32 = mybir.dt.int32
ALU = mybir.AluOpType
AF = mybir.ActivationFunctionType
AX = mybir.AxisListType

D = 1024
DIN = 3592
S = 8192
NT = 64
NOWN = 16
NS = 16
NPG = 16
EPS = 1e-6
SCALE = 0.125
C_Q, C_K, C_V, C_F, C_ZA, C_U, C_GV, C_ZC = 0, 512, 1024, 1536, 1544, 2056, 2568, 3080

K_ID, K_U, K_ONE, K_L, K_MW, K_IOTA, K_OH4, K_EB, K_BM, K_ID8, K_OHT, K_CS, K_END = (
    0, 128, 256, 384, 512, 1024, 1025, 1029, 1285, 1797, 1805, 2829, 3085)

ENGS = ("pe", "act", "dve", "pool", "sp")


class Res:
    __slots__ = ("w", "r")

    def __init__(self):
        self.w = None
        self.r = []


class Op:
    __slots__ = ("eng", "fn", "deps", "signaled", "count", "dma", "sem", "prev_on_sem")

    def __init__(self, eng, fn, dma):
        self.eng = eng
        self.fn = fn
        self.dma = dma
        self.deps = []
        self.signaled = dma
        self.count = None
        self.sem = None
        self.prev_on_sem = None


class Prog:
    def __init__(self, n_dma_sems=16):
        self.q = {e: [] for e in ENGS}
        self.n_dma_sems = n_dma_sems
        self.all_dma = []

    def add(self, eng, fn, reads=(), writes=(), dma=False):
        op = Op(eng, fn, dma)
        deps = {}
        for r in reads:
            if r.w is not None:
                deps[id(r.w)] = (r.w, "raw")
        for w in writes:
            if w.w is not None and id(w.w) not in deps:
                deps[id(w.w)] = (w.w, "waw")
            for rr in w.r:
                if id(rr) not in deps:
                    deps[id(rr)] = (rr, "war")
        for d, kind in deps.values():
            if d is op:
                continue
            if not d.dma and not dma and d.eng == eng:
                if eng == "pe":
                    continue
                if kind != "raw":
                    continue
            d.signaled = True
            op.deps.append(d)
        for r in reads:
            r.r.append(op)
        for w in writes:
            w.w = op
            w.r = []
        self.q[eng].append(op)
        if dma:
            self.all_dma.append(op)
        return op

    def emit(self, nc):
        stack = contextlib.ExitStack()
        with stack:
            esem = {e: stack.enter_context(nc.semaphore("s_" + e)) for e in ENGS}
            dsem = {e: [stack.enter_context(nc.semaphore("d_%s%d" % (e, i))) for i in range(self.n_dma_sems)]
                    for e in ("sp", "act", "pool")}
            for e in ENGS:
                c = 0
                dcount = [0] * self.n_dma_sems
                dlast = [None] * self.n_dma_sems
                k = 0
                for op in self.q[e]:
                    if op.dma:
                        s = k % self.n_dma_sems
                        k += 1
                        dcount[s] += 16
                        op.sem = dsem[e][s]
                        op.count = dcount[s]
                        op.prev_on_sem = dlast[s]
                        dlast[s] = op
                    elif op.signaled:
                        c += 1
                        op.sem = esem[e]
                        op.count = c
            block = stack.enter_context(nc.Block())

            def run(e):
                def body(eng):
                    waited = {}

                    def wait_for(d):
                        key = id(d.sem)
                        if waited.get(key, 0) >= d.count:
                            return
                        eng.wait_ge(d.sem, d.count)
                        waited[key] = d.count

                    for op in self.q[e]:
                        for d in op.deps:
                            wait_for(d)
                        if op.dma and op.prev_on_sem is not None:
                            wait_for(op.prev_on_sem)
                        ins = op.fn(eng)
                        if op.dma:
                            ins.then_inc(op.sem, 16)
                        elif op.signaled:
                            ins.then_inc(op.sem, 1)
                    if e == "sp":
                        last = {}
                        for d in self.all_dma:
                            last[id(d.sem)] = d
                        for d in last.values():
                            wait_for(d)
                return body

            block.tensor(run("pe"))
            block.scalar(run("act"))
            block.vector(run("dve"))
            block.gpsimd(run("pool"))
            block.sync(run("sp"))


def build_nc():
    nc = bass.Bass("TRN2", target_bir_lowering=False)

    def din(name, shape, dt=F32):
        return nc.dram_tensor(name, shape, dt, kind="ExternalInput").ap()

    def dout(name, shape, dt=F32):
        return nc.dram_tensor(name, shape, dt, kind="ExternalOutput").ap()

    x_all = din("x_all", [S, D])
    x_own = din("x_own", [NOWN * 128, D])
    x_s = din("x_s", [NS, D])
    ck = din("ck", [2560 * 128, 512])
    cv = din("cv", [2560 * 128, 512])
    cl = din("cl", [2560 * 128, 8])
    pt = din("pt", [128, 2 * NS], I32)
    g_norm = din("g_norm", [D])
    w_in = din("w_in", [D, DIN])
    b_f = din("b_f", [8])
    g_q = din("g_q", [64])
    g_k = din("g_k", [64])
    g_v = din("g_v", [512])
    w_s = din("w_s", [4, 128, 128])
    b_s = din("b_s", [4, 128])
    w_out = din("w_out", [D, D])
    consts_d = din("consts", [128, K_END])

    y_own = dout("y_own", [NOWN * 128, D])
    k_own = dout("k_own", [NOWN * 128, 512])
    v_own = dout("v_own", [NOWN * 128, 512])
    l_own = dout("l_own", [NOWN * 128, 8])
    y_s = dout("y_s", [NS, D])
    k_s = dout("k_s", [NS, 512])
    v_s = dout("v_s", [NS, 512])
    l_s = dout("l_s", [NS, 8])
    gv_s = dout("gv_s", [NS, 512])

    kT_scr = nc.dram_tensor("kT_scr", [4, 128, S], BF16, kind="Internal").ap()
    v_scr = nc.dram_tensor("v_scr", [4, 128, NT, 130], BF16, kind="Internal").ap()
    q_scr = nc.dram_tensor("q_scr", [NS, 512], F32, kind="Internal").ap()

    P = Prog()
    st = contextlib.ExitStack()
    with st:
        def sb(name, shape, dt):
            return st.enter_context(nc.sbuf_tensor(name, shape, dt))

        cst = sb("cst", [128, K_END], F32)
        identf = cst[:, K_ID:K_ID + 128]
        utri = cst[:, K_U:K_U + 128]
        ones = cst[:, K_ONE:K_ONE + 128]
        ltri = cst[:, K_L:K_L + 128]
        iota = cst[:, K_IOTA:K_IOTA + 1]
        oh4 = cst[:, K_OH4:K_OH4 + 4]
        identb = sb("identb", [128, 128], BF16)
        maskb = sb("maskb", [128, 4, 128], BF16)
        ohTb = sb("ohTb", [8, 8, 128], BF16)
        wout = sb("wout", [128, 8, D], BF16)
        g_bc = sb("g_bc", [128, D], F32)
        gv_bc = sb("gv_bc", [128, 512], F32)
        gq_bc = sb("gq_bc", [128, 64], F32)
        gk_bc = sb("gk_bc", [128, 64], F32)
        bf_bc = sb("bf_bc", [128, 8], F32)
        bs_t = sb("bs_t", [128, 4], F32)
        ws00 = sb("ws00", [128, 4], F32)
        bs0 = sb("bs0", [128, 4], F32)
        wsT = sb("wsT", [128, 4, 128], BF16)
        negc = sb("negc", [128, NT, 8], F32)
        carry = sb("carry", [128, NT, 8], F32)
        runc = sb("runc", [128, 8], F32)
        c_own = sb("c_own", [128, NOWN, 8], F32)
        xt = sb("xt", [128, D], F32)
        xn = sb("xn", [128, D], BF16)
        junk = xn
        xT = sb("xT", [128, 8, 128], BF16)
        xT2 = sb("xT2", [128, 8, 128], BF16)
        ss = sb("ss", [128, 1], F32)
        rr = sb("rr", [128, 1], F32)
        sq = sb("sq", [128, 512], F32)
        ssq = sb("ssq", [128, 8], F32)
        rk = sb("rk", [128, 8], F32)
        kn = sb("kn", [128, 512], F32)
        kf = sb("kf", [128, 512], F32)
        knb = sb("knb", [128, 512], BF16)
        vf = sb("vf", [128, 512], F32)
        zf = sb("zf", [128, 8], F32)
        lf = sb("lf", [128, 8], F32)
        lfs = sb("lfs", [128, 8], F32)
        arT = sb("arT", [128, 2080], F32)
        kT_stage = arT[:, 0:1024].bitcast(BF16).rearrange("p (h t) -> p h t", h=4)
        v_stage = arT[:, 1024:2064].bitcast(BF16).rearrange("p (a h d) -> p a h d", a=4, h=8)
        et = arT[:, 0:512]
        ub = arT[:, 512:1024]
        gvn = sb("gvn", [128, 512], BF16)
        gvnf = sb("gvnf", [128, 512], F32)
        gc = arT[:, 1024:1536]
        tmpm = arT[:, 1536:2048]
        arW = sb("arW", [128, 8 * DIN], BF16)
        arQ = sb("arQ", [128, 25600], BF16)

        Wb = arW[:, :].rearrange("p (c n) -> p c n", c=8)
        kTp = arW[:, 0:8192]
        vP = arW[:, 8192:8192 + 64 * 130].rearrange("p (t e) -> p t e", t=64)
        o = 8192 + 64 * 130
        pTb = [arW[:, o + i * 512:o + (i + 1) * 512] for i in range(4)]
        o += 4 * 512
        o_sb = arW[:, o:o + 1024].bitcast(F32)
        o += 1024
        a_rows = arW[:, o:o + 2048]
        o += 2048
        rcp_t = arW[:, o:o + 8].bitcast(F32)
        o += 8
        mT = arW[:, o:o + 1024].rearrange("p (c n) -> p c n", c=8)
        o += 1024
        yt = arW[:, o:o + 2048].bitcast(F32)
        o += 2048
        assert o <= 8 * DIN
        qT_st = arQ[:, 0:8192].rearrange("p (h n) -> p h n", h=4)
        ga_st = arQ[:, 8192:16384].rearrange("p (i n) -> p i n", i=NOWN)
        m_st = arQ[:, 16384:24576].rearrange("p (i n) -> p i n", i=NOWN)
        Kb = arQ[:, 0:8192].bitcast(F32).rearrange("p (g n) -> p g n", g=8)
        Vb = arQ[:, 8192:16384].bitcast(F32).rearrange("p (g n) -> p g n", g=8)
        Wstage = arQ[:, 0:2 * DIN].bitcast(F32)
        Lh = arQ[:, 16384:20992].bitcast(F32).rearrange("p (c t h) -> p c t h", c=2 * NS, t=9)
        Vbb = arQ[:, 20992:25088].rearrange("p (g n) -> p g n", g=8)
        ptb = arQ[:, 25088:25152].bitcast(I32)
        idx = arQ[:, 25152:25216].bitcast(I32)
        tot_t = sb("tot_t", [128, 256], F32)
        tsum = sb("tsum", [128, 256], F32)
        qb = sb("qb", [128, 512], F32)
        s_t = sb("s_t", [128, 128], F32)
        p_t = sb("p_t", [128, 128], BF16)
        psm = sb("psm", [128, 8], F32)
        mo = sb("mo", [8, 512], F32)
        md = sb("md", [8, 8], F32)
        qn_s = sb("qn_s", [NS, 512], F32)
        kn_s = sb("kn_s", [NS, 512], F32)
        vf_s = sb("vf_s", [NS, 512], F32)
        ga_s = sb("ga_s", [NS, 512], BF16)
        m_s = sb("m_s", [NS, 512], BF16)
        sm1 = sb("sm1", [NS, 512], F32)
        sm2 = sb("sm2", [NS, 8], F32)
        sm3 = sb("sm3", [NS, 8], F32)
        mrg_s = sb("mrg_s", [NS, D], BF16)
        mT_s = sb("mT_s", [128, 8, NS], BF16)

        pf = [st.enter_context(nc.psum_tensor("pf%d" % i, [128, 512], F32)) for i in range(8)]
        Rp = [Res() for _ in range(8)]

        def pbf(i):
            return pf[i][:, :].bitcast(BF16)

        R = {}

        def res(name):
            if name not in R:
                R[name] = Res()
            return R[name]

        cap = [None]

        def A(eng, fn, reads=(), writes=(), dma=False):
            if cap[0] is not None:
                cap[0].append((eng, fn, reads, writes, dma))
                return None
            rl = [res(r) if isinstance(r, str) else r for r in reads]
            wl = [res(w) if isinstance(w, str) else w for w in writes]
            return P.add(eng, fn, rl, wl, dma)

        def record(f, *args):
            cap[0] = []
            f(*args)
            lst = cap[0]
            cap[0] = None
            return lst

        def interleave(*lists):
            lists = [l for l in lists if l]
            pos = [0] * len(lists)
            left = sum(len(l) for l in lists)
            while left:
                for i, l in enumerate(lists):
                    if pos[i] < len(l):
                        A(*l[pos[i]])
                        pos[i] += 1
                        left -= 1

        A("sp", lambda e: e.dma_start(out=cst[:, :], in_=consts_d), writes=["cst"], dma=True)
        A("sp", lambda e: e.dma_start(out=g_bc[:, :], in_=g_norm.partition_broadcast(128)), writes=["g_bc"], dma=True)
        A("sp", lambda e: e.dma_start(out=gv_bc[:, :], in_=g_v.partition_broadcast(128)), writes=["gv_bc"], dma=True)
        A("sp", lambda e: e.dma_start(out=gq_bc[:, :], in_=g_q.partition_broadcast(128)), writes=["gq_bc"], dma=True)
        A("sp", lambda e: e.dma_start(out=gk_bc[:, :], in_=g_k.partition_broadcast(128)), writes=["gk_bc"], dma=True)
        A("sp", lambda e: e.dma_start(out=bf_bc[:, :], in_=b_f.partition_broadcast(128)), writes=["bf_bc"], dma=True)
        for g in range(4):
            A("sp", lambda e, g=g: e.dma_start(out=bs_t[:, g:g + 1], in_=b_s[g].rearrange("(t o) -> t o", o=1)),
              writes=["bs_t"], dma=True)
            A("sp", lambda e, g=g: e.dma_start(out=ws00[:, g:g + 1], in_=w_s[g, 0, 0:1].partition_broadcast(128)),
              writes=["ws00"], dma=True)
            A("sp", lambda e, g=g: e.dma_start(out=bs0[:, g:g + 1], in_=b_s[g, 0:1].partition_broadcast(128)),
              writes=["bs0"], dma=True)
        w_in_v = w_in.rearrange("(c p) n -> p c n", p=128)
        for c in range(8):
            A("sp", lambda e, c=c: e.dma_start(out=Wstage, in_=w_in[c * 128:(c + 1) * 128, :]), writes=["Kb"], dma=True)
            A("act", lambda e, c=c: e.activation(out=Wb[:, c, :], in_=Wstage, func=AF.Copy), reads=["Kb"], writes=["Wb"])
        A("pool", lambda e: e.dma_start(out=wout[:, :, :], in_=w_out.rearrange("(c p) n -> p c n", p=128)),
          writes=["wout"], dma=True)
        A("dve", lambda e: e.tensor_copy(out=identb[:, :], in_=identf), reads=["cst"], writes=["identb"])
        A("dve", lambda e: e.tensor_scalar(out=maskb[:, :, :].rearrange("p w t -> p (w t)"), in0=cst[:, K_MW:K_MW + 512],
                                           scalar1=1.0e4, scalar2=-1.0e4, op0=ALU.mult, op1=ALU.add),
          reads=["cst"], writes=["maskb"])
        A("dve", lambda e: e.tensor_copy(out=ohTb[:, :, :].rearrange("p h s -> p (h s)"), in_=cst[0:8, K_OHT:K_OHT + 1024]),
          reads=["cst"], writes=["ohTb"])
        A("dve", lambda e: e.memset(runc[:, :], 0.0), writes=["runc"])
        ws_t = tmpm.rearrange("p (g s) -> p g s", g=4)
        A("sp", lambda e: e.dma_start(out=ws_t, in_=w_s.rearrange("g t s -> t g s")), writes=["tmpm"], dma=True)
        for g in range(4):
            A("pe", lambda e, g=g: e.transpose(out=pf[0][:, g * 128:(g + 1) * 128], in_=ws_t[:, g, :], identity=identf),
              reads=["tmpm", "cst"], writes=[Rp[0]])
        for g in range(4):
            A("dve", lambda e, g=g: e.tensor_tensor(out=wsT[:, g, :], in0=pf[0][:, g * 128:(g + 1) * 128], in1=utri, op=ALU.mult),
              reads=[Rp[0], "cst"], writes=["wsT"])

        def rsqrt_act(dst, src, n, scale):
            A("act", lambda e: e.activation(out=dst, in_=src, func=AF.Ln, scale=scale, bias=EPS), reads=["tmp_r_in"], writes=["tmp_r"])
            A("act", lambda e: e.activation(out=dst, in_=dst, func=AF.Exp, scale=-0.5), reads=["tmp_r"], writes=["tmp_r"])

        def norm_T(src, n, xTb=None, xTr="xT"):
            xTb = xT if xTb is None else xTb
            A("sp", lambda e: e.dma_start(out=xt[:n, :], in_=src), writes=["xt"], dma=True)
            A("dve", lambda e: e.memset(ss[:n, :], 0.0), writes=["ss"])
            A("act", lambda e: e.activation(out=junk[:n, :], in_=xt[:n, :], func=AF.Square, accum_out=ss[:n, :]),
              reads=["xt", "ss"], writes=["xn", "ss"])
            A("act", lambda e: e.activation(out=rr[:n, :], in_=ss[:n, :], func=AF.Ln, scale=1.0 / D, bias=EPS),
              reads=["ss"], writes=["rr"])
            A("act", lambda e: e.activation(out=rr[:n, :], in_=rr[:n, :], func=AF.Exp, scale=-0.5),
              reads=["rr"], writes=["rr"])
            A("dve", lambda e: e.scalar_tensor_tensor(out=xn[:n, :], in0=xt[:n, :], scalar=rr[:n, 0:1], in1=g_bc[:n, :],
                                                      op0=ALU.mult, op1=ALU.mult),
              reads=["xt", "rr", "g_bc"], writes=["xn"])
            for c in range(8):
                A("pe", lambda e, c=c: e.transpose(out=pbf(0)[:, c * 128:c * 128 + n], in_=xn[:n, c * 128:(c + 1) * 128],
                                                   identity=identb[:n, :n]),
                  reads=["xn", "identb"], writes=[Rp[0]])
            A("act", lambda e: e.activation(out=xTb[:, :, 0:n], in_=pbf(0).rearrange("p (c t) -> p c t", c=8)[:, :, 0:n], func=AF.Copy),
              reads=[Rp[0]], writes=[xTr])

        def proj(bank, col0, width, n, xTb=None, xTr="xT"):
            xTb = xT if xTb is None else xTb
            for c in range(8):
                A("pe", lambda e, c=c: e.matmul(pf[bank][:n, 0:width], lhsT=xTb[:, c, 0:n], rhs=Wb[:, c, col0:col0 + width],
                                                start=(c == 0), stop=(c == 7)),
                  reads=[xTr, "Wb"], writes=[Rp[bank]])

        def headnorm(bank, n, nh, hd, gb, out_ap, out_res):
            A("act", lambda e: e.activation(out=sq[:n, :], in_=pf[bank][:n, :], func=AF.Square), reads=[Rp[bank]], writes=["sq"])
            A("dve", lambda e: e.tensor_reduce(out=ssq[:n, 0:nh], in_=sq[:n, :].rearrange("p (h d) -> p h d", h=nh),
                                               axis=AX.X, op=ALU.add), reads=["sq"], writes=["ssq"])
            A("act", lambda e: e.activation(out=rk[:n, 0:nh], in_=ssq[:n, 0:nh], func=AF.Ln, scale=1.0 / hd, bias=EPS),
              reads=["ssq"], writes=["rk"])
            A("act", lambda e: e.activation(out=rk[:n, 0:nh], in_=rk[:n, 0:nh], func=AF.Exp, scale=-0.5), reads=["rk"], writes=["rk"])
            A("dve", lambda e: e.tensor_tensor(out=kn[:n, :].rearrange("p (h d) -> p h d", h=nh),
                                               in0=pf[bank][:n, :].rearrange("p (h d) -> p h d", h=nh),
                                               in1=rk[:n, 0:nh].unsqueeze(2).to_broadcast([n, nh, hd]), op=ALU.mult),
              reads=[Rp[bank], "rk"], writes=["kn"])
            if hd == 64:
                in1 = gb[:n, :].unsqueeze(1).to_broadcast([n, nh, hd])
                A("dve", lambda e: e.tensor_tensor(out=out_ap.rearrange("p (h d) -> p h d", h=nh),
                                                   in0=kn[:n, :].rearrange("p (h d) -> p h d", h=nh), in1=in1, op=ALU.mult),
                  reads=["kn"], writes=[out_res])
            else:
                A("dve", lambda e: e.tensor_tensor(out=out_ap, in0=kn[:n, :], in1=gb[:n, :], op=ALU.mult),
                  reads=["kn"], writes=[out_res])

        def logf_of(bank, n):
            A("dve", lambda e: e.tensor_tensor(out=zf[:n, :], in0=pf[bank][:n, 0:8], in1=bf_bc[:n, :], op=ALU.add),
              reads=[Rp[bank], "bf_bc"], writes=["zf"])
            A("act", lambda e: e.activation(out=zf[:n, :], in_=zf[:n, :], func=AF.Exp, scale=-1.0), reads=["zf"], writes=["zf"])
            A("act", lambda e: e.activation(out=zf[:n, :], in_=zf[:n, :], func=AF.Ln, bias=1.0), reads=["zf"], writes=["zf"])
            A("dve", lambda e: e.tensor_scalar(out=lf[:n, :], in0=zf[:n, :], scalar1=-1.0, scalar2=None, op0=ALU.mult),
              reads=["zf"], writes=["lf"])

        def silu_from(bank, n, out_ap, out_res):
            A("act", lambda e: e.activation(out=et[:n, :], in_=pf[bank][:n, :], func=AF.Exp, scale=-1.0), reads=[Rp[bank]], writes=["et"])
            A("dve", lambda e: e.tensor_scalar(out=et[:n, :], in0=et[:n, :], scalar1=1.0, scalar2=None, op0=ALU.add),
              reads=["et"], writes=["et"])
            A("dve", lambda e: e.reciprocal(out=et[:n, :], in_=et[:n, :]), reads=["et"], writes=["et"])
            A("dve", lambda e: e.tensor_tensor(out=out_ap, in0=pf[bank][:n, :], in1=et[:n, :], op=ALU.mult),
              reads=[Rp[bank], "et"], writes=[out_res])

        def rest_proj(n, ga_out, ga_res, m_out, m_res, sample):
            proj(3, C_ZA, 512, n)
            silu_from(3, n, ga_out, ga_res)
            proj(4, C_U, 512, n)
            A("act", lambda e: e.activation(out=ub[:n, :], in_=pf[4][:n, :], func=AF.Copy), reads=[Rp[4]], writes=["ub"])
            proj(3, C_GV, 512, n)
            if sample:
                headnorm(3, n, 4, 128, gv_bc, gvnf[:n, :], "gvnf")
                A("sp", lambda e: e.dma_start(out=gv_s, in_=gvnf[:n, :]), reads=["gvnf"], dma=True)
            else:
                headnorm(3, n, 4, 128, gv_bc, gvn[:n, :], "gvn")
                for g in range(4):
                    A("pe", lambda e, g=g: e.matmul(pf[5][:n, g * 128:(g + 1) * 128], lhsT=wsT[:, g, :],
                                                    rhs=gvn[:, g * 128:(g + 1) * 128], start=True, stop=True),
                      reads=["wsT", "gvn"], writes=[Rp[5]])
            proj(4, C_ZC, 512, n)
            silu_from(4, n, gc[:n, :], "gc")
            for g in range(4):
                sl = slice(g * 128, (g + 1) * 128)
                if sample:
                    A("dve", lambda e, g=g, sl=sl: e.tensor_scalar(out=tmpm[:n, sl], in0=gvnf[:n, sl], scalar1=ws00[:n, g:g + 1],
                                                                   scalar2=bs0[:n, g:g + 1], op0=ALU.mult, op1=ALU.add),
                      reads=["gvnf", "ws00", "bs0"], writes=["tmpm"])
                    A("dve", lambda e, sl=sl: e.tensor_tensor(out=tmpm[:n, sl], in0=tmpm[:n, sl], in1=ub[:n, sl], op=ALU.mult),
                      reads=["tmpm", "ub"], writes=["tmpm"])
                else:
                    A("dve", lambda e, g=g, sl=sl: e.scalar_tensor_tensor(out=tmpm[:n, sl], in0=pf[5][:n, sl], scalar=bs_t[:n, g:g + 1],
                                                                          in1=ub[:n, sl], op0=ALU.add, op1=ALU.mult),
                      reads=[Rp[5], "bs_t", "ub"], writes=["tmpm"])
            A("dve", lambda e: e.tensor_tensor(out=m_out, in0=tmpm[:n, :], in1=gc[:n, :], op=ALU.mult),
              reads=["tmpm", "gc"], writes=[m_res])

        n = NS
        norm_T(x_s, n)
        proj(1, C_Q, 512, n)
        headnorm(1, n, 8, 64, gq_bc, qn_s[:n, :], "qn_s")
        A("sp", lambda e: e.dma_start(out=q_scr, in_=qn_s[:n, :]), reads=["qn_s"], writes=["q_scr"], dma=True)
        proj(1, C_K, 512, n)
        headnorm(1, n, 8, 64, gk_bc, kn_s[:n, :], "kn_s")
        A("sp", lambda e: e.dma_start(out=k_s, in_=kn_s[:n, :]), reads=["kn_s"], dma=True)
        proj(2, C_V, 512, n)
        A("act", lambda e: e.activation(out=vf_s[:n, :], in_=pf[2][:n, :], func=AF.Copy), reads=[Rp[2]], writes=["vf_s"])
        A("sp", lambda e: e.dma_start(out=v_s, in_=vf_s[:n, :]), reads=["vf_s"], dma=True)
        proj(2, C_F, 8, n)
        logf_of(2, n)
        A("dve", lambda e: e.tensor_copy(out=lfs[:n, :], in_=lf[:n, :]), reads=["lf"], writes=["lfs"])
        A("sp", lambda e: e.dma_start(out=l_s, in_=lfs[:n, :]), reads=["lfs"], dma=True)
        rest_proj(n, ga_s[:n, :], "ga_s", m_s[:n, :], "m_s", True)

        ck2 = ck.rearrange("(r t) n -> r (t n)", t=8)
        cv2 = cv.rearrange("(r t) n -> r (t n)", t=8)
        cl2 = cl.rearrange("(r t) n -> r (t n)", t=8)
        A("sp", lambda e: e.dma_start(out=ptb, in_=pt), writes=["ptb"], dma=True)
        A("dve", lambda e: e.tensor_scalar(out=idx, in0=ptb, scalar1=16.0, scalar2=iota, op0=ALU.mult, op1=ALU.add),
          reads=["ptb", "cst"], writes=["idx"])
        A("dve", lambda e: e.memset(arQ[:, 16384:20992].bitcast(F32), 0.0), writes=["Lall"])
        for col in range(2 * NS):
            A("pool", lambda e, col=col: e.indirect_dma_start(
                out=arQ[:, 16384:20992].bitcast(F32)[:, col * 72:col * 72 + 64], out_offset=None, in_=cl2,
                in_offset=bass.IndirectOffsetOnAxis(ap=idx[:, col:col + 1], axis=0)),
              reads=["idx"], writes=["Lall"], dma=True)

        def compute_E():
            A("dve", lambda e: e.tensor_reduce(out=tot_t[:, :].rearrange("p (c h) -> p c h", h=8),
                                               in_=Lh[:, :, 0:8, :].rearrange("p c t h -> p c h t"), axis=AX.X, op=ALU.add),
              reads=["Lall"], writes=["tot_t"])
            A("pe", lambda e: e.matmul(pf[3][:, 0:256], lhsT=ltri, rhs=tot_t[:, :], start=True, stop=True),
              reads=["tot_t", "cst"], writes=[Rp[3]])
            A("pe", lambda e: e.matmul(pf[4][:, 0:128], lhsT=ones,
                                       rhs=tot_t[:, :].rearrange("p (b f h) -> p b f h", f=2, h=8)[:, :, 1, :], start=True, stop=True),
              reads=["tot_t", "cst"], writes=[Rp[4]])
            A("dve", lambda e: e.tensor_copy(out=tsum[:, :], in_=pf[3][:, 0:256]), reads=[Rp[3]], writes=["tsum"])
            A("dve", lambda e: e.tensor_tensor(out=tsum[:, :].rearrange("p (b f h) -> p b f h", f=2, h=8)[:, :, 0, :],
                                               in0=tsum[:, :].rearrange("p (b f h) -> p b f h", f=2, h=8)[:, :, 0, :],
                                               in1=pf[4][:, 0:128].rearrange("p (b h) -> p b h", h=8), op=ALU.add),
              reads=[Rp[4], "tsum"], writes=["tsum"])
            for t in range(6, -1, -1):
                A("dve", lambda e, t=t: e.tensor_tensor(out=Lh[:, :, t, :], in0=Lh[:, :, t, :], in1=Lh[:, :, t + 1, :], op=ALU.add),
                  reads=["Lall"], writes=["Lall"])
            for t in range(1, 9):
                A("dve", lambda e, t=t: e.tensor_tensor(out=Lh[:, :, t, :], in0=Lh[:, :, t, :],
                                                        in1=tsum[:, :].rearrange("p (c h) -> p c h", h=8), op=ALU.add),
                  reads=["Lall", "tsum"], writes=["Lall"])

        def gather_half(b, hf):
            col = b * 2 + hf
            A("pool", lambda e: e.indirect_dma_start(
                out=Kb.rearrange("p g n -> p (g n)"), out_offset=None, in_=ck2,
                in_offset=bass.IndirectOffsetOnAxis(ap=idx[:, col:col + 1], axis=0)),
              reads=["idx"], writes=["Kb"], dma=True)
            A("pool", lambda e: e.indirect_dma_start(
                out=Vb.rearrange("p g n -> p (g n)"), out_offset=None, in_=cv2,
                in_offset=bass.IndirectOffsetOnAxis(ap=idx[:, col:col + 1], axis=0)),
              reads=["idx"], writes=["Vb"], dma=True)

        def sample_half(b, hf):
            hs = slice(hf * 64, hf * 64 + 64)
            if hf == 0:
                A("sp", lambda e: e.dma_start(out=qb[:, :], in_=q_scr[b].partition_broadcast(128)), reads=["q_scr"], writes=["qb"], dma=True)
            A("dve", lambda e: e.tensor_tensor(out=Kb, in0=Kb, in1=qb[:, :].unsqueeze(1).to_broadcast([128, 8, 512]), op=ALU.mult),
              reads=["Kb", "qb"], writes=["Kb"])
            A("dve", lambda e: e.tensor_reduce(out=s_t[:, hs], in_=Kb.rearrange("p g (h d) -> p (g h) d", h=8), axis=AX.X, op=ALU.add),
              reads=["Kb"], writes=["s_t"])
            A("dve", lambda e: e.scalar_tensor_tensor(out=s_t[:, hs].rearrange("p (g h) -> p g h", g=8), in0=s_t[:, hs].rearrange("p (g h) -> p g h", g=8),
                                                      scalar=SCALE, in1=Lh[:, b * 2 + hf, 1:9, :], op0=ALU.mult, op1=ALU.add),
              reads=["s_t", "Lall"], writes=["s_t"])
            A("act", lambda e: e.activation(out=p_t[:, hs], in_=s_t[:, hs], func=AF.Exp), reads=["s_t"], writes=["p_t"])
            A("act", lambda e: e.activation(out=Vbb, in_=Vb, func=AF.Copy), reads=["Vb"], writes=["Vbb"])
            for g in range(8):
                gg = hf * 8 + g
                A("pe", lambda e, g=g, gg=gg: e.matmul(pf[5][0:8, :], lhsT=p_t[:, gg * 8:(gg + 1) * 8], rhs=Vbb[:, g, :],
                                                       start=(gg == 0), stop=(gg == NPG - 1)),
                  reads=["p_t", "Vbb"], writes=[Rp[5]])
            if hf == 0:
                return
            A("dve", lambda e: e.tensor_reduce(out=psm[:, :], in_=p_t[:, :].rearrange("p (g h) -> p h g", g=NPG), axis=AX.X, op=ALU.add),
              reads=["p_t"], writes=["psm"])
            A("dve", lambda e: e.tensor_tensor(out=mo[:, :], in0=pf[5][0:8, :], in1=cst[0:8, K_BM:K_BM + 512], op=ALU.mult),
              reads=[Rp[5], "cst"], writes=["mo"])
            A("pe", lambda e: e.matmul(pf[3][0:NS, :], lhsT=cst[0:8, K_EB + b * 16:K_EB + (b + 1) * 16], rhs=mo[:, :],
                                       start=(b == 0), stop=(b == NS - 1)),
              reads=["mo", "cst"], writes=[Rp[3]])
            A("pe", lambda e: e.matmul(pf[4][0:NS, 0:8], lhsT=cst[:, K_CS + b * 16:K_CS + (b + 1) * 16], rhs=psm[:, :],
                                       start=(b == 0), stop=(b == NS - 1)),
              reads=["psm", "cst"], writes=[Rp[4]])

        xTs = ((xT, "xT"), (xT2, "xT2"))

        def pA_h1(I):
            xb, xr = xTs[I % 2]
            norm_T(x_all[I * 128:(I + 1) * 128, :], 128, xb, xr)

        def pA_k(I):
            xb, xr = xTs[I % 2]
            proj(1, C_K, 512, 128, xb, xr)
            headnorm(1, 128, 8, 64, gk_bc, knb[:, :], "knb")
            for hp in range(4):
                A("pe", lambda e, hp=hp: e.transpose(out=pbf(6)[:, hp * 128:(hp + 1) * 128], in_=knb[:, hp * 128:(hp + 1) * 128],
                                                     identity=identb[:, :]),
                  reads=["knb", "identb"], writes=[Rp[6]])
            r4 = I % 4
            A("act", lambda e: e.activation(out=kT_stage[:, :, r4 * 128:(r4 + 1) * 128],
                                            in_=pbf(6)[:, 0:512].rearrange("p (h t) -> p h t", h=4), func=AF.Copy),
              reads=[Rp[6]], writes=["kT_stage"])
            if r4 == 3:
                t0 = (I - 3) * 128
                A("sp", lambda e: e.dma_start(out=kT_scr[:, :, t0:t0 + 512].rearrange("h p t -> p h t"), in_=kT_stage[:, :, :]),
                  reads=["kT_stage"], writes=["kT_scr"], dma=True)

        def pA_vf(I):
            xb, xr = xTs[I % 2]
            r4 = I % 4
            proj(2, C_V, 512, 128, xb, xr)
            A("act", lambda e: e.activation(out=v_stage[:, r4, :, 0:64], in_=pf[2][:, :].rearrange("p (h d) -> p h d", h=8), func=AF.Copy),
              reads=[Rp[2]], writes=["v_stage"])
            proj(7, C_F, 8, 128, xb, xr)
            logf_of(7, 128)
            A("pe", lambda e: e.matmul(pf[7][:, 8:16], lhsT=utri, rhs=lf[:, :], start=True, stop=True), reads=["lf", "cst"], writes=[Rp[7]])
            A("pe", lambda e: e.matmul(pf[7][:, 16:24], lhsT=ones, rhs=lf[:, :], start=True, stop=True), reads=["lf", "cst"], writes=[Rp[7]])
            A("dve", lambda e: e.tensor_copy(out=carry[:, I, :], in_=runc[:, :]), reads=["runc"], writes=["carry"])
            A("dve", lambda e: e.scalar_tensor_tensor(out=negc[:, I, :], in0=pf[7][:, 8:16], scalar=-1.0, in1=runc[:, :],
                                                      op0=ALU.mult, op1=ALU.subtract),
              reads=[Rp[7], "runc"], writes=["negc"])
            A("dve", lambda e: e.tensor_tensor(out=runc[:, :], in0=runc[:, :], in1=pf[7][:, 16:24], op=ALU.add),
              reads=[Rp[7], "runc"], writes=["runc"])
            if r4 == 3:
                for hp in range(4):
                    A("sp", lambda e, hp=hp: e.dma_start(
                        out=v_scr[hp, :, I - 3:I + 1, :].rearrange("p a (h d) -> p a h d", h=2),
                        in_=v_stage[:, :, 2 * hp:2 * hp + 2, :]),
                      reads=["v_stage"], writes=["v_scr"], dma=True)

        cw = sb("cw", [128, 32], F32)
        cwo = sb("cwo", [128, 8], F32)
        fz_t = sb("fz_t", [128, 4], F32)

        def fence(names):
            A("dve", lambda e: e.memset(fz_t[:, :], 0.0), writes=names)

        fence(["et", "ub", "gc", "tmpm", "kT_stage", "v_stage"])
        A("dve", lambda e: e.memset(arT[:, 1024:2064].bitcast(BF16), 1.0), writes=["v_stage"])
        sched = {}
        for hs_ in range(2 * NS):
            sched.setdefault(15 + (hs_ * 3) // 2, []).append(hs_)
        gather_half(0, 0)
        pA_h1(0)
        for I in range(NT):
            chains = [record(pA_k, I), record(pA_vf, I)]
            if I + 1 < NT:
                chains.insert(0, record(pA_h1, I + 1))
            for hs_ in sched.get(I, []):
                chains.append(record(sample_half, hs_ // 2, hs_ % 2))
            interleave(*chains)
            for hs_ in sched.get(I, []):
                if hs_ + 1 < 2 * NS:
                    gather_half((hs_ + 1) // 2, (hs_ + 1) % 2)
            if I == 13:
                compute_E()

        n = NS
        A("sp", lambda e: e.dma_start(out=xt[:n, :], in_=x_s), writes=["xt"], dma=True)
        A("dve", lambda e: e.tensor_tensor(out=sm1[:, :], in0=qn_s[:, :], in1=kn_s[:, :], op=ALU.mult), reads=["qn_s", "kn_s"], writes=["sm1"])
        A("dve", lambda e: e.tensor_reduce(out=sm2[:, :], in_=sm1[:, :].rearrange("p (h d) -> p h d", h=8), axis=AX.X, op=ALU.add),
          reads=["sm1"], writes=["sm2"])
        A("dve", lambda e: e.scalar_tensor_tensor(out=sm2[:, :], in0=sm2[:, :], scalar=SCALE, in1=lfs[:n, :], op0=ALU.mult, op1=ALU.subtract),
          reads=["sm2", "lfs"], writes=["sm2"])
        A("act", lambda e: e.activation(out=sm2[:, :], in_=sm2[:, :], func=AF.Exp), reads=["sm2"], writes=["sm2"])
        A("dve", lambda e: e.tensor_tensor(out=sm1[:, :].rearrange("p (h d) -> p h d", h=8), in0=vf_s[:, :].rearrange("p (h d) -> p h d", h=8),
                                           in1=sm2[:, :].unsqueeze(2).to_broadcast([n, 8, 64]), op=ALU.mult),
          reads=["vf_s", "sm2"], writes=["sm1"])
        A("dve", lambda e: e.tensor_tensor(out=sm1[:, :], in0=sm1[:, :], in1=pf[3][0:n, :], op=ALU.add), reads=["sm1", Rp[3]], writes=["sm1"])
        A("dve", lambda e: e.tensor_tensor(out=sm3[:, :], in0=pf[4][0:n, 0:8], in1=sm2[:, :], op=ALU.add), reads=[Rp[4], "sm2"], writes=["sm3"])
        A("dve", lambda e: e.reciprocal(out=sm3[:, :], in_=sm3[:, :]), reads=["sm3"], writes=["sm3"])
        A("dve", lambda e: e.tensor_tensor(out=sm1[:, :].rearrange("p (h d) -> p h d", h=8), in0=sm1[:, :].rearrange("p (h d) -> p h d", h=8),
                                           in1=sm3[:, :].unsqueeze(2).to_broadcast([n, 8, 64]), op=ALU.mult),
          reads=["sm1", "sm3"], writes=["sm1"])
        A("dve", lambda e: e.tensor_tensor(out=mrg_s[:, 0:512], in0=sm1[:, :], in1=ga_s[:, :], op=ALU.mult), reads=["sm1", "ga_s"], writes=["mrg_s"])
        A("dve", lambda e: e.tensor_copy(out=mrg_s[:, 512:1024], in_=m_s[:, :]), reads=["m_s"], writes=["mrg_s"])
        for c in range(8):
            A("pe", lambda e, c=c: e.transpose(out=pbf(0)[:, c * 128:c * 128 + n], in_=mrg_s[:n, c * 128:(c + 1) * 128], identity=identb[:n, :n]),
              reads=["mrg_s", "identb"], writes=[Rp[0]])
        A("act", lambda e: e.activation(out=mT_s[:, :, :], in_=pbf(0).rearrange("p (c t) -> p c t", c=8)[:, :, 0:n], func=AF.Copy),
          reads=[Rp[0]], writes=["mT_s"])
        for half in range(2):
            for c in range(8):
                A("pe", lambda e, c=c, half=half: e.matmul(pf[1 + half][:n, :], lhsT=mT_s[:, c, :], rhs=wout[:, c, half * 512:(half + 1) * 512],
                                                           start=(c == 0), stop=(c == 7)),
                  reads=["mT_s", "wout"], writes=[Rp[1 + half]])
            A("dve", lambda e, half=half: e.tensor_tensor(out=xt[:n, half * 512:(half + 1) * 512], in0=pf[1 + half][:n, :],
                                                          in1=xt[:n, half * 512:(half + 1) * 512], op=ALU.add),
              reads=[Rp[1 + half], "xt"], writes=["xt"])
        A("sp", lambda e: e.dma_start(out=y_s, in_=xt[:n, :]), reads=["xt"], dma=True)

        fence(["Kb", "Vb", "Vbb", "Lall", "ptb", "idx", "qT_st", "ga_st", "m_st", "et", "ub", "gc", "tmpm", "kT_stage", "v_stage"])
        for i in range(NOWN):
            rows = slice(i * 128, (i + 1) * 128)
            norm_T(x_own[rows, :], 128)
            proj(1, C_Q, 512, 128)
            headnorm(1, 128, 8, 64, gq_bc, knb[:, :], "knb")
            for hp in range(4):
                A("pe", lambda e, hp=hp: e.transpose(out=pbf(6)[:, hp * 128:(hp + 1) * 128], in_=knb[:, hp * 128:(hp + 1) * 128],
                                                     identity=identb[:, :]),
                  reads=["knb", "identb"], writes=[Rp[6]])
            A("act", lambda e, i=i: e.activation(out=qT_st[:, :, i * 128:(i + 1) * 128],
                                                 in_=pbf(6)[:, 0:512].rearrange("p (h t) -> p h t", h=4), func=AF.Copy),
              reads=[Rp[6]], writes=["qT_st"])
            proj(1, C_K, 512, 128)
            headnorm(1, 128, 8, 64, gk_bc, kf[:, :], "kf")
            A("sp", lambda e, rows=rows: e.dma_start(out=k_own[rows, :], in_=kf[:, :]), reads=["kf"], dma=True)
            proj(2, C_V, 512, 128)
            A("act", lambda e: e.activation(out=vf[:, :], in_=pf[2][:, :], func=AF.Copy), reads=[Rp[2]], writes=["vf"])
            A("sp", lambda e, rows=rows: e.dma_start(out=v_own[rows, :], in_=vf[:, :]), reads=["vf"], dma=True)
            proj(7, C_F, 8, 128)
            logf_of(7, 128)
            A("sp", lambda e, rows=rows: e.dma_start(out=l_own[rows, :], in_=lf[:, :]), reads=["lf"], dma=True)
            A("pe", lambda e: e.matmul(pf[7][:, 8:16], lhsT=utri, rhs=lf[:, :], start=True, stop=True), reads=["lf", "cst"], writes=[Rp[7]])
            A("dve", lambda e, i=i: e.tensor_tensor(out=cw[:, :].rearrange("p (h w) -> p h w", h=8),
                                                    in0=carry[:, 4 * i:4 * i + 4, :].rearrange("p w h -> p h w"),
                                                    in1=oh4.unsqueeze(1).to_broadcast([128, 8, 4]), op=ALU.mult),
              reads=["carry", "cst"], writes=["cw"])
            A("dve", lambda e: e.tensor_reduce(out=cwo[:, :], in_=cw[:, :].rearrange("p (h w) -> p h w", h=8), axis=AX.X, op=ALU.add),
              reads=["cw"], writes=["cwo"])
            A("dve", lambda e, i=i: e.tensor_tensor(out=c_own[:, i, :], in0=pf[7][:, 8:16], in1=cwo[:, :], op=ALU.add),
              reads=[Rp[7], "cwo"], writes=["c_own"])
            rest_proj(128, ga_st[:, i, :], "ga_st", m_st[:, i, :], "m_st", False)

        fence(["Wb", "kTp", "vP", "pTb0", "pTb1", "pTb2", "pTb3", "o_sb", "a_rows", "rcp_t", "mT", "yt"])
        for i0 in range(0, NOWN, 4):
            for k in range(4):
                A("pe", lambda e, i0=i0, k=k: e.transpose(out=pf[7][0:8, k * 128:(k + 1) * 128], in_=c_own[:, i0 + k, :], identity=identf),
                  reads=["c_own", "cst"], writes=[Rp[7]])
            A("act", lambda e, i0=i0: e.activation(out=a_rows[0:8, i0 * 128:i0 * 128 + 512], in_=pf[7][0:8, :], func=AF.Copy, scale=1.0 / SCALE),
              reads=[Rp[7]], writes=["a_rows"])
        SB = (0, 1, 7)
        LA = 2
        its = []
        for h in range(8):
            for qg in range(4):
                nJ = 16 * qg + 16
                for J in range(nJ):
                    its.append((h, qg, J, nJ))

        def emit_qk(n):
            h, qg, J, nJ = its[n]
            hp, half = h // 2, h % 2
            rws = slice(half * 64, half * 64 + 64)
            if half == 0 and qg == 0 and J == 0:
                A("sp", lambda e, hp=hp: e.dma_start(out=kTp, in_=kT_scr[hp]), reads=["kT_scr"], writes=["kTp"], dma=True)
                A("sp", lambda e, hp=hp: e.dma_start(out=vP, in_=v_scr[hp]), reads=["v_scr"], writes=["vP"], dma=True)
            kmin = max(0, (J - 16 * qg) // 4) if J >= 16 * qg else 0
            c0 = kmin * 128
            q0 = qg * 512 + c0
            q1 = qg * 512 + 512
            sbk = SB[n % 3]
            pb = n % 4
            win = J >= 16 * qg
            A("pe", lambda e: e.matmul(pf[sbk][:, c0:512], lhsT=kTp[rws, J * 128:(J + 1) * 128], rhs=qT_st[rws, hp, q0:q1],
                                       start=True, stop=False),
              reads=["kTp", "qT_st"], writes=[Rp[sbk]])
            A("pe", lambda e: e.matmul(pf[sbk][:, c0:512], lhsT=ohTb[0:8, h, :], rhs=a_rows[0:8, q0:q1], start=False, stop=(not win)),
              reads=["ohTb", "a_rows"], writes=[Rp[sbk]])
            if win:
                w = (J - 16 * qg) % 4
                A("pe", lambda e: e.matmul(pf[sbk][:, c0:c0 + 128], lhsT=identb[:, :], rhs=maskb[:, w, :], start=False, stop=True),
                  reads=["identb", "maskb"], writes=[Rp[sbk]])
            A("act", lambda e: e.activation(out=pTb[pb][:, c0:512], in_=pf[sbk][:, c0:512], func=AF.Exp, scale=SCALE,
                                            bias=negc[:, J, h:h + 1]),
              reads=[Rp[sbk], "negc"], writes=["pTb%d" % pb])

        def emit_pv(n):
            h, qg, J, nJ = its[n]
            half = h % 2
            kmin = max(0, (J - 16 * qg) // 4) if J >= 16 * qg else 0
            c0 = kmin * 128
            pb = n % 4
            A("pe", lambda e: e.matmul(pf[2 + qg][0:65, c0:512], lhsT=vP[:, J, half * 65:(half + 1) * 65], rhs=pTb[pb][:, c0:512],
                                       start=(J == 0), stop=(J == nJ - 1)),
              reads=["vP", "pTb%d" % pb], writes=[Rp[2 + qg]])
            if J == nJ - 1:
                A("act", lambda e: e.activation(out=o_sb[0:65, :], in_=pf[2 + qg][0:65, :], func=AF.Copy),
                  reads=[Rp[2 + qg]], writes=["o_sb"])
                for k in range(4):
                    A("pe", lambda e, k=k: e.transpose(out=pf[6][:, k * 65:k * 65 + 65], in_=o_sb[0:65, k * 128:(k + 1) * 128],
                                                       identity=identf[0:65, 0:65]),
                      reads=["o_sb", "cst"], writes=[Rp[6]])
                A("dve", lambda e: e.reciprocal(out=rcp_t[:, 0:4], in_=pf[6][:, 0:260].rearrange("p (k e) -> p k e", k=4)[:, :, 64]),
                  reads=[Rp[6]], writes=["rcp_t"])
                for k in range(4):
                    A("dve", lambda e, k=k: e.scalar_tensor_tensor(
                        out=ga_st[:, 4 * qg + k, h * 64:(h + 1) * 64], in0=pf[6][:, k * 65:k * 65 + 64], scalar=rcp_t[:, k:k + 1],
                        in1=ga_st[:, 4 * qg + k, h * 64:(h + 1) * 64], op0=ALU.mult, op1=ALU.mult),
                      reads=[Rp[6], "rcp_t", "ga_st"], writes=["ga_st"])

        pend = []
        for nn in range(len(its)):
            h_, qg_, J_, _ = its[nn]
            if h_ % 2 == 0 and qg_ == 0 and J_ == 0:
                for m in pend:
                    emit_pv(m)
                pend = []
            emit_qk(nn)
            pend.append(nn)
            if len(pend) > LA:
                emit_pv(pend.pop(0))
        for m in pend:
            emit_pv(m)

        for i in range(NOWN):
            rows = slice(i * 128, (i + 1) * 128)
            for c in range(8):
                src = ga_st[:, i, c * 128:(c + 1) * 128] if c < 4 else m_st[:, i, (c - 4) * 128:(c - 3) * 128]
                A("pe", lambda e, c=c, src=src: e.transpose(out=pbf(6)[:, c * 128:(c + 1) * 128], in_=src, identity=identb[:, :]),
                  reads=["ga_st", "m_st", "identb"], writes=[Rp[6]])
            A("act", lambda e: e.activation(out=mT[:, :, :], in_=pbf(6).rearrange("p (c t) -> p c t", c=8), func=AF.Copy),
              reads=[Rp[6]], writes=["mT"])
            A("sp", lambda e, rows=rows: e.dma_start(out=xt[:, :], in_=x_own[rows, :]), writes=["xt"], dma=True)
            for half in range(2):
                for c in range(8):
                    A("pe", lambda e, c=c, half=half: e.matmul(pf[half][:, :], lhsT=mT[:, c, :], rhs=wout[:, c, half * 512:(half + 1) * 512],
                                                               start=(c == 0), stop=(c == 7)),
                      reads=["mT", "wout"], writes=[Rp[half]])
                A("dve", lambda e, half=half: e.tensor_tensor(out=yt[:, half * 512:(half + 1) * 512], in0=pf[half][:, :],
                                                              in1=xt[:, half * 512:(half + 1) * 512], op=ALU.add),
                  reads=[Rp[half], "xt"], writes=["yt"])
            A("sp", lambda e, rows=rows: e.dma_start(out=y_own[rows, :], in_=yt[:, :]), reads=["yt"], dma=True)

        P.emit(nc)
    return nc


def _consts(j):
    c = np.zeros((128, K_END), np.float32)
    p = np.arange(128)
    c[:, K_ID:K_ID + 128] = np.eye(128)
    c[:, K_U:K_U + 128] = (p[:, None] <= p[None, :])
    c[:, K_ONE:K_ONE + 128] = 1.0
    c[:, K_L:K_L + 128] = (p[:, None] > p[None, :])
    for w in range(4):
        if w < j:
            m = np.ones((128, 128))
        elif w == j:
            m = (p[:, None] <= p[None, :])
        else:
            m = np.zeros((128, 128))
        c[:, K_MW + w * 128:K_MW + (w + 1) * 128] = m
    c[:, K_IOTA] = p % 16
    c[:, K_OH4 + j] = 1.0
    for b in range(16):
        c[0:8, K_EB + b * 16 + b] = 1.0
        c[:, K_CS + b * 16 + b] = 1.0
    for h in range(8):
        c[h, K_BM + h * 64:K_BM + (h + 1) * 64] = 1.0
        c[h, K_ID8 + h] = 1.0
        c[h, K_OHT + h * 128:K_OHT + (h + 1) * 128] = 1.0
    return c


def kernel(x_prompt, x_sample, cache_k, cache_v, cache_logf, page_table, g_norm, w_in,
           b_f, g_q, g_k, g_v, w_s, b_s, w_out):
    f = lambda a: np.ascontiguousarray(np.asarray(a))
    x_prompt = f(x_prompt); x_sample = f(x_sample)
    ckr = f(cache_k).reshape(2560 * 128, 512)
    cvr = f(cache_v).reshape(2560 * 128, 512)
    clr = f(cache_logf).reshape(2560 * 128, 8)
    ptab = f(page_table).astype(np.int32)
    nc = build_nc()
    in_maps = []
    for c in range(8):
        b, j = c // 4, c % 4
        xa = x_prompt[b]
        xo = np.ascontiguousarray(xa.reshape(16, 4, 128, D)[:, j].reshape(NOWN * 128, D))
        in_maps.append(dict(
            x_all=xa, x_own=xo, x_s=np.ascontiguousarray(x_sample[16 * c:16 * c + 16, 0, :]),
            ck=ckr, cv=cvr, cl=clr,
            pt=np.ascontiguousarray(ptab[16 * c:16 * c + 16].reshape(16, 2, 8)[:, :, np.arange(128) // 16].transpose(2, 0, 1).reshape(128, 32)),
            g_norm=f(g_norm)[0], w_in=f(w_in)[0], b_f=f(b_f)[0], g_q=f(g_q)[0], g_k=f(g_k)[0], g_v=f(g_v)[0],
            w_s=f(w_s)[0], b_s=f(b_s)[0], w_out=f(w_out)[0], consts=_consts(j)))
    res = run_bass_kernel_spmd(nc, in_maps, core_ids=list(range(8)))
    R = res.results
    yp = np.zeros((2, S, D), np.float32)
    kp = np.zeros((1, 2, S, 8, 64), np.float32)
    vp = np.zeros((1, 2, S, 8, 64), np.float32)
    lp = np.zeros((1, 2, S, 8), np.float32)
    ys = np.zeros((128, 1, D), np.float32)
    ks = np.zeros((1, 128, 1, 8, 64), np.float32)
    vs = np.zeros((1, 128, 1, 8, 64), np.float32)
    ls = np.zeros((1, 128, 1, 8), np.float32)
    gs = np.zeros((1, 128, 1, 4, 128), np.float32)
    for c in range(8):
        b, j = c // 4, c % 4
        r = R[c]
        yp[b].reshape(16, 4, 128, D)[:, j] = r["y_own"].reshape(16, 128, D)
        kp[0, b].reshape(16, 4, 128, 512)[:, j] = r["k_own"].reshape(16, 128, 512)
        vp[0, b].reshape(16, 4, 128, 512)[:, j] = r["v_own"].reshape(16, 128, 512)
        lp[0, b].reshape(16, 4, 128, 8)[:, j] = r["l_own"].reshape(16, 128, 8)
        sl = slice(16 * c, 16 * c + 16)
        ys[sl, 0] = r["y_s"]
        ks[0, sl, 0] = r["k_s"].reshape(16, 8, 64)
        vs[0, sl, 0] = r["v_s"].reshape(16, 8, 64)
        ls[0, sl, 0] = r["l_s"]
        gs[0, sl, 0] = r["gv_s"].reshape(16, 4, 128)
    return (yp, ys, kp, vp, lp, ks, vs, ls, gs)
```

```python
import contextlib
import numpy as np
import concourse.bass as bass
import concourse.mybir as mybir
from concourse.bass_utils import run_bass_kernel_spmd

F32 = mybir.dt.float32
BF16 = mybir.dt.bfloat16
I32 = mybir.dt.int32
ALU = mybir.AluOpType
AF = mybir.ActivationFunctionType
AX = mybir.AxisListType

D = 1024
DIN = 3592
S = 8192
NT = 64
NOWN = 16
NS = 16
NPG = 16
EPS = 1e-6
SCALE = 0.125
C_Q, C_K, C_V, C_F, C_ZA, C_U, C_GV, C_ZC = 0, 512, 1024, 1536, 1544, 2056, 2568, 3080

K_ID, K_U, K_ONE, K_L, K_MW, K_IOTA, K_OH4, K_EB, K_BM, K_ID8, K_OHT, K_CS, K_END = (
    0, 128, 256, 384, 512, 1024, 1025, 1029, 1285, 1797, 1805, 2829, 3085)

ENGS = ("pe", "act", "dve", "pool", "sp")


class Res:
    __slots__ = ("w", "r")

    def __init__(self):
        self.w = None
        self.r = []


class Op:
    __slots__ = ("eng", "fn", "deps", "signaled", "count", "dma", "sem", "prev_on_sem")

    def __init__(self, eng, fn, dma):
        self.eng = eng
        self.fn = fn
        self.dma = dma
        self.deps = []
        self.signaled = dma
        self.count = None
        self.sem = None
        self.prev_on_sem = None


class Prog:
    def __init__(self, n_dma_sems=16):
        self.q = {e: [] for e in ENGS}
        self.n_dma_sems = n_dma_sems
        self.all_dma = []

    def add(self, eng, fn, reads=(), writes=(), dma=False):
        op = Op(eng, fn, dma)
        deps = {}
        for r in reads:
            if r.w is not None:
                deps[id(r.w)] = (r.w, "raw")
        for w in writes:
            if w.w is not None and id(w.w) not in deps:
                deps[id(w.w)] = (w.w, "waw")
            for rr in w.r:
                if id(rr) not in deps:
                    deps[id(rr)] = (rr, "war")
        for d, kind in deps.values():
            if d is op:
                continue
            if not d.dma and not dma and d.eng == eng:
                if eng == "pe":
                    continue
                if kind != "raw":
                    continue
            d.signaled = True
            op.deps.append(d)
        for r in reads:
            r.r.append(op)
        for w in writes:
            w.w = op
            w.r = []
        self.q[eng].append(op)
        if dma:
            self.all_dma.append(op)
        return op

    def emit(self, nc):
        stack = contextlib.ExitStack()
        with stack:
            esem = {e: stack.enter_context(nc.semaphore("s_" + e)) for e in ENGS}
            dsem = {e: [stack.enter_context(nc.semaphore("d_%s%d" % (e, i))) for i in range(self.n_dma_sems)]
                    for e in ("sp", "act", "pool")}
            for e in ENGS:
                c = 0
                dcount = [0] * self.n_dma_sems
                dlast = [None] * self.n_dma_sems
                k = 0
                for op in self.q[e]:
                    if op.dma:
                        s = k % self.n_dma_sems
                        k += 1
                        dcount[s] += 16
                        op.sem = dsem[e][s]
                        op.count = dcount[s]
                        op.prev_on_sem = dlast[s]
                        dlast[s] = op
                    elif op.signaled:
                        c += 1
                        op.sem = esem[e]
                        op.count = c
            block = stack.enter_context(nc.Block())

            def run(e):
                def body(eng):
                    waited = {}

                    def wait_for(d):
                        key = id(d.sem)
                        if waited.get(key, 0) >= d.count:
                            return
                        eng.wait_ge(d.sem, d.count)
                        waited[key] = d.count

                    for op in self.q[e]:
                        for d in op.deps:
                            wait_for(d)
                        if op.dma and op.prev_on_sem is not None:
                            wait_for(op.prev_on_sem)
                        ins = op.fn(eng)
                        if op.dma:
                            ins.then_inc(op.sem, 16)
                        elif op.signaled:
                            ins.then_inc(op.sem, 1)
                    if e == "sp":
                        last = {}
                        for d in self.all_dma:
                            last[id(d.sem)] = d
                        for d in last.values():
                            wait_for(d)
                return body

            block.tensor(run("pe"))
            block.scalar(run("act"))
            block.vector(run("dve"))
            block.gpsimd(run("pool"))
            block.sync(run("sp"))


def build_nc():
    nc = bass.Bass("TRN2", target_bir_lowering=False)

    def din(name, shape, dt=F32):
        return nc.dram_tensor(name, shape, dt, kind="ExternalInput").ap()

    def dout(name, shape, dt=F32):
        return nc.dram_tensor(name, shape, dt, kind="ExternalOutput").ap()

    x_all = din("x_all", [S, D])
    x_own = din("x_own", [NOWN * 128, D])
    x_s = din("x_s", [NS, D])
    ck = din("ck", [2560 * 128, 512])
    cv = din("cv", [2560 * 128, 512])
    cl = din("cl", [2560 * 128, 8])
    pt = din("pt", [128, 2 * NS], I32)
    g_norm = din("g_norm", [D])
    w_in = din("w_in", [D, DIN])
    b_f = din("b_f", [8])
    g_q = din("g_q", [64])
    g_k = din("g_k", [64])
    g_v = din("g_v", [512])
    w_s = din("w_s", [4, 128, 128])
    b_s = din("b_s", [4, 128])
    w_out = din("w_out", [D, D])
    consts_d = din("consts", [128, K_END])

    y_own = dout("y_own", [NOWN * 128, D])
    k_own = dout("k_own", [NOWN * 128, 512])
    v_own = dout("v_own", [NOWN * 128, 512])
    l_own = dout("l_own", [NOWN * 128, 8])
    y_s = dout("y_s", [NS, D])
    k_s = dout("k_s", [NS, 512])
    v_s = dout("v_s", [NS, 512])
    l_s = dout("l_s", [NS, 8])
    gv_s = dout("gv_s", [NS, 512])

    kT_scr = nc.dram_tensor("kT_scr", [4, 128, S], BF16, kind="Internal").ap()
    v_scr = nc.dram_tensor("v_scr", [4, 128, NT, 130], BF16, kind="Internal").ap()
    q_scr = nc.dram_tensor("q_scr", [NS, 512], F32, kind="Internal").ap()

    P = Prog()
    st = contextlib.ExitStack()
    with st:
        def sb(name, shape, dt):
            return st.enter_context(nc.sbuf_tensor(name, shape, dt))

        cst = sb("cst", [128, K_END], F32)
        identf = cst[:, K_ID:K_ID + 128]
        utri = cst[:, K_U:K_U + 128]
        ones = cst[:, K_ONE:K_ONE + 128]
        ltri = cst[:, K_L:K_L + 128]
        iota = cst[:, K_IOTA:K_IOTA + 1]
        oh4 = cst[:, K_OH4:K_OH4 + 4]
        identb = sb("identb", [128, 128], BF16)
        maskb = sb("maskb", [128, 4, 128], BF16)
        ohTb = sb("ohTb", [8, 8, 128], BF16)
        wout = sb("wout", [128, 8, D], BF16)
        g_bc = sb("g_bc", [128, D], F32)
        gv_bc = sb("gv_bc", [128, 512], F32)
        gq_bc = sb("gq_bc", [128, 64], F32)
        gk_bc = sb("gk_bc", [128, 64], F32)
        bf_bc = sb("bf_bc", [128, 8], F32)
        bs_t = sb("bs_t", [128, 4], F32)
        ws00 = sb("ws00", [128, 4], F32)
        bs0 = sb("bs0", [128, 4], F32)
        wsT = sb("wsT", [128, 4, 128], BF16)
        negc = sb("negc", [128, NT, 8], F32)
        carry = sb("carry", [128, NT, 8], F32)
        runc = sb("runc", [128, 8], F32)
        c_own = sb("c_own", [128, NOWN, 8], F32)
        xt = sb("xt", [128, D], F32)
        xn = sb("xn", [128, D], BF16)
        junk = xn
        xT = sb("xT", [128, 8, 128], BF16)
        xT2 = sb("xT2", [128, 8, 128], BF16)
        ss = sb("ss", [128, 1], F32)
        rr = sb("rr", [128, 1], F32)
        sq = sb("sq", [128, 512], F32)
        ssq = sb("ssq", [128, 8], F32)
        rk = sb("rk", [128, 8], F32)
        kn = sb("kn", [128, 512], F32)
        kf = sb("kf", [128, 512], F32)
        knb = sb("knb", [128, 512], BF16)
        vf = sb("vf", [128, 512], F32)
        zf = sb("zf", [128, 8], F32)
        lf = sb("lf", [128, 8], F32)
        lfs = sb("lfs", [128, 8], F32)
        arT = sb("arT", [128, 2080], F32)
        kT_stage = arT[:, 0:1024].bitcast(BF16).rearrange("p (h t) -> p h t", h=4)
        v_stage = arT[:, 1024:2064].bitcast(BF16).rearrange("p (a h d) -> p a h d", a=4, h=8)
        et = arT[:, 0:512]
        ub = arT[:, 512:1024]
        gvn = sb("gvn", [128, 512], BF16)
        gvnf = sb("gvnf", [128, 512], F32)
        gc = arT[:, 1024:1536]
        tmpm = arT[:, 1536:2048]
        arW = sb("arW", [128, 8 * DIN], BF16)
        arQ = sb("arQ", [128, 25600], BF16)

        Wb = arW[:, :].rearrange("p (c n) -> p c n", c=8)
        kTp = arW[:, 0:8192]
        vP = arW[:, 8192:8192 + 64 * 130].rearrange("p (t e) -> p t e", t=64)
        o = 8192 + 64 * 130
        pTb = [arW[:, o + i * 512:o + (i + 1) * 512] for i in range(4)]
        o += 4 * 512
        o_sb = arW[:, o:o + 1024].bitcast(F32)
        o += 1024
        a_rows = arW[:, o:o + 2048]
        o += 2048
        rcp_t = arW[:, o:o + 8].bitcast(F32)
        o += 8
        mT = arW[:, o:o + 1024].rearrange("p (c n) -> p c n", c=8)
        o += 1024
        yt = arW[:, o:o + 2048].bitcast(F32)
        o += 2048
        qTh = arW[:, o:o + 2048]
        o += 2048
        assert o <= 8 * DIN
        qT_st = arQ[:, 0:8192].rearrange("p (h n) -> p h n", h=4)
        ga_st = arQ[:, 8192:16384].rearrange("p (i n) -> p i n", i=NOWN)
        m_st = arQ[:, 16384:24576].rearrange("p (i n) -> p i n", i=NOWN)
        Kb = arQ[:, 0:8192].bitcast(F32).rearrange("p (g n) -> p g n", g=8)
        Vb = arQ[:, 8192:16384].bitcast(F32).rearrange("p (g n) -> p g n", g=8)
        Wstage = arQ[:, 0:2 * DIN].bitcast(F32)
        Lh = arQ[:, 16384:20992].bitcast(F32).rearrange("p (c t h) -> p c t h", c=2 * NS, t=9)
        Vbb = arQ[:, 20992:25088].rearrange("p (g n) -> p g n", g=8)
        ptb = arQ[:, 25088:25152].bitcast(I32)
        idx = arQ[:, 25152:25216].bitcast(I32)
        tot_t = sb("tot_t", [128, 256], F32)
        tsum = sb("tsum", [128, 256], F32)
        qb = sb("qb", [128, 512], F32)
        s_t = sb("s_t", [128, 128], F32)
        p_t = sb("p_t", [128, 128], BF16)
        psm = sb("psm", [128, 8], F32)
        mo = sb("mo", [8, 512], F32)
        md = sb("md", [8, 8], F32)
        qn_s = sb("qn_s", [NS, 512], F32)
        kn_s = sb("kn_s", [NS, 512], F32)
        vf_s = sb("vf_s", [NS, 512], F32)
        ga_s = sb("ga_s", [NS, 512], BF16)
        m_s = sb("m_s", [NS, 512], BF16)
        sm1 = sb("sm1", [NS, 512], F32)
        sm2 = sb("sm2", [NS, 8], F32)
        sm3 = sb("sm3", [NS, 8], F32)
        mrg_s = sb("mrg_s", [NS, D], BF16)
        mT_s = sb("mT_s", [128, 8, NS], BF16)

        pf = [st.enter_context(nc.psum_tensor("pf%d" % i, [128, 512], F32)) for i in range(8)]
        Rp = [Res() for _ in range(8)]

        def pbf(i):
            return pf[i][:, :].bitcast(BF16)

        R = {}

        def res(name):
            if name not in R:
                R[name] = Res()
            return R[name]

        cap = [None]

        def A(eng, fn, reads=(), writes=(), dma=False):
            if cap[0] is not None:
                cap[0].append((eng, fn, reads, writes, dma))
                return None
            rl = [res(r) if isinstance(r, str) else r for r in reads]
            wl = [res(w) if isinstance(w, str) else w for w in writes]
            return P.add(eng, fn, rl, wl, dma)

        def record(f, *args):
            cap[0] = []
            f(*args)
            lst = cap[0]
            cap[0] = None
            return lst

        def interleave(*lists):
            lists = [l for l in lists if l]
            pos = [0] * len(lists)
            left = sum(len(l) for l in lists)
            while left:
                for i, l in enumerate(lists):
                    if pos[i] < len(l):
                        A(*l[pos[i]])
                        pos[i] += 1
                        left -= 1

        A("sp", lambda e: e.dma_start(out=cst[:, :], in_=consts_d), writes=["cst"], dma=True)
        A("sp", lambda e: e.dma_start(out=g_bc[:, :], in_=g_norm.partition_broadcast(128)), writes=["g_bc"], dma=True)
        A("sp", lambda e: e.dma_start(out=gv_bc[:, :], in_=g_v.partition_broadcast(128)), writes=["gv_bc"], dma=True)
        A("sp", lambda e: e.dma_start(out=gq_bc[:, :], in_=g_q.partition_broadcast(128)), writes=["gq_bc"], dma=True)
        A("sp", lambda e: e.dma_start(out=gk_bc[:, :], in_=g_k.partition_broadcast(128)), writes=["gk_bc"], dma=True)
        A("sp", lambda e: e.dma_start(out=bf_bc[:, :], in_=b_f.partition_broadcast(128)), writes=["bf_bc"], dma=True)
        for g in range(4):
            A("sp", lambda e, g=g: e.dma_start(out=bs_t[:, g:g + 1], in_=b_s[g].rearrange("(t o) -> t o", o=1)),
              writes=["bs_t"], dma=True)
            A("sp", lambda e, g=g: e.dma_start(out=ws00[:, g:g + 1], in_=w_s[g, 0, 0:1].partition_broadcast(128)),
              writes=["ws00"], dma=True)
            A("sp", lambda e, g=g: e.dma_start(out=bs0[:, g:g + 1], in_=b_s[g, 0:1].partition_broadcast(128)),
              writes=["bs0"], dma=True)
        w_in_v = w_in.rearrange("(c p) n -> p c n", p=128)
        for c in range(8):
            A("sp", lambda e, c=c: e.dma_start(out=Wstage, in_=w_in[c * 128:(c + 1) * 128, :]), writes=["Kb"], dma=True)
            A("act", lambda e, c=c: e.activation(out=Wb[:, c, :], in_=Wstage, func=AF.Copy), reads=["Kb"], writes=["Wb"])
        A("pool", lambda e: e.dma_start(out=wout[:, :, :], in_=w_out.rearrange("(c p) n -> p c n", p=128)),
          writes=["wout"], dma=True)
        A("dve", lambda e: e.tensor_copy(out=identb[:, :], in_=identf), reads=["cst"], writes=["identb"])
        A("dve", lambda e: e.tensor_scalar(out=maskb[:, :, :].rearrange("p w t -> p (w t)"), in0=cst[:, K_MW:K_MW + 512],
                                           scalar1=1.0e4, scalar2=-1.0e4, op0=ALU.mult, op1=ALU.add),
          reads=["cst"], writes=["maskb"])
        A("dve", lambda e: e.tensor_copy(out=ohTb[:, :, :].rearrange("p h s -> p (h s)"), in_=cst[0:8, K_OHT:K_OHT + 1024]),
          reads=["cst"], writes=["ohTb"])
        A("dve", lambda e: e.memset(runc[:, :], 0.0), writes=["runc"])
        ws_t = tmpm.rearrange("p (g s) -> p g s", g=4)
        A("sp", lambda e: e.dma_start(out=ws_t, in_=w_s.rearrange("g t s -> t g s")), writes=["tmpm"], dma=True)
        for g in range(4):
            A("pe", lambda e, g=g: e.transpose(out=pf[0][:, g * 128:(g + 1) * 128], in_=ws_t[:, g, :], identity=identf),
              reads=["tmpm", "cst"], writes=[Rp[0]])
        for g in range(4):
            A("dve", lambda e, g=g: e.tensor_tensor(out=wsT[:, g, :], in0=pf[0][:, g * 128:(g + 1) * 128], in1=utri, op=ALU.mult),
              reads=[Rp[0], "cst"], writes=["wsT"])

        def rsqrt_act(dst, src, n, scale):
            A("act", lambda e: e.activation(out=dst, in_=src, func=AF.Ln, scale=scale, bias=EPS), reads=["tmp_r_in"], writes=["tmp_r"])
            A("act", lambda e: e.activation(out=dst, in_=dst, func=AF.Exp, scale=-0.5), reads=["tmp_r"], writes=["tmp_r"])

        def norm_T(src, n, xTb=None, xTr="xT"):
            xTb = xT if xTb is None else xTb
            A("sp", lambda e: e.dma_start(out=xt[:n, :], in_=src), writes=["xt"], dma=True)
            A("dve", lambda e: e.memset(ss[:n, :], 0.0), writes=["ss"])
            A("act", lambda e: e.activation(out=junk[:n, :], in_=xt[:n, :], func=AF.Square, accum_out=ss[:n, :]),
              reads=["xt", "ss"], writes=["xn", "ss"])
            A("act", lambda e: e.activation(out=rr[:n, :], in_=ss[:n, :], func=AF.Ln, scale=1.0 / D, bias=EPS),
              reads=["ss"], writes=["rr"])
            A("act", lambda e: e.activation(out=rr[:n, :], in_=rr[:n, :], func=AF.Exp, scale=-0.5),
              reads=["rr"], writes=["rr"])
            A("dve", lambda e: e.scalar_tensor_tensor(out=xn[:n, :], in0=xt[:n, :], scalar=rr[:n, 0:1], in1=g_bc[:n, :],
                                                      op0=ALU.mult, op1=ALU.mult),
              reads=["xt", "rr", "g_bc"], writes=["xn"])
            for c in range(8):
                A("pe", lambda e, c=c: e.transpose(out=pbf(0)[:, c * 128:c * 128 + n], in_=xn[:n, c * 128:(c + 1) * 128],
                                                   identity=identb[:n, :n]),
                  reads=["xn", "identb"], writes=[Rp[0]])
            A("act", lambda e: e.activation(out=xTb[:, :, 0:n], in_=pbf(0).rearrange("p (c t) -> p c t", c=8)[:, :, 0:n], func=AF.Copy),
              reads=[Rp[0]], writes=[xTr])

        def proj(bank, col0, width, n, xTb=None, xTr="xT"):
            xTb = xT if xTb is None else xTb
            for c in range(8):
                A("pe", lambda e, c=c: e.matmul(pf[bank][:n, 0:width], lhsT=xTb[:, c, 0:n], rhs=Wb[:, c, col0:col0 + width],
                                                start=(c == 0), stop=(c == 7)),
                  reads=[xTr, "Wb"], writes=[Rp[bank]])

        def headnorm(bank, n, nh, hd, gb, out_ap, out_res):
            A("act", lambda e: e.activation(out=sq[:n, :], in_=pf[bank][:n, :], func=AF.Square), reads=[Rp[bank]], writes=["sq"])
            A("dve", lambda e: e.tensor_reduce(out=ssq[:n, 0:nh], in_=sq[:n, :].rearrange("p (h d) -> p h d", h=nh),
                                               axis=AX.X, op=ALU.add), reads=["sq"], writes=["ssq"])
            A("act", lambda e: e.activation(out=rk[:n, 0:nh], in_=ssq[:n, 0:nh], func=AF.Ln, scale=1.0 / hd, bias=EPS),
              reads=["ssq"], writes=["rk"])
            A("act", lambda e: e.activation(out=rk[:n, 0:nh], in_=rk[:n, 0:nh], func=AF.Exp, scale=-0.5), reads=["rk"], writes=["rk"])
            A("dve", lambda e: e.tensor_tensor(out=kn[:n, :].rearrange("p (h d) -> p h d", h=nh),
                                               in0=pf[bank][:n, :].rearrange("p (h d) -> p h d", h=nh),
                                               in1=rk[:n, 0:nh].unsqueeze(2).to_broadcast([n, nh, hd]), op=ALU.mult),
              reads=[Rp[bank], "rk"], writes=["kn"])
            if hd == 64:
                in1 = gb[:n, :].unsqueeze(1).to_broadcast([n, nh, hd])
                A("dve", lambda e: e.tensor_tensor(out=out_ap.rearrange("p (h d) -> p h d", h=nh),
                                                   in0=kn[:n, :].rearrange("p (h d) -> p h d", h=nh), in1=in1, op=ALU.mult),
                  reads=["kn"], writes=[out_res])
            else:
                A("dve", lambda e: e.tensor_tensor(out=out_ap, in0=kn[:n, :], in1=gb[:n, :], op=ALU.mult),
                  reads=["kn"], writes=[out_res])

        def logf_of(bank, n):
            A("dve", lambda e: e.tensor_tensor(out=zf[:n, :], in0=pf[bank][:n, 0:8], in1=bf_bc[:n, :], op=ALU.add),
              reads=[Rp[bank], "bf_bc"], writes=["zf"])
            A("act", lambda e: e.activation(out=zf[:n, :], in_=zf[:n, :], func=AF.Exp, scale=-1.0), reads=["zf"], writes=["zf"])
            A("act", lambda e: e.activation(out=zf[:n, :], in_=zf[:n, :], func=AF.Ln, bias=1.0), reads=["zf"], writes=["zf"])
            A("dve", lambda e: e.tensor_scalar(out=lf[:n, :], in0=zf[:n, :], scalar1=-1.0, scalar2=None, op0=ALU.mult),
              reads=["zf"], writes=["lf"])

        def silu_from(bank, n, out_ap, out_res):
            A("act", lambda e: e.activation(out=et[:n, :], in_=pf[bank][:n, :], func=AF.Exp, scale=-1.0), reads=[Rp[bank]], writes=["et"])
            A("dve", lambda e: e.tensor_scalar(out=et[:n, :], in0=et[:n, :], scalar1=1.0, scalar2=None, op0=ALU.add),
              reads=["et"], writes=["et"])
            A("dve", lambda e: e.reciprocal(out=et[:n, :], in_=et[:n, :]), reads=["et"], writes=["et"])
            A("dve", lambda e: e.tensor_tensor(out=out_ap, in0=pf[bank][:n, :], in1=et[:n, :], op=ALU.mult),
              reads=[Rp[bank], "et"], writes=[out_res])

        def rest_proj(n, ga_out, ga_res, m_out, m_res, sample):
            proj(3, C_ZA, 512, n)
            silu_from(3, n, ga_out, ga_res)
            proj(4, C_U, 512, n)
            A("act", lambda e: e.activation(out=ub[:n, :], in_=pf[4][:n, :], func=AF.Copy), reads=[Rp[4]], writes=["ub"])
            proj(3, C_GV, 512, n)
            if sample:
                headnorm(3, n, 4, 128, gv_bc, gvnf[:n, :], "gvnf")
                A("sp", lambda e: e.dma_start(out=gv_s, in_=gvnf[:n, :]), reads=["gvnf"], dma=True)
            else:
                headnorm(3, n, 4, 128, gv_bc, gvn[:n, :], "gvn")
                for g in range(4):
                    A("pe", lambda e, g=g: e.matmul(pf[5][:n, g * 128:(g + 1) * 128], lhsT=wsT[:, g, :],
                                                    rhs=gvn[:, g * 128:(g + 1) * 128], start=True, stop=True),
                      reads=["wsT", "gvn"], writes=[Rp[5]])
            proj(4, C_ZC, 512, n)
            silu_from(4, n, gc[:n, :], "gc")
            for g in range(4):
                sl = slice(g * 128, (g + 1) * 128)
                if sample:
                    A("dve", lambda e, g=g, sl=sl: e.tensor_scalar(out=tmpm[:n, sl], in0=gvnf[:n, sl], scalar1=ws00[:n, g:g + 1],
                                                                   scalar2=bs0[:n, g:g + 1], op0=ALU.mult, op1=ALU.add),
                      reads=["gvnf", "ws00", "bs0"], writes=["tmpm"])
                    A("dve", lambda e, sl=sl: e.tensor_tensor(out=tmpm[:n, sl], in0=tmpm[:n, sl], in1=ub[:n, sl], op=ALU.mult),
                      reads=["tmpm", "ub"], writes=["tmpm"])
                else:
                    A("dve", lambda e, g=g, sl=sl: e.scalar_tensor_tensor(out=tmpm[:n, sl], in0=pf[5][:n, sl], scalar=bs_t[:n, g:g + 1],
                                                                          in1=ub[:n, sl], op0=ALU.add, op1=ALU.mult),
                      reads=[Rp[5], "bs_t", "ub"], writes=["tmpm"])
            A("dve", lambda e: e.tensor_tensor(out=m_out, in0=tmpm[:n, :], in1=gc[:n, :], op=ALU.mult),
              reads=["tmpm", "gc"], writes=[m_res])

        n = NS
        norm_T(x_s, n)
        proj(1, C_Q, 512, n)
        headnorm(1, n, 8, 64, gq_bc, qn_s[:n, :], "qn_s")
        A("sp", lambda e: e.dma_start(out=q_scr, in_=qn_s[:n, :]), reads=["qn_s"], writes=["q_scr"], dma=True)
        proj(1, C_K, 512, n)
        headnorm(1, n, 8, 64, gk_bc, kn_s[:n, :], "kn_s")
        A("sp", lambda e: e.dma_start(out=k_s, in_=kn_s[:n, :]), reads=["kn_s"], dma=True)
        proj(2, C_V, 512, n)
        A("act", lambda e: e.activation(out=vf_s[:n, :], in_=pf[2][:n, :], func=AF.Copy), reads=[Rp[2]], writes=["vf_s"])
        A("sp", lambda e: e.dma_start(out=v_s, in_=vf_s[:n, :]), reads=["vf_s"], dma=True)
        proj(2, C_F, 8, n)
        logf_of(2, n)
        A("dve", lambda e: e.tensor_copy(out=lfs[:n, :], in_=lf[:n, :]), reads=["lf"], writes=["lfs"])
        A("sp", lambda e: e.dma_start(out=l_s, in_=lfs[:n, :]), reads=["lfs"], dma=True)
        rest_proj(n, ga_s[:n, :], "ga_s", m_s[:n, :], "m_s", True)

        ck2 = ck.rearrange("(r t) n -> r (t n)", t=8)
        cv2 = cv.rearrange("(r t) n -> r (t n)", t=8)
        cl2 = cl.rearrange("(r t) n -> r (t n)", t=8)
        A("sp", lambda e: e.dma_start(out=ptb, in_=pt), writes=["ptb"], dma=True)
        A("dve", lambda e: e.tensor_scalar(out=idx, in0=ptb, scalar1=16.0, scalar2=iota, op0=ALU.mult, op1=ALU.add),
          reads=["ptb", "cst"], writes=["idx"])
        A("dve", lambda e: e.memset(arQ[:, 16384:20992].bitcast(F32), 0.0), writes=["Lall"])
        for col in range(2 * NS):
            A("pool", lambda e, col=col: e.indirect_dma_start(
                out=arQ[:, 16384:20992].bitcast(F32)[:, col * 72:col * 72 + 64], out_offset=None, in_=cl2,
                in_offset=bass.IndirectOffsetOnAxis(ap=idx[:, col:col + 1], axis=0)),
              reads=["idx"], writes=["Lall"], dma=True)

        def compute_E():
            A("dve", lambda e: e.tensor_reduce(out=tot_t[:, :].rearrange("p (c h) -> p c h", h=8),
                                               in_=Lh[:, :, 0:8, :].rearrange("p c t h -> p c h t"), axis=AX.X, op=ALU.add),
              reads=["Lall"], writes=["tot_t"])
            A("pe", lambda e: e.matmul(pf[3][:, 0:256], lhsT=ltri, rhs=tot_t[:, :], start=True, stop=True),
              reads=["tot_t", "cst"], writes=[Rp[3]])
            A("pe", lambda e: e.matmul(pf[4][:, 0:128], lhsT=ones,
                                       rhs=tot_t[:, :].rearrange("p (b f h) -> p b f h", f=2, h=8)[:, :, 1, :], start=True, stop=True),
              reads=["tot_t", "cst"], writes=[Rp[4]])
            A("dve", lambda e: e.tensor_copy(out=tsum[:, :], in_=pf[3][:, 0:256]), reads=[Rp[3]], writes=["tsum"])
            A("dve", lambda e: e.tensor_tensor(out=tsum[:, :].rearrange("p (b f h) -> p b f h", f=2, h=8)[:, :, 0, :],
                                               in0=tsum[:, :].rearrange("p (b f h) -> p b f h", f=2, h=8)[:, :, 0, :],
                                               in1=pf[4][:, 0:128].rearrange("p (b h) -> p b h", h=8), op=ALU.add),
              reads=[Rp[4], "tsum"], writes=["tsum"])
            for t in range(6, -1, -1):
                A("dve", lambda e, t=t: e.tensor_tensor(out=Lh[:, :, t, :], in0=Lh[:, :, t, :], in1=Lh[:, :, t + 1, :], op=ALU.add),
                  reads=["Lall"], writes=["Lall"])
            for t in range(1, 9):
                A("dve", lambda e, t=t: e.tensor_tensor(out=Lh[:, :, t, :], in0=Lh[:, :, t, :],
                                                        in1=tsum[:, :].rearrange("p (c h) -> p c h", h=8), op=ALU.add),
                  reads=["Lall", "tsum"], writes=["Lall"])

        def gather_half(b, hf):
            col = b * 2 + hf
            A("pool", lambda e: e.indirect_dma_start(
                out=Kb.rearrange("p g n -> p (g n)"), out_offset=None, in_=ck2,
                in_offset=bass.IndirectOffsetOnAxis(ap=idx[:, col:col + 1], axis=0)),
              reads=["idx"], writes=["Kb"], dma=True)
            A("pool", lambda e: e.indirect_dma_start(
                out=Vb.rearrange("p g n -> p (g n)"), out_offset=None, in_=cv2,
                in_offset=bass.IndirectOffsetOnAxis(ap=idx[:, col:col + 1], axis=0)),
              reads=["idx"], writes=["Vb"], dma=True)

        def sample_half(b, hf):
            hs = slice(hf * 64, hf * 64 + 64)
            if hf == 0:
                A("sp", lambda e: e.dma_start(out=qb[:, :], in_=q_scr[b].partition_broadcast(128)), reads=["q_scr"], writes=["qb"], dma=True)
            A("dve", lambda e: e.tensor_tensor(out=Kb, in0=Kb, in1=qb[:, :].unsqueeze(1).to_broadcast([128, 8, 512]), op=ALU.mult),
              reads=["Kb", "qb"], writes=["Kb"])
            A("dve", lambda e: e.tensor_reduce(out=s_t[:, hs], in_=Kb.rearrange("p g (h d) -> p (g h) d", h=8), axis=AX.X, op=ALU.add),
              reads=["Kb"], writes=["s_t"])
            A("dve", lambda e: e.scalar_tensor_tensor(out=s_t[:, hs].rearrange("p (g h) -> p g h", g=8), in0=s_t[:, hs].rearrange("p (g h) -> p g h", g=8),
                                                      scalar=SCALE, in1=Lh[:, b * 2 + hf, 1:9, :], op0=ALU.mult, op1=ALU.add),
              reads=["s_t", "Lall"], writes=["s_t"])
            A("act", lambda e: e.activation(out=p_t[:, hs], in_=s_t[:, hs], func=AF.Exp), reads=["s_t"], writes=["p_t"])
            A("act", lambda e: e.activation(out=Vbb, in_=Vb, func=AF.Copy), reads=["Vb"], writes=["Vbb"])
            for g in range(8):
                gg = hf * 8 + g
                A("pe", lambda e, g=g, gg=gg: e.matmul(pf[5][0:8, :], lhsT=p_t[:, gg * 8:(gg + 1) * 8], rhs=Vbb[:, g, :],
                                                       start=(gg == 0), stop=(gg == NPG - 1)),
                  reads=["p_t", "Vbb"], writes=[Rp[5]])
            if hf == 0:
                return
            A("dve", lambda e: e.tensor_reduce(out=psm[:, :], in_=p_t[:, :].rearrange("p (g h) -> p h g", g=NPG), axis=AX.X, op=ALU.add),
              reads=["p_t"], writes=["psm"])
            A("dve", lambda e: e.tensor_tensor(out=mo[:, :], in0=pf[5][0:8, :], in1=cst[0:8, K_BM:K_BM + 512], op=ALU.mult),
              reads=[Rp[5], "cst"], writes=["mo"])
            A("pe", lambda e: e.matmul(pf[3][0:NS, :], lhsT=cst[0:8, K_EB + b * 16:K_EB + (b + 1) * 16], rhs=mo[:, :],
                                       start=(b == 0), stop=(b == NS - 1)),
              reads=["mo", "cst"], writes=[Rp[3]])
            A("pe", lambda e: e.matmul(pf[4][0:NS, 0:8], lhsT=cst[:, K_CS + b * 16:K_CS + (b + 1) * 16], rhs=psm[:, :],
                                       start=(b == 0), stop=(b == NS - 1)),
              reads=["psm", "cst"], writes=[Rp[4]])

        xTs = ((xT, "xT"), (xT2, "xT2"))

        def pA_h1(I):
            xb, xr = xTs[I % 2]
            norm_T(x_all[I * 128:(I + 1) * 128, :], 128, xb, xr)

        def pA_k(I):
            xb, xr = xTs[I % 2]
            proj(1, C_K, 512, 128, xb, xr)
            headnorm(1, 128, 8, 64, gk_bc, knb[:, :], "knb")
            for hp in range(4):
                A("pe", lambda e, hp=hp: e.transpose(out=pbf(6)[:, hp * 128:(hp + 1) * 128], in_=knb[:, hp * 128:(hp + 1) * 128],
                                                     identity=identb[:, :]),
                  reads=["knb", "identb"], writes=[Rp[6]])
            r4 = I % 4
            A("act", lambda e: e.activation(out=kT_stage[:, :, r4 * 128:(r4 + 1) * 128],
                                            in_=pbf(6)[:, 0:512].rearrange("p (h t) -> p h t", h=4), func=AF.Copy),
              reads=[Rp[6]], writes=["kT_stage"])
            if r4 == 3:
                t0 = (I - 3) * 128
                A("sp", lambda e: e.dma_start(out=kT_scr[:, :, t0:t0 + 512].rearrange("h p t -> p h t"), in_=kT_stage[:, :, :]),
                  reads=["kT_stage"], writes=["kT_scr"], dma=True)

        def pA_vf(I):
            xb, xr = xTs[I % 2]
            r4 = I % 4
            proj(2, C_V, 512, 128, xb, xr)
            A("act", lambda e: e.activation(out=v_stage[:, r4, :, 0:64], in_=pf[2][:, :].rearrange("p (h d) -> p h d", h=8), func=AF.Copy),
              reads=[Rp[2]], writes=["v_stage"])
            proj(7, C_F, 8, 128, xb, xr)
            logf_of(7, 128)
            A("pe", lambda e: e.matmul(pf[7][:, 8:16], lhsT=utri, rhs=lf[:, :], start=True, stop=True), reads=["lf", "cst"], writes=[Rp[7]])
            A("pe", lambda e: e.matmul(pf[7][:, 16:24], lhsT=ones, rhs=lf[:, :], start=True, stop=True), reads=["lf", "cst"], writes=[Rp[7]])
            A("dve", lambda e: e.tensor_copy(out=carry[:, I, :], in_=runc[:, :]), reads=["runc"], writes=["carry"])
            A("dve", lambda e: e.scalar_tensor_tensor(out=negc[:, I, :], in0=pf[7][:, 8:16], scalar=-1.0, in1=runc[:, :],
                                                      op0=ALU.mult, op1=ALU.subtract),
              reads=[Rp[7], "runc"], writes=["negc"])
            A("dve", lambda e: e.tensor_tensor(out=runc[:, :], in0=runc[:, :], in1=pf[7][:, 16:24], op=ALU.add),
              reads=[Rp[7], "runc"], writes=["runc"])
            if r4 == 3:
                for hp in range(4):
                    A("sp", lambda e, hp=hp: e.dma_start(
                        out=v_scr[hp, :, I - 3:I + 1, :].rearrange("p a (h d) -> p a h d", h=2),
                        in_=v_stage[:, :, 2 * hp:2 * hp + 2, :]),
                      reads=["v_stage"], writes=["v_scr"], dma=True)

        cw = sb("cw", [128, 32], F32)
        cwo = sb("cwo", [128, 8], F32)
        fz_t = sb("fz_t", [128, 4], F32)

        def fence(names):
            A("dve", lambda e: e.memset(fz_t[:, :], 0.0), writes=names)

        fence(["et", "ub", "gc", "tmpm", "kT_stage", "v_stage"])
        A("dve", lambda e: e.memset(arT[:, 1024:2064].bitcast(BF16), 1.0), writes=["v_stage"])
        sched = {}
        for hs_ in range(2 * NS):
            sched.setdefault(15 + (hs_ * 3) // 2, []).append(hs_)
        gather_half(0, 0)
        pA_h1(0)
        for I in range(NT):
            chains = [record(pA_k, I), record(pA_vf, I)]
            if I + 1 < NT:
                chains.insert(0, record(pA_h1, I + 1))
            for hs_ in sched.get(I, []):
                chains.append(record(sample_half, hs_ // 2, hs_ % 2))
            interleave(*chains)
            for hs_ in sched.get(I, []):
                if hs_ + 1 < 2 * NS:
                    gather_half((hs_ + 1) // 2, (hs_ + 1) % 2)
            if I == 13:
                compute_E()

        n = NS
        A("sp", lambda e: e.dma_start(out=xt[:n, :], in_=x_s), writes=["xt"], dma=True)
        A("dve", lambda e: e.tensor_tensor(out=sm1[:, :], in0=qn_s[:, :], in1=kn_s[:, :], op=ALU.mult), reads=["qn_s", "kn_s"], writes=["sm1"])
        A("dve", lambda e: e.tensor_reduce(out=sm2[:, :], in_=sm1[:, :].rearrange("p (h d) -> p h d", h=8), axis=AX.X, op=ALU.add),
          reads=["sm1"], writes=["sm2"])
        A("dve", lambda e: e.scalar_tensor_tensor(out=sm2[:, :], in0=sm2[:, :], scalar=SCALE, in1=lfs[:n, :], op0=ALU.mult, op1=ALU.subtract),
          reads=["sm2", "lfs"], writes=["sm2"])
        A("act", lambda e: e.activation(out=sm2[:, :], in_=sm2[:, :], func=AF.Exp), reads=["sm2"], writes=["sm2"])
        A("dve", lambda e: e.tensor_tensor(out=sm1[:, :].rearrange("p (h d) -> p h d", h=8), in0=vf_s[:, :].rearrange("p (h d) -> p h d", h=8),
                                           in1=sm2[:, :].unsqueeze(2).to_broadcast([n, 8, 64]), op=ALU.mult),
          reads=["vf_s", "sm2"], writes=["sm1"])
        A("dve", lambda e: e.tensor_tensor(out=sm1[:, :], in0=sm1[:, :], in1=pf[3][0:n, :], op=ALU.add), reads=["sm1", Rp[3]], writes=["sm1"])
        A("dve", lambda e: e.tensor_tensor(out=sm3[:, :], in0=pf[4][0:n, 0:8], in1=sm2[:, :], op=ALU.add), reads=[Rp[4], "sm2"], writes=["sm3"])
        A("dve", lambda e: e.reciprocal(out=sm3[:, :], in_=sm3[:, :]), reads=["sm3"], writes=["sm3"])
        A("dve", lambda e: e.tensor_tensor(out=sm1[:, :].rearrange("p (h d) -> p h d", h=8), in0=sm1[:, :].rearrange("p (h d) -> p h d", h=8),
                                           in1=sm3[:, :].unsqueeze(2).to_broadcast([n, 8, 64]), op=ALU.mult),
          reads=["sm1", "sm3"], writes=["sm1"])
        A("dve", lambda e: e.tensor_tensor(out=mrg_s[:, 0:512], in0=sm1[:, :], in1=ga_s[:, :], op=ALU.mult), reads=["sm1", "ga_s"], writes=["mrg_s"])
        A("dve", lambda e: e.tensor_copy(out=mrg_s[:, 512:1024], in_=m_s[:, :]), reads=["m_s"], writes=["mrg_s"])
        for c in range(8):
            A("pe", lambda e, c=c: e.transpose(out=pbf(0)[:, c * 128:c * 128 + n], in_=mrg_s[:n, c * 128:(c + 1) * 128], identity=identb[:n, :n]),
              reads=["mrg_s", "identb"], writes=[Rp[0]])
        A("act", lambda e: e.activation(out=mT_s[:, :, :], in_=pbf(0).rearrange("p (c t) -> p c t", c=8)[:, :, 0:n], func=AF.Copy),
          reads=[Rp[0]], writes=["mT_s"])
        for half in range(2):
            for c in range(8):
                A("pe", lambda e, c=c, half=half: e.matmul(pf[1 + half][:n, :], lhsT=mT_s[:, c, :], rhs=wout[:, c, half * 512:(half + 1) * 512],
                                                           start=(c == 0), stop=(c == 7)),
                  reads=["mT_s", "wout"], writes=[Rp[1 + half]])
            A("dve", lambda e, half=half: e.tensor_tensor(out=xt[:n, half * 512:(half + 1) * 512], in0=pf[1 + half][:n, :],
                                                          in1=xt[:n, half * 512:(half + 1) * 512], op=ALU.add),
              reads=[Rp[1 + half], "xt"], writes=["xt"])
        A("sp", lambda e: e.dma_start(out=y_s, in_=xt[:n, :]), reads=["xt"], dma=True)

        fence(["Kb", "Vb", "Vbb", "Lall", "ptb", "idx", "qT_st", "ga_st", "m_st", "et", "ub", "gc", "tmpm", "kT_stage", "v_stage"])
        for i in range(NOWN):
            rows = slice(i * 128, (i + 1) * 128)
            norm_T(x_own[rows, :], 128)
            proj(1, C_Q, 512, 128)
            headnorm(1, 128, 8, 64, gq_bc, knb[:, :], "knb")
            for hp in range(4):
                A("pe", lambda e, hp=hp: e.transpose(out=pbf(6)[:, hp * 128:(hp + 1) * 128], in_=knb[:, hp * 128:(hp + 1) * 128],
                                                     identity=identb[:, :]),
                  reads=["knb", "identb"], writes=[Rp[6]])
            A("act", lambda e, i=i: e.activation(out=qT_st[:, :, i * 128:(i + 1) * 128],
                                                 in_=pbf(6)[:, 0:512].rearrange("p (h t) -> p h t", h=4), func=AF.Copy),
              reads=[Rp[6]], writes=["qT_st"])
            proj(1, C_K, 512, 128)
            headnorm(1, 128, 8, 64, gk_bc, kf[:, :], "kf")
            A("sp", lambda e, rows=rows: e.dma_start(out=k_own[rows, :], in_=kf[:, :]), reads=["kf"], dma=True)
            proj(2, C_V, 512, 128)
            A("act", lambda e: e.activation(out=vf[:, :], in_=pf[2][:, :], func=AF.Copy), reads=[Rp[2]], writes=["vf"])
            A("sp", lambda e, rows=rows: e.dma_start(out=v_own[rows, :], in_=vf[:, :]), reads=["vf"], dma=True)
            proj(7, C_F, 8, 128)
            logf_of(7, 128)
            A("sp", lambda e, rows=rows: e.dma_start(out=l_own[rows, :], in_=lf[:, :]), reads=["lf"], dma=True)
            A("pe", lambda e: e.matmul(pf[7][:, 8:16], lhsT=utri, rhs=lf[:, :], start=True, stop=True), reads=["lf", "cst"], writes=[Rp[7]])
            A("dve", lambda e, i=i: e.tensor_tensor(out=cw[:, :].rearrange("p (h w) -> p h w", h=8),
                                                    in0=carry[:, 4 * i:4 * i + 4, :].rearrange("p w h -> p h w"),
                                                    in1=oh4.unsqueeze(1).to_broadcast([128, 8, 4]), op=ALU.mult),
              reads=["carry", "cst"], writes=["cw"])
            A("dve", lambda e: e.tensor_reduce(out=cwo[:, :], in_=cw[:, :].rearrange("p (h w) -> p h w", h=8), axis=AX.X, op=ALU.add),
              reads=["cw"], writes=["cwo"])
            A("dve", lambda e, i=i: e.tensor_tensor(out=c_own[:, i, :], in0=pf[7][:, 8:16], in1=cwo[:, :], op=ALU.add),
              reads=[Rp[7], "cwo"], writes=["c_own"])
            rest_proj(128, ga_st[:, i, :], "ga_st", m_st[:, i, :], "m_st", False)

        fence(["Wb", "kTp", "vP", "pTb0", "pTb1", "pTb2", "pTb3", "o_sb", "a_rows", "rcp_t", "mT", "yt", "qTh"])
        A("dve", lambda e: e.memset(kTp[64:65, :], 1.0), writes=["kTp"])
        for i0 in range(0, NOWN, 4):
            for k in range(4):
                A("pe", lambda e, i0=i0, k=k: e.transpose(out=pf[7][0:8, k * 128:(k + 1) * 128], in_=c_own[:, i0 + k, :], identity=identf),
                  reads=["c_own", "cst"], writes=[Rp[7]])
            A("act", lambda e, i0=i0: e.activation(out=a_rows[0:8, i0 * 128:i0 * 128 + 512], in_=pf[7][0:8, :], func=AF.Copy, scale=1.0 / SCALE),
              reads=[Rp[7]], writes=["a_rows"])
        SB = (0, 1, 7)
        LA = 2
        its = []
        for h in range(8):
            for qg in range(4):
                nJ = 16 * qg + 16
                for J in range(nJ):
                    its.append((h, qg, J, nJ))

        def emit_qk(n):
            h, qg, J, nJ = its[n]
            hp, half = h // 2, h % 2
            rws = slice(half * 64, half * 64 + 64)
            if qg == 0 and J == 0:
                A("sp", lambda e: e.dma_start(out=kTp[0:64, :], in_=kT_scr[hp, half * 64:(half + 1) * 64, :]),
                  reads=["kT_scr"], writes=["kTp"], dma=True)
                A("sp", lambda e: e.dma_start(out=qTh[0:64, :], in_=qT_st[half * 64:(half + 1) * 64, hp, :]),
                  reads=["qT_st"], writes=["qTh"], dma=True)
                A("sp", lambda e: e.dma_start(out=qTh[64:65, :], in_=a_rows[h:h + 1, :]),
                  reads=["a_rows"], writes=["qTh"], dma=True)
                if half == 0:
                    A("sp", lambda e, hp=hp: e.dma_start(out=vP, in_=v_scr[hp]), reads=["v_scr"], writes=["vP"], dma=True)
            kmin = max(0, (J - 16 * qg) // 4) if J >= 16 * qg else 0
            c0 = kmin * 128
            q0 = qg * 512 + c0
            q1 = qg * 512 + 512
            sbk = SB[n % 3]
            pb = n % 4
            win = J >= 16 * qg
            A("pe", lambda e: e.matmul(pf[sbk][:, c0:512], lhsT=kTp[0:65, J * 128:(J + 1) * 128], rhs=qTh[0:65, q0:q1],
                                       start=True, stop=(not win)),
              reads=["kTp", "qTh"], writes=[Rp[sbk]])
            if win:
                w = (J - 16 * qg) % 4
                A("pe", lambda e: e.matmul(pf[sbk][:, c0:c0 + 128], lhsT=identb[:, :], rhs=maskb[:, w, :], start=False, stop=True),
                  reads=["identb", "maskb"], writes=[Rp[sbk]])
            A("act", lambda e: e.activation(out=pTb[pb][:, c0:512], in_=pf[sbk][:, c0:512], func=AF.Exp, scale=SCALE,
                                            bias=negc[:, J, h:h + 1]),
              reads=[Rp[sbk], "negc"], writes=["pTb%d" % pb])

        def emit_pv(n):
            h, qg, J, nJ = its[n]
            half = h % 2
            kmin = max(0, (J - 16 * qg) // 4) if J >= 16 * qg else 0
            c0 = kmin * 128
            pb = n % 4
            A("pe", lambda e: e.matmul(pf[2 + qg][0:65, c0:512], lhsT=vP[:, J, half * 65:(half + 1) * 65], rhs=pTb[pb][:, c0:512],
                                       start=(J == 0), stop=(J == nJ - 1)),
              reads=["vP", "pTb%d" % pb], writes=[Rp[2 + qg]])
            if J == nJ - 1:
                A("act", lambda e: e.activation(out=o_sb[0:65, :], in_=pf[2 + qg][0:65, :], func=AF.Copy),
                  reads=[Rp[2 + qg]], writes=["o_sb"])
                for k in range(4):
                    A("pe", lambda e, k=k: e.transpose(out=pf[6][:, k * 65:k * 65 + 65], in_=o_sb[0:65, k * 128:(k + 1) * 128],
                                                       identity=identf[0:65, 0:65]),
                      reads=["o_sb", "cst"], writes=[Rp[6]])
                A("dve", lambda e: e.reciprocal(out=rcp_t[:, 0:4], in_=pf[6][:, 0:260].rearrange("p (k e) -> p k e", k=4)[:, :, 64]),
                  reads=[Rp[6]], writes=["rcp_t"])
                for k in range(4):
                    A("dve", lambda e, k=k: e.scalar_tensor_tensor(
                        out=ga_st[:, 4 * qg + k, h * 64:(h + 1) * 64], in0=pf[6][:, k * 65:k * 65 + 64], scalar=rcp_t[:, k:k + 1],
                        in1=ga_st[:, 4 * qg + k, h * 64:(h + 1) * 64], op0=ALU.mult, op1=ALU.mult),
                      reads=[Rp[6], "rcp_t", "ga_st"], writes=["ga_st"])

        pend = []
        for nn in range(len(its)):
            h_, qg_, J_, _ = its[nn]
            if h_ % 2 == 0 and qg_ == 0 and J_ == 0:
                for m in pend:
                    emit_pv(m)
                pend = []
            emit_qk(nn)
            pend.append(nn)
            if len(pend) > LA:
                emit_pv(pend.pop(0))
        for m in pend:
            emit_pv(m)

        for i in range(NOWN):
            rows = slice(i * 128, (i + 1) * 128)
            for c in range(8):
                src = ga_st[:, i, c * 128:(c + 1) * 128] if c < 4 else m_st[:, i, (c - 4) * 128:(c - 3) * 128]
                A("pe", lambda e, c=c, src=src: e.transpose(out=pbf(6)[:, c * 128:(c + 1) * 128], in_=src, identity=identb[:, :]),
                  reads=["ga_st", "m_st", "identb"], writes=[Rp[6]])
            A("act", lambda e: e.activation(out=mT[:, :, :], in_=pbf(6).rearrange("p (c t) -> p c t", c=8), func=AF.Copy),
              reads=[Rp[6]], writes=["mT"])
            A("sp", lambda e, rows=rows: e.dma_start(out=xt[:, :], in_=x_own[rows, :]), writes=["xt"], dma=True)
            for half in range(2):
                for c in range(8):
                    A("pe", lambda e, c=c, half=half: e.matmul(pf[half][:, :], lhsT=mT[:, c, :], rhs=wout[:, c, half * 512:(half + 1) * 512],
                                                               start=(c == 0), stop=(c == 7)),
                      reads=["mT", "wout"], writes=[Rp[half]])
                A("dve", lambda e, half=half: e.tensor_tensor(out=yt[:, half * 512:(half + 1) * 512], in0=pf[half][:, :],
                                                              in1=xt[:, half * 512:(half + 1) * 512], op=ALU.add),
                  reads=[Rp[half], "xt"], writes=["yt"])
            A("sp", lambda e, rows=rows: e.dma_start(out=y_own[rows, :], in_=yt[:, :]), reads=["yt"], dma=True)

        P.emit(nc)
    return nc


def _consts(j):
    c = np.zeros((128, K_END), np.float32)
    p = np.arange(128)
    c[:, K_ID:K_ID + 128] = np.eye(128)
    c[:, K_U:K_U + 128] = (p[:, None] <= p[None, :])
    c[:, K_ONE:K_ONE + 128] = 1.0
    c[:, K_L:K_L + 128] = (p[:, None] > p[None, :])
    for w in range(4):
        if w < j:
            m = np.ones((128, 128))
        elif w == j:
            m = (p[:, None] <= p[None, :])
        else:
            m = np.zeros((128, 128))
        c[:, K_MW + w * 128:K_MW + (w + 1) * 128] = m
    c[:, K_IOTA] = p % 16
    c[:, K_OH4 + j] = 1.0
    for b in range(16):
        c[0:8, K_EB + b * 16 + b] = 1.0
        c[:, K_CS + b * 16 + b] = 1.0
    for h in range(8):
        c[h, K_BM + h * 64:K_BM + (h + 1) * 64] = 1.0
        c[h, K_ID8 + h] = 1.0
        c[h, K_OHT + h * 128:K_OHT + (h + 1) * 128] = 1.0
    return c


def kernel(x_prompt, x_sample, cache_k, cache_v, cache_logf, page_table, g_norm, w_in,
           b_f, g_q, g_k, g_v, w_s, b_s, w_out):
    f = lambda a: np.ascontiguousarray(np.asarray(a))
    x_prompt = f(x_prompt); x_sample = f(x_sample)
    ckr = f(cache_k).reshape(2560 * 128, 512)
    cvr = f(cache_v).reshape(2560 * 128, 512)
    clr = f(cache_logf).reshape(2560 * 128, 8)
    ptab = f(page_table).astype(np.int32)
    nc = build_nc()
    in_maps = []
    for c in range(8):
        b, j = c // 4, c % 4
        xa = x_prompt[b]
        xo = np.ascontiguousarray(xa.reshape(16, 4, 128, D)[:, j].reshape(NOWN * 128, D))
        in_maps.append(dict(
            x_all=xa, x_own=xo, x_s=np.ascontiguousarray(x_sample[16 * c:16 * c + 16, 0, :]),
            ck=ckr, cv=cvr, cl=clr,
            pt=np.ascontiguousarray(ptab[16 * c:16 * c + 16].reshape(16, 2, 8)[:, :, np.arange(128) // 16].transpose(2, 0, 1).reshape(128, 32)),
            g_norm=f(g_norm)[0], w_in=f(w_in)[0], b_f=f(b_f)[0], g_q=f(g_q)[0], g_k=f(g_k)[0], g_v=f(g_v)[0],
            w_s=f(w_s)[0], b_s=f(b_s)[0], w_out=f(w_out)[0], consts=_consts(j)))
    res = run_bass_kernel_spmd(nc, in_maps, core_ids=list(range(8)))
    R = res.results
    yp = np.zeros((2, S, D), np.float32)
    kp = np.zeros((1, 2, S, 8, 64), np.float32)
    vp = np.zeros((1, 2, S, 8, 64), np.float32)
    lp = np.zeros((1, 2, S, 8), np.float32)
    ys = np.zeros((128, 1, D), np.float32)
    ks = np.zeros((1, 128, 1, 8, 64), np.float32)
    vs = np.zeros((1, 128, 1, 8, 64), np.float32)
    ls = np.zeros((1, 128, 1, 8), np.float32)
    gs = np.zeros((1, 128, 1, 4, 128), np.float32)
    for c in range(8):
        b, j = c // 4, c % 4
        r = R[c]
        yp[b].reshape(16, 4, 128, D)[:, j] = r["y_own"].reshape(16, 128, D)
        kp[0, b].reshape(16, 4, 128, 512)[:, j] = r["k_own"].reshape(16, 128, 512)
        vp[0, b].reshape(16, 4, 128, 512)[:, j] = r["v_own"].reshape(16, 128, 512)
        lp[0, b].reshape(16, 4, 128, 8)[:, j] = r["l_own"].reshape(16, 128, 8)
        sl = slice(16 * c, 16 * c + 16)
        ys[sl, 0] = r["y_s"]
        ks[0, sl, 0] = r["k_s"].reshape(16, 8, 64)
        vs[0, sl, 0] = r["v_s"].reshape(16, 8, 64)
        ls[0, sl, 0] = r["l_s"]
        gs[0, sl, 0] = r["gv_s"].reshape(16, 4, 128)
    return (yp, ys, kp, vp, lp, ks, vs, ls, gs)
```

```python
import contextlib
import numpy as np
import concourse.bass as bass
import concourse.mybir as mybir
from concourse.bass_utils import run_bass_kernel_spmd

F32 = mybir.dt.float32
BF16 = mybir.dt.bfloat16
I32 = mybir.dt.int32
ALU = mybir.AluOpType
AF = mybir.ActivationFunctionType
AX = mybir.AxisListType

D = 1024
DIN = 3592
S = 8192
NT = 64
NOWN = 16
NS = 16
NPG = 16
EPS = 1e-6
SCALE = 0.125
C_Q, C_K, C_V, C_F, C_ZA, C_U, C_GV, C_ZC = 0, 512, 1024, 1536, 1544, 2056, 2568, 3080

K_ID, K_U, K_ONE, K_L, K_MW, K_IOTA, K_OH4, K_EB, K_BM, K_ID8, K_OHT, K_CS, K_END = (
    0, 128, 256, 384, 512, 1024, 1025, 1029, 1285, 1797, 1805, 2829, 3085)

ENGS = ("pe", "act", "dve", "pool", "sp")


class Res:
    __slots__ = ("w", "r")

    def __init__(self):
        self.w = None
        self.r = []


class Op:
    __slots__ = ("eng", "fn", "deps", "signaled", "count", "dma", "sem", "prev_on_sem")

    def __init__(self, eng, fn, dma):
        self.eng = eng
        self.fn = fn
        self.dma = dma
        self.deps = []
        self.signaled = dma
        self.count = None
        self.sem = None
        self.prev_on_sem = None


class Prog:
    def __init__(self, n_dma_sems=16):
        self.q = {e: [] for e in ENGS}
        self.n_dma_sems = n_dma_sems
        self.all_dma = []

    def add(self, eng, fn, reads=(), writes=(), dma=False):
        op = Op(eng, fn, dma)
        deps = {}
        for r in reads:
            if r.w is not None:
                deps[id(r.w)] = (r.w, "raw")
        for w in writes:
            if w.w is not None and id(w.w) not in deps:
                deps[id(w.w)] = (w.w, "waw")
            for rr in w.r:
                if id(rr) not in deps:
                    deps[id(rr)] = (rr, "war")
        for d, kind in deps.values():
            if d is op:
                continue
            if not d.dma and not dma and d.eng == eng:
                if eng == "pe":
                    continue
                if kind != "raw":
                    continue
            d.signaled = True
            op.deps.append(d)
        for r in reads:
            r.r.append(op)
        for w in writes:
            w.w = op
            w.r = []
        self.q[eng].append(op)
        if dma:
            self.all_dma.append(op)
        return op

    def emit(self, nc):
        stack = contextlib.ExitStack()
        with stack:
            esem = {e: stack.enter_context(nc.semaphore("s_" + e)) for e in ENGS}
            dsem = {e: [stack.enter_context(nc.semaphore("d_%s%d" % (e, i))) for i in range(self.n_dma_sems)]
                    for e in ("sp", "act", "pool")}
            for e in ENGS:
                c = 0
                dcount = [0] * self.n_dma_sems
                dlast = [None] * self.n_dma_sems
                k = 0
                for op in self.q[e]:
                    if op.dma:
                        s = k % self.n_dma_sems
                        k += 1
                        dcount[s] += 16
                        op.sem = dsem[e][s]
                        op.count = dcount[s]
                        op.prev_on_sem = dlast[s]
                        dlast[s] = op
                    elif op.signaled:
                        c += 1
                        op.sem = esem[e]
                        op.count = c
            block = stack.enter_context(nc.Block())

            def run(e):
                def body(eng):
                    waited = {}

                    def wait_for(d):
                        key = id(d.sem)
                        if waited.get(key, 0) >= d.count:
                            return
                        eng.wait_ge(d.sem, d.count)
                        waited[key] = d.count

                    for op in self.q[e]:
                        for d in op.deps:
                            wait_for(d)
                        if op.dma and op.prev_on_sem is not None:
                            wait_for(op.prev_on_sem)
                        ins = op.fn(eng)
                        if op.dma:
                            ins.then_inc(op.sem, 16)
                        elif op.signaled:
                            ins.then_inc(op.sem, 1)
                    if e == "sp":
                        last = {}
                        for d in self.all_dma:
                            last[id(d.sem)] = d
                        for d in last.values():
                            wait_for(d)
                return body

            block.tensor(run("pe"))
            block.scalar(run("act"))
            block.vector(run("dve"))
            block.gpsimd(run("pool"))
            block.sync(run("sp"))


def build_nc():
    nc = bass.Bass("TRN2", target_bir_lowering=False)

    def din(name, shape, dt=F32):
        return nc.dram_tensor(name, shape, dt, kind="ExternalInput").ap()

    def dout(name, shape, dt=F32):
        return nc.dram_tensor(name, shape, dt, kind="ExternalOutput").ap()

    x_all = din("x_all", [S, D])
    x_own = din("x_own", [NOWN * 128, D])
    x_s = din("x_s", [NS, D])
    ck = din("ck", [2560 * 128, 512])
    cv = din("cv", [2560 * 128, 512])
    cl = din("cl", [2560 * 128, 8])
    pt = din("pt", [128, 2 * NS], I32)
    g_norm = din("g_norm", [D])
    w_in = din("w_in", [D, DIN])
    b_f = din("b_f", [8])
    g_q = din("g_q", [64])
    g_k = din("g_k", [64])
    g_v = din("g_v", [512])
    w_s = din("w_s", [4, 128, 128])
    b_s = din("b_s", [4, 128])
    w_out = din("w_out", [D, D])
    consts_d = din("consts", [128, K_END])

    y_own = dout("y_own", [NOWN * 128, D])
    k_own = dout("k_own", [NOWN * 128, 512])
    v_own = dout("v_own", [NOWN * 128, 512])
    l_own = dout("l_own", [NOWN * 128, 8])
    y_s = dout("y_s", [NS, D])
    k_s = dout("k_s", [NS, 512])
    v_s = dout("v_s", [NS, 512])
    l_s = dout("l_s", [NS, 8])
    gv_s = dout("gv_s", [NS, 512])

    kT_scr = nc.dram_tensor("kT_scr", [4, 128, S], BF16, kind="Internal").ap()
    v_scr = nc.dram_tensor("v_scr", [4, 128, NT, 130], BF16, kind="Internal").ap()
    q_scr = nc.dram_tensor("q_scr", [NS, 512], F32, kind="Internal").ap()

    P = Prog()
    st = contextlib.ExitStack()
    with st:
        def sb(name, shape, dt):
            return st.enter_context(nc.sbuf_tensor(name, shape, dt))

        cst = sb("cst", [128, K_END], F32)
        identf = cst[:, K_ID:K_ID + 128]
        utri = cst[:, K_U:K_U + 128]
        ones = cst[:, K_ONE:K_ONE + 128]
        ltri = cst[:, K_L:K_L + 128]
        iota = cst[:, K_IOTA:K_IOTA + 1]
        oh4 = cst[:, K_OH4:K_OH4 + 4]
        identb = sb("identb", [128, 128], BF16)
        maskb = sb("maskb", [128, 4, 128], BF16)
        ohTb = sb("ohTb", [8, 8, 128], BF16)
        wout = sb("wout", [128, 8, D], BF16)
        g_bc = sb("g_bc", [128, D], F32)
        gv_bc = sb("gv_bc", [128, 512], F32)
        gq_bc = sb("gq_bc", [128, 64], F32)
        gk_bc = sb("gk_bc", [128, 64], F32)
        bf_bc = sb("bf_bc", [128, 8], F32)
        bs_t = sb("bs_t", [128, 4], F32)
        ws00 = sb("ws00", [128, 4], F32)
        bs0 = sb("bs0", [128, 4], F32)
        wsT = sb("wsT", [128, 4, 128], BF16)
        negc = sb("negc", [128, NT, 8], F32)
        carry = sb("carry", [128, NT, 8], F32)
        runc = sb("runc", [128, 8], F32)
        c_own = sb("c_own", [128, NOWN, 8], F32)
        xt = sb("xt", [128, D], F32)
        xn = sb("xn", [128, D], BF16)
        junk = xn
        xT = sb("xT", [128, 8, 128], BF16)
        xT2 = sb("xT2", [128, 8, 128], BF16)
        ss = sb("ss", [128, 1], F32)
        rr = sb("rr", [128, 1], F32)
        sq = sb("sq", [128, 512], F32)
        ssq = sb("ssq", [128, 8], F32)
        rk = sb("rk", [128, 8], F32)
        kn = sb("kn", [128, 512], F32)
        kf = sb("kf", [128, 512], F32)
        knb = sb("knb", [128, 512], BF16)
        vf = sb("vf", [128, 512], F32)
        zf = sb("zf", [128, 8], F32)
        lf = sb("lf", [128, 8], F32)
        lfs = sb("lfs", [128, 8], F32)
        arT = sb("arT", [128, 2080], F32)
        kT_stage = arT[:, 0:1024].bitcast(BF16).rearrange("p (h t) -> p h t", h=4)
        v_stage = arT[:, 1024:2064].bitcast(BF16).rearrange("p (a h d) -> p a h d", a=4, h=8)
        et = arT[:, 0:512]
        ub = arT[:, 512:1024]
        gvn = sb("gvn", [128, 512], BF16)
        gvnf = sb("gvnf", [128, 512], F32)
        gc = arT[:, 1024:1536]
        tmpm = arT[:, 1536:2048]
        arW = sb("arW", [128, 8 * DIN], BF16)
        arQ = sb("arQ", [128, 25600], BF16)

        Wb = arW[:, :].rearrange("p (c n) -> p c n", c=8)
        kTp = arW[:, 0:8192]
        vP = arW[:, 8192:8192 + 64 * 130].rearrange("p (t e) -> p t e", t=64)
        o = 8192 + 64 * 130
        pTb = [arW[:, o + i * 512:o + (i + 1) * 512] for i in range(4)]
        o += 4 * 512
        o_sb = arW[:, o:o + 1024].bitcast(F32)
        o += 1024
        a_rows = arW[:, o:o + 2048]
        o += 2048
        rcp_t = arW[:, o:o + 8].bitcast(F32)
        o += 8
        mT = arW[:, o:o + 1024].rearrange("p (c n) -> p c n", c=8)
        o += 1024
        yt = arW[:, o:o + 2048].bitcast(F32)
        o += 2048
        qTh = arW[:, o:o + 2048]
        o += 2048
        assert o <= 8 * DIN
        qT_st = arQ[:, 0:8192].rearrange("p (h n) -> p h n", h=4)
        ga_st = arQ[:, 8192:16384].rearrange("p (i n) -> p i n", i=NOWN)
        m_st = arQ[:, 16384:24576].rearrange("p (i n) -> p i n", i=NOWN)
        Kb = arQ[:, 0:8192].bitcast(F32).rearrange("p (g n) -> p g n", g=8)
        Vb = arQ[:, 8192:16384].bitcast(F32).rearrange("p (g n) -> p g n", g=8)
        Wstage = arQ[:, 0:2 * DIN].bitcast(F32)
        Lh = arQ[:, 16384:20992].bitcast(F32).rearrange("p (c t h) -> p c t h", c=2 * NS, t=9)
        Vbb = arQ[:, 20992:25088].rearrange("p (g n) -> p g n", g=8)
        ptb = arQ[:, 25088:25152].bitcast(I32)
        idx = arQ[:, 25152:25216].bitcast(I32)
        tot_t = sb("tot_t", [128, 256], F32)
        tsum = sb("tsum", [128, 256], F32)
        qb = sb("qb", [128, 512], F32)
        s_t = sb("s_t", [128, 128], F32)
        p_t = sb("p_t", [128, 128], BF16)
        psm = sb("psm", [128, 8], F32)
        mo = sb("mo", [8, 512], F32)
        md = sb("md", [8, 8], F32)
        qn_s = sb("qn_s", [NS, 512], F32)
        kn_s = sb("kn_s", [NS, 512], F32)
        vf_s = sb("vf_s", [NS, 512], F32)
        ga_s = sb("ga_s", [NS, 512], BF16)
        m_s = sb("m_s", [NS, 512], BF16)
        sm1 = sb("sm1", [NS, 512], F32)
        sm2 = sb("sm2", [NS, 8], F32)
        sm3 = sb("sm3", [NS, 8], F32)
        mrg_s = sb("mrg_s", [NS, D], BF16)
        mT_s = sb("mT_s", [128, 8, NS], BF16)

        pf = [st.enter_context(nc.psum_tensor("pf%d" % i, [128, 512], F32)) for i in range(8)]
        Rp = [Res() for _ in range(8)]

        def pbf(i):
            return pf[i][:, :].bitcast(BF16)

        R = {}

        def res(name):
            if name not in R:
                R[name] = Res()
            return R[name]

        cap = [None]

        def A(eng, fn, reads=(), writes=(), dma=False):
            if cap[0] is not None:
                cap[0].append((eng, fn, reads, writes, dma))
                return None
            rl = [res(r) if isinstance(r, str) else r for r in reads]
            wl = [res(w) if isinstance(w, str) else w for w in writes]
            return P.add(eng, fn, rl, wl, dma)

        def record(f, *args):
            cap[0] = []
            f(*args)
            lst = cap[0]
            cap[0] = None
            return lst

        def interleave(*lists):
            lists = [l for l in lists if l]
            pos = [0] * len(lists)
            left = sum(len(l) for l in lists)
            while left:
                for i, l in enumerate(lists):
                    if pos[i] < len(l):
                        A(*l[pos[i]])
                        pos[i] += 1
                        left -= 1

        A("sp", lambda e: e.dma_start(out=cst[:, :], in_=consts_d), writes=["cst"], dma=True)
        A("sp", lambda e: e.dma_start(out=g_bc[:, :], in_=g_norm.partition_broadcast(128)), writes=["g_bc"], dma=True)
        A("sp", lambda e: e.dma_start(out=gv_bc[:, :], in_=g_v.partition_broadcast(128)), writes=["gv_bc"], dma=True)
        A("sp", lambda e: e.dma_start(out=gq_bc[:, :], in_=g_q.partition_broadcast(128)), writes=["gq_bc"], dma=True)
        A("sp", lambda e: e.dma_start(out=gk_bc[:, :], in_=g_k.partition_broadcast(128)), writes=["gk_bc"], dma=True)
        A("sp", lambda e: e.dma_start(out=bf_bc[:, :], in_=b_f.partition_broadcast(128)), writes=["bf_bc"], dma=True)
        for g in range(4):
            A("sp", lambda e, g=g: e.dma_start(out=bs_t[:, g:g + 1], in_=b_s[g].rearrange("(t o) -> t o", o=1)),
              writes=["bs_t"], dma=True)
            A("sp", lambda e, g=g: e.dma_start(out=ws00[:, g:g + 1], in_=w_s[g, 0, 0:1].partition_broadcast(128)),
              writes=["ws00"], dma=True)
            A("sp", lambda e, g=g: e.dma_start(out=bs0[:, g:g + 1], in_=b_s[g, 0:1].partition_broadcast(128)),
              writes=["bs0"], dma=True)
        w_in_v = w_in.rearrange("(c p) n -> p c n", p=128)
        for c in range(8):
            A("sp", lambda e, c=c: e.dma_start(out=Wstage, in_=w_in[c * 128:(c + 1) * 128, :]), writes=["Kb"], dma=True)
            A("act", lambda e, c=c: e.activation(out=Wb[:, c, :], in_=Wstage, func=AF.Copy), reads=["Kb"], writes=["Wb"])
        A("pool", lambda e: e.dma_start(out=wout[:, :, :], in_=w_out.rearrange("(c p) n -> p c n", p=128)),
          writes=["wout"], dma=True)
        A("dve", lambda e: e.tensor_copy(out=identb[:, :], in_=identf), reads=["cst"], writes=["identb"])
        A("dve", lambda e: e.tensor_scalar(out=maskb[:, :, :].rearrange("p w t -> p (w t)"), in0=cst[:, K_MW:K_MW + 512],
                                           scalar1=1.0e4, scalar2=-1.0e4, op0=ALU.mult, op1=ALU.add),
          reads=["cst"], writes=["maskb"])
        A("dve", lambda e: e.tensor_copy(out=ohTb[:, :, :].rearrange("p h s -> p (h s)"), in_=cst[0:8, K_OHT:K_OHT + 1024]),
          reads=["cst"], writes=["ohTb"])
        A("dve", lambda e: e.memset(runc[:, :], 0.0), writes=["runc"])
        ws_t = tmpm.rearrange("p (g s) -> p g s", g=4)
        A("sp", lambda e: e.dma_start(out=ws_t, in_=w_s.rearrange("g t s -> t g s")), writes=["tmpm"], dma=True)
        for g in range(4):
            A("pe", lambda e, g=g: e.transpose(out=pf[0][:, g * 128:(g + 1) * 128], in_=ws_t[:, g, :], identity=identf),
              reads=["tmpm", "cst"], writes=[Rp[0]])
        for g in range(4):
            A("dve", lambda e, g=g: e.tensor_tensor(out=wsT[:, g, :], in0=pf[0][:, g * 128:(g + 1) * 128], in1=utri, op=ALU.mult),
              reads=[Rp[0], "cst"], writes=["wsT"])

        def rsqrt_act(dst, src, n, scale):
            A("act", lambda e: e.activation(out=dst, in_=src, func=AF.Ln, scale=scale, bias=EPS), reads=["tmp_r_in"], writes=["tmp_r"])
            A("act", lambda e: e.activation(out=dst, in_=dst, func=AF.Exp, scale=-0.5), reads=["tmp_r"], writes=["tmp_r"])

        def norm_T(src, n, xTb=None, xTr="xT"):
            xTb = xT if xTb is None else xTb
            A("sp", lambda e: e.dma_start(out=xt[:n, :], in_=src), writes=["xt"], dma=True)
            A("dve", lambda e: e.memset(ss[:n, :], 0.0), writes=["ss"])
            A("act", lambda e: e.activation(out=junk[:n, :], in_=xt[:n, :], func=AF.Square, accum_out=ss[:n, :]),
              reads=["xt", "ss"], writes=["xn", "ss"])
            A("act", lambda e: e.activation(out=rr[:n, :], in_=ss[:n, :], func=AF.Ln, scale=1.0 / D, bias=EPS),
              reads=["ss"], writes=["rr"])
            A("act", lambda e: e.activation(out=rr[:n, :], in_=rr[:n, :], func=AF.Exp, scale=-0.5),
              reads=["rr"], writes=["rr"])
            A("dve", lambda e: e.scalar_tensor_tensor(out=xn[:n, :], in0=xt[:n, :], scalar=rr[:n, 0:1], in1=g_bc[:n, :],
                                                      op0=ALU.mult, op1=ALU.mult),
              reads=["xt", "rr", "g_bc"], writes=["xn"])
            for c in range(8):
                A("pe", lambda e, c=c: e.transpose(out=pbf(0)[:, c * 128:c * 128 + n], in_=xn[:n, c * 128:(c + 1) * 128],
                                                   identity=identb[:n, :n]),
                  reads=["xn", "identb"], writes=[Rp[0]])
            A("act", lambda e: e.activation(out=xTb[:, :, 0:n], in_=pbf(0).rearrange("p (c t) -> p c t", c=8)[:, :, 0:n], func=AF.Copy),
              reads=[Rp[0]], writes=[xTr])

        def proj(bank, col0, width, n, xTb=None, xTr="xT"):
            xTb = xT if xTb is None else xTb
            for c in range(8):
                A("pe", lambda e, c=c: e.matmul(pf[bank][:n, 0:width], lhsT=xTb[:, c, 0:n], rhs=Wb[:, c, col0:col0 + width],
                                                start=(c == 0), stop=(c == 7)),
                  reads=[xTr, "Wb"], writes=[Rp[bank]])

        def headnorm(bank, n, nh, hd, gb, out_ap, out_res):
            A("act", lambda e: e.activation(out=sq[:n, :], in_=pf[bank][:n, :], func=AF.Square), reads=[Rp[bank]], writes=["sq"])
            A("dve", lambda e: e.tensor_reduce(out=ssq[:n, 0:nh], in_=sq[:n, :].rearrange("p (h d) -> p h d", h=nh),
                                               axis=AX.X, op=ALU.add), reads=["sq"], writes=["ssq"])
            A("act", lambda e: e.activation(out=rk[:n, 0:nh], in_=ssq[:n, 0:nh], func=AF.Ln, scale=1.0 / hd, bias=EPS),
              reads=["ssq"], writes=["rk"])
            A("act", lambda e: e.activation(out=rk[:n, 0:nh], in_=rk[:n, 0:nh], func=AF.Exp, scale=-0.5), reads=["rk"], writes=["rk"])
            A("dve", lambda e: e.tensor_tensor(out=kn[:n, :].rearrange("p (h d) -> p h d", h=nh),
                                               in0=pf[bank][:n, :].rearrange("p (h d) -> p h d", h=nh),
                                               in1=rk[:n, 0:nh].unsqueeze(2).to_broadcast([n, nh, hd]), op=ALU.mult),
              reads=[Rp[bank], "rk"], writes=["kn"])
            if hd == 64:
                in1 = gb[:n, :].unsqueeze(1).to_broadcast([n, nh, hd])
                A("dve", lambda e: e.tensor_tensor(out=out_ap.rearrange("p (h d) -> p h d", h=nh),
                                                   in0=kn[:n, :].rearrange("p (h d) -> p h d", h=nh), in1=in1, op=ALU.mult),
                  reads=["kn"], writes=[out_res])
            else:
                A("dve", lambda e: e.tensor_tensor(out=out_ap, in0=kn[:n, :], in1=gb[:n, :], op=ALU.mult),
                  reads=["kn"], writes=[out_res])

        def logf_of(bank, n):
            A("dve", lambda e: e.tensor_tensor(out=zf[:n, :], in0=pf[bank][:n, 0:8], in1=bf_bc[:n, :], op=ALU.add),
              reads=[Rp[bank], "bf_bc"], writes=["zf"])
            A("act", lambda e: e.activation(out=zf[:n, :], in_=zf[:n, :], func=AF.Exp, scale=-1.0), reads=["zf"], writes=["zf"])
            A("act", lambda e: e.activation(out=zf[:n, :], in_=zf[:n, :], func=AF.Ln, bias=1.0), reads=["zf"], writes=["zf"])
            A("dve", lambda e: e.tensor_scalar(out=lf[:n, :], in0=zf[:n, :], scalar1=-1.0, scalar2=None, op0=ALU.mult),
              reads=["zf"], writes=["lf"])

        def silu_from(bank, n, out_ap, out_res):
            A("act", lambda e: e.activation(out=et[:n, :], in_=pf[bank][:n, :], func=AF.Exp, scale=-1.0), reads=[Rp[bank]], writes=["et"])
            A("dve", lambda e: e.tensor_scalar(out=et[:n, :], in0=et[:n, :], scalar1=1.0, scalar2=None, op0=ALU.add),
              reads=["et"], writes=["et"])
            A("dve", lambda e: e.reciprocal(out=et[:n, :], in_=et[:n, :]), reads=["et"], writes=["et"])
            A("dve", lambda e: e.tensor_tensor(out=out_ap, in0=pf[bank][:n, :], in1=et[:n, :], op=ALU.mult),
              reads=[Rp[bank], "et"], writes=[out_res])

        def rest_proj(n, ga_out, ga_res, m_out, m_res, sample):
            proj(3, C_ZA, 512, n)
            silu_from(3, n, ga_out, ga_res)
            proj(4, C_U, 512, n)
            A("act", lambda e: e.activation(out=ub[:n, :], in_=pf[4][:n, :], func=AF.Copy), reads=[Rp[4]], writes=["ub"])
            proj(3, C_GV, 512, n)
            if sample:
                headnorm(3, n, 4, 128, gv_bc, gvnf[:n, :], "gvnf")
                A("sp", lambda e: e.dma_start(out=gv_s, in_=gvnf[:n, :]), reads=["gvnf"], dma=True)
            else:
                headnorm(3, n, 4, 128, gv_bc, gvn[:n, :], "gvn")
                for g in range(4):
                    A("pe", lambda e, g=g: e.matmul(pf[5][:n, g * 128:(g + 1) * 128], lhsT=wsT[:, g, :],
                                                    rhs=gvn[:, g * 128:(g + 1) * 128], start=True, stop=True),
                      reads=["wsT", "gvn"], writes=[Rp[5]])
            proj(4, C_ZC, 512, n)
            silu_from(4, n, gc[:n, :], "gc")
            for g in range(4):
                sl = slice(g * 128, (g + 1) * 128)
                if sample:
                    A("dve", lambda e, g=g, sl=sl: e.tensor_scalar(out=tmpm[:n, sl], in0=gvnf[:n, sl], scalar1=ws00[:n, g:g + 1],
                                                                   scalar2=bs0[:n, g:g + 1], op0=ALU.mult, op1=ALU.add),
                      reads=["gvnf", "ws00", "bs0"], writes=["tmpm"])
                    A("dve", lambda e, sl=sl: e.tensor_tensor(out=tmpm[:n, sl], in0=tmpm[:n, sl], in1=ub[:n, sl], op=ALU.mult),
                      reads=["tmpm", "ub"], writes=["tmpm"])
                else:
                    A("dve", lambda e, g=g, sl=sl: e.scalar_tensor_tensor(out=tmpm[:n, sl], in0=pf[5][:n, sl], scalar=bs_t[:n, g:g + 1],
                                                                          in1=ub[:n, sl], op0=ALU.add, op1=ALU.mult),
                      reads=[Rp[5], "bs_t", "ub"], writes=["tmpm"])
            A("dve", lambda e: e.tensor_tensor(out=m_out, in0=tmpm[:n, :], in1=gc[:n, :], op=ALU.mult),
              reads=["tmpm", "gc"], writes=[m_res])

        n = NS
        norm_T(x_s, n)
        proj(1, C_Q, 512, n)
        headnorm(1, n, 8, 64, gq_bc, qn_s[:n, :], "qn_s")
        A("sp", lambda e: e.dma_start(out=q_scr, in_=qn_s[:n, :]), reads=["qn_s"], writes=["q_scr"], dma=True)
        proj(1, C_K, 512, n)
        headnorm(1, n, 8, 64, gk_bc, kn_s[:n, :], "kn_s")
        A("sp", lambda e: e.dma_start(out=k_s, in_=kn_s[:n, :]), reads=["kn_s"], dma=True)
        proj(2, C_V, 512, n)
        A("act", lambda e: e.activation(out=vf_s[:n, :], in_=pf[2][:n, :], func=AF.Copy), reads=[Rp[2]], writes=["vf_s"])
        A("sp", lambda e: e.dma_start(out=v_s, in_=vf_s[:n, :]), reads=["vf_s"], dma=True)
        proj(2, C_F, 8, n)
        logf_of(2, n)
        A("dve", lambda e: e.tensor_copy(out=lfs[:n, :], in_=lf[:n, :]), reads=["lf"], writes=["lfs"])
        A("sp", lambda e: e.dma_start(out=l_s, in_=lfs[:n, :]), reads=["lfs"], dma=True)
        rest_proj(n, ga_s[:n, :], "ga_s", m_s[:n, :], "m_s", True)

        ck2 = ck.rearrange("(r t) n -> r (t n)", t=8)
        cv2 = cv.rearrange("(r t) n -> r (t n)", t=8)
        cl2 = cl.rearrange("(r t) n -> r (t n)", t=8)
        A("sp", lambda e: e.dma_start(out=ptb, in_=pt), writes=["ptb"], dma=True)
        A("dve", lambda e: e.tensor_scalar(out=idx, in0=ptb, scalar1=16.0, scalar2=iota, op0=ALU.mult, op1=ALU.add),
          reads=["ptb", "cst"], writes=["idx"])
        A("dve", lambda e: e.memset(arQ[:, 16384:20992].bitcast(F32), 0.0), writes=["Lall"])
        for col in range(2 * NS):
            A("pool", lambda e, col=col: e.indirect_dma_start(
                out=arQ[:, 16384:20992].bitcast(F32)[:, col * 72:col * 72 + 64], out_offset=None, in_=cl2,
                in_offset=bass.IndirectOffsetOnAxis(ap=idx[:, col:col + 1], axis=0)),
              reads=["idx"], writes=["Lall"], dma=True)

        def compute_E():
            A("dve", lambda e: e.tensor_reduce(out=tot_t[:, :].rearrange("p (c h) -> p c h", h=8),
                                               in_=Lh[:, :, 0:8, :].rearrange("p c t h -> p c h t"), axis=AX.X, op=ALU.add),
              reads=["Lall"], writes=["tot_t"])
            A("pe", lambda e: e.matmul(pf[3][:, 0:256], lhsT=ltri, rhs=tot_t[:, :], start=True, stop=True),
              reads=["tot_t", "cst"], writes=[Rp[3]])
            A("pe", lambda e: e.matmul(pf[4][:, 0:128], lhsT=ones,
                                       rhs=tot_t[:, :].rearrange("p (b f h) -> p b f h", f=2, h=8)[:, :, 1, :], start=True, stop=True),
              reads=["tot_t", "cst"], writes=[Rp[4]])
            A("dve", lambda e: e.tensor_copy(out=tsum[:, :], in_=pf[3][:, 0:256]), reads=[Rp[3]], writes=["tsum"])
            A("dve", lambda e: e.tensor_tensor(out=tsum[:, :].rearrange("p (b f h) -> p b f h", f=2, h=8)[:, :, 0, :],
                                               in0=tsum[:, :].rearrange("p (b f h) -> p b f h", f=2, h=8)[:, :, 0, :],
                                               in1=pf[4][:, 0:128].rearrange("p (b h) -> p b h", h=8), op=ALU.add),
              reads=[Rp[4], "tsum"], writes=["tsum"])
            for t in range(6, -1, -1):
                A("dve", lambda e, t=t: e.tensor_tensor(out=Lh[:, :, t, :], in0=Lh[:, :, t, :], in1=Lh[:, :, t + 1, :], op=ALU.add),
                  reads=["Lall"], writes=["Lall"])
            for t in range(1, 9):
                A("dve", lambda e, t=t: e.tensor_tensor(out=Lh[:, :, t, :], in0=Lh[:, :, t, :],
                                                        in1=tsum[:, :].rearrange("p (c h) -> p c h", h=8), op=ALU.add),
                  reads=["Lall", "tsum"], writes=["Lall"])

        def gather_half(b, hf):
            col = b * 2 + hf
            A("pool", lambda e: e.indirect_dma_start(
                out=Kb.rearrange("p g n -> p (g n)"), out_offset=None, in_=ck2,
                in_offset=bass.IndirectOffsetOnAxis(ap=idx[:, col:col + 1], axis=0)),
              reads=["idx"], writes=["Kb"], dma=True)
            A("pool", lambda e: e.indirect_dma_start(
                out=Vb.rearrange("p g n -> p (g n)"), out_offset=None, in_=cv2,
                in_offset=bass.IndirectOffsetOnAxis(ap=idx[:, col:col + 1], axis=0)),
              reads=["idx"], writes=["Vb"], dma=True)

        def sample_half(b, hf):
            hs = slice(hf * 64, hf * 64 + 64)
            if hf == 0:
                A("sp", lambda e: e.dma_start(out=qb[:, :], in_=q_scr[b].partition_broadcast(128)), reads=["q_scr"], writes=["qb"], dma=True)
            A("dve", lambda e: e.tensor_tensor(out=Kb, in0=Kb, in1=qb[:, :].unsqueeze(1).to_broadcast([128, 8, 512]), op=ALU.mult),
              reads=["Kb", "qb"], writes=["Kb"])
            A("dve", lambda e: e.tensor_reduce(out=s_t[:, hs], in_=Kb.rearrange("p g (h d) -> p (g h) d", h=8), axis=AX.X, op=ALU.add),
              reads=["Kb"], writes=["s_t"])
            A("dve", lambda e: e.scalar_tensor_tensor(out=s_t[:, hs].rearrange("p (g h) -> p g h", g=8), in0=s_t[:, hs].rearrange("p (g h) -> p g h", g=8),
                                                      scalar=SCALE, in1=Lh[:, b * 2 + hf, 1:9, :], op0=ALU.mult, op1=ALU.add),
              reads=["s_t", "Lall"], writes=["s_t"])
            A("act", lambda e: e.activation(out=p_t[:, hs], in_=s_t[:, hs], func=AF.Exp), reads=["s_t"], writes=["p_t"])
            A("act", lambda e: e.activation(out=Vbb, in_=Vb, func=AF.Copy), reads=["Vb"], writes=["Vbb"])
            for g in range(8):
                gg = hf * 8 + g
                A("pe", lambda e, g=g, gg=gg: e.matmul(pf[5][0:8, :], lhsT=p_t[:, gg * 8:(gg + 1) * 8], rhs=Vbb[:, g, :],
                                                       start=(gg == 0), stop=(gg == NPG - 1)),
                  reads=["p_t", "Vbb"], writes=[Rp[5]])
            if hf == 0:
                return
            A("dve", lambda e: e.tensor_reduce(out=psm[:, :], in_=p_t[:, :].rearrange("p (g h) -> p h g", g=NPG), axis=AX.X, op=ALU.add),
              reads=["p_t"], writes=["psm"])
            A("dve", lambda e: e.tensor_tensor(out=mo[:, :], in0=pf[5][0:8, :], in1=cst[0:8, K_BM:K_BM + 512], op=ALU.mult),
              reads=[Rp[5], "cst"], writes=["mo"])
            A("pe", lambda e: e.matmul(pf[3][0:NS, :], lhsT=cst[0:8, K_EB + b * 16:K_EB + (b + 1) * 16], rhs=mo[:, :],
                                       start=(b == 0), stop=(b == NS - 1)),
              reads=["mo", "cst"], writes=[Rp[3]])
            A("pe", lambda e: e.matmul(pf[4][0:NS, 0:8], lhsT=cst[:, K_CS + b * 16:K_CS + (b + 1) * 16], rhs=psm[:, :],
                                       start=(b == 0), stop=(b == NS - 1)),
              reads=["psm", "cst"], writes=[Rp[4]])

        xTs = ((xT, "xT"), (xT2, "xT2"))

        def pA_h1(I):
            xb, xr = xTs[I % 2]
            norm_T(x_all[I * 128:(I + 1) * 128, :], 128, xb, xr)

        def pA_k(I):
            xb, xr = xTs[I % 2]
            proj(1, C_K, 512, 128, xb, xr)
            headnorm(1, 128, 8, 64, gk_bc, knb[:, :], "knb")
            for hp in range(4):
                A("pe", lambda e, hp=hp: e.transpose(out=pbf(6)[:, hp * 128:(hp + 1) * 128], in_=knb[:, hp * 128:(hp + 1) * 128],
                                                     identity=identb[:, :]),
                  reads=["knb", "identb"], writes=[Rp[6]])
            r4 = I % 4
            sl = r4 // 2
            A("act", lambda e: e.activation(out=kT_stage[:, :, r4 * 128:(r4 + 1) * 128],
                                            in_=pbf(6)[:, 0:512].rearrange("p (h t) -> p h t", h=4), func=AF.Copy),
              reads=[Rp[6]], writes=["kT_stage%d" % sl])
            if r4 % 2 == 1:
                t0 = (I - 1) * 128
                A("sp", lambda e: e.dma_start(out=kT_scr[:, :, t0:t0 + 256].rearrange("h p t -> p h t"),
                                              in_=kT_stage[:, :, sl * 256:(sl + 1) * 256]),
                  reads=["kT_stage%d" % sl], writes=["kT_scr"], dma=True)

        def pA_vf(I):
            xb, xr = xTs[I % 2]
            r4 = I % 4
            proj(2, C_V, 512, 128, xb, xr)
            sl = r4 // 2
            A("act", lambda e: e.activation(out=v_stage[:, r4, :, 0:64], in_=pf[2][:, :].rearrange("p (h d) -> p h d", h=8), func=AF.Copy),
              reads=[Rp[2]], writes=["v_stage%d" % sl])
            proj(7, C_F, 8, 128, xb, xr)
            logf_of(7, 128)
            A("pe", lambda e: e.matmul(pf[7][:, 8:16], lhsT=utri, rhs=lf[:, :], start=True, stop=True), reads=["lf", "cst"], writes=[Rp[7]])
            A("pe", lambda e: e.matmul(pf[7][:, 16:24], lhsT=ones, rhs=lf[:, :], start=True, stop=True), reads=["lf", "cst"], writes=[Rp[7]])
            A("dve", lambda e: e.tensor_copy(out=carry[:, I, :], in_=runc[:, :]), reads=["runc"], writes=["carry"])
            A("dve", lambda e: e.scalar_tensor_tensor(out=negc[:, I, :], in0=pf[7][:, 8:16], scalar=-1.0, in1=runc[:, :],
                                                      op0=ALU.mult, op1=ALU.subtract),
              reads=[Rp[7], "runc"], writes=["negc"])
            A("dve", lambda e: e.tensor_tensor(out=runc[:, :], in0=runc[:, :], in1=pf[7][:, 16:24], op=ALU.add),
              reads=[Rp[7], "runc"], writes=["runc"])
            if r4 % 2 == 1:
                for hp in range(4):
                    A("sp", lambda e, hp=hp: e.dma_start(
                        out=v_scr[hp, :, I - 1:I + 1, :].rearrange("p a (h d) -> p a h d", h=2),
                        in_=v_stage[:, 2 * sl:2 * sl + 2, 2 * hp:2 * hp + 2, :]),
                      reads=["v_stage%d" % sl], writes=["v_scr"], dma=True)

        cw = sb("cw", [128, 32], F32)
        cwo = sb("cwo", [128, 8], F32)
        fz_t = sb("fz_t", [128, 4], F32)

        def fence(names):
            A("dve", lambda e: e.memset(fz_t[:, :], 0.0), writes=names)

        fence(["et", "ub", "gc", "tmpm", "kT_stage0", "kT_stage1", "v_stage0", "v_stage1"])
        A("dve", lambda e: e.memset(arT[:, 1024:2064].bitcast(BF16), 1.0), writes=["v_stage0", "v_stage1"])
        sched = {}
        for hs_ in range(2 * NS):
            sched.setdefault(15 + (hs_ * 3) // 2, []).append(hs_)
        gather_half(0, 0)
        pA_h1(0)
        for I in range(NT):
            chains = [record(pA_k, I), record(pA_vf, I)]
            if I + 1 < NT:
                chains.insert(0, record(pA_h1, I + 1))
            for hs_ in sched.get(I, []):
                chains.append(record(sample_half, hs_ // 2, hs_ % 2))
            interleave(*chains)
            for hs_ in sched.get(I, []):
                if hs_ + 1 < 2 * NS:
                    gather_half((hs_ + 1) // 2, (hs_ + 1) % 2)
            if I == 13:
                compute_E()

        n = NS
        A("sp", lambda e: e.dma_start(out=xt[:n, :], in_=x_s), writes=["xt"], dma=True)
        A("dve", lambda e: e.tensor_tensor(out=sm1[:, :], in0=qn_s[:, :], in1=kn_s[:, :], op=ALU.mult), reads=["qn_s", "kn_s"], writes=["sm1"])
        A("dve", lambda e: e.tensor_reduce(out=sm2[:, :], in_=sm1[:, :].rearrange("p (h d) -> p h d", h=8), axis=AX.X, op=ALU.add),
          reads=["sm1"], writes=["sm2"])
        A("dve", lambda e: e.scalar_tensor_tensor(out=sm2[:, :], in0=sm2[:, :], scalar=SCALE, in1=lfs[:n, :], op0=ALU.mult, op1=ALU.subtract),
          reads=["sm2", "lfs"], writes=["sm2"])
        A("act", lambda e: e.activation(out=sm2[:, :], in_=sm2[:, :], func=AF.Exp), reads=["sm2"], writes=["sm2"])
        A("dve", lambda e: e.tensor_tensor(out=sm1[:, :].rearrange("p (h d) -> p h d", h=8), in0=vf_s[:, :].rearrange("p (h d) -> p h d", h=8),
                                           in1=sm2[:, :].unsqueeze(2).to_broadcast([n, 8, 64]), op=ALU.mult),
          reads=["vf_s", "sm2"], writes=["sm1"])
        A("dve", lambda e: e.tensor_tensor(out=sm1[:, :], in0=sm1[:, :], in1=pf[3][0:n, :], op=ALU.add), reads=["sm1", Rp[3]], writes=["sm1"])
        A("dve", lambda e: e.tensor_tensor(out=sm3[:, :], in0=pf[4][0:n, 0:8], in1=sm2[:, :], op=ALU.add), reads=[Rp[4], "sm2"], writes=["sm3"])
        A("dve", lambda e: e.reciprocal(out=sm3[:, :], in_=sm3[:, :]), reads=["sm3"], writes=["sm3"])
        A("dve", lambda e: e.tensor_tensor(out=sm1[:, :].rearrange("p (h d) -> p h d", h=8), in0=sm1[:, :].rearrange("p (h d) -> p h d", h=8),
                                           in1=sm3[:, :].unsqueeze(2).to_broadcast([n, 8, 64]), op=ALU.mult),
          reads=["sm1", "sm3"], writes=["sm1"])
        A("dve", lambda e: e.tensor_tensor(out=mrg_s[:, 0:512], in0=sm1[:, :], in1=ga_s[:, :], op=ALU.mult), reads=["sm1", "ga_s"], writes=["mrg_s"])
        A("dve", lambda e: e.tensor_copy(out=mrg_s[:, 512:1024], in_=m_s[:, :]), reads=["m_s"], writes=["mrg_s"])
        for c in range(8):
            A("pe", lambda e, c=c: e.transpose(out=pbf(0)[:, c * 128:c * 128 + n], in_=mrg_s[:n, c * 128:(c + 1) * 128], identity=identb[:n, :n]),
              reads=["mrg_s", "identb"], writes=[Rp[0]])
        A("act", lambda e: e.activation(out=mT_s[:, :, :], in_=pbf(0).rearrange("p (c t) -> p c t", c=8)[:, :, 0:n], func=AF.Copy),
          reads=[Rp[0]], writes=["mT_s"])
        for half in range(2):
            for c in range(8):
                A("pe", lambda e, c=c, half=half: e.matmul(pf[1 + half][:n, :], lhsT=mT_s[:, c, :], rhs=wout[:, c, half * 512:(half + 1) * 512],
                                                           start=(c == 0), stop=(c == 7)),
                  reads=["mT_s", "wout"], writes=[Rp[1 + half]])
            A("dve", lambda e, half=half: e.tensor_tensor(out=xt[:n, half * 512:(half + 1) * 512], in0=pf[1 + half][:n, :],
                                                          in1=xt[:n, half * 512:(half + 1) * 512], op=ALU.add),
              reads=[Rp[1 + half], "xt"], writes=["xt"])
        A("sp", lambda e: e.dma_start(out=y_s, in_=xt[:n, :]), reads=["xt"], dma=True)

        fence(["Kb", "Vb", "Vbb", "Lall", "ptb", "idx", "qT_st", "ga_st", "m_st", "et", "ub", "gc", "tmpm", "kT_stage0", "kT_stage1", "v_stage0", "v_stage1", "xT", "xT2"])
        def pB_h1(i):
            xb, xr = xTs[i % 2]
            norm_T(x_own[i * 128:(i + 1) * 128, :], 128, xb, xr)

        def pB_X(i):
            xb, xr = xTs[i % 2]
            rows = slice(i * 128, (i + 1) * 128)
            proj(1, C_Q, 512, 128, xb, xr)
            headnorm(1, 128, 8, 64, gq_bc, knb[:, :], "knb")
            for hp in range(4):
                A("pe", lambda e, hp=hp: e.transpose(out=pbf(6)[:, hp * 128:(hp + 1) * 128], in_=knb[:, hp * 128:(hp + 1) * 128],
                                                     identity=identb[:, :]),
                  reads=["knb", "identb"], writes=[Rp[6]])
            A("act", lambda e: e.activation(out=qT_st[:, :, i * 128:(i + 1) * 128],
                                            in_=pbf(6)[:, 0:512].rearrange("p (h t) -> p h t", h=4), func=AF.Copy),
              reads=[Rp[6]], writes=["qT_st"])
            proj(1, C_K, 512, 128, xb, xr)
            headnorm(1, 128, 8, 64, gk_bc, kf[:, :], "kf")
            A("sp", lambda e: e.dma_start(out=k_own[rows, :], in_=kf[:, :]), reads=["kf"], dma=True)
            proj(3, C_GV, 512, 128, xb, xr)
            headnorm(3, 128, 4, 128, gv_bc, gvn[:, :], "gvn")
            for g in range(4):
                A("pe", lambda e, g=g: e.matmul(pf[5][:, g * 128:(g + 1) * 128], lhsT=wsT[:, g, :],
                                                rhs=gvn[:, g * 128:(g + 1) * 128], start=True, stop=True),
                  reads=["wsT", "gvn"], writes=[Rp[5]])

        def pB_Y(i):
            xb, xr = xTs[i % 2]
            rows = slice(i * 128, (i + 1) * 128)
            proj(2, C_V, 512, 128, xb, xr)
            A("act", lambda e: e.activation(out=vf[:, :], in_=pf[2][:, :], func=AF.Copy), reads=[Rp[2]], writes=["vf"])
            A("sp", lambda e: e.dma_start(out=v_own[rows, :], in_=vf[:, :]), reads=["vf"], dma=True)
            proj(7, C_F, 8, 128, xb, xr)
            logf_of(7, 128)
            A("sp", lambda e: e.dma_start(out=l_own[rows, :], in_=lf[:, :]), reads=["lf"], dma=True)
            A("pe", lambda e: e.matmul(pf[7][:, 8:16], lhsT=utri, rhs=lf[:, :], start=True, stop=True), reads=["lf", "cst"], writes=[Rp[7]])
            A("dve", lambda e: e.tensor_tensor(out=cw[:, :].rearrange("p (h w) -> p h w", h=8),
                                               in0=carry[:, 4 * i:4 * i + 4, :].rearrange("p w h -> p h w"),
                                               in1=oh4.unsqueeze(1).to_broadcast([128, 8, 4]), op=ALU.mult),
              reads=["carry", "cst"], writes=["cw"])
            A("dve", lambda e: e.tensor_reduce(out=cwo[:, :], in_=cw[:, :].rearrange("p (h w) -> p h w", h=8), axis=AX.X, op=ALU.add),
              reads=["cw"], writes=["cwo"])
            A("dve", lambda e: e.tensor_tensor(out=c_own[:, i, :], in0=pf[7][:, 8:16], in1=cwo[:, :], op=ALU.add),
              reads=[Rp[7], "cwo"], writes=["c_own"])

        def pB_Z(i):
            xb, xr = xTs[i % 2]
            proj(4, C_ZA, 512, 128, xb, xr)
            silu_from(4, 128, ga_st[:, i, :], "ga_st")
            proj(4, C_U, 512, 128, xb, xr)
            A("act", lambda e: e.activation(out=ub[:, :], in_=pf[4][:, :], func=AF.Copy), reads=[Rp[4]], writes=["ub"])
            proj(4, C_ZC, 512, 128, xb, xr)
            silu_from(4, 128, gc[:, :], "gc")

        def pB_tail(i):
            for g in range(4):
                sl = slice(g * 128, (g + 1) * 128)
                A("dve", lambda e, g=g, sl=sl: e.scalar_tensor_tensor(out=tmpm[:, sl], in0=pf[5][:, sl], scalar=bs_t[:, g:g + 1],
                                                                      in1=ub[:, sl], op0=ALU.add, op1=ALU.mult),
                  reads=[Rp[5], "bs_t", "ub"], writes=["tmpm"])
            A("dve", lambda e: e.tensor_tensor(out=m_st[:, i, :], in0=tmpm[:, :], in1=gc[:, :], op=ALU.mult),
              reads=["tmpm", "gc"], writes=["m_st"])

        pB_h1(0)
        for i in range(NOWN):
            chains = [record(pB_X, i), record(pB_Y, i), record(pB_Z, i)]
            if i + 1 < NOWN:
                chains.insert(0, record(pB_h1, i + 1))
            interleave(*chains)
            pB_tail(i)

        fence(["Wb", "kTp", "vP", "pTb0", "pTb1", "pTb2", "pTb3", "o_sb", "a_rows", "rcp_t", "mT", "yt", "qTh"])
        A("dve", lambda e: e.memset(kTp[64:65, :], 1.0), writes=["kTp"])
        for i0 in range(0, NOWN, 4):
            for k in range(4):
                A("pe", lambda e, i0=i0, k=k: e.transpose(out=pf[7][0:8, k * 128:(k + 1) * 128], in_=c_own[:, i0 + k, :], identity=identf),
                  reads=["c_own", "cst"], writes=[Rp[7]])
            A("act", lambda e, i0=i0: e.activation(out=a_rows[0:8, i0 * 128:i0 * 128 + 512], in_=pf[7][0:8, :], func=AF.Copy, scale=1.0 / SCALE),
              reads=[Rp[7]], writes=["a_rows"])
        SB = (0, 1, 7)
        LA = 2
        its = []
        for h in range(8):
            for qg in range(4):
                nJ = 16 * qg + 16
                for J in range(nJ):
                    its.append((h, qg, J, nJ))

        def emit_qk(n):
            h, qg, J, nJ = its[n]
            hp, half = h // 2, h % 2
            rws = slice(half * 64, half * 64 + 64)
            if qg == 0 and J == 0:
                A("sp", lambda e: e.dma_start(out=kTp[0:64, :], in_=kT_scr[hp, half * 64:(half + 1) * 64, :]),
                  reads=["kT_scr"], writes=["kTp"], dma=True)
                A("sp", lambda e: e.dma_start(out=qTh[0:64, :], in_=qT_st[half * 64:(half + 1) * 64, hp, :]),
                  reads=["qT_st"], writes=["qTh"], dma=True)
                A("sp", lambda e: e.dma_start(out=qTh[64:65, :], in_=a_rows[h:h + 1, :]),
                  reads=["a_rows"], writes=["qTh"], dma=True)
                if half == 0:
                    A("sp", lambda e, hp=hp: e.dma_start(out=vP, in_=v_scr[hp]), reads=["v_scr"], writes=["vP"], dma=True)
            kmin = max(0, (J - 16 * qg) // 4) if J >= 16 * qg else 0
            c0 = kmin * 128
            q0 = qg * 512 + c0
            q1 = qg * 512 + 512
            sbk = SB[n % 3]
            pb = n % 4
            win = J >= 16 * qg
            A("pe", lambda e: e.matmul(pf[sbk][:, c0:512], lhsT=kTp[0:65, J * 128:(J + 1) * 128], rhs=qTh[0:65, q0:q1],
                                       start=True, stop=(not win)),
              reads=["kTp", "qTh"], writes=[Rp[sbk]])
            if win:
                w = (J - 16 * qg) % 4
                A("pe", lambda e: e.matmul(pf[sbk][:, c0:c0 + 128], lhsT=identb[:, :], rhs=maskb[:, w, :], start=False, stop=True),
                  reads=["identb", "maskb"], writes=[Rp[sbk]])
            A("act", lambda e: e.activation(out=pTb[pb][:, c0:512], in_=pf[sbk][:, c0:512], func=AF.Exp, scale=SCALE,
                                            bias=negc[:, J, h:h + 1]),
              reads=[Rp[sbk], "negc"], writes=["pTb%d" % pb])

        def emit_pv(n):
            h, qg, J, nJ = its[n]
            half = h % 2
            kmin = max(0, (J - 16 * qg) // 4) if J >= 16 * qg else 0
            c0 = kmin * 128
            pb = n % 4
            A("pe", lambda e: e.matmul(pf[2 + qg][0:65, c0:512], lhsT=vP[:, J, half * 65:(half + 1) * 65], rhs=pTb[pb][:, c0:512],
                                       start=(J == 0), stop=(J == nJ - 1)),
              reads=["vP", "pTb%d" % pb], writes=[Rp[2 + qg]])
            if J == nJ - 1:
                A("act", lambda e: e.activation(out=o_sb[0:65, :], in_=pf[2 + qg][0:65, :], func=AF.Copy),
                  reads=[Rp[2 + qg]], writes=["o_sb"])
                for k in range(4):
                    A("pe", lambda e, k=k: e.transpose(out=pf[6][:, k * 65:k * 65 + 65], in_=o_sb[0:65, k * 128:(k + 1) * 128],
                                                       identity=identf[0:65, 0:65]),
                      reads=["o_sb", "cst"], writes=[Rp[6]])
                A("dve", lambda e: e.reciprocal(out=rcp_t[:, 0:4], in_=pf[6][:, 0:260].rearrange("p (k e) -> p k e", k=4)[:, :, 64]),
                  reads=[Rp[6]], writes=["rcp_t"])
                for k in range(4):
                    A("dve", lambda e, k=k: e.scalar_tensor_tensor(
                        out=ga_st[:, 4 * qg + k, h * 64:(h + 1) * 64], in0=pf[6][:, k * 65:k * 65 + 64], scalar=rcp_t[:, k:k + 1],
                        in1=ga_st[:, 4 * qg + k, h * 64:(h + 1) * 64], op0=ALU.mult, op1=ALU.mult),
                      reads=[Rp[6], "rcp_t", "ga_st"], writes=["ga_st"])

        pend = []
        for nn in range(len(its)):
            h_, qg_, J_, _ = its[nn]
            if h_ % 2 == 0 and qg_ == 0 and J_ == 0:
                for m in pend:
                    emit_pv(m)
                pend = []
            emit_qk(nn)
            pend.append(nn)
            if len(pend) > LA:
                emit_pv(pend.pop(0))
        for m in pend:
            emit_pv(m)

        for i in range(NOWN):
            rows = slice(i * 128, (i + 1) * 128)
            for c in range(8):
                src = ga_st[:, i, c * 128:(c + 1) * 128] if c < 4 else m_st[:, i, (c - 4) * 128:(c - 3) * 128]
                A("pe", lambda e, c=c, src=src: e.transpose(out=pbf(6)[:, c * 128:(c + 1) * 128], in_=src, identity=identb[:, :]),
                  reads=["ga_st", "m_st", "identb"], writes=[Rp[6]])
            A("act", lambda e: e.activation(out=mT[:, :, :], in_=pbf(6).rearrange("p (c t) -> p c t", c=8), func=AF.Copy),
              reads=[Rp[6]], writes=["mT"])
            A("sp", lambda e, rows=rows: e.dma_start(out=xt[:, :], in_=x_own[rows, :]), writes=["xt"], dma=True)
            for half in range(2):
                for c in range(8):
                    A("pe", lambda e, c=c, half=half: e.matmul(pf[half][:, :], lhsT=mT[:, c, :], rhs=wout[:, c, half * 512:(half + 1) * 512],
                                                               start=(c == 0), stop=(c == 7)),
                      reads=["mT", "wout"], writes=[Rp[half]])
                A("dve", lambda e, half=half: e.tensor_tensor(out=yt[:, half * 512:(half + 1) * 512], in0=pf[half][:, :],
                                                              in1=xt[:, half * 512:(half + 1) * 512], op=ALU.add),
                  reads=[Rp[half], "xt"], writes=["yt"])
            A("sp", lambda e, rows=rows: e.dma_start(out=y_own[rows, :], in_=yt[:, :]), reads=["yt"], dma=True)

        P.emit(nc)
    return nc


def _consts(j):
    c = np.zeros((128, K_END), np.float32)
    p = np.arange(128)
    c[:, K_ID:K_ID + 128] = np.eye(128)
    c[:, K_U:K_U + 128] = (p[:, None] <= p[None, :])
    c[:, K_ONE:K_ONE + 128] = 1.0
    c[:, K_L:K_L + 128] = (p[:, None] > p[None, :])
    for w in range(4):
        if w < j:
            m = np.ones((128, 128))
        elif w == j:
            m = (p[:, None] <= p[None, :])
        else:
            m = np.zeros((128, 128))
        c[:, K_MW + w * 128:K_MW + (w + 1) * 128] = m
    c[:, K_IOTA] = p % 16
    c[:, K_OH4 + j] = 1.0
    for b in range(16):
        c[0:8, K_EB + b * 16 + b] = 1.0
        c[:, K_CS + b * 16 + b] = 1.0
    for h in range(8):
        c[h, K_BM + h * 64:K_BM + (h + 1) * 64] = 1.0
        c[h, K_ID8 + h] = 1.0
        c[h, K_OHT + h * 128:K_OHT + (h + 1) * 128] = 1.0
    return c


def kernel(x_prompt, x_sample, cache_k, cache_v, cache_logf, page_table, g_norm, w_in,
           b_f, g_q, g_k, g_v, w_s, b_s, w_out):
    f = lambda a: np.ascontiguousarray(np.asarray(a))
    x_prompt = f(x_prompt); x_sample = f(x_sample)
    ckr = f(cache_k).reshape(2560 * 128, 512)
    cvr = f(cache_v).reshape(2560 * 128, 512)
    clr = f(cache_logf).reshape(2560 * 128, 8)
    ptab = f(page_table).astype(np.int32)
    nc = build_nc()
    in_maps = []
    for c in range(8):
        b, j = c // 4, c % 4
        xa = x_prompt[b]
        xo = np.ascontiguousarray(xa.reshape(16, 4, 128, D)[:, j].reshape(NOWN * 128, D))
        in_maps.append(dict(
            x_all=xa, x_own=xo, x_s=np.ascontiguousarray(x_sample[16 * c:16 * c + 16, 0, :]),
            ck=ckr, cv=cvr, cl=clr,
            pt=np.ascontiguousarray(ptab[16 * c:16 * c + 16].reshape(16, 2, 8)[:, :, np.arange(128) // 16].transpose(2, 0, 1).reshape(128, 32)),
            g_norm=f(g_norm)[0], w_in=f(w_in)[0], b_f=f(b_f)[0], g_q=f(g_q)[0], g_k=f(g_k)[0], g_v=f(g_v)[0],
            w_s=f(w_s)[0], b_s=f(b_s)[0], w_out=f(w_out)[0], consts=_consts(j)))
    res = run_bass_kernel_spmd(nc, in_maps, core_ids=list(range(8)))
    R = res.results
    yp = np.zeros((2, S, D), np.float32)
    kp = np.zeros((1, 2, S, 8, 64), np.float32)
    vp = np.zeros((1, 2, S, 8, 64), np.float32)
    lp = np.zeros((1, 2, S, 8), np.float32)
    ys = np.zeros((128, 1, D), np.float32)
    ks = np.zeros((1, 128, 1, 8, 64), np.float32)
    vs = np.zeros((1, 128, 1, 8, 64), np.float32)
    ls = np.zeros((1, 128, 1, 8), np.float32)
    gs = np.zeros((1, 128, 1, 4, 128), np.float32)
    for c in range(8):
        b, j = c // 4, c % 4
        r = R[c]
        yp[b].reshape(16, 4, 128, D)[:, j] = r["y_own"].reshape(16, 128, D)
        kp[0, b].reshape(16, 4, 128, 512)[:, j] = r["k_own"].reshape(16, 128, 512)
        vp[0, b].reshape(16, 4, 128, 512)[:, j] = r["v_own"].reshape(16, 128, 512)
        lp[0, b].reshape(16, 4, 128, 8)[:, j] = r["l_own"].reshape(16, 128, 8)
        sl = slice(16 * c, 16 * c + 16)
        ys[sl, 0] = r["y_s"]
        ks[0, sl, 0] = r["k_s"].reshape(16, 8, 64)
        vs[0, sl, 0] = r["v_s"].reshape(16, 8, 64)
        ls[0, sl, 0] = r["l_s"]
        gs[0, sl, 0] = r["gv_s"].reshape(16, 4, 128)
    return (yp, ys, kp, vp, lp, ks, vs, ls, gs)
```

```python
import contextlib
import numpy as np
import concourse.bass as bass
import concourse.mybir as mybir
from concourse.bass_utils import run_bass_kernel_spmd

F32 = mybir.dt.float32
BF16 = mybir.dt.bfloat16
I32 = mybir.dt.int32
ALU = mybir.AluOpType
AF = mybir.ActivationFunctionType
AX = mybir.AxisListType

D = 1024
DIN = 3592
S = 8192
NT = 64
NOWN = 16
NS = 16
NPG = 16
EPS = 1e-6
SCALE = 0.125
C_Q, C_K, C_V, C_F, C_ZA, C_U, C_GV, C_ZC = 0, 512, 1024, 1536, 1544, 2056, 2568, 3080

K_ID, K_U, K_ONE, K_L, K_MW, K_IOTA, K_OH4, K_EB, K_BM, K_ID8, K_OHT, K_CS, K_END = (
    0, 128, 256, 384, 512, 1024, 1025, 1029, 1285, 1797, 1805, 2829, 3085)

ENGS = ("pe", "act", "dve", "pool", "sp")


class Res:
    __slots__ = ("w", "r")

    def __init__(self):
        self.w = None
        self.r = []


class Op:
    __slots__ = ("eng", "fn", "deps", "signaled", "count", "dma", "sem", "prev_on_sem")

    def __init__(self, eng, fn, dma):
        self.eng = eng
        self.fn = fn
        self.dma = dma
        self.deps = []
        self.signaled = dma
        self.count = None
        self.sem = None
        self.prev_on_sem = None


class Prog:
    def __init__(self, n_dma_sems=16):
        self.q = {e: [] for e in ENGS}
        self.n_dma_sems = n_dma_sems
        self.all_dma = []

    def add(self, eng, fn, reads=(), writes=(), dma=False):
        op = Op(eng, fn, dma)
        deps = {}
        for r in reads:
            if r.w is not None:
                deps[id(r.w)] = (r.w, "raw")
        for w in writes:
            if w.w is not None and id(w.w) not in deps:
                deps[id(w.w)] = (w.w, "waw")
            for rr in w.r:
                if id(rr) not in deps:
                    deps[id(rr)] = (rr, "war")
        for d, kind in deps.values():
            if d is op:
                continue
            if not d.dma and not dma and d.eng == eng:
                if eng == "pe":
                    continue
                if kind != "raw":
                    continue
            d.signaled = True
            op.deps.append(d)
        for r in reads:
            r.r.append(op)
        for w in writes:
            w.w = op
            w.r = []
        self.q[eng].append(op)
        if dma:
            self.all_dma.append(op)
        return op

    def emit(self, nc):
        stack = contextlib.ExitStack()
        with stack:
            esem = {e: stack.enter_context(nc.semaphore("s_" + e)) for e in ENGS}
            dsem = {e: [stack.enter_context(nc.semaphore("d_%s%d" % (e, i))) for i in range(self.n_dma_sems)]
                    for e in ("sp", "act", "pool")}
            for e in ENGS:
                c = 0
                dcount = [0] * self.n_dma_sems
                dlast = [None] * self.n_dma_sems
                k = 0
                for op in self.q[e]:
                    if op.dma:
                        s = k % self.n_dma_sems
                        k += 1
                        dcount[s] += 16
                        op.sem = dsem[e][s]
                        op.count = dcount[s]
                        op.prev_on_sem = dlast[s]
                        dlast[s] = op
                    elif op.signaled:
                        c += 1
                        op.sem = esem[e]
                        op.count = c
            block = stack.enter_context(nc.Block())

            def run(e):
                def body(eng):
                    waited = {}

                    def wait_for(d):
                        key = id(d.sem)
                        if waited.get(key, 0) >= d.count:
                            return
                        eng.wait_ge(d.sem, d.count)
                        waited[key] = d.count

                    for op in self.q[e]:
                        for d in op.deps:
                            wait_for(d)
                        if op.dma and op.prev_on_sem is not None:
                            wait_for(op.prev_on_sem)
                        ins = op.fn(eng)
                        if op.dma:
                            ins.then_inc(op.sem, 16)
                        elif op.signaled:
                            ins.then_inc(op.sem, 1)
                    if e == "sp":
                        last = {}
                        for d in self.all_dma:
                            last[id(d.sem)] = d
                        for d in last.values():
                            wait_for(d)
                return body

            block.tensor(run("pe"))
            block.scalar(run("act"))
            block.vector(run("dve"))
            block.gpsimd(run("pool"))
            block.sync(run("sp"))


def build_nc():
    nc = bass.Bass("TRN2", target_bir_lowering=False)

    def din(name, shape, dt=F32):
        return nc.dram_tensor(name, shape, dt, kind="ExternalInput").ap()

    def dout(name, shape, dt=F32):
        return nc.dram_tensor(name, shape, dt, kind="ExternalOutput").ap()

    x_all = din("x_all", [S, D])
    x_own = din("x_own", [NOWN * 128, D])
    x_s = din("x_s", [NS, D])
    ck = din("ck", [2560 * 128, 512])
    cv = din("cv", [2560 * 128, 512])
    cl = din("cl", [2560 * 128, 8])
    pt = din("pt", [128, 2 * NS], I32)
    g_norm = din("g_norm", [D])
    w_in = din("w_in", [D, DIN])
    b_f = din("b_f", [8])
    g_q = din("g_q", [64])
    g_k = din("g_k", [64])
    g_v = din("g_v", [512])
    w_s = din("w_s", [4, 128, 128])
    b_s = din("b_s", [4, 128])
    w_out = din("w_out", [D, D])
    consts_d = din("consts", [128, K_END])

    y_own = dout("y_own", [NOWN * 128, D])
    k_own = dout("k_own", [NOWN * 128, 512])
    v_own = dout("v_own", [NOWN * 128, 512])
    l_own = dout("l_own", [NOWN * 128, 8])
    y_s = dout("y_s", [NS, D])
    k_s = dout("k_s", [NS, 512])
    v_s = dout("v_s", [NS, 512])
    l_s = dout("l_s", [NS, 8])
    gv_s = dout("gv_s", [NS, 512])

    kT_scr = nc.dram_tensor("kT_scr", [4, 128, S], BF16, kind="Internal").ap()
    v_scr = nc.dram_tensor("v_scr", [4, 128, NT, 130], BF16, kind="Internal").ap()
    q_scr = nc.dram_tensor("q_scr", [NS, 512], F32, kind="Internal").ap()

    P = Prog()
    st = contextlib.ExitStack()
    with st:
        def sb(name, shape, dt):
            return st.enter_context(nc.sbuf_tensor(name, shape, dt))

        cst = sb("cst", [128, K_END], F32)
        identf = cst[:, K_ID:K_ID + 128]
        utri = cst[:, K_U:K_U + 128]
        ones = cst[:, K_ONE:K_ONE + 128]
        ltri = cst[:, K_L:K_L + 128]
        iota = cst[:, K_IOTA:K_IOTA + 1]
        oh4 = cst[:, K_OH4:K_OH4 + 4]
        identb = sb("identb", [128, 128], BF16)
        maskb = sb("maskb", [128, 4, 128], BF16)
        ohTb = sb("ohTb", [8, 8, 128], BF16)
        wout = sb("wout", [128, 8, D], BF16)
        g_bc = sb("g_bc", [128, D], F32)
        gv_bc = sb("gv_bc", [128, 512], F32)
        gq_bc = sb("gq_bc", [128, 64], F32)
        gk_bc = sb("gk_bc", [128, 64], F32)
        bf_bc = sb("bf_bc", [128, 8], F32)
        bs_t = sb("bs_t", [128, 4], F32)
        ws00 = sb("ws00", [128, 4], F32)
        bs0 = sb("bs0", [128, 4], F32)
        wsT = sb("wsT", [128, 4, 128], BF16)
        negc = sb("negc", [128, NT, 8], F32)
        carry = sb("carry", [128, NT, 8], F32)
        runc = sb("runc", [128, 8], F32)
        c_own = sb("c_own", [128, NOWN, 8], F32)
        xt = sb("xt", [128, D], F32)
        xn = sb("xn", [128, D], BF16)
        junk = xn
        xT = sb("xT", [128, 8, 128], BF16)
        xT2 = sb("xT2", [128, 8, 128], BF16)
        ss = sb("ss", [128, 1], F32)
        rr = sb("rr", [128, 1], F32)
        sq = sb("sq", [128, 512], F32)
        ssq = sb("ssq", [128, 8], F32)
        rk = sb("rk", [128, 8], F32)
        kn = sb("kn", [128, 512], F32)
        kf = sb("kf", [128, 512], F32)
        knb = sb("knb", [128, 512], BF16)
        vf = sb("vf", [128, 512], F32)
        zf = sb("zf", [128, 8], F32)
        lf = sb("lf", [128, 8], F32)
        lfs = sb("lfs", [128, 8], F32)
        arT = sb("arT", [128, 2080], F32)
        kT_stage = arT[:, 0:1024].bitcast(BF16).rearrange("p (h t) -> p h t", h=4)
        v_stage = arT[:, 1024:2064].bitcast(BF16).rearrange("p (s q a h d) -> p s q a h d", s=2, q=4, a=2, h=2)
        et = arT[:, 0:512]
        ub = arT[:, 512:1024]
        gvn = sb("gvn", [128, 512], BF16)
        gvnf = sb("gvnf", [128, 512], F32)
        gc = arT[:, 1024:1536]
        tmpm = arT[:, 1536:2048]
        arW = sb("arW", [128, 8 * DIN], BF16)
        arQ = sb("arQ", [128, 25600], BF16)

        Wb = arW[:, :].rearrange("p (c n) -> p c n", c=8)
        kTp = arW[:, 0:8192]
        vP = arW[:, 8192:8192 + 64 * 130].rearrange("p (t e) -> p t e", t=64)
        o = 8192 + 64 * 130
        pTb = [arW[:, o + i * 512:o + (i + 1) * 512] for i in range(4)]
        o += 4 * 512
        o_sb = arW[:, o:o + 1024].bitcast(F32)
        o += 1024
        a_rows = arW[:, o:o + 2048]
        o += 2048
        rcp_t = arW[:, o:o + 8].bitcast(F32)
        o += 8
        mT = arW[:, o:o + 1024].rearrange("p (c n) -> p c n", c=8)
        o += 1024
        yt = arW[:, o:o + 2048].bitcast(F32)
        o += 2048
        qTh = arW[:, o:o + 2048]
        o += 2048
        assert o <= 8 * DIN
        qT_st = arQ[:, 0:8192].rearrange("p (h n) -> p h n", h=4)
        ga_st = arQ[:, 8192:16384].rearrange("p (i n) -> p i n", i=NOWN)
        m_st = arQ[:, 16384:24576].rearrange("p (i n) -> p i n", i=NOWN)
        Kb = arQ[:, 0:8192].bitcast(F32).rearrange("p (g n) -> p g n", g=8)
        Vb = arQ[:, 8192:16384].bitcast(F32).rearrange("p (g n) -> p g n", g=8)
        Wstage = arQ[:, 0:2 * DIN].bitcast(F32)
        Lh = arQ[:, 16384:20992].bitcast(F32).rearrange("p (c t h) -> p c t h", c=2 * NS, t=9)
        Vbb = arQ[:, 20992:25088].rearrange("p (g n) -> p g n", g=8)
        ptb = arQ[:, 25088:25152].bitcast(I32)
        idx = arQ[:, 25152:25216].bitcast(I32)
        tot_t = sb("tot_t", [128, 256], F32)
        tsum = sb("tsum", [128, 256], F32)
        qb = sb("qb", [128, 512], F32)
        s_t = sb("s_t", [128, 128], F32)
        p_t = sb("p_t", [128, 128], BF16)
        psm = sb("psm", [128, 8], F32)
        mo = sb("mo", [8, 512], F32)
        md = sb("md", [8, 8], F32)
        qn_s = sb("qn_s", [NS, 512], F32)
        kn_s = sb("kn_s", [NS, 512], F32)
        vf_s = sb("vf_s", [NS, 512], F32)
        ga_s = sb("ga_s", [NS, 512], BF16)
        m_s = sb("m_s", [NS, 512], BF16)
        sm1 = sb("sm1", [NS, 512], F32)
        sm2 = sb("sm2", [NS, 8], F32)
        sm3 = sb("sm3", [NS, 8], F32)
        mrg_s = sb("mrg_s", [NS, D], BF16)
        mT_s = sb("mT_s", [128, 8, NS], BF16)

        pf = [st.enter_context(nc.psum_tensor("pf%d" % i, [128, 512], F32)) for i in range(8)]
        Rp = [Res() for _ in range(8)]

        def pbf(i):
            return pf[i][:, :].bitcast(BF16)

        R = {}

        def res(name):
            if name not in R:
                R[name] = Res()
            return R[name]

        cap = [None]

        def A(eng, fn, reads=(), writes=(), dma=False):
            if cap[0] is not None:
                cap[0].append((eng, fn, reads, writes, dma))
                return None
            rl = [res(r) if isinstance(r, str) else r for r in reads]
            wl = [res(w) if isinstance(w, str) else w for w in writes]
            return P.add(eng, fn, rl, wl, dma)

        def record(f, *args):
            cap[0] = []
            f(*args)
            lst = cap[0]
            cap[0] = None
            return lst

        def interleave(*lists):
            lists = [l for l in lists if l]
            pos = [0] * len(lists)
            left = sum(len(l) for l in lists)
            while left:
                for i, l in enumerate(lists):
                    if pos[i] < len(l):
                        A(*l[pos[i]])
                        pos[i] += 1
                        left -= 1

        A("sp", lambda e: e.dma_start(out=cst[:, :], in_=consts_d), writes=["cst"], dma=True)
        A("sp", lambda e: e.dma_start(out=g_bc[:, :], in_=g_norm.partition_broadcast(128)), writes=["g_bc"], dma=True)
        A("sp", lambda e: e.dma_start(out=gv_bc[:, :], in_=g_v.partition_broadcast(128)), writes=["gv_bc"], dma=True)
        A("sp", lambda e: e.dma_start(out=gq_bc[:, :], in_=g_q.partition_broadcast(128)), writes=["gq_bc"], dma=True)
        A("sp", lambda e: e.dma_start(out=gk_bc[:, :], in_=g_k.partition_broadcast(128)), writes=["gk_bc"], dma=True)
        A("sp", lambda e: e.dma_start(out=bf_bc[:, :], in_=b_f.partition_broadcast(128)), writes=["bf_bc"], dma=True)
        for g in range(4):
            A("sp", lambda e, g=g: e.dma_start(out=bs_t[:, g:g + 1], in_=b_s[g].rearrange("(t o) -> t o", o=1)),
              writes=["bs_t"], dma=True)
            A("sp", lambda e, g=g: e.dma_start(out=ws00[:, g:g + 1], in_=w_s[g, 0, 0:1].partition_broadcast(128)),
              writes=["ws00"], dma=True)
            A("sp", lambda e, g=g: e.dma_start(out=bs0[:, g:g + 1], in_=b_s[g, 0:1].partition_broadcast(128)),
              writes=["bs0"], dma=True)
        w_in_v = w_in.rearrange("(c p) n -> p c n", p=128)
        for c in range(8):
            A("sp", lambda e, c=c: e.dma_start(out=Wstage, in_=w_in[c * 128:(c + 1) * 128, :]), writes=["Kb"], dma=True)
            A("act", lambda e, c=c: e.activation(out=Wb[:, c, :], in_=Wstage, func=AF.Copy), reads=["Kb"], writes=["Wb"])
        A("pool", lambda e: e.dma_start(out=wout[:, :, :], in_=w_out.rearrange("(c p) n -> p c n", p=128)),
          writes=["wout"], dma=True)
        A("dve", lambda e: e.tensor_copy(out=identb[:, :], in_=identf), reads=["cst"], writes=["identb"])
        A("dve", lambda e: e.tensor_scalar(out=maskb[:, :, :].rearrange("p w t -> p (w t)"), in0=cst[:, K_MW:K_MW + 512],
                                           scalar1=1.0e4, scalar2=-1.0e4, op0=ALU.mult, op1=ALU.add),
          reads=["cst"], writes=["maskb"])
        A("dve", lambda e: e.tensor_copy(out=ohTb[:, :, :].rearrange("p h s -> p (h s)"), in_=cst[0:8, K_OHT:K_OHT + 1024]),
          reads=["cst"], writes=["ohTb"])
        A("dve", lambda e: e.memset(runc[:, :], 0.0), writes=["runc"])
        ws_t = tmpm.rearrange("p (g s) -> p g s", g=4)
        A("sp", lambda e: e.dma_start(out=ws_t, in_=w_s.rearrange("g t s -> t g s")), writes=["tmpm"], dma=True)
        for g in range(4):
            A("pe", lambda e, g=g: e.transpose(out=pf[0][:, g * 128:(g + 1) * 128], in_=ws_t[:, g, :], identity=identf),
              reads=["tmpm", "cst"], writes=[Rp[0]])
        for g in range(4):
            A("dve", lambda e, g=g: e.tensor_tensor(out=wsT[:, g, :], in0=pf[0][:, g * 128:(g + 1) * 128], in1=utri, op=ALU.mult),
              reads=[Rp[0], "cst"], writes=["wsT"])

        def rsqrt_act(dst, src, n, scale):
            A("act", lambda e: e.activation(out=dst, in_=src, func=AF.Ln, scale=scale, bias=EPS), reads=["tmp_r_in"], writes=["tmp_r"])
            A("act", lambda e: e.activation(out=dst, in_=dst, func=AF.Exp, scale=-0.5), reads=["tmp_r"], writes=["tmp_r"])

        def norm_T(src, n, xTb=None, xTr="xT", ldq="sp"):
            xTb = xT if xTb is None else xTb
            A(ldq, lambda e: e.dma_start(out=xt[:n, :], in_=src), writes=["xt"], dma=True)
            A("dve", lambda e: e.memset(ss[:n, :], 0.0), writes=["ss"])
            A("act", lambda e: e.activation(out=junk[:n, :], in_=xt[:n, :], func=AF.Square, accum_out=ss[:n, :]),
              reads=["xt", "ss"], writes=["xn", "ss"])
            A("act", lambda e: e.activation(out=rr[:n, :], in_=ss[:n, :], func=AF.Ln, scale=1.0 / D, bias=EPS),
              reads=["ss"], writes=["rr"])
            A("act", lambda e: e.activation(out=rr[:n, :], in_=rr[:n, :], func=AF.Exp, scale=-0.5),
              reads=["rr"], writes=["rr"])
            A("dve", lambda e: e.scalar_tensor_tensor(out=xn[:n, :], in0=xt[:n, :], scalar=rr[:n, 0:1], in1=g_bc[:n, :],
                                                      op0=ALU.mult, op1=ALU.mult),
              reads=["xt", "rr", "g_bc"], writes=["xn"])
            for c in range(8):
                A("pe", lambda e, c=c: e.transpose(out=pbf(0)[:, c * 128:c * 128 + n], in_=xn[:n, c * 128:(c + 1) * 128],
                                                   identity=identb[:n, :n]),
                  reads=["xn", "identb"], writes=[Rp[0]])
            A("act", lambda e: e.activation(out=xTb[:, :, 0:n], in_=pbf(0).rearrange("p (c t) -> p c t", c=8)[:, :, 0:n], func=AF.Copy),
              reads=[Rp[0]], writes=[xTr])

        def proj(bank, col0, width, n, xTb=None, xTr="xT"):
            xTb = xT if xTb is None else xTb
            for c in range(8):
                A("pe", lambda e, c=c: e.matmul(pf[bank][:n, 0:width], lhsT=xTb[:, c, 0:n], rhs=Wb[:, c, col0:col0 + width],
                                                start=(c == 0), stop=(c == 7)),
                  reads=[xTr, "Wb"], writes=[Rp[bank]])

        def headnorm(bank, n, nh, hd, gb, out_ap, out_res):
            A("act", lambda e: e.activation(out=sq[:n, :], in_=pf[bank][:n, :], func=AF.Square), reads=[Rp[bank]], writes=["sq"])
            A("dve", lambda e: e.tensor_reduce(out=ssq[:n, 0:nh], in_=sq[:n, :].rearrange("p (h d) -> p h d", h=nh),
                                               axis=AX.X, op=ALU.add), reads=["sq"], writes=["ssq"])
            A("act", lambda e: e.activation(out=rk[:n, 0:nh], in_=ssq[:n, 0:nh], func=AF.Ln, scale=1.0 / hd, bias=EPS),
              reads=["ssq"], writes=["rk"])
            A("act", lambda e: e.activation(out=rk[:n, 0:nh], in_=rk[:n, 0:nh], func=AF.Exp, scale=-0.5), reads=["rk"], writes=["rk"])
            A("dve", lambda e: e.tensor_tensor(out=kn[:n, :].rearrange("p (h d) -> p h d", h=nh),
                                               in0=pf[bank][:n, :].rearrange("p (h d) -> p h d", h=nh),
                                               in1=rk[:n, 0:nh].unsqueeze(2).to_broadcast([n, nh, hd]), op=ALU.mult),
              reads=[Rp[bank], "rk"], writes=["kn"])
            if hd == 64:
                in1 = gb[:n, :].unsqueeze(1).to_broadcast([n, nh, hd])
                A("dve", lambda e: e.tensor_tensor(out=out_ap.rearrange("p (h d) -> p h d", h=nh),
                                                   in0=kn[:n, :].rearrange("p (h d) -> p h d", h=nh), in1=in1, op=ALU.mult),
                  reads=["kn"], writes=[out_res])
            else:
                A("dve", lambda e: e.tensor_tensor(out=out_ap, in0=kn[:n, :], in1=gb[:n, :], op=ALU.mult),
                  reads=["kn"], writes=[out_res])

        def logf_of(bank, n):
            A("dve", lambda e: e.tensor_tensor(out=zf[:n, :], in0=pf[bank][:n, 0:8], in1=bf_bc[:n, :], op=ALU.add),
              reads=[Rp[bank], "bf_bc"], writes=["zf"])
            A("act", lambda e: e.activation(out=zf[:n, :], in_=zf[:n, :], func=AF.Exp, scale=-1.0), reads=["zf"], writes=["zf"])
            A("act", lambda e: e.activation(out=zf[:n, :], in_=zf[:n, :], func=AF.Ln, bias=1.0), reads=["zf"], writes=["zf"])
            A("dve", lambda e: e.tensor_scalar(out=lf[:n, :], in0=zf[:n, :], scalar1=-1.0, scalar2=None, op0=ALU.mult),
              reads=["zf"], writes=["lf"])

        def silu_from(bank, n, out_ap, out_res):
            A("act", lambda e: e.activation(out=et[:n, :], in_=pf[bank][:n, :], func=AF.Exp, scale=-1.0), reads=[Rp[bank]], writes=["et"])
            A("dve", lambda e: e.tensor_scalar(out=et[:n, :], in0=et[:n, :], scalar1=1.0, scalar2=None, op0=ALU.add),
              reads=["et"], writes=["et"])
            A("dve", lambda e: e.reciprocal(out=et[:n, :], in_=et[:n, :]), reads=["et"], writes=["et"])
            A("dve", lambda e: e.tensor_tensor(out=out_ap, in0=pf[bank][:n, :], in1=et[:n, :], op=ALU.mult),
              reads=[Rp[bank], "et"], writes=[out_res])

        def rest_proj(n, ga_out, ga_res, m_out, m_res, sample):
            proj(3, C_ZA, 512, n)
            silu_from(3, n, ga_out, ga_res)
            proj(4, C_U, 512, n)
            A("act", lambda e: e.activation(out=ub[:n, :], in_=pf[4][:n, :], func=AF.Copy), reads=[Rp[4]], writes=["ub"])
            proj(3, C_GV, 512, n)
            if sample:
                headnorm(3, n, 4, 128, gv_bc, gvnf[:n, :], "gvnf")
                A("sp", lambda e: e.dma_start(out=gv_s, in_=gvnf[:n, :]), reads=["gvnf"], dma=True)
            else:
                headnorm(3, n, 4, 128, gv_bc, gvn[:n, :], "gvn")
                for g in range(4):
                    A("pe", lambda e, g=g: e.matmul(pf[5][:n, g * 128:(g + 1) * 128], lhsT=wsT[:, g, :],
                                                    rhs=gvn[:, g * 128:(g + 1) * 128], start=True, stop=True),
                      reads=["wsT", "gvn"], writes=[Rp[5]])
            proj(4, C_ZC, 512, n)
            silu_from(4, n, gc[:n, :], "gc")
            for g in range(4):
                sl = slice(g * 128, (g + 1) * 128)
                if sample:
                    A("dve", lambda e, g=g, sl=sl: e.tensor_scalar(out=tmpm[:n, sl], in0=gvnf[:n, sl], scalar1=ws00[:n, g:g + 1],
                                                                   scalar2=bs0[:n, g:g + 1], op0=ALU.mult, op1=ALU.add),
                      reads=["gvnf", "ws00", "bs0"], writes=["tmpm"])
                    A("dve", lambda e, sl=sl: e.tensor_tensor(out=tmpm[:n, sl], in0=tmpm[:n, sl], in1=ub[:n, sl], op=ALU.mult),
                      reads=["tmpm", "ub"], writes=["tmpm"])
                else:
                    A("dve", lambda e, g=g, sl=sl: e.scalar_tensor_tensor(out=tmpm[:n, sl], in0=pf[5][:n, sl], scalar=bs_t[:n, g:g + 1],
                                                                          in1=ub[:n, sl], op0=ALU.add, op1=ALU.mult),
                      reads=[Rp[5], "bs_t", "ub"], writes=["tmpm"])
            A("dve", lambda e: e.tensor_tensor(out=m_out, in0=tmpm[:n, :], in1=gc[:n, :], op=ALU.mult),
              reads=["tmpm", "gc"], writes=[m_res])

        n = NS
        norm_T(x_s, n)
        proj(1, C_Q, 512, n)
        headnorm(1, n, 8, 64, gq_bc, qn_s[:n, :], "qn_s")
        A("sp", lambda e: e.dma_start(out=q_scr, in_=qn_s[:n, :]), reads=["qn_s"], writes=["q_scr"], dma=True)
        proj(1, C_K, 512, n)
        headnorm(1, n, 8, 64, gk_bc, kn_s[:n, :], "kn_s")
        A("sp", lambda e: e.dma_start(out=k_s, in_=kn_s[:n, :]), reads=["kn_s"], dma=True)
        proj(2, C_V, 512, n)
        A("act", lambda e: e.activation(out=vf_s[:n, :], in_=pf[2][:n, :], func=AF.Copy), reads=[Rp[2]], writes=["vf_s"])
        A("sp", lambda e: e.dma_start(out=v_s, in_=vf_s[:n, :]), reads=["vf_s"], dma=True)
        proj(2, C_F, 8, n)
        logf_of(2, n)
        A("dve", lambda e: e.tensor_copy(out=lfs[:n, :], in_=lf[:n, :]), reads=["lf"], writes=["lfs"])
        A("sp", lambda e: e.dma_start(out=l_s, in_=lfs[:n, :]), reads=["lfs"], dma=True)
        rest_proj(n, ga_s[:n, :], "ga_s", m_s[:n, :], "m_s", True)

        ck2 = ck.rearrange("(r t) n -> r (t n)", t=8)
        cv2 = cv.rearrange("(r t) n -> r (t n)", t=8)
        cl2 = cl.rearrange("(r t) n -> r (t n)", t=8)
        A("sp", lambda e: e.dma_start(out=ptb, in_=pt), writes=["ptb"], dma=True)
        A("dve", lambda e: e.tensor_scalar(out=idx, in0=ptb, scalar1=16.0, scalar2=iota, op0=ALU.mult, op1=ALU.add),
          reads=["ptb", "cst"], writes=["idx"])
        A("dve", lambda e: e.memset(arQ[:, 16384:20992].bitcast(F32), 0.0), writes=["Lall"])
        for col in range(2 * NS):
            A("pool", lambda e, col=col: e.indirect_dma_start(
                out=arQ[:, 16384:20992].bitcast(F32)[:, col * 72:col * 72 + 64], out_offset=None, in_=cl2,
                in_offset=bass.IndirectOffsetOnAxis(ap=idx[:, col:col + 1], axis=0)),
              reads=["idx"], writes=["Lall"], dma=True)

        def compute_E():
            A("dve", lambda e: e.tensor_reduce(out=tot_t[:, :].rearrange("p (c h) -> p c h", h=8),
                                               in_=Lh[:, :, 0:8, :].rearrange("p c t h -> p c h t"), axis=AX.X, op=ALU.add),
              reads=["Lall"], writes=["tot_t"])
            A("pe", lambda e: e.matmul(pf[3][:, 0:256], lhsT=ltri, rhs=tot_t[:, :], start=True, stop=True),
              reads=["tot_t", "cst"], writes=[Rp[3]])
            A("pe", lambda e: e.matmul(pf[4][:, 0:128], lhsT=ones,
                                       rhs=tot_t[:, :].rearrange("p (b f h) -> p b f h", f=2, h=8)[:, :, 1, :], start=True, stop=True),
              reads=["tot_t", "cst"], writes=[Rp[4]])
            A("dve", lambda e: e.tensor_copy(out=tsum[:, :], in_=pf[3][:, 0:256]), reads=[Rp[3]], writes=["tsum"])
            A("dve", lambda e: e.tensor_tensor(out=tsum[:, :].rearrange("p (b f h) -> p b f h", f=2, h=8)[:, :, 0, :],
                                               in0=tsum[:, :].rearrange("p (b f h) -> p b f h", f=2, h=8)[:, :, 0, :],
                                               in1=pf[4][:, 0:128].rearrange("p (b h) -> p b h", h=8), op=ALU.add),
              reads=[Rp[4], "tsum"], writes=["tsum"])
            for t in range(6, -1, -1):
                A("dve", lambda e, t=t: e.tensor_tensor(out=Lh[:, :, t, :], in0=Lh[:, :, t, :], in1=Lh[:, :, t + 1, :], op=ALU.add),
                  reads=["Lall"], writes=["Lall"])
            for t in range(1, 9):
                A("dve", lambda e, t=t: e.tensor_tensor(out=Lh[:, :, t, :], in0=Lh[:, :, t, :],
                                                        in1=tsum[:, :].rearrange("p (c h) -> p c h", h=8), op=ALU.add),
                  reads=["Lall", "tsum"], writes=["Lall"])

        def gather_half(b, hf):
            col = b * 2 + hf
            A("pool", lambda e: e.indirect_dma_start(
                out=Kb.rearrange("p g n -> p (g n)"), out_offset=None, in_=ck2,
                in_offset=bass.IndirectOffsetOnAxis(ap=idx[:, col:col + 1], axis=0)),
              reads=["idx"], writes=["Kb"], dma=True)
            A("pool", lambda e: e.indirect_dma_start(
                out=Vb.rearrange("p g n -> p (g n)"), out_offset=None, in_=cv2,
                in_offset=bass.IndirectOffsetOnAxis(ap=idx[:, col:col + 1], axis=0)),
              reads=["idx"], writes=["Vb"], dma=True)

        def sample_half(b, hf):
            hs = slice(hf * 64, hf * 64 + 64)
            if hf == 0:
                A("sp", lambda e: e.dma_start(out=qb[:, :], in_=q_scr[b].partition_broadcast(128)), reads=["q_scr"], writes=["qb"], dma=True)
            A("dve", lambda e: e.tensor_tensor(out=Kb, in0=Kb, in1=qb[:, :].unsqueeze(1).to_broadcast([128, 8, 512]), op=ALU.mult),
              reads=["Kb", "qb"], writes=["Kb"])
            A("dve", lambda e: e.tensor_reduce(out=s_t[:, hs], in_=Kb.rearrange("p g (h d) -> p (g h) d", h=8), axis=AX.X, op=ALU.add),
              reads=["Kb"], writes=["s_t"])
            A("dve", lambda e: e.scalar_tensor_tensor(out=s_t[:, hs].rearrange("p (g h) -> p g h", g=8), in0=s_t[:, hs].rearrange("p (g h) -> p g h", g=8),
                                                      scalar=SCALE, in1=Lh[:, b * 2 + hf, 1:9, :], op0=ALU.mult, op1=ALU.add),
              reads=["s_t", "Lall"], writes=["s_t"])
            A("act", lambda e: e.activation(out=p_t[:, hs], in_=s_t[:, hs], func=AF.Exp), reads=["s_t"], writes=["p_t"])
            A("act", lambda e: e.activation(out=Vbb, in_=Vb, func=AF.Copy), reads=["Vb"], writes=["Vbb"])
            for g in range(8):
                gg = hf * 8 + g
                A("pe", lambda e, g=g, gg=gg: e.matmul(pf[5][0:8, :], lhsT=p_t[:, gg * 8:(gg + 1) * 8], rhs=Vbb[:, g, :],
                                                       start=(gg == 0), stop=(gg == NPG - 1)),
                  reads=["p_t", "Vbb"], writes=[Rp[5]])
            if hf == 0:
                return
            A("dve", lambda e: e.tensor_reduce(out=psm[:, :], in_=p_t[:, :].rearrange("p (g h) -> p h g", g=NPG), axis=AX.X, op=ALU.add),
              reads=["p_t"], writes=["psm"])
            A("dve", lambda e: e.tensor_tensor(out=mo[:, :], in0=pf[5][0:8, :], in1=cst[0:8, K_BM:K_BM + 512], op=ALU.mult),
              reads=[Rp[5], "cst"], writes=["mo"])
            A("pe", lambda e: e.matmul(pf[3][0:NS, :], lhsT=cst[0:8, K_EB + b * 16:K_EB + (b + 1) * 16], rhs=mo[:, :],
                                       start=(b == 0), stop=(b == NS - 1)),
              reads=["mo", "cst"], writes=[Rp[3]])
            A("pe", lambda e: e.matmul(pf[4][0:NS, 0:8], lhsT=cst[:, K_CS + b * 16:K_CS + (b + 1) * 16], rhs=psm[:, :],
                                       start=(b == 0), stop=(b == NS - 1)),
              reads=["psm", "cst"], writes=[Rp[4]])

        xTs = ((xT, "xT"), (xT2, "xT2"))

        def pA_h1(I):
            xb, xr = xTs[I % 2]
            norm_T(x_all[I * 128:(I + 1) * 128, :], 128, xb, xr, "act")

        def pA_k(I):
            xb, xr = xTs[I % 2]
            proj(1, C_K, 512, 128, xb, xr)
            headnorm(1, 128, 8, 64, gk_bc, knb[:, :], "knb")
            for hp in range(4):
                A("pe", lambda e, hp=hp: e.transpose(out=pbf(6)[:, hp * 128:(hp + 1) * 128], in_=knb[:, hp * 128:(hp + 1) * 128],
                                                     identity=identb[:, :]),
                  reads=["knb", "identb"], writes=[Rp[6]])
            r4 = I % 4
            sl = r4 // 2
            A("act", lambda e: e.activation(out=kT_stage[:, :, r4 * 128:(r4 + 1) * 128],
                                            in_=pbf(6)[:, 0:512].rearrange("p (h t) -> p h t", h=4), func=AF.Copy),
              reads=[Rp[6]], writes=["kT_stage%d" % sl])
            if r4 % 2 == 1:
                t0 = (I - 1) * 128
                A("sp", lambda e: e.dma_start(out=kT_scr[:, :, t0:t0 + 256].rearrange("h p t -> p h t"),
                                              in_=kT_stage[:, :, sl * 256:(sl + 1) * 256]),
                  reads=["kT_stage%d" % sl], writes=["kT_scr"], dma=True)

        def pA_vf(I):
            xb, xr = xTs[I % 2]
            r4 = I % 4
            proj(2, C_V, 512, 128, xb, xr)
            sl = r4 // 2
            A("act", lambda e: e.activation(out=v_stage[:, sl, :, r4 % 2, :, 0:64],
                                            in_=pf[2][:, :].rearrange("p (q h d) -> p q h d", q=4, h=2), func=AF.Copy),
              reads=[Rp[2]], writes=["v_stage%d" % sl])
            proj(7, C_F, 8, 128, xb, xr)
            logf_of(7, 128)
            A("pe", lambda e: e.matmul(pf[7][:, 8:16], lhsT=utri, rhs=lf[:, :], start=True, stop=True), reads=["lf", "cst"], writes=[Rp[7]])
            A("pe", lambda e: e.matmul(pf[7][:, 16:24], lhsT=ones, rhs=lf[:, :], start=True, stop=True), reads=["lf", "cst"], writes=[Rp[7]])
            A("dve", lambda e: e.tensor_copy(out=carry[:, I, :], in_=runc[:, :]), reads=["runc"], writes=["carry"])
            A("dve", lambda e: e.scalar_tensor_tensor(out=negc[:, I, :], in0=pf[7][:, 8:16], scalar=-1.0, in1=runc[:, :],
                                                      op0=ALU.mult, op1=ALU.subtract),
              reads=[Rp[7], "runc"], writes=["negc"])
            A("dve", lambda e: e.tensor_tensor(out=runc[:, :], in0=runc[:, :], in1=pf[7][:, 16:24], op=ALU.add),
              reads=[Rp[7], "runc"], writes=["runc"])
            if r4 % 2 == 1:
                A("sp", lambda e: e.dma_start(
                    out=v_scr[:, :, I - 1:I + 1, :].rearrange("q p a e -> p q (a e)"),
                    in_=v_stage[:, sl, :, :, :, :].rearrange("p q a h d -> p q (a h d)")),
                  reads=["v_stage%d" % sl], writes=["v_scr"], dma=True)

        cw = sb("cw", [128, 32], F32)
        cwo = sb("cwo", [128, 8], F32)
        fz_t = sb("fz_t", [128, 4], F32)

        def fence(names):
            A("dve", lambda e: e.memset(fz_t[:, :], 0.0), writes=names)

        fence(["et", "ub", "gc", "tmpm", "kT_stage0", "kT_stage1", "v_stage0", "v_stage1"])
        A("dve", lambda e: e.memset(arT[:, 1024:2064].bitcast(BF16), 1.0), writes=["v_stage0", "v_stage1"])
        sched = {}
        for hs_ in range(2 * NS):
            sched.setdefault(15 + (hs_ * 3) // 2, []).append(hs_)
        gather_half(0, 0)
        pA_h1(0)
        for I in range(NT):
            chains = [record(pA_k, I), record(pA_vf, I)]
            if I + 1 < NT:
                chains.insert(0, record(pA_h1, I + 1))
            for hs_ in sched.get(I, []):
                chains.append(record(sample_half, hs_ // 2, hs_ % 2))
            interleave(*chains)
            for hs_ in sched.get(I, []):
                if hs_ + 1 < 2 * NS:
                    gather_half((hs_ + 1) // 2, (hs_ + 1) % 2)
            if I == 13:
                compute_E()

        n = NS
        A("sp", lambda e: e.dma_start(out=xt[:n, :], in_=x_s), writes=["xt"], dma=True)
        A("dve", lambda e: e.tensor_tensor(out=sm1[:, :], in0=qn_s[:, :], in1=kn_s[:, :], op=ALU.mult), reads=["qn_s", "kn_s"], writes=["sm1"])
        A("dve", lambda e: e.tensor_reduce(out=sm2[:, :], in_=sm1[:, :].rearrange("p (h d) -> p h d", h=8), axis=AX.X, op=ALU.add),
          reads=["sm1"], writes=["sm2"])
        A("dve", lambda e: e.scalar_tensor_tensor(out=sm2[:, :], in0=sm2[:, :], scalar=SCALE, in1=lfs[:n, :], op0=ALU.mult, op1=ALU.subtract),
          reads=["sm2", "lfs"], writes=["sm2"])
        A("act", lambda e: e.activation(out=sm2[:, :], in_=sm2[:, :], func=AF.Exp), reads=["sm2"], writes=["sm2"])
        A("dve", lambda e: e.tensor_tensor(out=sm1[:, :].rearrange("p (h d) -> p h d", h=8), in0=vf_s[:, :].rearrange("p (h d) -> p h d", h=8),
                                           in1=sm2[:, :].unsqueeze(2).to_broadcast([n, 8, 64]), op=ALU.mult),
          reads=["vf_s", "sm2"], writes=["sm1"])
        A("dve", lambda e: e.tensor_tensor(out=sm1[:, :], in0=sm1[:, :], in1=pf[3][0:n, :], op=ALU.add), reads=["sm1", Rp[3]], writes=["sm1"])
        A("dve", lambda e: e.tensor_tensor(out=sm3[:, :], in0=pf[4][0:n, 0:8], in1=sm2[:, :], op=ALU.add), reads=[Rp[4], "sm2"], writes=["sm3"])
        A("dve", lambda e: e.reciprocal(out=sm3[:, :], in_=sm3[:, :]), reads=["sm3"], writes=["sm3"])
        A("dve", lambda e: e.tensor_tensor(out=sm1[:, :].rearrange("p (h d) -> p h d", h=8), in0=sm1[:, :].rearrange("p (h d) -> p h d", h=8),
                                           in1=sm3[:, :].unsqueeze(2).to_broadcast([n, 8, 64]), op=ALU.mult),
          reads=["sm1", "sm3"], writes=["sm1"])
        A("dve", lambda e: e.tensor_tensor(out=mrg_s[:, 0:512], in0=sm1[:, :], in1=ga_s[:, :], op=ALU.mult), reads=["sm1", "ga_s"], writes=["mrg_s"])
        A("dve", lambda e: e.tensor_copy(out=mrg_s[:, 512:1024], in_=m_s[:, :]), reads=["m_s"], writes=["mrg_s"])
        for c in range(8):
            A("pe", lambda e, c=c: e.transpose(out=pbf(0)[:, c * 128:c * 128 + n], in_=mrg_s[:n, c * 128:(c + 1) * 128], identity=identb[:n, :n]),
              reads=["mrg_s", "identb"], writes=[Rp[0]])
        A("act", lambda e: e.activation(out=mT_s[:, :, :], in_=pbf(0).rearrange("p (c t) -> p c t", c=8)[:, :, 0:n], func=AF.Copy),
          reads=[Rp[0]], writes=["mT_s"])
        for half in range(2):
            for c in range(8):
                A("pe", lambda e, c=c, half=half: e.matmul(pf[1 + half][:n, :], lhsT=mT_s[:, c, :], rhs=wout[:, c, half * 512:(half + 1) * 512],
                                                           start=(c == 0), stop=(c == 7)),
                  reads=["mT_s", "wout"], writes=[Rp[1 + half]])
            A("dve", lambda e, half=half: e.tensor_tensor(out=xt[:n, half * 512:(half + 1) * 512], in0=pf[1 + half][:n, :],
                                                          in1=xt[:n, half * 512:(half + 1) * 512], op=ALU.add),
              reads=[Rp[1 + half], "xt"], writes=["xt"])
        A("sp", lambda e: e.dma_start(out=y_s, in_=xt[:n, :]), reads=["xt"], dma=True)

        fence(["Kb", "Vb", "Vbb", "Lall", "ptb", "idx", "qT_st", "ga_st", "m_st", "et", "ub", "gc", "tmpm", "kT_stage0", "kT_stage1", "v_stage0", "v_stage1", "xT", "xT2"])
        def pB_h1(i):
            xb, xr = xTs[i % 2]
            norm_T(x_own[i * 128:(i + 1) * 128, :], 128, xb, xr, "act")

        def pB_X(i):
            xb, xr = xTs[i % 2]
            rows = slice(i * 128, (i + 1) * 128)
            proj(1, C_Q, 512, 128, xb, xr)
            headnorm(1, 128, 8, 64, gq_bc, knb[:, :], "knb")
            for hp in range(4):
                A("pe", lambda e, hp=hp: e.transpose(out=pbf(6)[:, hp * 128:(hp + 1) * 128], in_=knb[:, hp * 128:(hp + 1) * 128],
                                                     identity=identb[:, :]),
                  reads=["knb", "identb"], writes=[Rp[6]])
            A("act", lambda e: e.activation(out=qT_st[:, :, i * 128:(i + 1) * 128],
                                            in_=pbf(6)[:, 0:512].rearrange("p (h t) -> p h t", h=4), func=AF.Copy),
              reads=[Rp[6]], writes=["qT_st"])
            proj(1, C_K, 512, 128, xb, xr)
            headnorm(1, 128, 8, 64, gk_bc, kf[:, :], "kf")
            A("sp", lambda e: e.dma_start(out=k_own[rows, :], in_=kf[:, :]), reads=["kf"], dma=True)
            proj(3, C_GV, 512, 128, xb, xr)
            headnorm(3, 128, 4, 128, gv_bc, gvn[:, :], "gvn")
            for g in range(4):
                A("pe", lambda e, g=g: e.matmul(pf[5][:, g * 128:(g + 1) * 128], lhsT=wsT[:, g, :],
                                                rhs=gvn[:, g * 128:(g + 1) * 128], start=True, stop=True),
                  reads=["wsT", "gvn"], writes=[Rp[5]])

        def pB_Y(i):
            xb, xr = xTs[i % 2]
            rows = slice(i * 128, (i + 1) * 128)
            proj(2, C_V, 512, 128, xb, xr)
            A("act", lambda e: e.activation(out=vf[:, :], in_=pf[2][:, :], func=AF.Copy), reads=[Rp[2]], writes=["vf"])
            A("sp", lambda e: e.dma_start(out=v_own[rows, :], in_=vf[:, :]), reads=["vf"], dma=True)
            proj(7, C_F, 8, 128, xb, xr)
            logf_of(7, 128)
            A("sp", lambda e: e.dma_start(out=l_own[rows, :], in_=lf[:, :]), reads=["lf"], dma=True)
            A("pe", lambda e: e.matmul(pf[7][:, 8:16], lhsT=utri, rhs=lf[:, :], start=True, stop=True), reads=["lf", "cst"], writes=[Rp[7]])
            A("dve", lambda e: e.tensor_tensor(out=cw[:, :].rearrange("p (h w) -> p h w", h=8),
                                               in0=carry[:, 4 * i:4 * i + 4, :].rearrange("p w h -> p h w"),
                                               in1=oh4.unsqueeze(1).to_broadcast([128, 8, 4]), op=ALU.mult),
              reads=["carry", "cst"], writes=["cw"])
            A("dve", lambda e: e.tensor_reduce(out=cwo[:, :], in_=cw[:, :].rearrange("p (h w) -> p h w", h=8), axis=AX.X, op=ALU.add),
              reads=["cw"], writes=["cwo"])
            A("dve", lambda e: e.tensor_tensor(out=c_own[:, i, :], in0=pf[7][:, 8:16], in1=cwo[:, :], op=ALU.add),
              reads=[Rp[7], "cwo"], writes=["c_own"])

        def pB_Z(i):
            xb, xr = xTs[i % 2]
            proj(4, C_ZA, 512, 128, xb, xr)
            silu_from(4, 128, ga_st[:, i, :], "ga_st")
            proj(4, C_U, 512, 128, xb, xr)
            A("act", lambda e: e.activation(out=ub[:, :], in_=pf[4][:, :], func=AF.Copy), reads=[Rp[4]], writes=["ub"])
            proj(4, C_ZC, 512, 128, xb, xr)
            silu_from(4, 128, gc[:, :], "gc")

        def pB_tail(i):
            for g in range(4):
                sl = slice(g * 128, (g + 1) * 128)
                A("dve", lambda e, g=g, sl=sl: e.scalar_tensor_tensor(out=tmpm[:, sl], in0=pf[5][:, sl], scalar=bs_t[:, g:g + 1],
                                                                      in1=ub[:, sl], op0=ALU.add, op1=ALU.mult),
                  reads=[Rp[5], "bs_t", "ub"], writes=["tmpm"])
            A("dve", lambda e: e.tensor_tensor(out=m_st[:, i, :], in0=tmpm[:, :], in1=gc[:, :], op=ALU.mult),
              reads=["tmpm", "gc"], writes=["m_st"])

        pB_h1(0)
        for i in range(NOWN):
            chains = [record(pB_X, i), record(pB_Y, i), record(pB_Z, i)]
            if i + 1 < NOWN:
                chains.insert(0, record(pB_h1, i + 1))
            interleave(*chains)
            pB_tail(i)

        fence(["Wb", "kTp", "vP", "pTb0", "pTb1", "pTb2", "pTb3", "o_sb", "a_rows", "rcp_t", "mT", "yt", "qTh"])
        A("dve", lambda e: e.memset(kTp[64:65, :], 1.0), writes=["kTp"])
        for i0 in range(0, NOWN, 4):
            for k in range(4):
                A("pe", lambda e, i0=i0, k=k: e.transpose(out=pf[7][0:8, k * 128:(k + 1) * 128], in_=c_own[:, i0 + k, :], identity=identf),
                  reads=["c_own", "cst"], writes=[Rp[7]])
            A("act", lambda e, i0=i0: e.activation(out=a_rows[0:8, i0 * 128:i0 * 128 + 512], in_=pf[7][0:8, :], func=AF.Copy, scale=1.0 / SCALE),
              reads=[Rp[7]], writes=["a_rows"])
        SB = (0, 1, 7)
        LA = 2
        its = []
        for h in range(8):
            for qg in range(4):
                nJ = 16 * qg + 16
                for J in range(nJ):
                    its.append((h, qg, J, nJ))

        def emit_qk(n):
            h, qg, J, nJ = its[n]
            hp, half = h // 2, h % 2
            rws = slice(half * 64, half * 64 + 64)
            if qg == 0 and J == 0:
                A("sp", lambda e: e.dma_start(out=kTp[0:64, :], in_=kT_scr[hp, half * 64:(half + 1) * 64, :]),
                  reads=["kT_scr"], writes=["kTp"], dma=True)
                A("sp", lambda e: e.dma_start(out=qTh[0:64, :], in_=qT_st[half * 64:(half + 1) * 64, hp, :]),
                  reads=["qT_st"], writes=["qTh"], dma=True)
                A("sp", lambda e: e.dma_start(out=qTh[64:65, :], in_=a_rows[h:h + 1, :]),
                  reads=["a_rows"], writes=["qTh"], dma=True)
                if half == 0:
                    A("sp", lambda e, hp=hp: e.dma_start(out=vP, in_=v_scr[hp]), reads=["v_scr"], writes=["vP"], dma=True)
            kmin = max(0, (J - 16 * qg) // 4) if J >= 16 * qg else 0
            c0 = kmin * 128
            q0 = qg * 512 + c0
            q1 = qg * 512 + 512
            sbk = SB[n % 3]
            pb = n % 4
            win = J >= 16 * qg
            A("pe", lambda e: e.matmul(pf[sbk][:, c0:512], lhsT=kTp[0:65, J * 128:(J + 1) * 128], rhs=qTh[0:65, q0:q1],
                                       start=True, stop=(not win)),
              reads=["kTp", "qTh"], writes=[Rp[sbk]])
            if win:
                w = (J - 16 * qg) % 4
                A("pe", lambda e: e.matmul(pf[sbk][:, c0:c0 + 128], lhsT=identb[:, :], rhs=maskb[:, w, :], start=False, stop=True),
                  reads=["identb", "maskb"], writes=[Rp[sbk]])
            A("act", lambda e: e.activation(out=pTb[pb][:, c0:512], in_=pf[sbk][:, c0:512], func=AF.Exp, scale=SCALE,
                                            bias=negc[:, J, h:h + 1]),
              reads=[Rp[sbk], "negc"], writes=["pTb%d" % pb])

        def emit_pv(n):
            h, qg, J, nJ = its[n]
            half = h % 2
            kmin = max(0, (J - 16 * qg) // 4) if J >= 16 * qg else 0
            c0 = kmin * 128
            pb = n % 4
            A("pe", lambda e: e.matmul(pf[2 + qg][0:65, c0:512], lhsT=vP[:, J, half * 65:(half + 1) * 65], rhs=pTb[pb][:, c0:512],
                                       start=(J == 0), stop=(J == nJ - 1)),
              reads=["vP", "pTb%d" % pb], writes=[Rp[2 + qg]])
            if J == nJ - 1:
                A("act", lambda e: e.activation(out=o_sb[0:65, :], in_=pf[2 + qg][0:65, :], func=AF.Copy),
                  reads=[Rp[2 + qg]], writes=["o_sb"])
                for k in range(4):
                    A("pe", lambda e, k=k: e.transpose(out=pf[6][:, k * 65:k * 65 + 65], in_=o_sb[0:65, k * 128:(k + 1) * 128],
                                                       identity=identf[0:65, 0:65]),
                      reads=["o_sb", "cst"], writes=[Rp[6]])
                A("dve", lambda e: e.reciprocal(out=rcp_t[:, 0:4], in_=pf[6][:, 0:260].rearrange("p (k e) -> p k e", k=4)[:, :, 64]),
                  reads=[Rp[6]], writes=["rcp_t"])
                for k in range(4):
                    A("dve", lambda e, k=k: e.scalar_tensor_tensor(
                        out=ga_st[:, 4 * qg + k, h * 64:(h + 1) * 64], in0=pf[6][:, k * 65:k * 65 + 64], scalar=rcp_t[:, k:k + 1],
                        in1=ga_st[:, 4 * qg + k, h * 64:(h + 1) * 64], op0=ALU.mult, op1=ALU.mult),
                      reads=[Rp[6], "rcp_t", "ga_st"], writes=["ga_st"])

        pend = []
        for nn in range(len(its)):
            h_, qg_, J_, _ = its[nn]
            if h_ % 2 == 0 and qg_ == 0 and J_ == 0:
                for m in pend:
                    emit_pv(m)
                pend = []
            emit_qk(nn)
            pend.append(nn)
            if len(pend) > LA:
                emit_pv(pend.pop(0))
        for m in pend:
            emit_pv(m)

        for i in range(NOWN):
            rows = slice(i * 128, (i + 1) * 128)
            for c in range(8):
                src = ga_st[:, i, c * 128:(c + 1) * 128] if c < 4 else m_st[:, i, (c - 4) * 128:(c - 3) * 128]
                A("pe", lambda e, c=c, src=src: e.transpose(out=pbf(6)[:, c * 128:(c + 1) * 128], in_=src, identity=identb[:, :]),
                  reads=["ga_st", "m_st", "identb"], writes=[Rp[6]])
            A("act", lambda e: e.activation(out=mT[:, :, :], in_=pbf(6).rearrange("p (c t) -> p c t", c=8), func=AF.Copy),
              reads=[Rp[6]], writes=["mT"])
            A("sp", lambda e, rows=rows: e.dma_start(out=xt[:, :], in_=x_own[rows, :]), writes=["xt"], dma=True)
            for half in range(2):
                for c in range(8):
                    A("pe", lambda e, c=c, half=half: e.matmul(pf[half][:, :], lhsT=mT[:, c, :], rhs=wout[:, c, half * 512:(half + 1) * 512],
                                                               start=(c == 0), stop=(c == 7)),
                      reads=["mT", "wout"], writes=[Rp[half]])
                A("dve", lambda e, half=half: e.tensor_tensor(out=yt[:, half * 512:(half + 1) * 512], in0=pf[half][:, :],
                                                              in1=xt[:, half * 512:(half + 1) * 512], op=ALU.add),
                  reads=[Rp[half], "xt"], writes=["yt"])
            A("sp", lambda e, rows=rows: e.dma_start(out=y_own[rows, :], in_=yt[:, :]), reads=["yt"], dma=True)

        P.emit(nc)
    return nc


def _consts(j):
    c = np.zeros((128, K_END), np.float32)
    p = np.arange(128)
    c[:, K_ID:K_ID + 128] = np.eye(128)
    c[:, K_U:K_U + 128] = (p[:, None] <= p[None, :])
    c[:, K_ONE:K_ONE + 128] = 1.0
    c[:, K_L:K_L + 128] = (p[:, None] > p[None, :])
    for w in range(4):
        if w < j:
            m = np.ones((128, 128))
        elif w == j:
            m = (p[:, None] <= p[None, :])
        else:
            m = np.zeros((128, 128))
        c[:, K_MW + w * 128:K_MW + (w + 1) * 128] = m
    c[:, K_IOTA] = p % 16
    c[:, K_OH4 + j] = 1.0
    for b in range(16):
        c[0:8, K_EB + b * 16 + b] = 1.0
        c[:, K_CS + b * 16 + b] = 1.0
    for h in range(8):
        c[h, K_BM + h * 64:K_BM + (h + 1) * 64] = 1.0
        c[h, K_ID8 + h] = 1.0
        c[h, K_OHT + h * 128:K_OHT + (h + 1) * 128] = 1.0
    return c


def kernel(x_prompt, x_sample, cache_k, cache_v, cache_logf, page_table, g_norm, w_in,
           b_f, g_q, g_k, g_v, w_s, b_s, w_out):
    f = lambda a: np.ascontiguousarray(np.asarray(a))
    x_prompt = f(x_prompt); x_sample = f(x_sample)
    ckr = f(cache_k).reshape(2560 * 128, 512)
    cvr = f(cache_v).reshape(2560 * 128, 512)
    clr = f(cache_logf).reshape(2560 * 128, 8)
    ptab = f(page_table).astype(np.int32)
    nc = build_nc()
    in_maps = []
    for c in range(8):
        b, j = c // 4, c % 4
        xa = x_prompt[b]
        xo = np.ascontiguousarray(xa.reshape(16, 4, 128, D)[:, j].reshape(NOWN * 128, D))
        in_maps.append(dict(
            x_all=xa, x_own=xo, x_s=np.ascontiguousarray(x_sample[16 * c:16 * c + 16, 0, :]),
            ck=ckr, cv=cvr, cl=clr,
            pt=np.ascontiguousarray(ptab[16 * c:16 * c + 16].reshape(16, 2, 8)[:, :, np.arange(128) // 16].transpose(2, 0, 1).reshape(128, 32)),
            g_norm=f(g_norm)[0], w_in=f(w_in)[0], b_f=f(b_f)[0], g_q=f(g_q)[0], g_k=f(g_k)[0], g_v=f(g_v)[0],
            w_s=f(w_s)[0], b_s=f(b_s)[0], w_out=f(w_out)[0], consts=_consts(j)))
    res = run_bass_kernel_spmd(nc, in_maps, core_ids=list(range(8)))
    R = res.results
    yp = np.zeros((2, S, D), np.float32)
    kp = np.zeros((1, 2, S, 8, 64), np.float32)
    vp = np.zeros((1, 2, S, 8, 64), np.float32)
    lp = np.zeros((1, 2, S, 8), np.float32)
    ys = np.zeros((128, 1, D), np.float32)
    ks = np.zeros((1, 128, 1, 8, 64), np.float32)
    vs = np.zeros((1, 128, 1, 8, 64), np.float32)
    ls = np.zeros((1, 128, 1, 8), np.float32)
    gs = np.zeros((1, 128, 1, 4, 128), np.float32)
    for c in range(8):
        b, j = c // 4, c % 4
        r = R[c]
        yp[b].reshape(16, 4, 128, D)[:, j] = r["y_own"].reshape(16, 128, D)
        kp[0, b].reshape(16, 4, 128, 512)[:, j] = r["k_own"].reshape(16, 128, 512)
        vp[0, b].reshape(16, 4, 128, 512)[:, j] = r["v_own"].reshape(16, 128, 512)
        lp[0, b].reshape(16, 4, 128, 8)[:, j] = r["l_own"].reshape(16, 128, 8)
        sl = slice(16 * c, 16 * c + 16)
        ys[sl, 0] = r["y_s"]
        ks[0, sl, 0] = r["k_s"].reshape(16, 8, 64)
        vs[0, sl, 0] = r["v_s"].reshape(16, 8, 64)
        ls[0, sl, 0] = r["l_s"]
        gs[0, sl, 0] = r["gv_s"].reshape(16, 4, 128)
    return (yp, ys, kp, vp, lp, ks, vs, ls, gs)
```

```python
import contextlib
import numpy as np
import concourse.bass as bass
import concourse.mybir as mybir
from concourse.bass_utils import run_bass_kernel_spmd

F32 = mybir.dt.float32
BF16 = mybir.dt.bfloat16
I32 = mybir.dt.int32
ALU = mybir.AluOpType
AF = mybir.ActivationFunctionType
AX = mybir.AxisListType

D = 1024
DIN = 3592
S = 8192
NT = 64
NOWN = 16
NS = 16
NPG = 16
EPS = 1e-6
SCALE = 0.125
C_Q, C_K, C_V, C_F, C_ZA, C_U, C_GV, C_ZC = 0, 512, 1024, 1536, 1544, 2056, 2568, 3080

K_ID, K_U, K_ONE, K_L, K_MW, K_IOTA, K_OH4, K_EB, K_BM, K_ID8, K_OHT, K_CS, K_END = (
    0, 128, 256, 384, 512, 1024, 1025, 1029, 1285, 1797, 1805, 2829, 3085)

ENGS = ("pe", "act", "dve", "pool", "sp")


class Res:
    __slots__ = ("w", "r")

    def __init__(self):
        self.w = None
        self.r = []


class Op:
    __slots__ = ("eng", "fn", "deps", "signaled", "count", "dma", "sem", "prev_on_sem")

    def __init__(self, eng, fn, dma):
        self.eng = eng
        self.fn = fn
        self.dma = dma
        self.deps = []
        self.signaled = dma
        self.count = None
        self.sem = None
        self.prev_on_sem = None


class Prog:
    def __init__(self, n_dma_sems=16):
        self.q = {e: [] for e in ENGS}
        self.n_dma_sems = n_dma_sems
        self.all_dma = []

    def add(self, eng, fn, reads=(), writes=(), dma=False):
        op = Op(eng, fn, dma)
        deps = {}
        for r in reads:
            if r.w is not None:
                deps[id(r.w)] = (r.w, "raw")
        for w in writes:
            if w.w is not None and id(w.w) not in deps:
                deps[id(w.w)] = (w.w, "waw")
            for rr in w.r:
                if id(rr) not in deps:
                    deps[id(rr)] = (rr, "war")
        for d, kind in deps.values():
            if d is op:
                continue
            if not d.dma and not dma and d.eng == eng:
                if eng == "pe":
                    continue
                if kind != "raw":
                    continue
            d.signaled = True
            op.deps.append(d)
        for r in reads:
            r.r.append(op)
        for w in writes:
            w.w = op
            w.r = []
        self.q[eng].append(op)
        if dma:
            self.all_dma.append(op)
        return op

    def emit(self, nc):
        stack = contextlib.ExitStack()
        with stack:
            esem = {e: stack.enter_context(nc.semaphore("s_" + e)) for e in ENGS}
            dsem = {e: [stack.enter_context(nc.semaphore("d_%s%d" % (e, i))) for i in range(self.n_dma_sems)]
                    for e in ("sp", "act", "pool")}
            for e in ENGS:
                c = 0
                dcount = [0] * self.n_dma_sems
                dlast = [None] * self.n_dma_sems
                k = 0
                for op in self.q[e]:
                    if op.dma:
                        s = k % self.n_dma_sems
                        k += 1
                        dcount[s] += 16
                        op.sem = dsem[e][s]
                        op.count = dcount[s]
                        op.prev_on_sem = dlast[s]
                        dlast[s] = op
                    elif op.signaled:
                        c += 1
                        op.sem = esem[e]
                        op.count = c
            block = stack.enter_context(nc.Block())

            def run(e):
                def body(eng):
                    waited = {}

                    def wait_for(d):
                        key = id(d.sem)
                        if waited.get(key, 0) >= d.count:
                            return
                        eng.wait_ge(d.sem, d.count)
                        waited[key] = d.count

                    for op in self.q[e]:
                        for d in op.deps:
                            wait_for(d)
                        if op.dma and op.prev_on_sem is not None:
                            wait_for(op.prev_on_sem)
                        ins = op.fn(eng)
                        if op.dma:
                            ins.then_inc(op.sem, 16)
                        elif op.signaled:
                            ins.then_inc(op.sem, 1)
                    if e == "sp":
                        last = {}
                        for d in self.all_dma:
                            last[id(d.sem)] = d
                        for d in last.values():
                            wait_for(d)
                return body

            block.tensor(run("pe"))
            block.scalar(run("act"))
            block.vector(run("dve"))
            block.gpsimd(run("pool"))
            block.sync(run("sp"))


def build_nc():
    nc = bass.Bass("TRN2", target_bir_lowering=False)

    def din(name, shape, dt=F32):
        return nc.dram_tensor(name, shape, dt, kind="ExternalInput").ap()

    def dout(name, shape, dt=F32):
        return nc.dram_tensor(name, shape, dt, kind="ExternalOutput").ap()

    x_all = din("x_all", [S, D])
    x_own = din("x_own", [NOWN * 128, D])
    x_s = din("x_s", [NS, D])
    ck = din("ck", [2560 * 128, 512])
    cv = din("cv", [2560 * 128, 512])
    cl = din("cl", [2560 * 128, 8])
    pt = din("pt", [128, 2 * NS], I32)
    g_norm = din("g_norm", [D])
    w_in = din("w_in", [D, DIN])
    b_f = din("b_f", [8])
    g_q = din("g_q", [64])
    g_k = din("g_k", [64])
    g_v = din("g_v", [512])
    w_s = din("w_s", [4, 128, 128])
    b_s = din("b_s", [4, 128])
    w_out = din("w_out", [D, D])
    consts_d = din("consts", [128, K_END])

    y_own = dout("y_own", [NOWN * 128, D])
    k_own = dout("k_own", [NOWN * 128, 512])
    v_own = dout("v_own", [NOWN * 128, 512])
    l_own = dout("l_own", [NOWN * 128, 8])
    y_s = dout("y_s", [NS, D])
    k_s = dout("k_s", [NS, 512])
    v_s = dout("v_s", [NS, 512])
    l_s = dout("l_s", [NS, 8])
    gv_s = dout("gv_s", [NS, 512])

    kT_scr = nc.dram_tensor("kT_scr", [4, 128, S], BF16, kind="Internal").ap()
    v_scr = nc.dram_tensor("v_scr", [4, 128, NT, 130], BF16, kind="Internal").ap()
    q_scr = nc.dram_tensor("q_scr", [NS, 512], F32, kind="Internal").ap()

    P = Prog()
    st = contextlib.ExitStack()
    with st:
        def sb(name, shape, dt):
            return st.enter_context(nc.sbuf_tensor(name, shape, dt))

        cst = sb("cst", [128, K_END], F32)
        identf = cst[:, K_ID:K_ID + 128]
        utri = cst[:, K_U:K_U + 128]
        ones = cst[:, K_ONE:K_ONE + 128]
        ltri = cst[:, K_L:K_L + 128]
        iota = cst[:, K_IOTA:K_IOTA + 1]
        oh4 = cst[:, K_OH4:K_OH4 + 4]
        identb = sb("identb", [128, 128], BF16)
        maskb = sb("maskb", [128, 4, 128], BF16)
        ohTb = sb("ohTb", [8, 8, 128], BF16)
        wout = sb("wout", [128, 8, D], BF16)
        g_bc = sb("g_bc", [128, D], F32)
        gv_bc = sb("gv_bc", [128, 512], F32)
        gq_bc = sb("gq_bc", [128, 64], F32)
        gk_bc = sb("gk_bc", [128, 64], F32)
        bf_bc = sb("bf_bc", [128, 8], F32)
        bs_t = sb("bs_t", [128, 4], F32)
        ws00 = sb("ws00", [128, 4], F32)
        bs0 = sb("bs0", [128, 4], F32)
        wsT = sb("wsT", [128, 4, 128], BF16)
        negc = sb("negc", [128, NT, 8], F32)
        carry = sb("carry", [128, NT, 8], F32)
        runc = sb("runc", [128, 8], F32)
        c_own = sb("c_own", [128, NOWN, 8], F32)
        xt = sb("xt", [128, D], F32)
        xn = sb("xn", [128, D], BF16)
        junk = xn
        xT = sb("xT", [128, 8, 128], BF16)
        xT2 = sb("xT2", [128, 8, 128], BF16)
        ss = sb("ss", [128, 1], F32)
        rr = sb("rr", [128, 1], F32)
        sq = sb("sq", [128, 512], F32)
        ssq = sb("ssq", [128, 8], F32)
        rk = sb("rk", [128, 8], F32)
        kn = sb("kn", [128, 512], F32)
        kf = sb("kf", [128, 512], F32)
        knb = sb("knb", [128, 512], BF16)
        vf = sb("vf", [128, 512], F32)
        zf = sb("zf", [128, 8], F32)
        lf = sb("lf", [128, 8], F32)
        lfs = sb("lfs", [128, 8], F32)
        arT = sb("arT", [128, 2080], F32)
        kT_stage = arT[:, 0:1024].bitcast(BF16).rearrange("p (h t) -> p h t", h=4)
        v_stage = arT[:, 1024:2064].bitcast(BF16).rearrange("p (s q a h d) -> p s q a h d", s=2, q=4, a=2, h=2)
        et = arT[:, 0:512]
        ub = arT[:, 512:1024]
        gvn = sb("gvn", [128, 512], BF16)
        gvnf = sb("gvnf", [128, 512], F32)
        gc = arT[:, 1024:1536]
        tmpm = arT[:, 1536:2048]
        arW = sb("arW", [128, 8 * DIN], BF16)
        arQ = sb("arQ", [128, 25600], BF16)

        Wb = arW[:, :].rearrange("p (c n) -> p c n", c=8)
        kTp = arW[:, 0:8192]
        vP = arW[:, 8192:8192 + 64 * 130].rearrange("p (t e) -> p t e", t=64)
        o = 8192 + 64 * 130
        pTb = [arW[:, o + i * 512:o + (i + 1) * 512] for i in range(4)]
        o += 4 * 512
        o_sb = arW[:, o:o + 1024].bitcast(F32)
        o += 1024
        a_rows = arW[:, o:o + 2048]
        o += 2048
        rcp_t = arW[:, o:o + 8].bitcast(F32)
        o += 8
        mT = arW[:, o:o + 1024].rearrange("p (c n) -> p c n", c=8)
        o += 1024
        yt = arW[:, o:o + 2048].bitcast(F32)
        o += 2048
        qTh = arW[:, o:o + 2048]
        o += 2048
        assert o <= 8 * DIN
        qT_st = arQ[:, 0:8192].rearrange("p (h n) -> p h n", h=4)
        ga_st = arQ[:, 8192:16384].rearrange("p (i n) -> p i n", i=NOWN)
        m_st = arQ[:, 16384:24576].rearrange("p (i n) -> p i n", i=NOWN)
        Kb = arQ[:, 0:8192].bitcast(F32).rearrange("p (g n) -> p g n", g=8)
        Vb = arQ[:, 8192:16384].bitcast(F32).rearrange("p (g n) -> p g n", g=8)
        Wstage = arQ[:, 0:2 * DIN].bitcast(F32)
        Lh = arQ[:, 16384:20992].bitcast(F32).rearrange("p (c t h) -> p c t h", c=2 * NS, t=9)
        Vbb = arQ[:, 20992:25088].rearrange("p (g n) -> p g n", g=8)
        ptb = arQ[:, 25088:25152].bitcast(I32)
        idx = arQ[:, 25152:25216].bitcast(I32)
        tot_t = sb("tot_t", [128, 256], F32)
        tsum = sb("tsum", [128, 256], F32)
        qb = sb("qb", [128, 512], F32)
        s_t = sb("s_t", [128, 128], F32)
        p_t = sb("p_t", [128, 128], BF16)
        psm = sb("psm", [128, 8], F32)
        mo = sb("mo", [8, 512], F32)
        md = sb("md", [8, 8], F32)
        qn_s = sb("qn_s", [NS, 512], F32)
        kn_s = sb("kn_s", [NS, 512], F32)
        vf_s = sb("vf_s", [NS, 512], F32)
        ga_s = sb("ga_s", [NS, 512], BF16)
        m_s = sb("m_s", [NS, 512], BF16)
        sm1 = sb("sm1", [NS, 512], F32)
        sm2 = sb("sm2", [NS, 8], F32)
        sm3 = sb("sm3", [NS, 8], F32)
        mrg_s = sb("mrg_s", [NS, D], BF16)
        mT_s = sb("mT_s", [128, 8, NS], BF16)

        pf = [st.enter_context(nc.psum_tensor("pf%d" % i, [128, 512], F32)) for i in range(8)]
        Rp = [Res() for _ in range(8)]

        def pbf(i):
            return pf[i][:, :].bitcast(BF16)

        R = {}

        def res(name):
            if name not in R:
                R[name] = Res()
            return R[name]

        cap = [None]

        def A(eng, fn, reads=(), writes=(), dma=False):
            if cap[0] is not None:
                cap[0].append((eng, fn, reads, writes, dma))
                return None
            rl = [res(r) if isinstance(r, str) else r for r in reads]
            wl = [res(w) if isinstance(w, str) else w for w in writes]
            return P.add(eng, fn, rl, wl, dma)

        def record(f, *args):
            cap[0] = []
            f(*args)
            lst = cap[0]
            cap[0] = None
            return lst

        def interleave(*lists):
            lists = [l for l in lists if l]
            pos = [0] * len(lists)
            left = sum(len(l) for l in lists)
            while left:
                for i, l in enumerate(lists):
                    if pos[i] < len(l):
                        A(*l[pos[i]])
                        pos[i] += 1
                        left -= 1

        A("sp", lambda e: e.dma_start(out=cst[:, :], in_=consts_d), writes=["cst"], dma=True)
        A("sp", lambda e: e.dma_start(out=g_bc[:, :], in_=g_norm.partition_broadcast(128)), writes=["g_bc"], dma=True)
        A("sp", lambda e: e.dma_start(out=gv_bc[:, :], in_=g_v.partition_broadcast(128)), writes=["gv_bc"], dma=True)
        A("sp", lambda e: e.dma_start(out=gq_bc[:, :], in_=g_q.partition_broadcast(128)), writes=["gq_bc"], dma=True)
        A("sp", lambda e: e.dma_start(out=gk_bc[:, :], in_=g_k.partition_broadcast(128)), writes=["gk_bc"], dma=True)
        A("sp", lambda e: e.dma_start(out=bf_bc[:, :], in_=b_f.partition_broadcast(128)), writes=["bf_bc"], dma=True)
        for g in range(4):
            A("sp", lambda e, g=g: e.dma_start(out=bs_t[:, g:g + 1], in_=b_s[g].rearrange("(t o) -> t o", o=1)),
              writes=["bs_t"], dma=True)
            A("sp", lambda e, g=g: e.dma_start(out=ws00[:, g:g + 1], in_=w_s[g, 0, 0:1].partition_broadcast(128)),
              writes=["ws00"], dma=True)
            A("sp", lambda e, g=g: e.dma_start(out=bs0[:, g:g + 1], in_=b_s[g, 0:1].partition_broadcast(128)),
              writes=["bs0"], dma=True)
        w_in_v = w_in.rearrange("(c p) n -> p c n", p=128)
        for c in range(8):
            A("sp", lambda e, c=c: e.dma_start(out=Wstage, in_=w_in[c * 128:(c + 1) * 128, :]), writes=["Kb"], dma=True)
            A("act", lambda e, c=c: e.activation(out=Wb[:, c, :], in_=Wstage, func=AF.Copy), reads=["Kb"], writes=["Wb"])
        A("pool", lambda e: e.dma_start(out=wout[:, :, :], in_=w_out.rearrange("(c p) n -> p c n", p=128)),
          writes=["wout"], dma=True)
        A("dve", lambda e: e.tensor_copy(out=identb[:, :], in_=identf), reads=["cst"], writes=["identb"])
        A("dve", lambda e: e.tensor_scalar(out=maskb[:, :, :].rearrange("p w t -> p (w t)"), in0=cst[:, K_MW:K_MW + 512],
                                           scalar1=1.0e4, scalar2=-1.0e4, op0=ALU.mult, op1=ALU.add),
          reads=["cst"], writes=["maskb"])
        A("dve", lambda e: e.tensor_copy(out=ohTb[:, :, :].rearrange("p h s -> p (h s)"), in_=cst[0:8, K_OHT:K_OHT + 1024]),
          reads=["cst"], writes=["ohTb"])
        A("dve", lambda e: e.memset(runc[:, :], 0.0), writes=["runc"])
        ws_t = tmpm.rearrange("p (g s) -> p g s", g=4)
        A("sp", lambda e: e.dma_start(out=ws_t, in_=w_s.rearrange("g t s -> t g s")), writes=["tmpm"], dma=True)
        for g in range(4):
            A("pe", lambda e, g=g: e.transpose(out=pf[0][:, g * 128:(g + 1) * 128], in_=ws_t[:, g, :], identity=identf),
              reads=["tmpm", "cst"], writes=[Rp[0]])
        for g in range(4):
            A("dve", lambda e, g=g: e.tensor_tensor(out=wsT[:, g, :], in0=pf[0][:, g * 128:(g + 1) * 128], in1=utri, op=ALU.mult),
              reads=[Rp[0], "cst"], writes=["wsT"])

        def rsqrt_act(dst, src, n, scale):
            A("act", lambda e: e.activation(out=dst, in_=src, func=AF.Ln, scale=scale, bias=EPS), reads=["tmp_r_in"], writes=["tmp_r"])
            A("act", lambda e: e.activation(out=dst, in_=dst, func=AF.Exp, scale=-0.5), reads=["tmp_r"], writes=["tmp_r"])

        def norm_T(src, n, xTb=None, xTr="xT", ldq="sp"):
            xTb = xT if xTb is None else xTb
            A(ldq, lambda e: e.dma_start(out=xt[:n, :], in_=src), writes=["xt"], dma=True)
            A("dve", lambda e: e.memset(ss[:n, :], 0.0), writes=["ss"])
            A("act", lambda e: e.activation(out=junk[:n, :], in_=xt[:n, :], func=AF.Square, accum_out=ss[:n, :]),
              reads=["xt", "ss"], writes=["xn", "ss"])
            A("act", lambda e: e.activation(out=rr[:n, :], in_=ss[:n, :], func=AF.Ln, scale=1.0 / D, bias=EPS),
              reads=["ss"], writes=["rr"])
            A("act", lambda e: e.activation(out=rr[:n, :], in_=rr[:n, :], func=AF.Exp, scale=-0.5),
              reads=["rr"], writes=["rr"])
            A("dve", lambda e: e.scalar_tensor_tensor(out=xn[:n, :], in0=xt[:n, :], scalar=rr[:n, 0:1], in1=g_bc[:n, :],
                                                      op0=ALU.mult, op1=ALU.mult),
              reads=["xt", "rr", "g_bc"], writes=["xn"])
            for c in range(8):
                A("pe", lambda e, c=c: e.transpose(out=pbf(0)[:, c * 128:c * 128 + n], in_=xn[:n, c * 128:(c + 1) * 128],
                                                   identity=identb[:n, :n]),
                  reads=["xn", "identb"], writes=[Rp[0]])
            A("act", lambda e: e.activation(out=xTb[:, :, 0:n], in_=pbf(0).rearrange("p (c t) -> p c t", c=8)[:, :, 0:n], func=AF.Copy),
              reads=[Rp[0]], writes=[xTr])

        def proj(bank, col0, width, n, xTb=None, xTr="xT"):
            xTb = xT if xTb is None else xTb
            for c in range(8):
                A("pe", lambda e, c=c: e.matmul(pf[bank][:n, 0:width], lhsT=xTb[:, c, 0:n], rhs=Wb[:, c, col0:col0 + width],
                                                start=(c == 0), stop=(c == 7)),
                  reads=[xTr, "Wb"], writes=[Rp[bank]])

        def headnorm(bank, n, nh, hd, gb, out_ap, out_res):
            A("act", lambda e: e.activation(out=sq[:n, :], in_=pf[bank][:n, :], func=AF.Square), reads=[Rp[bank]], writes=["sq"])
            A("dve", lambda e: e.tensor_reduce(out=ssq[:n, 0:nh], in_=sq[:n, :].rearrange("p (h d) -> p h d", h=nh),
                                               axis=AX.X, op=ALU.add), reads=["sq"], writes=["ssq"])
            A("act", lambda e: e.activation(out=rk[:n, 0:nh], in_=ssq[:n, 0:nh], func=AF.Ln, scale=1.0 / hd, bias=EPS),
              reads=["ssq"], writes=["rk"])
            A("act", lambda e: e.activation(out=rk[:n, 0:nh], in_=rk[:n, 0:nh], func=AF.Exp, scale=-0.5), reads=["rk"], writes=["rk"])
            A("dve", lambda e: e.tensor_tensor(out=kn[:n, :].rearrange("p (h d) -> p h d", h=nh),
                                               in0=pf[bank][:n, :].rearrange("p (h d) -> p h d", h=nh),
                                               in1=rk[:n, 0:nh].unsqueeze(2).to_broadcast([n, nh, hd]), op=ALU.mult),
              reads=[Rp[bank], "rk"], writes=["kn"])
            if hd == 64:
                in1 = gb[:n, :].unsqueeze(1).to_broadcast([n, nh, hd])
                A("dve", lambda e: e.tensor_tensor(out=out_ap.rearrange("p (h d) -> p h d", h=nh),
                                                   in0=kn[:n, :].rearrange("p (h d) -> p h d", h=nh), in1=in1, op=ALU.mult),
                  reads=["kn"], writes=[out_res])
            else:
                A("dve", lambda e: e.tensor_tensor(out=out_ap, in0=kn[:n, :], in1=gb[:n, :], op=ALU.mult),
                  reads=["kn"], writes=[out_res])

        def logf_of(bank, n):
            A("dve", lambda e: e.tensor_tensor(out=zf[:n, :], in0=pf[bank][:n, 0:8], in1=bf_bc[:n, :], op=ALU.add),
              reads=[Rp[bank], "bf_bc"], writes=["zf"])
            A("act", lambda e: e.activation(out=zf[:n, :], in_=zf[:n, :], func=AF.Exp, scale=-1.0), reads=["zf"], writes=["zf"])
            A("act", lambda e: e.activation(out=zf[:n, :], in_=zf[:n, :], func=AF.Ln, bias=1.0), reads=["zf"], writes=["zf"])
            A("dve", lambda e: e.tensor_scalar(out=lf[:n, :], in0=zf[:n, :], scalar1=-1.0, scalar2=None, op0=ALU.mult),
              reads=["zf"], writes=["lf"])

        def silu_from(bank, n, out_ap, out_res):
            A("act", lambda e: e.activation(out=et[:n, :], in_=pf[bank][:n, :], func=AF.Exp, scale=-1.0), reads=[Rp[bank]], writes=["et"])
            A("dve", lambda e: e.tensor_scalar(out=et[:n, :], in0=et[:n, :], scalar1=1.0, scalar2=None, op0=ALU.add),
              reads=["et"], writes=["et"])
            A("dve", lambda e: e.reciprocal(out=et[:n, :], in_=et[:n, :]), reads=["et"], writes=["et"])
            A("dve", lambda e: e.tensor_tensor(out=out_ap, in0=pf[bank][:n, :], in1=et[:n, :], op=ALU.mult),
              reads=[Rp[bank], "et"], writes=[out_res])

        def rest_proj(n, ga_out, ga_res, m_out, m_res, sample):
            proj(3, C_ZA, 512, n)
            silu_from(3, n, ga_out, ga_res)
            proj(4, C_U, 512, n)
            A("act", lambda e: e.activation(out=ub[:n, :], in_=pf[4][:n, :], func=AF.Copy), reads=[Rp[4]], writes=["ub"])
            proj(3, C_GV, 512, n)
            if sample:
                headnorm(3, n, 4, 128, gv_bc, gvnf[:n, :], "gvnf")
                A("sp", lambda e: e.dma_start(out=gv_s, in_=gvnf[:n, :]), reads=["gvnf"], dma=True)
            else:
                headnorm(3, n, 4, 128, gv_bc, gvn[:n, :], "gvn")
                for g in range(4):
                    A("pe", lambda e, g=g: e.matmul(pf[5][:n, g * 128:(g + 1) * 128], lhsT=wsT[:, g, :],
                                                    rhs=gvn[:, g * 128:(g + 1) * 128], start=True, stop=True),
                      reads=["wsT", "gvn"], writes=[Rp[5]])
            proj(4, C_ZC, 512, n)
            silu_from(4, n, gc[:n, :], "gc")
            for g in range(4):
                sl = slice(g * 128, (g + 1) * 128)
                if sample:
                    A("dve", lambda e, g=g, sl=sl: e.tensor_scalar(out=tmpm[:n, sl], in0=gvnf[:n, sl], scalar1=ws00[:n, g:g + 1],
                                                                   scalar2=bs0[:n, g:g + 1], op0=ALU.mult, op1=ALU.add),
                      reads=["gvnf", "ws00", "bs0"], writes=["tmpm"])
                    A("dve", lambda e, sl=sl: e.tensor_tensor(out=tmpm[:n, sl], in0=tmpm[:n, sl], in1=ub[:n, sl], op=ALU.mult),
                      reads=["tmpm", "ub"], writes=["tmpm"])
                else:
                    A("dve", lambda e, g=g, sl=sl: e.scalar_tensor_tensor(out=tmpm[:n, sl], in0=pf[5][:n, sl], scalar=bs_t[:n, g:g + 1],
                                                                          in1=ub[:n, sl], op0=ALU.add, op1=ALU.mult),
                      reads=[Rp[5], "bs_t", "ub"], writes=["tmpm"])
            A("dve", lambda e: e.tensor_tensor(out=m_out, in0=tmpm[:n, :], in1=gc[:n, :], op=ALU.mult),
              reads=["tmpm", "gc"], writes=[m_res])

        n = NS
        norm_T(x_s, n)
        proj(1, C_Q, 512, n)
        headnorm(1, n, 8, 64, gq_bc, qn_s[:n, :], "qn_s")
        A("sp", lambda e: e.dma_start(out=q_scr, in_=qn_s[:n, :]), reads=["qn_s"], writes=["q_scr"], dma=True)
        proj(1, C_K, 512, n)
        headnorm(1, n, 8, 64, gk_bc, kn_s[:n, :], "kn_s")
        A("sp", lambda e: e.dma_start(out=k_s, in_=kn_s[:n, :]), reads=["kn_s"], dma=True)
        proj(2, C_V, 512, n)
        A("act", lambda e: e.activation(out=vf_s[:n, :], in_=pf[2][:n, :], func=AF.Copy), reads=[Rp[2]], writes=["vf_s"])
        A("sp", lambda e: e.dma_start(out=v_s, in_=vf_s[:n, :]), reads=["vf_s"], dma=True)
        proj(2, C_F, 8, n)
        logf_of(2, n)
        A("dve", lambda e: e.tensor_copy(out=lfs[:n, :], in_=lf[:n, :]), reads=["lf"], writes=["lfs"])
        A("sp", lambda e: e.dma_start(out=l_s, in_=lfs[:n, :]), reads=["lfs"], dma=True)
        rest_proj(n, ga_s[:n, :], "ga_s", m_s[:n, :], "m_s", True)

        ck2 = ck.rearrange("(r t) n -> r (t n)", t=8)
        cv2 = cv.rearrange("(r t) n -> r (t n)", t=8)
        cl2 = cl.rearrange("(r t) n -> r (t n)", t=8)
        A("sp", lambda e: e.dma_start(out=ptb, in_=pt), writes=["ptb"], dma=True)
        A("dve", lambda e: e.tensor_scalar(out=idx, in0=ptb, scalar1=16.0, scalar2=iota, op0=ALU.mult, op1=ALU.add),
          reads=["ptb", "cst"], writes=["idx"])
        A("dve", lambda e: e.memset(arQ[:, 16384:20992].bitcast(F32), 0.0), writes=["Lall"])
        for col in range(2 * NS):
            A("pool", lambda e, col=col: e.indirect_dma_start(
                out=arQ[:, 16384:20992].bitcast(F32)[:, col * 72:col * 72 + 64], out_offset=None, in_=cl2,
                in_offset=bass.IndirectOffsetOnAxis(ap=idx[:, col:col + 1], axis=0)),
              reads=["idx"], writes=["Lall"], dma=True)

        def compute_E():
            A("dve", lambda e: e.tensor_reduce(out=tot_t[:, :].rearrange("p (c h) -> p c h", h=8),
                                               in_=Lh[:, :, 0:8, :].rearrange("p c t h -> p c h t"), axis=AX.X, op=ALU.add),
              reads=["Lall"], writes=["tot_t"])
            A("pe", lambda e: e.matmul(pf[3][:, 0:256], lhsT=ltri, rhs=tot_t[:, :], start=True, stop=True),
              reads=["tot_t", "cst"], writes=[Rp[3]])
            A("pe", lambda e: e.matmul(pf[4][:, 0:128], lhsT=ones,
                                       rhs=tot_t[:, :].rearrange("p (b f h) -> p b f h", f=2, h=8)[:, :, 1, :], start=True, stop=True),
              reads=["tot_t", "cst"], writes=[Rp[4]])
            A("dve", lambda e: e.tensor_copy(out=tsum[:, :], in_=pf[3][:, 0:256]), reads=[Rp[3]], writes=["tsum"])
            A("dve", lambda e: e.tensor_tensor(out=tsum[:, :].rearrange("p (b f h) -> p b f h", f=2, h=8)[:, :, 0, :],
                                               in0=tsum[:, :].rearrange("p (b f h) -> p b f h", f=2, h=8)[:, :, 0, :],
                                               in1=pf[4][:, 0:128].rearrange("p (b h) -> p b h", h=8), op=ALU.add),
              reads=[Rp[4], "tsum"], writes=["tsum"])
            for t in range(6, -1, -1):
                A("dve", lambda e, t=t: e.tensor_tensor(out=Lh[:, :, t, :], in0=Lh[:, :, t, :], in1=Lh[:, :, t + 1, :], op=ALU.add),
                  reads=["Lall"], writes=["Lall"])
            for t in range(1, 9):
                A("dve", lambda e, t=t: e.tensor_tensor(out=Lh[:, :, t, :], in0=Lh[:, :, t, :],
                                                        in1=tsum[:, :].rearrange("p (c h) -> p c h", h=8), op=ALU.add),
                  reads=["Lall", "tsum"], writes=["Lall"])

        def gather_half(b, hf):
            col = b * 2 + hf
            A("pool", lambda e: e.indirect_dma_start(
                out=Kb.rearrange("p g n -> p (g n)"), out_offset=None, in_=ck2,
                in_offset=bass.IndirectOffsetOnAxis(ap=idx[:, col:col + 1], axis=0)),
              reads=["idx"], writes=["Kb"], dma=True)
            A("pool", lambda e: e.indirect_dma_start(
                out=Vb.rearrange("p g n -> p (g n)"), out_offset=None, in_=cv2,
                in_offset=bass.IndirectOffsetOnAxis(ap=idx[:, col:col + 1], axis=0)),
              reads=["idx"], writes=["Vb"], dma=True)

        def sample_half(b, hf):
            hs = slice(hf * 64, hf * 64 + 64)
            if hf == 0:
                A("sp", lambda e: e.dma_start(out=qb[:, :], in_=q_scr[b].partition_broadcast(128)), reads=["q_scr"], writes=["qb"], dma=True)
            A("dve", lambda e: e.tensor_tensor(out=Kb, in0=Kb, in1=qb[:, :].unsqueeze(1).to_broadcast([128, 8, 512]), op=ALU.mult),
              reads=["Kb", "qb"], writes=["Kb"])
            A("dve", lambda e: e.tensor_reduce(out=s_t[:, hs], in_=Kb.rearrange("p g (h d) -> p (g h) d", h=8), axis=AX.X, op=ALU.add),
              reads=["Kb"], writes=["s_t"])
            A("dve", lambda e: e.scalar_tensor_tensor(out=s_t[:, hs].rearrange("p (g h) -> p g h", g=8), in0=s_t[:, hs].rearrange("p (g h) -> p g h", g=8),
                                                      scalar=SCALE, in1=Lh[:, b * 2 + hf, 1:9, :], op0=ALU.mult, op1=ALU.add),
              reads=["s_t", "Lall"], writes=["s_t"])
            A("act", lambda e: e.activation(out=p_t[:, hs], in_=s_t[:, hs], func=AF.Exp), reads=["s_t"], writes=["p_t"])
            A("act", lambda e: e.activation(out=Vbb, in_=Vb, func=AF.Copy), reads=["Vb"], writes=["Vbb"])
            for g in range(8):
                gg = hf * 8 + g
                A("pe", lambda e, g=g, gg=gg: e.matmul(pf[5][0:8, :], lhsT=p_t[:, gg * 8:(gg + 1) * 8], rhs=Vbb[:, g, :],
                                                       start=(gg == 0), stop=(gg == NPG - 1)),
                  reads=["p_t", "Vbb"], writes=[Rp[5]])
            if hf == 0:
                return
            A("dve", lambda e: e.tensor_reduce(out=psm[:, :], in_=p_t[:, :].rearrange("p (g h) -> p h g", g=NPG), axis=AX.X, op=ALU.add),
              reads=["p_t"], writes=["psm"])
            A("dve", lambda e: e.tensor_tensor(out=mo[:, :], in0=pf[5][0:8, :], in1=cst[0:8, K_BM:K_BM + 512], op=ALU.mult),
              reads=[Rp[5], "cst"], writes=["mo"])
            A("pe", lambda e: e.matmul(pf[3][0:NS, :], lhsT=cst[0:8, K_EB + b * 16:K_EB + (b + 1) * 16], rhs=mo[:, :],
                                       start=(b == 0), stop=(b == NS - 1)),
              reads=["mo", "cst"], writes=[Rp[3]])
            A("pe", lambda e: e.matmul(pf[4][0:NS, 0:8], lhsT=cst[:, K_CS + b * 16:K_CS + (b + 1) * 16], rhs=psm[:, :],
                                       start=(b == 0), stop=(b == NS - 1)),
              reads=["psm", "cst"], writes=[Rp[4]])

        xTs = ((xT, "xT"), (xT2, "xT2"))

        def pA_h1(I):
            xb, xr = xTs[I % 2]
            norm_T(x_all[I * 128:(I + 1) * 128, :], 128, xb, xr, "act")

        def pA_k(I):
            xb, xr = xTs[I % 2]
            proj(1, C_K, 512, 128, xb, xr)
            headnorm(1, 128, 8, 64, gk_bc, knb[:, :], "knb")

        def pA_kb(I):
            for hp in range(4):
                A("pe", lambda e, hp=hp: e.transpose(out=pbf(6)[:, hp * 128:(hp + 1) * 128], in_=knb[:, hp * 128:(hp + 1) * 128],
                                                     identity=identb[:, :]),
                  reads=["knb", "identb"], writes=[Rp[6]])
            r4 = I % 4
            sl = r4 // 2
            A("act", lambda e: e.activation(out=kT_stage[:, :, r4 * 128:(r4 + 1) * 128],
                                            in_=pbf(6)[:, 0:512].rearrange("p (h t) -> p h t", h=4), func=AF.Copy),
              reads=[Rp[6]], writes=["kT_stage%d" % sl])
            if r4 % 2 == 1:
                t0 = (I - 1) * 128
                A("sp", lambda e: e.dma_start(out=kT_scr[:, :, t0:t0 + 256].rearrange("h p t -> p h t"),
                                              in_=kT_stage[:, :, sl * 256:(sl + 1) * 256]),
                  reads=["kT_stage%d" % sl], writes=["kT_scr"], dma=True)

        def pA_vf(I):
            xb, xr = xTs[I % 2]
            r4 = I % 4
            proj(2, C_V, 512, 128, xb, xr)
            sl = r4 // 2
            A("act", lambda e: e.activation(out=v_stage[:, sl, :, r4 % 2, :, 0:64],
                                            in_=pf[2][:, :].rearrange("p (q h d) -> p q h d", q=4, h=2), func=AF.Copy),
              reads=[Rp[2]], writes=["v_stage%d" % sl])
            proj(7, C_F, 8, 128, xb, xr)
            logf_of(7, 128)
            A("pe", lambda e: e.matmul(pf[7][:, 8:16], lhsT=utri, rhs=lf[:, :], start=True, stop=True), reads=["lf", "cst"], writes=[Rp[7]])
            A("pe", lambda e: e.matmul(pf[7][:, 16:24], lhsT=ones, rhs=lf[:, :], start=True, stop=True), reads=["lf", "cst"], writes=[Rp[7]])
            A("dve", lambda e: e.tensor_copy(out=carry[:, I, :], in_=runc[:, :]), reads=["runc"], writes=["carry"])
            A("dve", lambda e: e.scalar_tensor_tensor(out=negc[:, I, :], in0=pf[7][:, 8:16], scalar=-1.0, in1=runc[:, :],
                                                      op0=ALU.mult, op1=ALU.subtract),
              reads=[Rp[7], "runc"], writes=["negc"])
            A("dve", lambda e: e.tensor_tensor(out=runc[:, :], in0=runc[:, :], in1=pf[7][:, 16:24], op=ALU.add),
              reads=[Rp[7], "runc"], writes=["runc"])
            if r4 % 2 == 1:
                A("sp", lambda e: e.dma_start(
                    out=v_scr[:, :, I - 1:I + 1, :].rearrange("q p a e -> p q (a e)"),
                    in_=v_stage[:, sl, :, :, :, :].rearrange("p q a h d -> p q (a h d)")),
                  reads=["v_stage%d" % sl], writes=["v_scr"], dma=True)

        cw = sb("cw", [128, 32], F32)
        cwo = sb("cwo", [128, 8], F32)
        fz_t = sb("fz_t", [128, 4], F32)

        def fence(names):
            A("dve", lambda e: e.memset(fz_t[:, :], 0.0), writes=names)

        fence(["et", "ub", "gc", "tmpm", "kT_stage0", "kT_stage1", "v_stage0", "v_stage1"])
        A("dve", lambda e: e.memset(arT[:, 1024:2064].bitcast(BF16), 1.0), writes=["v_stage0", "v_stage1"])
        sched = {}
        for hs_ in range(2 * NS):
            sched.setdefault(15 + (hs_ * 3) // 2, []).append(hs_)
        gather_half(0, 0)
        pA_h1(0)
        for I in range(NT):
            chains = [record(pA_k, I), record(pA_vf, I)]
            if I > 0:
                chains.append(record(pA_kb, I - 1))
            if I + 1 < NT:
                chains.insert(0, record(pA_h1, I + 1))
            for hs_ in sched.get(I, []):
                chains.append(record(sample_half, hs_ // 2, hs_ % 2))
            interleave(*chains)
            for hs_ in sched.get(I, []):
                if hs_ + 1 < 2 * NS:
                    gather_half((hs_ + 1) // 2, (hs_ + 1) % 2)
            if I == 13:
                compute_E()
        pA_kb(NT - 1)

        n = NS
        A("sp", lambda e: e.dma_start(out=xt[:n, :], in_=x_s), writes=["xt"], dma=True)
        A("dve", lambda e: e.tensor_tensor(out=sm1[:, :], in0=qn_s[:, :], in1=kn_s[:, :], op=ALU.mult), reads=["qn_s", "kn_s"], writes=["sm1"])
        A("dve", lambda e: e.tensor_reduce(out=sm2[:, :], in_=sm1[:, :].rearrange("p (h d) -> p h d", h=8), axis=AX.X, op=ALU.add),
          reads=["sm1"], writes=["sm2"])
        A("dve", lambda e: e.scalar_tensor_tensor(out=sm2[:, :], in0=sm2[:, :], scalar=SCALE, in1=lfs[:n, :], op0=ALU.mult, op1=ALU.subtract),
          reads=["sm2", "lfs"], writes=["sm2"])
        A("act", lambda e: e.activation(out=sm2[:, :], in_=sm2[:, :], func=AF.Exp), reads=["sm2"], writes=["sm2"])
        A("dve", lambda e: e.tensor_tensor(out=sm1[:, :].rearrange("p (h d) -> p h d", h=8), in0=vf_s[:, :].rearrange("p (h d) -> p h d", h=8),
                                           in1=sm2[:, :].unsqueeze(2).to_broadcast([n, 8, 64]), op=ALU.mult),
          reads=["vf_s", "sm2"], writes=["sm1"])
        A("dve", lambda e: e.tensor_tensor(out=sm1[:, :], in0=sm1[:, :], in1=pf[3][0:n, :], op=ALU.add), reads=["sm1", Rp[3]], writes=["sm1"])
        A("dve", lambda e: e.tensor_tensor(out=sm3[:, :], in0=pf[4][0:n, 0:8], in1=sm2[:, :], op=ALU.add), reads=[Rp[4], "sm2"], writes=["sm3"])
        A("dve", lambda e: e.reciprocal(out=sm3[:, :], in_=sm3[:, :]), reads=["sm3"], writes=["sm3"])
        A("dve", lambda e: e.tensor_tensor(out=sm1[:, :].rearrange("p (h d) -> p h d", h=8), in0=sm1[:, :].rearrange("p (h d) -> p h d", h=8),
                                           in1=sm3[:, :].unsqueeze(2).to_broadcast([n, 8, 64]), op=ALU.mult),
          reads=["sm1", "sm3"], writes=["sm1"])
        A("dve", lambda e: e.tensor_tensor(out=mrg_s[:, 0:512], in0=sm1[:, :], in1=ga_s[:, :], op=ALU.mult), reads=["sm1", "ga_s"], writes=["mrg_s"])
        A("dve", lambda e: e.tensor_copy(out=mrg_s[:, 512:1024], in_=m_s[:, :]), reads=["m_s"], writes=["mrg_s"])
        for c in range(8):
            A("pe", lambda e, c=c: e.transpose(out=pbf(0)[:, c * 128:c * 128 + n], in_=mrg_s[:n, c * 128:(c + 1) * 128], identity=identb[:n, :n]),
              reads=["mrg_s", "identb"], writes=[Rp[0]])
        A("act", lambda e: e.activation(out=mT_s[:, :, :], in_=pbf(0).rearrange("p (c t) -> p c t", c=8)[:, :, 0:n], func=AF.Copy),
          reads=[Rp[0]], writes=["mT_s"])
        for half in range(2):
            for c in range(8):
                A("pe", lambda e, c=c, half=half: e.matmul(pf[1 + half][:n, :], lhsT=mT_s[:, c, :], rhs=wout[:, c, half * 512:(half + 1) * 512],
                                                           start=(c == 0), stop=(c == 7)),
                  reads=["mT_s", "wout"], writes=[Rp[1 + half]])
            A("dve", lambda e, half=half: e.tensor_tensor(out=xt[:n, half * 512:(half + 1) * 512], in0=pf[1 + half][:n, :],
                                                          in1=xt[:n, half * 512:(half + 1) * 512], op=ALU.add),
              reads=[Rp[1 + half], "xt"], writes=["xt"])
        A("sp", lambda e: e.dma_start(out=y_s, in_=xt[:n, :]), reads=["xt"], dma=True)

        fence(["Kb", "Vb", "Vbb", "Lall", "ptb", "idx", "qT_st", "ga_st", "m_st", "et", "ub", "gc", "tmpm", "kT_stage0", "kT_stage1", "v_stage0", "v_stage1", "xT", "xT2"])
        def pB_h1(i):
            xb, xr = xTs[i % 2]
            norm_T(x_own[i * 128:(i + 1) * 128, :], 128, xb, xr, "act")

        def pB_X(i):
            xb, xr = xTs[i % 2]
            rows = slice(i * 128, (i + 1) * 128)
            proj(1, C_Q, 512, 128, xb, xr)
            headnorm(1, 128, 8, 64, gq_bc, knb[:, :], "knb")
            for hp in range(4):
                A("pe", lambda e, hp=hp: e.transpose(out=pbf(6)[:, hp * 128:(hp + 1) * 128], in_=knb[:, hp * 128:(hp + 1) * 128],
                                                     identity=identb[:, :]),
                  reads=["knb", "identb"], writes=[Rp[6]])
            A("act", lambda e: e.activation(out=qT_st[:, :, i * 128:(i + 1) * 128],
                                            in_=pbf(6)[:, 0:512].rearrange("p (h t) -> p h t", h=4), func=AF.Copy),
              reads=[Rp[6]], writes=["qT_st"])
            proj(1, C_K, 512, 128, xb, xr)
            headnorm(1, 128, 8, 64, gk_bc, kf[:, :], "kf")
            A("sp", lambda e: e.dma_start(out=k_own[rows, :], in_=kf[:, :]), reads=["kf"], dma=True)
            proj(3, C_GV, 512, 128, xb, xr)
            headnorm(3, 128, 4, 128, gv_bc, gvn[:, :], "gvn")
            for g in range(4):
                A("pe", lambda e, g=g: e.matmul(pf[5][:, g * 128:(g + 1) * 128], lhsT=wsT[:, g, :],
                                                rhs=gvn[:, g * 128:(g + 1) * 128], start=True, stop=True),
                  reads=["wsT", "gvn"], writes=[Rp[5]])

        def pB_Y(i):
            xb, xr = xTs[i % 2]
            rows = slice(i * 128, (i + 1) * 128)
            proj(2, C_V, 512, 128, xb, xr)
            A("act", lambda e: e.activation(out=vf[:, :], in_=pf[2][:, :], func=AF.Copy), reads=[Rp[2]], writes=["vf"])
            A("sp", lambda e: e.dma_start(out=v_own[rows, :], in_=vf[:, :]), reads=["vf"], dma=True)
            proj(7, C_F, 8, 128, xb, xr)
            logf_of(7, 128)
            A("sp", lambda e: e.dma_start(out=l_own[rows, :], in_=lf[:, :]), reads=["lf"], dma=True)
            A("pe", lambda e: e.matmul(pf[7][:, 8:16], lhsT=utri, rhs=lf[:, :], start=True, stop=True), reads=["lf", "cst"], writes=[Rp[7]])
            A("dve", lambda e: e.tensor_tensor(out=cw[:, :].rearrange("p (h w) -> p h w", h=8),
                                               in0=carry[:, 4 * i:4 * i + 4, :].rearrange("p w h -> p h w"),
                                               in1=oh4.unsqueeze(1).to_broadcast([128, 8, 4]), op=ALU.mult),
              reads=["carry", "cst"], writes=["cw"])
            A("dve", lambda e: e.tensor_reduce(out=cwo[:, :], in_=cw[:, :].rearrange("p (h w) -> p h w", h=8), axis=AX.X, op=ALU.add),
              reads=["cw"], writes=["cwo"])
            A("dve", lambda e: e.tensor_tensor(out=c_own[:, i, :], in0=pf[7][:, 8:16], in1=cwo[:, :], op=ALU.add),
              reads=[Rp[7], "cwo"], writes=["c_own"])

        def pB_Z(i):
            xb, xr = xTs[i % 2]
            proj(4, C_ZA, 512, 128, xb, xr)
            silu_from(4, 128, ga_st[:, i, :], "ga_st")
            proj(4, C_U, 512, 128, xb, xr)
            A("act", lambda e: e.activation(out=ub[:, :], in_=pf[4][:, :], func=AF.Copy), reads=[Rp[4]], writes=["ub"])
            proj(4, C_ZC, 512, 128, xb, xr)
            silu_from(4, 128, gc[:, :], "gc")

        def pB_tail(i):
            for g in range(4):
                sl = slice(g * 128, (g + 1) * 128)
                A("dve", lambda e, g=g, sl=sl: e.scalar_tensor_tensor(out=tmpm[:, sl], in0=pf[5][:, sl], scalar=bs_t[:, g:g + 1],
                                                                      in1=ub[:, sl], op0=ALU.add, op1=ALU.mult),
                  reads=[Rp[5], "bs_t", "ub"], writes=["tmpm"])
            A("dve", lambda e: e.tensor_tensor(out=m_st[:, i, :], in0=tmpm[:, :], in1=gc[:, :], op=ALU.mult),
              reads=["tmpm", "gc"], writes=["m_st"])

        pB_h1(0)
        for i in range(NOWN):
            chains = [record(pB_X, i), record(pB_Y, i), record(pB_Z, i)]
            if i + 1 < NOWN:
                chains.insert(0, record(pB_h1, i + 1))
            interleave(*chains)
            pB_tail(i)

        fence(["Wb", "kTp", "vP", "pTb0", "pTb1", "pTb2", "pTb3", "o_sb", "a_rows", "rcp_t", "mT", "yt", "qTh"])
        A("dve", lambda e: e.memset(kTp[64:65, :], 1.0), writes=["kTp"])
        for i0 in range(0, NOWN, 4):
            for k in range(4):
                A("pe", lambda e, i0=i0, k=k: e.transpose(out=pf[7][0:8, k * 128:(k + 1) * 128], in_=c_own[:, i0 + k, :], identity=identf),
                  reads=["c_own", "cst"], writes=[Rp[7]])
            A("act", lambda e, i0=i0: e.activation(out=a_rows[0:8, i0 * 128:i0 * 128 + 512], in_=pf[7][0:8, :], func=AF.Copy, scale=1.0 / SCALE),
              reads=[Rp[7]], writes=["a_rows"])
        SB = (0, 1, 7)
        LA = 2
        its = []
        for h in range(8):
            for qg in range(4):
                nJ = 16 * qg + 16
                for J in range(nJ):
                    its.append((h, qg, J, nJ))

        def emit_qk(n):
            h, qg, J, nJ = its[n]
            hp, half = h // 2, h % 2
            rws = slice(half * 64, half * 64 + 64)
            if qg == 0 and J == 0:
                A("sp", lambda e: e.dma_start(out=kTp[0:64, :], in_=kT_scr[hp, half * 64:(half + 1) * 64, :]),
                  reads=["kT_scr"], writes=["kTp"], dma=True)
                A("sp", lambda e: e.dma_start(out=qTh[0:64, :], in_=qT_st[half * 64:(half + 1) * 64, hp, :]),
                  reads=["qT_st"], writes=["qTh"], dma=True)
                A("sp", lambda e: e.dma_start(out=qTh[64:65, :], in_=a_rows[h:h + 1, :]),
                  reads=["a_rows"], writes=["qTh"], dma=True)
                if half == 0:
                    A("sp", lambda e, hp=hp: e.dma_start(out=vP, in_=v_scr[hp]), reads=["v_scr"], writes=["vP"], dma=True)
            kmin = max(0, (J - 16 * qg) // 4) if J >= 16 * qg else 0
            c0 = kmin * 128
            q0 = qg * 512 + c0
            q1 = qg * 512 + 512
            sbk = SB[n % 3]
            pb = n % 4
            win = J >= 16 * qg
            A("pe", lambda e: e.matmul(pf[sbk][:, c0:512], lhsT=kTp[0:65, J * 128:(J + 1) * 128], rhs=qTh[0:65, q0:q1],
                                       start=True, stop=(not win)),
              reads=["kTp", "qTh"], writes=[Rp[sbk]])
            if win:
                w = (J - 16 * qg) % 4
                A("pe", lambda e: e.matmul(pf[sbk][:, c0:c0 + 128], lhsT=identb[:, :], rhs=maskb[:, w, :], start=False, stop=True),
                  reads=["identb", "maskb"], writes=[Rp[sbk]])
            A("act", lambda e: e.activation(out=pTb[pb][:, c0:512], in_=pf[sbk][:, c0:512], func=AF.Exp, scale=SCALE,
                                            bias=negc[:, J, h:h + 1]),
              reads=[Rp[sbk], "negc"], writes=["pTb%d" % pb])

        def emit_pv(n):
            h, qg, J, nJ = its[n]
            half = h % 2
            kmin = max(0, (J - 16 * qg) // 4) if J >= 16 * qg else 0
            c0 = kmin * 128
            pb = n % 4
            A("pe", lambda e: e.matmul(pf[2 + qg][0:65, c0:512], lhsT=vP[:, J, half * 65:(half + 1) * 65], rhs=pTb[pb][:, c0:512],
                                       start=(J == 0), stop=(J == nJ - 1)),
              reads=["vP", "pTb%d" % pb], writes=[Rp[2 + qg]])
            if J == nJ - 1:
                A("act", lambda e: e.activation(out=o_sb[0:65, :], in_=pf[2 + qg][0:65, :], func=AF.Copy),
                  reads=[Rp[2 + qg]], writes=["o_sb"])
                for k in range(4):
                    A("pe", lambda e, k=k: e.transpose(out=pf[6][:, k * 65:k * 65 + 65], in_=o_sb[0:65, k * 128:(k + 1) * 128],
                                                       identity=identf[0:65, 0:65]),
                      reads=["o_sb", "cst"], writes=[Rp[6]])
                A("dve", lambda e: e.reciprocal(out=rcp_t[:, 0:4], in_=pf[6][:, 0:260].rearrange("p (k e) -> p k e", k=4)[:, :, 64]),
                  reads=[Rp[6]], writes=["rcp_t"])
                for k in range(4):
                    A("dve", lambda e, k=k: e.scalar_tensor_tensor(
                        out=ga_st[:, 4 * qg + k, h * 64:(h + 1) * 64], in0=pf[6][:, k * 65:k * 65 + 64], scalar=rcp_t[:, k:k + 1],
                        in1=ga_st[:, 4 * qg + k, h * 64:(h + 1) * 64], op0=ALU.mult, op1=ALU.mult),
                      reads=[Rp[6], "rcp_t", "ga_st"], writes=["ga_st"])

        pend = []
        for nn in range(len(its)):
            h_, qg_, J_, _ = its[nn]
            if h_ % 2 == 0 and qg_ == 0 and J_ == 0:
                for m in pend:
                    emit_pv(m)
                pend = []
            emit_qk(nn)
            pend.append(nn)
            if len(pend) > LA:
                emit_pv(pend.pop(0))
        for m in pend:
            emit_pv(m)

        for i in range(NOWN):
            rows = slice(i * 128, (i + 1) * 128)
            for c in range(8):
                src = ga_st[:, i, c * 128:(c + 1) * 128] if c < 4 else m_st[:, i, (c - 4) * 128:(c - 3) * 128]
                A("pe", lambda e, c=c, src=src: e.transpose(out=pbf(6)[:, c * 128:(c + 1) * 128], in_=src, identity=identb[:, :]),
                  reads=["ga_st", "m_st", "identb"], writes=[Rp[6]])
            A("act", lambda e: e.activation(out=mT[:, :, :], in_=pbf(6).rearrange("p (c t) -> p c t", c=8), func=AF.Copy),
              reads=[Rp[6]], writes=["mT"])
            A("sp", lambda e, rows=rows: e.dma_start(out=xt[:, :], in_=x_own[rows, :]), writes=["xt"], dma=True)
            for half in range(2):
                for c in range(8):
                    A("pe", lambda e, c=c, half=half: e.matmul(pf[half][:, :], lhsT=mT[:, c, :], rhs=wout[:, c, half * 512:(half + 1) * 512],
                                                               start=(c == 0), stop=(c == 7)),
                      reads=["mT", "wout"], writes=[Rp[half]])
                A("dve", lambda e, half=half: e.tensor_tensor(out=yt[:, half * 512:(half + 1) * 512], in0=pf[half][:, :],
                                                              in1=xt[:, half * 512:(half + 1) * 512], op=ALU.add),
                  reads=[Rp[half], "xt"], writes=["yt"])
            A("sp", lambda e, rows=rows: e.dma_start(out=y_own[rows, :], in_=yt[:, :]), reads=["yt"], dma=True)

        P.emit(nc)
    return nc


def _consts(j):
    c = np.zeros((128, K_END), np.float32)
    p = np.arange(128)
    c[:, K_ID:K_ID + 128] = np.eye(128)
    c[:, K_U:K_U + 128] = (p[:, None] <= p[None, :])
    c[:, K_ONE:K_ONE + 128] = 1.0
    c[:, K_L:K_L + 128] = (p[:, None] > p[None, :])
    for w in range(4):
        if w < j:
            m = np.ones((128, 128))
        elif w == j:
            m = (p[:, None] <= p[None, :])
        else:
            m = np.zeros((128, 128))
        c[:, K_MW + w * 128:K_MW + (w + 1) * 128] = m
    c[:, K_IOTA] = p % 16
    c[:, K_OH4 + j] = 1.0
    for b in range(16):
        c[0:8, K_EB + b * 16 + b] = 1.0
        c[:, K_CS + b * 16 + b] = 1.0
    for h in range(8):
        c[h, K_BM + h * 64:K_BM + (h + 1) * 64] = 1.0
        c[h, K_ID8 + h] = 1.0
        c[h, K_OHT + h * 128:K_OHT + (h + 1) * 128] = 1.0
    return c


def kernel(x_prompt, x_sample, cache_k, cache_v, cache_logf, page_table, g_norm, w_in,
           b_f, g_q, g_k, g_v, w_s, b_s, w_out):
    f = lambda a: np.ascontiguousarray(np.asarray(a))
    x_prompt = f(x_prompt); x_sample = f(x_sample)
    ckr = f(cache_k).reshape(2560 * 128, 512)
    cvr = f(cache_v).reshape(2560 * 128, 512)
    clr = f(cache_logf).reshape(2560 * 128, 8)
    ptab = f(page_table).astype(np.int32)
    nc = build_nc()
    in_maps = []
    for c in range(8):
        b, j = c // 4, c % 4
        xa = x_prompt[b]
        xo = np.ascontiguousarray(xa.reshape(16, 4, 128, D)[:, j].reshape(NOWN * 128, D))
        in_maps.append(dict(
            x_all=xa, x_own=xo, x_s=np.ascontiguousarray(x_sample[16 * c:16 * c + 16, 0, :]),
            ck=ckr, cv=cvr, cl=clr,
            pt=np.ascontiguousarray(ptab[16 * c:16 * c + 16].reshape(16, 2, 8)[:, :, np.arange(128) // 16].transpose(2, 0, 1).reshape(128, 32)),
            g_norm=f(g_norm)[0], w_in=f(w_in)[0], b_f=f(b_f)[0], g_q=f(g_q)[0], g_k=f(g_k)[0], g_v=f(g_v)[0],
            w_s=f(w_s)[0], b_s=f(b_s)[0], w_out=f(w_out)[0], consts=_consts(j)))
    res = run_bass_kernel_spmd(nc, in_maps, core_ids=list(range(8)))
    R = res.results
    yp = np.zeros((2, S, D), np.float32)
    kp = np.zeros((1, 2, S, 8, 64), np.float32)
    vp = np.zeros((1, 2, S, 8, 64), np.float32)
    lp = np.zeros((1, 2, S, 8), np.float32)
    ys = np.zeros((128, 1, D), np.float32)
    ks = np.zeros((1, 128, 1, 8, 64), np.float32)
    vs = np.zeros((1, 128, 1, 8, 64), np.float32)
    ls = np.zeros((1, 128, 1, 8), np.float32)
    gs = np.zeros((1, 128, 1, 4, 128), np.float32)
    for c in range(8):
        b, j = c // 4, c % 4
        r = R[c]
        yp[b].reshape(16, 4, 128, D)[:, j] = r["y_own"].reshape(16, 128, D)
        kp[0, b].reshape(16, 4, 128, 512)[:, j] = r["k_own"].reshape(16, 128, 512)
        vp[0, b].reshape(16, 4, 128, 512)[:, j] = r["v_own"].reshape(16, 128, 512)
        lp[0, b].reshape(16, 4, 128, 8)[:, j] = r["l_own"].reshape(16, 128, 8)
        sl = slice(16 * c, 16 * c + 16)
        ys[sl, 0] = r["y_s"]
        ks[0, sl, 0] = r["k_s"].reshape(16, 8, 64)
        vs[0, sl, 0] = r["v_s"].reshape(16, 8, 64)
        ls[0, sl, 0] = r["l_s"]
        gs[0, sl, 0] = r["gv_s"].reshape(16, 4, 128)
    return (yp, ys, kp, vp, lp, ks, vs, ls, gs)
```

```python
import contextlib
import numpy as np
import concourse.bass as bass
import concourse.mybir as mybir
from concourse.bass_utils import run_bass_kernel_spmd

F32 = mybir.dt.float32
BF16 = mybir.dt.bfloat16
I32 = mybir.dt.int32
ALU = mybir.AluOpType
AF = mybir.ActivationFunctionType
AX = mybir.AxisListType

D = 1024
DIN = 3592
S = 8192
NT = 64
NOWN = 16
NS = 16
NPG = 16
EPS = 1e-6
SCALE = 0.125
C_Q, C_K, C_V, C_F, C_ZA, C_U, C_GV, C_ZC = 0, 512, 1024, 1536, 1544, 2056, 2568, 3080

K_ID, K_U, K_ONE, K_L, K_MW, K_IOTA, K_OH4, K_EB, K_BM, K_ID8, K_OHT, K_CS, K_END = (
    0, 128, 256, 384, 512, 1024, 1025, 1029, 1285, 1797, 1805, 2829, 3085)

ENGS = ("pe", "act", "dve", "pool", "sp")


class Res:
    __slots__ = ("w", "r")

    def __init__(self):
        self.w = None
        self.r = []


class Op:
    __slots__ = ("eng", "fn", "deps", "signaled", "count", "dma", "sem", "prev_on_sem")

    def __init__(self, eng, fn, dma):
        self.eng = eng
        self.fn = fn
        self.dma = dma
        self.deps = []
        self.signaled = dma
        self.count = None
        self.sem = None
        self.prev_on_sem = None


class Prog:
    def __init__(self, n_dma_sems=16):
        self.q = {e: [] for e in ENGS}
        self.n_dma_sems = n_dma_sems
        self.all_dma = []

    def add(self, eng, fn, reads=(), writes=(), dma=False):
        op = Op(eng, fn, dma)
        deps = {}
        for r in reads:
            if r.w is not None:
                deps[id(r.w)] = (r.w, "raw")
        for w in writes:
            if w.w is not None and id(w.w) not in deps:
                deps[id(w.w)] = (w.w, "waw")
            for rr in w.r:
                if id(rr) not in deps:
                    deps[id(rr)] = (rr, "war")
        for d, kind in deps.values():
            if d is op:
                continue
            if not d.dma and not dma and d.eng == eng:
                if eng == "pe":
                    continue
                if kind != "raw":
                    continue
            d.signaled = True
            op.deps.append(d)
        for r in reads:
            r.r.append(op)
        for w in writes:
            w.w = op
            w.r = []
        self.q[eng].append(op)
        if dma:
            self.all_dma.append(op)
        return op

    def emit(self, nc):
        stack = contextlib.ExitStack()
        with stack:
            esem = {e: stack.enter_context(nc.semaphore("s_" + e)) for e in ENGS}
            dsem = {e: [stack.enter_context(nc.semaphore("d_%s%d" % (e, i))) for i in range(self.n_dma_sems)]
                    for e in ("sp", "act", "pool")}
            for e in ENGS:
                c = 0
                dcount = [0] * self.n_dma_sems
                dlast = [None] * self.n_dma_sems
                k = 0
                for op in self.q[e]:
                    if op.dma:
                        s = k % self.n_dma_sems
                        k += 1
                        dcount[s] += 16
                        op.sem = dsem[e][s]
                        op.count = dcount[s]
                        op.prev_on_sem = dlast[s]
                        dlast[s] = op
                    elif op.signaled:
                        c += 1
                        op.sem = esem[e]
                        op.count = c
            block = stack.enter_context(nc.Block())

            def run(e):
                def body(eng):
                    waited = {}

                    def wait_for(d):
                        key = id(d.sem)
                        if waited.get(key, 0) >= d.count:
                            return
                        eng.wait_ge(d.sem, d.count)
                        waited[key] = d.count

                    for op in self.q[e]:
                        for d in op.deps:
                            wait_for(d)
                        if op.dma and op.prev_on_sem is not None:
                            wait_for(op.prev_on_sem)
                        ins = op.fn(eng)
                        if op.dma:
                            ins.then_inc(op.sem, 16)
                        elif op.signaled:
                            ins.then_inc(op.sem, 1)
                    if e == "sp":
                        last = {}
                        for d in self.all_dma:
                            last[id(d.sem)] = d
                        for d in last.values():
                            wait_for(d)
                return body

            block.tensor(run("pe"))
            block.scalar(run("act"))
            block.vector(run("dve"))
            block.gpsimd(run("pool"))
            block.sync(run("sp"))


def build_nc():
    nc = bass.Bass("TRN2", target_bir_lowering=False)

    def din(name, shape, dt=F32):
        return nc.dram_tensor(name, shape, dt, kind="ExternalInput").ap()

    def dout(name, shape, dt=F32):
        return nc.dram_tensor(name, shape, dt, kind="ExternalOutput").ap()

    x_all = din("x_all", [S, D])
    x_own = din("x_own", [NOWN * 128, D])
    x_s = din("x_s", [NS, D])
    ck = din("ck", [2560 * 128, 512])
    cv = din("cv", [2560 * 128, 512])
    cl = din("cl", [2560 * 128, 8])
    pt = din("pt", [128, 2 * NS], I32)
    g_norm = din("g_norm", [D])
    w_in = din("w_in", [D, DIN])
    b_f = din("b_f", [8])
    g_q = din("g_q", [64])
    g_k = din("g_k", [64])
    g_v = din("g_v", [512])
    w_s = din("w_s", [4, 128, 128])
    b_s = din("b_s", [4, 128])
    w_out = din("w_out", [D, D])
    consts_d = din("consts", [128, K_END])

    y_own = dout("y_own", [NOWN * 128, D])
    k_own = dout("k_own", [NOWN * 128, 512])
    v_own = dout("v_own", [NOWN * 128, 512])
    l_own = dout("l_own", [NOWN * 128, 8])
    y_s = dout("y_s", [NS, D])
    k_s = dout("k_s", [NS, 512])
    v_s = dout("v_s", [NS, 512])
    l_s = dout("l_s", [NS, 8])
    gv_s = dout("gv_s", [NS, 512])

    kT_scr = nc.dram_tensor("kT_scr", [4, 128, S], BF16, kind="Internal").ap()
    v_scr = nc.dram_tensor("v_scr", [4, 128, NT, 130], BF16, kind="Internal").ap()
    q_scr = nc.dram_tensor("q_scr", [NS, 512], F32, kind="Internal").ap()

    P = Prog()
    st = contextlib.ExitStack()
    with st:
        def sb(name, shape, dt):
            return st.enter_context(nc.sbuf_tensor(name, shape, dt))

        cst = sb("cst", [128, K_END], F32)
        identf = cst[:, K_ID:K_ID + 128]
        utri = cst[:, K_U:K_U + 128]
        ones = cst[:, K_ONE:K_ONE + 128]
        ltri = cst[:, K_L:K_L + 128]
        iota = cst[:, K_IOTA:K_IOTA + 1]
        oh4 = cst[:, K_OH4:K_OH4 + 4]
        identb = sb("identb", [128, 128], BF16)
        maskb = sb("maskb", [128, 4, 128], BF16)
        ohTb = sb("ohTb", [8, 8, 128], BF16)
        wout = sb("wout", [128, 8, D], BF16)
        g_bc = sb("g_bc", [128, D], F32)
        gv_bc = sb("gv_bc", [128, 512], F32)
        gq_bc = sb("gq_bc", [128, 64], F32)
        gk_bc = sb("gk_bc", [128, 64], F32)
        bf_bc = sb("bf_bc", [128, 8], F32)
        bs_t = sb("bs_t", [128, 4], F32)
        ws00 = sb("ws00", [128, 4], F32)
        bs0 = sb("bs0", [128, 4], F32)
        wsT = sb("wsT", [128, 4, 128], BF16)
        negc = sb("negc", [128, NT, 8], F32)
        carry = sb("carry", [128, NT, 8], F32)
        runc = sb("runc", [128, 8], F32)
        c_own = sb("c_own", [128, NOWN, 8], F32)
        xt = sb("xt", [128, D], F32)
        xn = sb("xn", [128, D], BF16)
        junk = xn
        xT = sb("xT", [128, 8, 128], BF16)
        xT2 = sb("xT2", [128, 8, 128], BF16)
        ss = sb("ss", [128, 1], F32)
        rr = sb("rr", [128, 1], F32)
        sq = sb("sq", [128, 512], F32)
        ssq = sb("ssq", [128, 8], F32)
        rk = sb("rk", [128, 8], F32)
        kn = sb("kn", [128, 512], F32)
        kf = sb("kf", [128, 512], F32)
        knb = sb("knb", [128, 512], BF16)
        vf = sb("vf", [128, 512], F32)
        zf = sb("zf", [128, 8], F32)
        lf = sb("lf", [128, 8], F32)
        lfs = sb("lfs", [128, 8], F32)
        arT = sb("arT", [128, 2080], F32)
        kT_stage = arT[:, 0:1024].bitcast(BF16).rearrange("p (h t) -> p h t", h=4)
        v_stage = arT[:, 1024:2064].bitcast(BF16).rearrange("p (s q a h d) -> p s q a h d", s=2, q=4, a=2, h=2)
        et = arT[:, 0:512]
        ub = arT[:, 512:1024]
        gvn = sb("gvn", [128, 512], BF16)
        gvnf = sb("gvnf", [128, 512], F32)
        gc = arT[:, 1024:1536]
        tmpm = arT[:, 1536:2048]
        arW = sb("arW", [128, 8 * DIN], BF16)
        arQ = sb("arQ", [128, 25600], BF16)

        Wb = arW[:, :].rearrange("p (c n) -> p c n", c=8)
        kTp = arW[:, 0:8192]
        vP = arW[:, 8192:8192 + 64 * 130].rearrange("p (t e) -> p t e", t=64)
        o = 8192 + 64 * 130
        pTb = [arW[:, o + i * 512:o + (i + 1) * 512] for i in range(4)]
        o += 4 * 512
        o_sb = arW[:, o:o + 1024].bitcast(F32)
        o += 1024
        a_rows = arW[:, o:o + 2048]
        o += 2048
        rcp_t = arW[:, o:o + 8].bitcast(F32)
        o += 8
        mT = arW[:, o:o + 1024].rearrange("p (c n) -> p c n", c=8)
        o += 1024
        yt = arW[:, o:o + 2048].bitcast(F32)
        o += 2048
        qTh = arW[:, o:o + 2048]
        o += 2048
        assert o <= 8 * DIN
        qT_st = arQ[:, 0:8192].rearrange("p (h n) -> p h n", h=4)
        ga_st = arQ[:, 8192:16384].rearrange("p (i n) -> p i n", i=NOWN)
        m_st = arQ[:, 16384:24576].rearrange("p (i n) -> p i n", i=NOWN)
        Kb = arQ[:, 0:8192].bitcast(F32).rearrange("p (g n) -> p g n", g=8)
        Vb = arQ[:, 8192:16384].bitcast(F32).rearrange("p (g n) -> p g n", g=8)
        Wstage = arQ[:, 0:2 * DIN].bitcast(F32)
        Lh = arQ[:, 16384:20992].bitcast(F32).rearrange("p (c t h) -> p c t h", c=2 * NS, t=9)
        Vbb = arQ[:, 20992:25088].rearrange("p (g n) -> p g n", g=8)
        ptb = arQ[:, 25088:25152].bitcast(I32)
        idx = arQ[:, 25152:25216].bitcast(I32)
        tot_t = sb("tot_t", [128, 256], F32)
        tsum = sb("tsum", [128, 256], F32)
        qb = sb("qb", [128, 512], F32)
        s_t = sb("s_t", [128, 128], F32)
        p_t = sb("p_t", [128, 128], BF16)
        psm = sb("psm", [128, 8], F32)
        mo = sb("mo", [8, 512], F32)
        md = sb("md", [8, 8], F32)
        qn_s = sb("qn_s", [NS, 512], F32)
        kn_s = sb("kn_s", [NS, 512], F32)
        vf_s = sb("vf_s", [NS, 512], F32)
        ga_s = sb("ga_s", [NS, 512], BF16)
        m_s = sb("m_s", [NS, 512], BF16)
        sm1 = sb("sm1", [NS, 512], F32)
        sm2 = sb("sm2", [NS, 8], F32)
        sm3 = sb("sm3", [NS, 8], F32)
        mrg_s = sb("mrg_s", [NS, D], BF16)
        mT_s = sb("mT_s", [128, 8, NS], BF16)

        pf = [st.enter_context(nc.psum_tensor("pf%d" % i, [128, 512], F32)) for i in range(8)]
        Rp = [Res() for _ in range(8)]

        def pbf(i):
            return pf[i][:, :].bitcast(BF16)

        R = {}

        def res(name):
            if name not in R:
                R[name] = Res()
            return R[name]

        cap = [None]

        def A(eng, fn, reads=(), writes=(), dma=False):
            if cap[0] is not None:
                cap[0].append((eng, fn, reads, writes, dma))
                return None
            rl = [res(r) if isinstance(r, str) else r for r in reads]
            wl = [res(w) if isinstance(w, str) else w for w in writes]
            return P.add(eng, fn, rl, wl, dma)

        def record(f, *args):
            cap[0] = []
            f(*args)
            lst = cap[0]
            cap[0] = None
            return lst

        def interleave(*lists):
            lists = [l for l in lists if l]
            pos = [0] * len(lists)
            left = sum(len(l) for l in lists)
            while left:
                for i, l in enumerate(lists):
                    if pos[i] < len(l):
                        A(*l[pos[i]])
                        pos[i] += 1
                        left -= 1

        A("sp", lambda e: e.dma_start(out=cst[:, :], in_=consts_d), writes=["cst"], dma=True)
        A("sp", lambda e: e.dma_start(out=g_bc[:, :], in_=g_norm.partition_broadcast(128)), writes=["g_bc"], dma=True)
        A("sp", lambda e: e.dma_start(out=gv_bc[:, :], in_=g_v.partition_broadcast(128)), writes=["gv_bc"], dma=True)
        A("sp", lambda e: e.dma_start(out=gq_bc[:, :], in_=g_q.partition_broadcast(128)), writes=["gq_bc"], dma=True)
        A("sp", lambda e: e.dma_start(out=gk_bc[:, :], in_=g_k.partition_broadcast(128)), writes=["gk_bc"], dma=True)
        A("sp", lambda e: e.dma_start(out=bf_bc[:, :], in_=b_f.partition_broadcast(128)), writes=["bf_bc"], dma=True)
        for g in range(4):
            A("sp", lambda e, g=g: e.dma_start(out=bs_t[:, g:g + 1], in_=b_s[g].rearrange("(t o) -> t o", o=1)),
              writes=["bs_t"], dma=True)
            A("sp", lambda e, g=g: e.dma_start(out=ws00[:, g:g + 1], in_=w_s[g, 0, 0:1].partition_broadcast(128)),
              writes=["ws00"], dma=True)
            A("sp", lambda e, g=g: e.dma_start(out=bs0[:, g:g + 1], in_=b_s[g, 0:1].partition_broadcast(128)),
              writes=["bs0"], dma=True)
        w_in_v = w_in.rearrange("(c p) n -> p c n", p=128)
        for c in range(8):
            A("sp", lambda e, c=c: e.dma_start(out=Wstage, in_=w_in[c * 128:(c + 1) * 128, :]), writes=["Kb"], dma=True)
            A("act", lambda e, c=c: e.activation(out=Wb[:, c, :], in_=Wstage, func=AF.Copy), reads=["Kb"], writes=["Wb"])
        A("pool", lambda e: e.dma_start(out=wout[:, :, :], in_=w_out.rearrange("(c p) n -> p c n", p=128)),
          writes=["wout"], dma=True)
        A("dve", lambda e: e.tensor_copy(out=identb[:, :], in_=identf), reads=["cst"], writes=["identb"])
        A("dve", lambda e: e.tensor_scalar(out=maskb[:, :, :].rearrange("p w t -> p (w t)"), in0=cst[:, K_MW:K_MW + 512],
                                           scalar1=1.0e4, scalar2=-1.0e4, op0=ALU.mult, op1=ALU.add),
          reads=["cst"], writes=["maskb"])
        A("dve", lambda e: e.tensor_copy(out=ohTb[:, :, :].rearrange("p h s -> p (h s)"), in_=cst[0:8, K_OHT:K_OHT + 1024]),
          reads=["cst"], writes=["ohTb"])
        A("dve", lambda e: e.memset(runc[:, :], 0.0), writes=["runc"])
        ws_t = tmpm.rearrange("p (g s) -> p g s", g=4)
        A("sp", lambda e: e.dma_start(out=ws_t, in_=w_s.rearrange("g t s -> t g s")), writes=["tmpm"], dma=True)
        for g in range(4):
            A("pe", lambda e, g=g: e.transpose(out=pf[0][:, g * 128:(g + 1) * 128], in_=ws_t[:, g, :], identity=identf),
              reads=["tmpm", "cst"], writes=[Rp[0]])
        for g in range(4):
            A("dve", lambda e, g=g: e.tensor_tensor(out=wsT[:, g, :], in0=pf[0][:, g * 128:(g + 1) * 128], in1=utri, op=ALU.mult),
              reads=[Rp[0], "cst"], writes=["wsT"])

        def rsqrt_act(dst, src, n, scale):
            A("act", lambda e: e.activation(out=dst, in_=src, func=AF.Ln, scale=scale, bias=EPS), reads=["tmp_r_in"], writes=["tmp_r"])
            A("act", lambda e: e.activation(out=dst, in_=dst, func=AF.Exp, scale=-0.5), reads=["tmp_r"], writes=["tmp_r"])

        def norm_T(src, n, xTb=None, xTr="xT", ldq="sp"):
            xTb = xT if xTb is None else xTb
            A(ldq, lambda e: e.dma_start(out=xt[:n, :], in_=src), writes=["xt"], dma=True)
            A("dve", lambda e: e.memset(ss[:n, :], 0.0), writes=["ss"])
            A("act", lambda e: e.activation(out=junk[:n, :], in_=xt[:n, :], func=AF.Square, accum_out=ss[:n, :]),
              reads=["xt", "ss"], writes=["xn", "ss"])
            A("act", lambda e: e.activation(out=rr[:n, :], in_=ss[:n, :], func=AF.Ln, scale=1.0 / D, bias=EPS),
              reads=["ss"], writes=["rr"])
            A("act", lambda e: e.activation(out=rr[:n, :], in_=rr[:n, :], func=AF.Exp, scale=-0.5),
              reads=["rr"], writes=["rr"])
            A("dve", lambda e: e.scalar_tensor_tensor(out=xn[:n, :], in0=xt[:n, :], scalar=rr[:n, 0:1], in1=g_bc[:n, :],
                                                      op0=ALU.mult, op1=ALU.mult),
              reads=["xt", "rr", "g_bc"], writes=["xn"])
            for c in range(8):
                A("pe", lambda e, c=c: e.transpose(out=pbf(0)[:, c * 128:c * 128 + n], in_=xn[:n, c * 128:(c + 1) * 128],
                                                   identity=identb[:n, :n]),
                  reads=["xn", "identb"], writes=[Rp[0]])
            A("act", lambda e: e.activation(out=xTb[:, :, 0:n], in_=pbf(0).rearrange("p (c t) -> p c t", c=8)[:, :, 0:n], func=AF.Copy),
              reads=[Rp[0]], writes=[xTr])

        def proj(bank, col0, width, n, xTb=None, xTr="xT"):
            xTb = xT if xTb is None else xTb
            for c in range(8):
                A("pe", lambda e, c=c: e.matmul(pf[bank][:n, 0:width], lhsT=xTb[:, c, 0:n], rhs=Wb[:, c, col0:col0 + width],
                                                start=(c == 0), stop=(c == 7)),
                  reads=[xTr, "Wb"], writes=[Rp[bank]])

        def headnorm(bank, n, nh, hd, gb, out_ap, out_res):
            A("act", lambda e: e.activation(out=sq[:n, :], in_=pf[bank][:n, :], func=AF.Square), reads=[Rp[bank]], writes=["sq"])
            A("dve", lambda e: e.tensor_reduce(out=ssq[:n, 0:nh], in_=sq[:n, :].rearrange("p (h d) -> p h d", h=nh),
                                               axis=AX.X, op=ALU.add), reads=["sq"], writes=["ssq"])
            A("act", lambda e: e.activation(out=rk[:n, 0:nh], in_=ssq[:n, 0:nh], func=AF.Ln, scale=1.0 / hd, bias=EPS),
              reads=["ssq"], writes=["rk"])
            A("act", lambda e: e.activation(out=rk[:n, 0:nh], in_=rk[:n, 0:nh], func=AF.Exp, scale=-0.5), reads=["rk"], writes=["rk"])
            A("dve", lambda e: e.tensor_tensor(out=kn[:n, :].rearrange("p (h d) -> p h d", h=nh),
                                               in0=pf[bank][:n, :].rearrange("p (h d) -> p h d", h=nh),
                                               in1=rk[:n, 0:nh].unsqueeze(2).to_broadcast([n, nh, hd]), op=ALU.mult),
              reads=[Rp[bank], "rk"], writes=["kn"])
            if hd == 64:
                in1 = gb[:n, :].unsqueeze(1).to_broadcast([n, nh, hd])
                A("dve", lambda e: e.tensor_tensor(out=out_ap.rearrange("p (h d) -> p h d", h=nh),
                                                   in0=kn[:n, :].rearrange("p (h d) -> p h d", h=nh), in1=in1, op=ALU.mult),
                  reads=["kn"], writes=[out_res])
            else:
                A("dve", lambda e: e.tensor_tensor(out=out_ap, in0=kn[:n, :], in1=gb[:n, :], op=ALU.mult),
                  reads=["kn"], writes=[out_res])

        def logf_of(bank, n):
            A("dve", lambda e: e.tensor_tensor(out=zf[:n, :], in0=pf[bank][:n, 0:8], in1=bf_bc[:n, :], op=ALU.add),
              reads=[Rp[bank], "bf_bc"], writes=["zf"])
            A("act", lambda e: e.activation(out=zf[:n, :], in_=zf[:n, :], func=AF.Exp, scale=-1.0), reads=["zf"], writes=["zf"])
            A("act", lambda e: e.activation(out=zf[:n, :], in_=zf[:n, :], func=AF.Ln, bias=1.0), reads=["zf"], writes=["zf"])
            A("dve", lambda e: e.tensor_scalar(out=lf[:n, :], in0=zf[:n, :], scalar1=-1.0, scalar2=None, op0=ALU.mult),
              reads=["zf"], writes=["lf"])

        def silu_from(bank, n, out_ap, out_res):
            A("act", lambda e: e.activation(out=et[:n, :], in_=pf[bank][:n, :], func=AF.Exp, scale=-1.0), reads=[Rp[bank]], writes=["et"])
            A("dve", lambda e: e.tensor_scalar(out=et[:n, :], in0=et[:n, :], scalar1=1.0, scalar2=None, op0=ALU.add),
              reads=["et"], writes=["et"])
            A("dve", lambda e: e.reciprocal(out=et[:n, :], in_=et[:n, :]), reads=["et"], writes=["et"])
            A("dve", lambda e: e.tensor_tensor(out=out_ap, in0=pf[bank][:n, :], in1=et[:n, :], op=ALU.mult),
              reads=[Rp[bank], "et"], writes=[out_res])

        def rest_proj(n, ga_out, ga_res, m_out, m_res, sample):
            proj(3, C_ZA, 512, n)
            silu_from(3, n, ga_out, ga_res)
            proj(4, C_U, 512, n)
            A("act", lambda e: e.activation(out=ub[:n, :], in_=pf[4][:n, :], func=AF.Copy), reads=[Rp[4]], writes=["ub"])
            proj(3, C_GV, 512, n)
            if sample:
                headnorm(3, n, 4, 128, gv_bc, gvnf[:n, :], "gvnf")
                A("sp", lambda e: e.dma_start(out=gv_s, in_=gvnf[:n, :]), reads=["gvnf"], dma=True)
            else:
                headnorm(3, n, 4, 128, gv_bc, gvn[:n, :], "gvn")
                for g in range(4):
                    A("pe", lambda e, g=g: e.matmul(pf[5][:n, g * 128:(g + 1) * 128], lhsT=wsT[:, g, :],
                                                    rhs=gvn[:, g * 128:(g + 1) * 128], start=True, stop=True),
                      reads=["wsT", "gvn"], writes=[Rp[5]])
            proj(4, C_ZC, 512, n)
            silu_from(4, n, gc[:n, :], "gc")
            for g in range(4):
                sl = slice(g * 128, (g + 1) * 128)
                if sample:
                    A("dve", lambda e, g=g, sl=sl: e.tensor_scalar(out=tmpm[:n, sl], in0=gvnf[:n, sl], scalar1=ws00[:n, g:g + 1],
                                                                   scalar2=bs0[:n, g:g + 1], op0=ALU.mult, op1=ALU.add),
                      reads=["gvnf", "ws00", "bs0"], writes=["tmpm"])
                    A("dve", lambda e, sl=sl: e.tensor_tensor(out=tmpm[:n, sl], in0=tmpm[:n, sl], in1=ub[:n, sl], op=ALU.mult),
                      reads=["tmpm", "ub"], writes=["tmpm"])
                else:
                    A("dve", lambda e, g=g, sl=sl: e.scalar_tensor_tensor(out=tmpm[:n, sl], in0=pf[5][:n, sl], scalar=bs_t[:n, g:g + 1],
                                                                          in1=ub[:n, sl], op0=ALU.add, op1=ALU.mult),
                      reads=[Rp[5], "bs_t", "ub"], writes=["tmpm"])
            A("dve", lambda e: e.tensor_tensor(out=m_out, in0=tmpm[:n, :], in1=gc[:n, :], op=ALU.mult),
              reads=["tmpm", "gc"], writes=[m_res])

        n = NS
        norm_T(x_s, n)
        proj(1, C_Q, 512, n)
        headnorm(1, n, 8, 64, gq_bc, qn_s[:n, :], "qn_s")
        A("sp", lambda e: e.dma_start(out=q_scr, in_=qn_s[:n, :]), reads=["qn_s"], writes=["q_scr"], dma=True)
        proj(1, C_K, 512, n)
        headnorm(1, n, 8, 64, gk_bc, kn_s[:n, :], "kn_s")
        A("sp", lambda e: e.dma_start(out=k_s, in_=kn_s[:n, :]), reads=["kn_s"], dma=True)
        proj(2, C_V, 512, n)
        A("act", lambda e: e.activation(out=vf_s[:n, :], in_=pf[2][:n, :], func=AF.Copy), reads=[Rp[2]], writes=["vf_s"])
        A("sp", lambda e: e.dma_start(out=v_s, in_=vf_s[:n, :]), reads=["vf_s"], dma=True)
        proj(2, C_F, 8, n)
        logf_of(2, n)
        A("dve", lambda e: e.tensor_copy(out=lfs[:n, :], in_=lf[:n, :]), reads=["lf"], writes=["lfs"])
        A("sp", lambda e: e.dma_start(out=l_s, in_=lfs[:n, :]), reads=["lfs"], dma=True)
        rest_proj(n, ga_s[:n, :], "ga_s", m_s[:n, :], "m_s", True)

        ck2 = ck.rearrange("(r t) n -> r (t n)", t=8)
        cv2 = cv.rearrange("(r t) n -> r (t n)", t=8)
        cl2 = cl.rearrange("(r t) n -> r (t n)", t=8)
        A("sp", lambda e: e.dma_start(out=ptb, in_=pt), writes=["ptb"], dma=True)
        A("dve", lambda e: e.tensor_scalar(out=idx, in0=ptb, scalar1=16.0, scalar2=iota, op0=ALU.mult, op1=ALU.add),
          reads=["ptb", "cst"], writes=["idx"])
        A("dve", lambda e: e.memset(arQ[:, 16384:20992].bitcast(F32), 0.0), writes=["Lall"])
        for col in range(2 * NS):
            A("pool", lambda e, col=col: e.indirect_dma_start(
                out=arQ[:, 16384:20992].bitcast(F32)[:, col * 72:col * 72 + 64], out_offset=None, in_=cl2,
                in_offset=bass.IndirectOffsetOnAxis(ap=idx[:, col:col + 1], axis=0)),
              reads=["idx"], writes=["Lall"], dma=True)

        def compute_E():
            A("dve", lambda e: e.tensor_reduce(out=tot_t[:, :].rearrange("p (c h) -> p c h", h=8),
                                               in_=Lh[:, :, 0:8, :].rearrange("p c t h -> p c h t"), axis=AX.X, op=ALU.add),
              reads=["Lall"], writes=["tot_t"])
            A("pe", lambda e: e.matmul(pf[3][:, 0:256], lhsT=ltri, rhs=tot_t[:, :], start=True, stop=True),
              reads=["tot_t", "cst"], writes=[Rp[3]])
            A("pe", lambda e: e.matmul(pf[4][:, 0:128], lhsT=ones,
                                       rhs=tot_t[:, :].rearrange("p (b f h) -> p b f h", f=2, h=8)[:, :, 1, :], start=True, stop=True),
              reads=["tot_t", "cst"], writes=[Rp[4]])
            A("dve", lambda e: e.tensor_copy(out=tsum[:, :], in_=pf[3][:, 0:256]), reads=[Rp[3]], writes=["tsum"])
            A("dve", lambda e: e.tensor_tensor(out=tsum[:, :].rearrange("p (b f h) -> p b f h", f=2, h=8)[:, :, 0, :],
                                               in0=tsum[:, :].rearrange("p (b f h) -> p b f h", f=2, h=8)[:, :, 0, :],
                                               in1=pf[4][:, 0:128].rearrange("p (b h) -> p b h", h=8), op=ALU.add),
              reads=[Rp[4], "tsum"], writes=["tsum"])
            for t in range(6, -1, -1):
                A("dve", lambda e, t=t: e.tensor_tensor(out=Lh[:, :, t, :], in0=Lh[:, :, t, :], in1=Lh[:, :, t + 1, :], op=ALU.add),
                  reads=["Lall"], writes=["Lall"])
            for t in range(1, 9):
                A("dve", lambda e, t=t: e.tensor_tensor(out=Lh[:, :, t, :], in0=Lh[:, :, t, :],
                                                        in1=tsum[:, :].rearrange("p (c h) -> p c h", h=8), op=ALU.add),
                  reads=["Lall", "tsum"], writes=["Lall"])

        def gather_half(b, hf):
            col = b * 2 + hf
            A("pool", lambda e: e.indirect_dma_start(
                out=Kb.rearrange("p g n -> p (g n)"), out_offset=None, in_=ck2,
                in_offset=bass.IndirectOffsetOnAxis(ap=idx[:, col:col + 1], axis=0)),
              reads=["idx"], writes=["Kb"], dma=True)
            A("pool", lambda e: e.indirect_dma_start(
                out=Vb.rearrange("p g n -> p (g n)"), out_offset=None, in_=cv2,
                in_offset=bass.IndirectOffsetOnAxis(ap=idx[:, col:col + 1], axis=0)),
              reads=["idx"], writes=["Vb"], dma=True)

        def sample_half(b, hf):
            hs = slice(hf * 64, hf * 64 + 64)
            if hf == 0:
                A("sp", lambda e: e.dma_start(out=qb[:, :], in_=q_scr[b].partition_broadcast(128)), reads=["q_scr"], writes=["qb"], dma=True)
            A("dve", lambda e: e.tensor_tensor(out=Kb, in0=Kb, in1=qb[:, :].unsqueeze(1).to_broadcast([128, 8, 512]), op=ALU.mult),
              reads=["Kb", "qb"], writes=["Kb"])
            A("dve", lambda e: e.tensor_reduce(out=s_t[:, hs], in_=Kb.rearrange("p g (h d) -> p (g h) d", h=8), axis=AX.X, op=ALU.add),
              reads=["Kb"], writes=["s_t"])
            A("dve", lambda e: e.scalar_tensor_tensor(out=s_t[:, hs].rearrange("p (g h) -> p g h", g=8), in0=s_t[:, hs].rearrange("p (g h) -> p g h", g=8),
                                                      scalar=SCALE, in1=Lh[:, b * 2 + hf, 1:9, :], op0=ALU.mult, op1=ALU.add),
              reads=["s_t", "Lall"], writes=["s_t"])
            A("act", lambda e: e.activation(out=p_t[:, hs], in_=s_t[:, hs], func=AF.Exp), reads=["s_t"], writes=["p_t"])
            A("act", lambda e: e.activation(out=Vbb, in_=Vb, func=AF.Copy), reads=["Vb"], writes=["Vbb"])
            for g in range(8):
                gg = hf * 8 + g
                A("pe", lambda e, g=g, gg=gg: e.matmul(pf[5][0:8, :], lhsT=p_t[:, gg * 8:(gg + 1) * 8], rhs=Vbb[:, g, :],
                                                       start=(gg == 0), stop=(gg == NPG - 1)),
                  reads=["p_t", "Vbb"], writes=[Rp[5]])
            if hf == 0:
                return
            A("dve", lambda e: e.tensor_reduce(out=psm[:, :], in_=p_t[:, :].rearrange("p (g h) -> p h g", g=NPG), axis=AX.X, op=ALU.add),
              reads=["p_t"], writes=["psm"])
            A("dve", lambda e: e.tensor_tensor(out=mo[:, :], in0=pf[5][0:8, :], in1=cst[0:8, K_BM:K_BM + 512], op=ALU.mult),
              reads=[Rp[5], "cst"], writes=["mo"])
            A("pe", lambda e: e.matmul(pf[3][0:NS, :], lhsT=cst[0:8, K_EB + b * 16:K_EB + (b + 1) * 16], rhs=mo[:, :],
                                       start=(b == 0), stop=(b == NS - 1)),
              reads=["mo", "cst"], writes=[Rp[3]])
            A("pe", lambda e: e.matmul(pf[4][0:NS, 0:8], lhsT=cst[:, K_CS + b * 16:K_CS + (b + 1) * 16], rhs=psm[:, :],
                                       start=(b == 0), stop=(b == NS - 1)),
              reads=["psm", "cst"], writes=[Rp[4]])

        xTs = ((xT, "xT"), (xT2, "xT2"))

        def pA_h1(I):
            xb, xr = xTs[I % 2]
            norm_T(x_all[I * 128:(I + 1) * 128, :], 128, xb, xr, "act")

        def pA_k(I):
            xb, xr = xTs[I % 2]
            proj(1, C_K, 512, 128, xb, xr)
            headnorm(1, 128, 8, 64, gk_bc, knb[:, :], "knb")

        def pA_kb(I):
            for hp in range(4):
                A("pe", lambda e, hp=hp: e.transpose(out=pbf(6)[:, hp * 128:(hp + 1) * 128], in_=knb[:, hp * 128:(hp + 1) * 128],
                                                     identity=identb[:, :]),
                  reads=["knb", "identb"], writes=[Rp[6]])
            r4 = I % 4
            sl = r4 // 2
            A("act", lambda e: e.activation(out=kT_stage[:, :, r4 * 128:(r4 + 1) * 128],
                                            in_=pbf(6)[:, 0:512].rearrange("p (h t) -> p h t", h=4), func=AF.Copy),
              reads=[Rp[6]], writes=["kT_stage%d" % sl])
            if r4 % 2 == 1:
                t0 = (I - 1) * 128
                A("sp", lambda e: e.dma_start(out=kT_scr[:, :, t0:t0 + 256].rearrange("h p t -> p h t"),
                                              in_=kT_stage[:, :, sl * 256:(sl + 1) * 256]),
                  reads=["kT_stage%d" % sl], writes=["kT_scr"], dma=True)

        def pA_vf(I):
            xb, xr = xTs[I % 2]
            r4 = I % 4
            proj(2, C_V, 512, 128, xb, xr)
            sl = r4 // 2
            A("act", lambda e: e.activation(out=v_stage[:, sl, :, r4 % 2, :, 0:64],
                                            in_=pf[2][:, :].rearrange("p (q h d) -> p q h d", q=4, h=2), func=AF.Copy),
              reads=[Rp[2]], writes=["v_stage%d" % sl])
            proj(7, C_F, 8, 128, xb, xr)
            logf_of(7, 128)
            if r4 % 2 == 1:
                A("sp", lambda e: e.dma_start(
                    out=v_scr[:, :, I - 1:I + 1, :].rearrange("q p a e -> p q (a e)"),
                    in_=v_stage[:, sl, :, :, :, :].rearrange("p q a h d -> p q (a h d)")),
                  reads=["v_stage%d" % sl], writes=["v_scr"], dma=True)

        def pA_vfb(I):
            A("pe", lambda e: e.matmul(pf[7][:, 8:16], lhsT=utri, rhs=lf[:, :], start=True, stop=True), reads=["lf", "cst"], writes=[Rp[7]])
            A("pe", lambda e: e.matmul(pf[7][:, 16:24], lhsT=ones, rhs=lf[:, :], start=True, stop=True), reads=["lf", "cst"], writes=[Rp[7]])
            A("dve", lambda e: e.tensor_copy(out=carry[:, I, :], in_=runc[:, :]), reads=["runc"], writes=["carry"])
            A("dve", lambda e: e.scalar_tensor_tensor(out=negc[:, I, :], in0=pf[7][:, 8:16], scalar=-1.0, in1=runc[:, :],
                                                      op0=ALU.mult, op1=ALU.subtract),
              reads=[Rp[7], "runc"], writes=["negc"])
            A("dve", lambda e: e.tensor_tensor(out=runc[:, :], in0=runc[:, :], in1=pf[7][:, 16:24], op=ALU.add),
              reads=[Rp[7], "runc"], writes=["runc"])

        cw = sb("cw", [128, 32], F32)
        cwo = sb("cwo", [128, 8], F32)
        fz_t = sb("fz_t", [128, 4], F32)

        def fence(names):
            A("dve", lambda e: e.memset(fz_t[:, :], 0.0), writes=names)

        fence(["et", "ub", "gc", "tmpm", "kT_stage0", "kT_stage1", "v_stage0", "v_stage1"])
        A("dve", lambda e: e.memset(arT[:, 1024:2064].bitcast(BF16), 1.0), writes=["v_stage0", "v_stage1"])
        sched = {}
        for hs_ in range(2 * NS):
            sched.setdefault(15 + (hs_ * 3) // 2, []).append(hs_)
        gather_half(0, 0)
        pA_h1(0)
        for I in range(NT):
            chains = [record(pA_k, I), record(pA_vf, I)]
            if I > 0:
                chains.append(record(pA_kb, I - 1))
                chains.append(record(pA_vfb, I - 1))
            if I + 1 < NT:
                chains.insert(0, record(pA_h1, I + 1))
            for hs_ in sched.get(I, []):
                chains.append(record(sample_half, hs_ // 2, hs_ % 2))
            interleave(*chains)
            for hs_ in sched.get(I, []):
                if hs_ + 1 < 2 * NS:
                    gather_half((hs_ + 1) // 2, (hs_ + 1) % 2)
            if I == 13:
                compute_E()
        pA_kb(NT - 1)
        pA_vfb(NT - 1)

        n = NS
        A("sp", lambda e: e.dma_start(out=xt[:n, :], in_=x_s), writes=["xt"], dma=True)
        A("dve", lambda e: e.tensor_tensor(out=sm1[:, :], in0=qn_s[:, :], in1=kn_s[:, :], op=ALU.mult), reads=["qn_s", "kn_s"], writes=["sm1"])
        A("dve", lambda e: e.tensor_reduce(out=sm2[:, :], in_=sm1[:, :].rearrange("p (h d) -> p h d", h=8), axis=AX.X, op=ALU.add),
          reads=["sm1"], writes=["sm2"])
        A("dve", lambda e: e.scalar_tensor_tensor(out=sm2[:, :], in0=sm2[:, :], scalar=SCALE, in1=lfs[:n, :], op0=ALU.mult, op1=ALU.subtract),
          reads=["sm2", "lfs"], writes=["sm2"])
        A("act", lambda e: e.activation(out=sm2[:, :], in_=sm2[:, :], func=AF.Exp), reads=["sm2"], writes=["sm2"])
        A("dve", lambda e: e.tensor_tensor(out=sm1[:, :].rearrange("p (h d) -> p h d", h=8), in0=vf_s[:, :].rearrange("p (h d) -> p h d", h=8),
                                           in1=sm2[:, :].unsqueeze(2).to_broadcast([n, 8, 64]), op=ALU.mult),
          reads=["vf_s", "sm2"], writes=["sm1"])
        A("dve", lambda e: e.tensor_tensor(out=sm1[:, :], in0=sm1[:, :], in1=pf[3][0:n, :], op=ALU.add), reads=["sm1", Rp[3]], writes=["sm1"])
        A("dve", lambda e: e.tensor_tensor(out=sm3[:, :], in0=pf[4][0:n, 0:8], in1=sm2[:, :], op=ALU.add), reads=[Rp[4], "sm2"], writes=["sm3"])
        A("dve", lambda e: e.reciprocal(out=sm3[:, :], in_=sm3[:, :]), reads=["sm3"], writes=["sm3"])
        A("dve", lambda e: e.tensor_tensor(out=sm1[:, :].rearrange("p (h d) -> p h d", h=8), in0=sm1[:, :].rearrange("p (h d) -> p h d", h=8),
                                           in1=sm3[:, :].unsqueeze(2).to_broadcast([n, 8, 64]), op=ALU.mult),
          reads=["sm1", "sm3"], writes=["sm1"])
        A("dve", lambda e: e.tensor_tensor(out=mrg_s[:, 0:512], in0=sm1[:, :], in1=ga_s[:, :], op=ALU.mult), reads=["sm1", "ga_s"], writes=["mrg_s"])
        A("dve", lambda e: e.tensor_copy(out=mrg_s[:, 512:1024], in_=m_s[:, :]), reads=["m_s"], writes=["mrg_s"])
        for c in range(8):
            A("pe", lambda e, c=c: e.transpose(out=pbf(0)[:, c * 128:c * 128 + n], in_=mrg_s[:n, c * 128:(c + 1) * 128], identity=identb[:n, :n]),
              reads=["mrg_s", "identb"], writes=[Rp[0]])
        A("act", lambda e: e.activation(out=mT_s[:, :, :], in_=pbf(0).rearrange("p (c t) -> p c t", c=8)[:, :, 0:n], func=AF.Copy),
          reads=[Rp[0]], writes=["mT_s"])
        for half in range(2):
            for c in range(8):
                A("pe", lambda e, c=c, half=half: e.matmul(pf[1 + half][:n, :], lhsT=mT_s[:, c, :], rhs=wout[:, c, half * 512:(half + 1) * 512],
                                                           start=(c == 0), stop=(c == 7)),
                  reads=["mT_s", "wout"], writes=[Rp[1 + half]])
            A("dve", lambda e, half=half: e.tensor_tensor(out=xt[:n, half * 512:(half + 1) * 512], in0=pf[1 + half][:n, :],
                                                          in1=xt[:n, half * 512:(half + 1) * 512], op=ALU.add),
              reads=[Rp[1 + half], "xt"], writes=["xt"])
        A("sp", lambda e: e.dma_start(out=y_s, in_=xt[:n, :]), reads=["xt"], dma=True)

        fence(["Kb", "Vb", "Vbb", "Lall", "ptb", "idx", "qT_st", "ga_st", "m_st", "et", "ub", "gc", "tmpm", "kT_stage0", "kT_stage1", "v_stage0", "v_stage1", "xT", "xT2"])
        def pB_h1(i):
            xb, xr = xTs[i % 2]
            norm_T(x_own[i * 128:(i + 1) * 128, :], 128, xb, xr, "act")

        def pB_X(i):
            xb, xr = xTs[i % 2]
            rows = slice(i * 128, (i + 1) * 128)
            proj(1, C_Q, 512, 128, xb, xr)
            headnorm(1, 128, 8, 64, gq_bc, knb[:, :], "knb")
            for hp in range(4):
                A("pe", lambda e, hp=hp: e.transpose(out=pbf(6)[:, hp * 128:(hp + 1) * 128], in_=knb[:, hp * 128:(hp + 1) * 128],
                                                     identity=identb[:, :]),
                  reads=["knb", "identb"], writes=[Rp[6]])
            A("act", lambda e: e.activation(out=qT_st[:, :, i * 128:(i + 1) * 128],
                                            in_=pbf(6)[:, 0:512].rearrange("p (h t) -> p h t", h=4), func=AF.Copy),
              reads=[Rp[6]], writes=["qT_st"])
            proj(1, C_K, 512, 128, xb, xr)
            headnorm(1, 128, 8, 64, gk_bc, kf[:, :], "kf")
            A("sp", lambda e: e.dma_start(out=k_own[rows, :], in_=kf[:, :]), reads=["kf"], dma=True)
            proj(3, C_GV, 512, 128, xb, xr)
            headnorm(3, 128, 4, 128, gv_bc, gvn[:, :], "gvn")
            for g in range(4):
                A("pe", lambda e, g=g: e.matmul(pf[5][:, g * 128:(g + 1) * 128], lhsT=wsT[:, g, :],
                                                rhs=gvn[:, g * 128:(g + 1) * 128], start=True, stop=True),
                  reads=["wsT", "gvn"], writes=[Rp[5]])

        def pB_Y(i):
            xb, xr = xTs[i % 2]
            rows = slice(i * 128, (i + 1) * 128)
            proj(2, C_V, 512, 128, xb, xr)
            A("act", lambda e: e.activation(out=vf[:, :], in_=pf[2][:, :], func=AF.Copy), reads=[Rp[2]], writes=["vf"])
            A("sp", lambda e: e.dma_start(out=v_own[rows, :], in_=vf[:, :]), reads=["vf"], dma=True)
            proj(7, C_F, 8, 128, xb, xr)
            logf_of(7, 128)
            A("sp", lambda e: e.dma_start(out=l_own[rows, :], in_=lf[:, :]), reads=["lf"], dma=True)
            A("pe", lambda e: e.matmul(pf[7][:, 8:16], lhsT=utri, rhs=lf[:, :], start=True, stop=True), reads=["lf", "cst"], writes=[Rp[7]])
            A("dve", lambda e: e.tensor_tensor(out=cw[:, :].rearrange("p (h w) -> p h w", h=8),
                                               in0=carry[:, 4 * i:4 * i + 4, :].rearrange("p w h -> p h w"),
                                               in1=oh4.unsqueeze(1).to_broadcast([128, 8, 4]), op=ALU.mult),
              reads=["carry", "cst"], writes=["cw"])
            A("dve", lambda e: e.tensor_reduce(out=cwo[:, :], in_=cw[:, :].rearrange("p (h w) -> p h w", h=8), axis=AX.X, op=ALU.add),
              reads=["cw"], writes=["cwo"])
            A("dve", lambda e: e.tensor_tensor(out=c_own[:, i, :], in0=pf[7][:, 8:16], in1=cwo[:, :], op=ALU.add),
              reads=[Rp[7], "cwo"], writes=["c_own"])

        def pB_Z(i):
            xb, xr = xTs[i % 2]
            proj(4, C_ZA, 512, 128, xb, xr)
            silu_from(4, 128, ga_st[:, i, :], "ga_st")
            proj(4, C_U, 512, 128, xb, xr)
            A("act", lambda e: e.activation(out=ub[:, :], in_=pf[4][:, :], func=AF.Copy), reads=[Rp[4]], writes=["ub"])
            proj(4, C_ZC, 512, 128, xb, xr)
            silu_from(4, 128, gc[:, :], "gc")

        def pB_tail(i):
            for g in range(4):
                sl = slice(g * 128, (g + 1) * 128)
                A("dve", lambda e, g=g, sl=sl: e.scalar_tensor_tensor(out=tmpm[:, sl], in0=pf[5][:, sl], scalar=bs_t[:, g:g + 1],
                                                                      in1=ub[:, sl], op0=ALU.add, op1=ALU.mult),
                  reads=[Rp[5], "bs_t", "ub"], writes=["tmpm"])
            A("dve", lambda e: e.tensor_tensor(out=m_st[:, i, :], in0=tmpm[:, :], in1=gc[:, :], op=ALU.mult),
              reads=["tmpm", "gc"], writes=["m_st"])

        pB_h1(0)
        for i in range(NOWN):
            chains = [record(pB_X, i), record(pB_Y, i), record(pB_Z, i)]
            if i + 1 < NOWN:
                chains.insert(0, record(pB_h1, i + 1))
            interleave(*chains)
            pB_tail(i)

        fence(["Wb", "kTp", "vP", "pTb0", "pTb1", "pTb2", "pTb3", "o_sb", "a_rows", "rcp_t", "mT", "yt", "qTh"])
        A("dve", lambda e: e.memset(kTp[64:65, :], 1.0), writes=["kTp"])
        for i0 in range(0, NOWN, 4):
            for k in range(4):
                A("pe", lambda e, i0=i0, k=k: e.transpose(out=pf[7][0:8, k * 128:(k + 1) * 128], in_=c_own[:, i0 + k, :], identity=identf),
                  reads=["c_own", "cst"], writes=[Rp[7]])
            A("act", lambda e, i0=i0: e.activation(out=a_rows[0:8, i0 * 128:i0 * 128 + 512], in_=pf[7][0:8, :], func=AF.Copy, scale=1.0 / SCALE),
              reads=[Rp[7]], writes=["a_rows"])
        SB = (0, 1, 7)
        LA = 2
        its = []
        for h in range(8):
            for qg in range(4):
                nJ = 16 * qg + 16
                for J in range(nJ):
                    its.append((h, qg, J, nJ))

        def emit_qk(n):
            h, qg, J, nJ = its[n]
            hp, half = h // 2, h % 2
            rws = slice(half * 64, half * 64 + 64)
            if qg == 0 and J == 0:
                A("sp", lambda e: e.dma_start(out=kTp[0:64, :], in_=kT_scr[hp, half * 64:(half + 1) * 64, :]),
                  reads=["kT_scr"], writes=["kTp"], dma=True)
                A("sp", lambda e: e.dma_start(out=qTh[0:64, :], in_=qT_st[half * 64:(half + 1) * 64, hp, :]),
                  reads=["qT_st"], writes=["qTh"], dma=True)
                A("sp", lambda e: e.dma_start(out=qTh[64:65, :], in_=a_rows[h:h + 1, :]),
                  reads=["a_rows"], writes=["qTh"], dma=True)
                if half == 0:
                    A("sp", lambda e, hp=hp: e.dma_start(out=vP, in_=v_scr[hp]), reads=["v_scr"], writes=["vP"], dma=True)
            kmin = max(0, (J - 16 * qg) // 4) if J >= 16 * qg else 0
            c0 = kmin * 128
            q0 = qg * 512 + c0
            q1 = qg * 512 + 512
            sbk = SB[n % 3]
            pb = n % 4
            win = J >= 16 * qg
            A("pe", lambda e: e.matmul(pf[sbk][:, c0:512], lhsT=kTp[0:65, J * 128:(J + 1) * 128], rhs=qTh[0:65, q0:q1],
                                       start=True, stop=(not win)),
              reads=["kTp", "qTh"], writes=[Rp[sbk]])
            if win:
                w = (J - 16 * qg) % 4
                A("pe", lambda e: e.matmul(pf[sbk][:, c0:c0 + 128], lhsT=identb[:, :], rhs=maskb[:, w, :], start=False, stop=True),
                  reads=["identb", "maskb"], writes=[Rp[sbk]])
            A("act", lambda e: e.activation(out=pTb[pb][:, c0:512], in_=pf[sbk][:, c0:512], func=AF.Exp, scale=SCALE,
                                            bias=negc[:, J, h:h + 1]),
              reads=[Rp[sbk], "negc"], writes=["pTb%d" % pb])

        def emit_pv(n):
            h, qg, J, nJ = its[n]
            half = h % 2
            kmin = max(0, (J - 16 * qg) // 4) if J >= 16 * qg else 0
            c0 = kmin * 128
            pb = n % 4
            A("pe", lambda e: e.matmul(pf[2 + qg][0:65, c0:512], lhsT=vP[:, J, half * 65:(half + 1) * 65], rhs=pTb[pb][:, c0:512],
                                       start=(J == 0), stop=(J == nJ - 1)),
              reads=["vP", "pTb%d" % pb], writes=[Rp[2 + qg]])
            if J == nJ - 1:
                A("act", lambda e: e.activation(out=o_sb[0:65, :], in_=pf[2 + qg][0:65, :], func=AF.Copy),
                  reads=[Rp[2 + qg]], writes=["o_sb"])
                for k in range(4):
                    A("pe", lambda e, k=k: e.transpose(out=pf[6][:, k * 65:k * 65 + 65], in_=o_sb[0:65, k * 128:(k + 1) * 128],
                                                       identity=identf[0:65, 0:65]),
                      reads=["o_sb", "cst"], writes=[Rp[6]])
                A("dve", lambda e: e.reciprocal(out=rcp_t[:, 0:4], in_=pf[6][:, 0:260].rearrange("p (k e) -> p k e", k=4)[:, :, 64]),
                  reads=[Rp[6]], writes=["rcp_t"])
                for k in range(4):
                    A("dve", lambda e, k=k: e.scalar_tensor_tensor(
                        out=ga_st[:, 4 * qg + k, h * 64:(h + 1) * 64], in0=pf[6][:, k * 65:k * 65 + 64], scalar=rcp_t[:, k:k + 1],
                        in1=ga_st[:, 4 * qg + k, h * 64:(h + 1) * 64], op0=ALU.mult, op1=ALU.mult),
                      reads=[Rp[6], "rcp_t", "ga_st"], writes=["ga_st"])

        pend = []
        for nn in range(len(its)):
            h_, qg_, J_, _ = its[nn]
            if h_ % 2 == 0 and qg_ == 0 and J_ == 0:
                for m in pend:
                    emit_pv(m)
                pend = []
            emit_qk(nn)
            pend.append(nn)
            if len(pend) > LA:
                emit_pv(pend.pop(0))
        for m in pend:
            emit_pv(m)

        for i in range(NOWN):
            rows = slice(i * 128, (i + 1) * 128)
            for c in range(8):
                src = ga_st[:, i, c * 128:(c + 1) * 128] if c < 4 else m_st[:, i, (c - 4) * 128:(c - 3) * 128]
                A("pe", lambda e, c=c, src=src: e.transpose(out=pbf(6)[:, c * 128:(c + 1) * 128], in_=src, identity=identb[:, :]),
                  reads=["ga_st", "m_st", "identb"], writes=[Rp[6]])
            A("act", lambda e: e.activation(out=mT[:, :, :], in_=pbf(6).rearrange("p (c t) -> p c t", c=8), func=AF.Copy),
              reads=[Rp[6]], writes=["mT"])
            A("sp", lambda e, rows=rows: e.dma_start(out=xt[:, :], in_=x_own[rows, :]), writes=["xt"], dma=True)
            for half in range(2):
                for c in range(8):
                    A("pe", lambda e, c=c, half=half: e.matmul(pf[half][:, :], lhsT=mT[:, c, :], rhs=wout[:, c, half * 512:(half + 1) * 512],
                                                               start=(c == 0), stop=(c == 7)),
                      reads=["mT", "wout"], writes=[Rp[half]])
                A("dve", lambda e, half=half: e.tensor_tensor(out=yt[:, half * 512:(half + 1) * 512], in0=pf[half][:, :],
                                                              in1=xt[:, half * 512:(half + 1) * 512], op=ALU.add),
                  reads=[Rp[half], "xt"], writes=["yt"])
            A("sp", lambda e, rows=rows: e.dma_start(out=y_own[rows, :], in_=yt[:, :]), reads=["yt"], dma=True)

        P.emit(nc)
    return nc


def _consts(j):
    c = np.zeros((128, K_END), np.float32)
    p = np.arange(128)
    c[:, K_ID:K_ID + 128] = np.eye(128)
    c[:, K_U:K_U + 128] = (p[:, None] <= p[None, :])
    c[:, K_ONE:K_ONE + 128] = 1.0
    c[:, K_L:K_L + 128] = (p[:, None] > p[None, :])
    for w in range(4):
        if w < j:
            m = np.ones((128, 128))
        elif w == j:
            m = (p[:, None] <= p[None, :])
        else:
            m = np.zeros((128, 128))
        c[:, K_MW + w * 128:K_MW + (w + 1) * 128] = m
    c[:, K_IOTA] = p % 16
    c[:, K_OH4 + j] = 1.0
    for b in range(16):
        c[0:8, K_EB + b * 16 + b] = 1.0
        c[:, K_CS + b * 16 + b] = 1.0
    for h in range(8):
        c[h, K_BM + h * 64:K_BM + (h + 1) * 64] = 1.0
        c[h, K_ID8 + h] = 1.0
        c[h, K_OHT + h * 128:K_OHT + (h + 1) * 128] = 1.0
    return c


def kernel(x_prompt, x_sample, cache_k, cache_v, cache_logf, page_table, g_norm, w_in,
           b_f, g_q, g_k, g_v, w_s, b_s, w_out):
    f = lambda a: np.ascontiguousarray(np.asarray(a))
    x_prompt = f(x_prompt); x_sample = f(x_sample)
    ckr = f(cache_k).reshape(2560 * 128, 512)
    cvr = f(cache_v).reshape(2560 * 128, 512)
    clr = f(cache_logf).reshape(2560 * 128, 8)
    ptab = f(page_table).astype(np.int32)
    nc = build_nc()
    in_maps = []
    for c in range(8):
        b, j = c // 4, c % 4
        xa = x_prompt[b]
        xo = np.ascontiguousarray(xa.reshape(16, 4, 128, D)[:, j].reshape(NOWN * 128, D))
        in_maps.append(dict(
            x_all=xa, x_own=xo, x_s=np.ascontiguousarray(x_sample[16 * c:16 * c + 16, 0, :]),
            ck=ckr, cv=cvr, cl=clr,
            pt=np.ascontiguousarray(ptab[16 * c:16 * c + 16].reshape(16, 2, 8)[:, :, np.arange(128) // 16].transpose(2, 0, 1).reshape(128, 32)),
            g_norm=f(g_norm)[0], w_in=f(w_in)[0], b_f=f(b_f)[0], g_q=f(g_q)[0], g_k=f(g_k)[0], g_v=f(g_v)[0],
            w_s=f(w_s)[0], b_s=f(b_s)[0], w_out=f(w_out)[0], consts=_consts(j)))
    res = run_bass_kernel_spmd(nc, in_maps, core_ids=list(range(8)))
    R = res.results
    yp = np.zeros((2, S, D), np.float32)
    kp = np.zeros((1, 2, S, 8, 64), np.float32)
    vp = np.zeros((1, 2, S, 8, 64), np.float32)
    lp = np.zeros((1, 2, S, 8), np.float32)
    ys = np.zeros((128, 1, D), np.float32)
    ks = np.zeros((1, 128, 1, 8, 64), np.float32)
    vs = np.zeros((1, 128, 1, 8, 64), np.float32)
    ls = np.zeros((1, 128, 1, 8), np.float32)
    gs = np.zeros((1, 128, 1, 4, 128), np.float32)
    for c in range(8):
        b, j = c // 4, c % 4
        r = R[c]
        yp[b].reshape(16, 4, 128, D)[:, j] = r["y_own"].reshape(16, 128, D)
        kp[0, b].reshape(16, 4, 128, 512)[:, j] = r["k_own"].reshape(16, 128, 512)
        vp[0, b].reshape(16, 4, 128, 512)[:, j] = r["v_own"].reshape(16, 128, 512)
        lp[0, b].reshape(16, 4, 128, 8)[:, j] = r["l_own"].reshape(16, 128, 8)
        sl = slice(16 * c, 16 * c + 16)
        ys[sl, 0] = r["y_s"]
        ks[0, sl, 0] = r["k_s"].reshape(16, 8, 64)
        vs[0, sl, 0] = r["v_s"].reshape(16, 8, 64)
        ls[0, sl, 0] = r["l_s"]
        gs[0, sl, 0] = r["gv_s"].reshape(16, 4, 128)
    return (yp, ys, kp, vp, lp, ks, vs, ls, gs)
```

```python
import contextlib
import numpy as np
import concourse.bass as bass
import concourse.mybir as mybir
from concourse.bass_utils import run_bass_kernel_spmd

F32 = mybir.dt.float32
BF16 = mybir.dt.bfloat16
I32 = mybir.dt.int32
ALU = mybir.AluOpType
AF = mybir.ActivationFunctionType
AX = mybir.AxisListType

D = 1024
DIN = 3592
S = 8192
NT = 64
NOWN = 16
NS = 16
NPG = 16
EPS = 1e-6
SCALE = 0.125
C_Q, C_K, C_V, C_F, C_ZA, C_U, C_GV, C_ZC = 0, 512, 1024, 1536, 1544, 2056, 2568, 3080

K_ID, K_U, K_ONE, K_L, K_MW, K_IOTA, K_OH4, K_EB, K_BM, K_ID8, K_OHT, K_CS, K_END = (
    0, 128, 256, 384, 512, 1024, 1025, 1029, 1285, 1797, 1805, 2829, 3085)

ENGS = ("pe", "act", "dve", "pool", "sp")


class Res:
    __slots__ = ("w", "r")

    def __init__(self):
        self.w = None
        self.r = []


class Op:
    __slots__ = ("eng", "fn", "deps", "signaled", "count", "dma", "sem", "prev_on_sem")

    def __init__(self, eng, fn, dma):
        self.eng = eng
        self.fn = fn
        self.dma = dma
        self.deps = []
        self.signaled = dma
        self.count = None
        self.sem = None
        self.prev_on_sem = None


class Prog:
    def __init__(self, n_dma_sems=16):
        self.q = {e: [] for e in ENGS}
        self.n_dma_sems = n_dma_sems
        self.all_dma = []

    def add(self, eng, fn, reads=(), writes=(), dma=False):
        op = Op(eng, fn, dma)
        deps = {}
        for r in reads:
            if r.w is not None:
                deps[id(r.w)] = (r.w, "raw")
        for w in writes:
            if w.w is not None and id(w.w) not in deps:
                deps[id(w.w)] = (w.w, "waw")
            for rr in w.r:
                if id(rr) not in deps:
                    deps[id(rr)] = (rr, "war")
        for d, kind in deps.values():
            if d is op:
                continue
            if not d.dma and not dma and d.eng == eng:
                if eng == "pe":
                    continue
                if kind != "raw":
                    continue
            d.signaled = True
            op.deps.append(d)
        for r in reads:
            r.r.append(op)
        for w in writes:
            w.w = op
            w.r = []
        self.q[eng].append(op)
        if dma:
            self.all_dma.append(op)
        return op

    def emit(self, nc):
        stack = contextlib.ExitStack()
        with stack:
            esem = {e: stack.enter_context(nc.semaphore("s_" + e)) for e in ENGS}
            dsem = {e: [stack.enter_context(nc.semaphore("d_%s%d" % (e, i))) for i in range(self.n_dma_sems)]
                    for e in ("sp", "act", "pool")}
            for e in ENGS:
                c = 0
                dcount = [0] * self.n_dma_sems
                dlast = [None] * self.n_dma_sems
                k = 0
                for op in self.q[e]:
                    if op.dma:
                        s = k % self.n_dma_sems
                        k += 1
                        dcount[s] += 16
                        op.sem = dsem[e][s]
                        op.count = dcount[s]
                        op.prev_on_sem = dlast[s]
                        dlast[s] = op
                    elif op.signaled:
                        c += 1
                        op.sem = esem[e]
                        op.count = c
            block = stack.enter_context(nc.Block())

            def run(e):
                def body(eng):
                    waited = {}

                    def wait_for(d):
                        key = id(d.sem)
                        if waited.get(key, 0) >= d.count:
                            return
                        eng.wait_ge(d.sem, d.count)
                        waited[key] = d.count

                    for op in self.q[e]:
                        for d in op.deps:
                            wait_for(d)
                        if op.dma and op.prev_on_sem is not None:
                            wait_for(op.prev_on_sem)
                        ins = op.fn(eng)
                        if op.dma:
                            ins.then_inc(op.sem, 16)
                        elif op.signaled:
                            ins.then_inc(op.sem, 1)
                    if e == "sp":
                        last = {}
                        for d in self.all_dma:
                            last[id(d.sem)] = d
                        for d in last.values():
                            wait_for(d)
                return body

            block.tensor(run("pe"))
            block.scalar(run("act"))
            block.vector(run("dve"))
            block.gpsimd(run("pool"))
            block.sync(run("sp"))


def build_nc():
    nc = bass.Bass("TRN2", target_bir_lowering=False)

    def din(name, shape, dt=F32):
        return nc.dram_tensor(name, shape, dt, kind="ExternalInput").ap()

    def dout(name, shape, dt=F32):
        return nc.dram_tensor(name, shape, dt, kind="ExternalOutput").ap()

    x_all = din("x_all", [S, D])
    x_own = din("x_own", [NOWN * 128, D])
    x_s = din("x_s", [NS, D])
    ck = din("ck", [2560 * 128, 512])
    cv = din("cv", [2560 * 128, 512])
    cl = din("cl", [2560 * 128, 8])
    pt = din("pt", [128, 2 * NS], I32)
    g_norm = din("g_norm", [D])
    w_in = din("w_in", [D, DIN])
    b_f = din("b_f", [8])
    g_q = din("g_q", [64])
    g_k = din("g_k", [64])
    g_v = din("g_v", [512])
    w_s = din("w_s", [4, 128, 128])
    b_s = din("b_s", [4, 128])
    w_out = din("w_out", [D, D])
    consts_d = din("consts", [128, K_END])

    y_own = dout("y_own", [NOWN * 128, D])
    k_own = dout("k_own", [NOWN * 128, 512])
    v_own = dout("v_own", [NOWN * 128, 512])
    l_own = dout("l_own", [NOWN * 128, 8])
    y_s = dout("y_s", [NS, D])
    k_s = dout("k_s", [NS, 512])
    v_s = dout("v_s", [NS, 512])
    l_s = dout("l_s", [NS, 8])
    gv_s = dout("gv_s", [NS, 512])

    kT_scr = nc.dram_tensor("kT_scr", [4, 128, S], BF16, kind="Internal").ap()
    v_scr = nc.dram_tensor("v_scr", [4, 128, NT, 130], BF16, kind="Internal").ap()
    q_scr = nc.dram_tensor("q_scr", [NS, 512], F32, kind="Internal").ap()

    P = Prog()
    st = contextlib.ExitStack()
    with st:
        def sb(name, shape, dt):
            return st.enter_context(nc.sbuf_tensor(name, shape, dt))

        cst = sb("cst", [128, K_END], F32)
        identf = cst[:, K_ID:K_ID + 128]
        utri = cst[:, K_U:K_U + 128]
        ones = cst[:, K_ONE:K_ONE + 128]
        ltri = cst[:, K_L:K_L + 128]
        iota = cst[:, K_IOTA:K_IOTA + 1]
        oh4 = cst[:, K_OH4:K_OH4 + 4]
        identb = sb("identb", [128, 128], BF16)
        maskb = sb("maskb", [128, 4, 128], BF16)
        ohTb = sb("ohTb", [8, 8, 128], BF16)
        wout = sb("wout", [128, 8, D], BF16)
        g_bc = sb("g_bc", [128, D], F32)
        gv_bc = sb("gv_bc", [128, 512], F32)
        gq_bc = sb("gq_bc", [128, 64], F32)
        gk_bc = sb("gk_bc", [128, 64], F32)
        bf_bc = sb("bf_bc", [128, 8], F32)
        bs_t = sb("bs_t", [128, 4], F32)
        ws00 = sb("ws00", [128, 4], F32)
        bs0 = sb("bs0", [128, 4], F32)
        wsT = sb("wsT", [128, 4, 128], BF16)
        negc = sb("negc", [128, NT, 8], F32)
        carry = sb("carry", [128, NT, 8], F32)
        runc = sb("runc", [128, 8], F32)
        c_own = sb("c_own", [128, NOWN, 8], F32)
        xt = sb("xt", [128, D], F32)
        xn = sb("xn", [128, D], BF16)
        junk = xn
        xT = sb("xT", [128, 8, 128], BF16)
        xT2 = sb("xT2", [128, 8, 128], BF16)
        ss = sb("ss", [128, 1], F32)
        rr = sb("rr", [128, 1], F32)
        sq = sb("sq", [128, 512], F32)
        ssq = sb("ssq", [128, 8], F32)
        rk = sb("rk", [128, 8], F32)
        kn = sb("kn", [128, 512], F32)
        kf = sb("kf", [128, 512], F32)
        knb = sb("knb", [128, 512], BF16)
        vf = sb("vf", [128, 512], F32)
        zf = sb("zf", [128, 8], F32)
        lf = sb("lf", [128, 8], F32)
        lfs = sb("lfs", [128, 8], F32)
        arT = sb("arT", [128, 2080], F32)
        kT_stage = arT[:, 0:1024].bitcast(BF16).rearrange("p (h t) -> p h t", h=4)
        v_stage = arT[:, 1024:2064].bitcast(BF16).rearrange("p (s q a h d) -> p s q a h d", s=2, q=4, a=2, h=2)
        et = arT[:, 0:512]
        ub = arT[:, 512:1024]
        gvn = sb("gvn", [128, 512], BF16)
        gvnf = sb("gvnf", [128, 512], F32)
        gc = arT[:, 1024:1536]
        tmpm = arT[:, 1536:2048]
        arW = sb("arW", [128, 8 * DIN], BF16)
        arQ = sb("arQ", [128, 25600], BF16)

        Wb = arW[:, :].rearrange("p (c n) -> p c n", c=8)
        kTp = arW[:, 0:8192]
        vP = arW[:, 8192:8192 + 64 * 130].rearrange("p (t e) -> p t e", t=64)
        o = 8192 + 64 * 130
        pTb = [arW[:, o + i * 512:o + (i + 1) * 512] for i in range(4)]
        o += 4 * 512
        o_sb = arW[:, o:o + 1024].bitcast(F32)
        o += 1024
        a_rows = arW[:, o:o + 2048]
        o += 2048
        rcp_t = arW[:, o:o + 8].bitcast(F32)
        o += 8
        mT = arW[:, o:o + 1024].rearrange("p (c n) -> p c n", c=8)
        o += 1024
        yt = arW[:, o:o + 2048].bitcast(F32)
        o += 2048
        qTh = arW[:, o:o + 2048]
        o += 2048
        mT2 = arW[:, o:o + 1024].rearrange("p (c n) -> p c n", c=8)
        o += 1024
        assert o <= 8 * DIN
        qT_st = arQ[:, 0:8192].rearrange("p (h n) -> p h n", h=4)
        ga_st = arQ[:, 8192:16384].rearrange("p (i n) -> p i n", i=NOWN)
        m_st = arQ[:, 16384:24576].rearrange("p (i n) -> p i n", i=NOWN)
        Kb = arQ[:, 0:8192].bitcast(F32).rearrange("p (g n) -> p g n", g=8)
        Vb = arQ[:, 8192:16384].bitcast(F32).rearrange("p (g n) -> p g n", g=8)
        Wstage = arQ[:, 0:2 * DIN].bitcast(F32)
        Lh = arQ[:, 16384:20992].bitcast(F32).rearrange("p (c t h) -> p c t h", c=2 * NS, t=9)
        Vbb = arQ[:, 20992:25088].rearrange("p (g n) -> p g n", g=8)
        ptb = arQ[:, 25088:25152].bitcast(I32)
        idx = arQ[:, 25152:25216].bitcast(I32)
        tot_t = sb("tot_t", [128, 256], F32)
        tsum = sb("tsum", [128, 256], F32)
        qb = sb("qb", [128, 512], F32)
        s_t = sb("s_t", [128, 128], F32)
        p_t = sb("p_t", [128, 128], BF16)
        psm = sb("psm", [128, 8], F32)
        mo = sb("mo", [8, 512], F32)
        md = sb("md", [8, 8], F32)
        qn_s = sb("qn_s", [NS, 512], F32)
        kn_s = sb("kn_s", [NS, 512], F32)
        vf_s = sb("vf_s", [NS, 512], F32)
        ga_s = sb("ga_s", [NS, 512], BF16)
        m_s = sb("m_s", [NS, 512], BF16)
        sm1 = sb("sm1", [NS, 512], F32)
        sm2 = sb("sm2", [NS, 8], F32)
        sm3 = sb("sm3", [NS, 8], F32)
        mrg_s = sb("mrg_s", [NS, D], BF16)
        mT_s = sb("mT_s", [128, 8, NS], BF16)

        pf = [st.enter_context(nc.psum_tensor("pf%d" % i, [128, 512], F32)) for i in range(8)]
        Rp = [Res() for _ in range(8)]

        def pbf(i):
            return pf[i][:, :].bitcast(BF16)

        R = {}

        def res(name):
            if name not in R:
                R[name] = Res()
            return R[name]

        cap = [None]

        def A(eng, fn, reads=(), writes=(), dma=False):
            if cap[0] is not None:
                cap[0].append((eng, fn, reads, writes, dma))
                return None
            rl = [res(r) if isinstance(r, str) else r for r in reads]
            wl = [res(w) if isinstance(w, str) else w for w in writes]
            return P.add(eng, fn, rl, wl, dma)

        def record(f, *args):
            cap[0] = []
            f(*args)
            lst = cap[0]
            cap[0] = None
            return lst

        def interleave(*lists):
            lists = [l for l in lists if l]
            pos = [0] * len(lists)
            left = sum(len(l) for l in lists)
            while left:
                for i, l in enumerate(lists):
                    if pos[i] < len(l):
                        A(*l[pos[i]])
                        pos[i] += 1
                        left -= 1

        A("sp", lambda e: e.dma_start(out=cst[:, :], in_=consts_d), writes=["cst"], dma=True)
        A("sp", lambda e: e.dma_start(out=g_bc[:, :], in_=g_norm.partition_broadcast(128)), writes=["g_bc"], dma=True)
        A("sp", lambda e: e.dma_start(out=gv_bc[:, :], in_=g_v.partition_broadcast(128)), writes=["gv_bc"], dma=True)
        A("sp", lambda e: e.dma_start(out=gq_bc[:, :], in_=g_q.partition_broadcast(128)), writes=["gq_bc"], dma=True)
        A("sp", lambda e: e.dma_start(out=gk_bc[:, :], in_=g_k.partition_broadcast(128)), writes=["gk_bc"], dma=True)
        A("sp", lambda e: e.dma_start(out=bf_bc[:, :], in_=b_f.partition_broadcast(128)), writes=["bf_bc"], dma=True)
        for g in range(4):
            A("sp", lambda e, g=g: e.dma_start(out=bs_t[:, g:g + 1], in_=b_s[g].rearrange("(t o) -> t o", o=1)),
              writes=["bs_t"], dma=True)
            A("sp", lambda e, g=g: e.dma_start(out=ws00[:, g:g + 1], in_=w_s[g, 0, 0:1].partition_broadcast(128)),
              writes=["ws00"], dma=True)
            A("sp", lambda e, g=g: e.dma_start(out=bs0[:, g:g + 1], in_=b_s[g, 0:1].partition_broadcast(128)),
              writes=["bs0"], dma=True)
        w_in_v = w_in.rearrange("(c p) n -> p c n", p=128)
        for c in range(8):
            A("sp", lambda e, c=c: e.dma_start(out=Wstage, in_=w_in[c * 128:(c + 1) * 128, :]), writes=["Kb"], dma=True)
            A("act", lambda e, c=c: e.activation(out=Wb[:, c, :], in_=Wstage, func=AF.Copy), reads=["Kb"], writes=["Wb"])
        A("pool", lambda e: e.dma_start(out=wout[:, :, :], in_=w_out.rearrange("(c p) n -> p c n", p=128)),
          writes=["wout"], dma=True)
        A("dve", lambda e: e.tensor_copy(out=identb[:, :], in_=identf), reads=["cst"], writes=["identb"])
        A("dve", lambda e: e.tensor_scalar(out=maskb[:, :, :].rearrange("p w t -> p (w t)"), in0=cst[:, K_MW:K_MW + 512],
                                           scalar1=1.0e4, scalar2=-1.0e4, op0=ALU.mult, op1=ALU.add),
          reads=["cst"], writes=["maskb"])
        A("dve", lambda e: e.tensor_copy(out=ohTb[:, :, :].rearrange("p h s -> p (h s)"), in_=cst[0:8, K_OHT:K_OHT + 1024]),
          reads=["cst"], writes=["ohTb"])
        A("dve", lambda e: e.memset(runc[:, :], 0.0), writes=["runc"])
        ws_t = tmpm.rearrange("p (g s) -> p g s", g=4)
        A("sp", lambda e: e.dma_start(out=ws_t, in_=w_s.rearrange("g t s -> t g s")), writes=["tmpm"], dma=True)
        for g in range(4):
            A("pe", lambda e, g=g: e.transpose(out=pf[0][:, g * 128:(g + 1) * 128], in_=ws_t[:, g, :], identity=identf),
              reads=["tmpm", "cst"], writes=[Rp[0]])
        for g in range(4):
            A("dve", lambda e, g=g: e.tensor_tensor(out=wsT[:, g, :], in0=pf[0][:, g * 128:(g + 1) * 128], in1=utri, op=ALU.mult),
              reads=[Rp[0], "cst"], writes=["wsT"])

        def rsqrt_act(dst, src, n, scale):
            A("act", lambda e: e.activation(out=dst, in_=src, func=AF.Ln, scale=scale, bias=EPS), reads=["tmp_r_in"], writes=["tmp_r"])
            A("act", lambda e: e.activation(out=dst, in_=dst, func=AF.Exp, scale=-0.5), reads=["tmp_r"], writes=["tmp_r"])

        def norm_T(src, n, xTb=None, xTr="xT", ldq="sp"):
            xTb = xT if xTb is None else xTb
            A(ldq, lambda e: e.dma_start(out=xt[:n, :], in_=src), writes=["xt"], dma=True)
            A("dve", lambda e: e.memset(ss[:n, :], 0.0), writes=["ss"])
            A("act", lambda e: e.activation(out=junk[:n, :], in_=xt[:n, :], func=AF.Square, accum_out=ss[:n, :]),
              reads=["xt", "ss"], writes=["xn", "ss"])
            A("act", lambda e: e.activation(out=rr[:n, :], in_=ss[:n, :], func=AF.Ln, scale=1.0 / D, bias=EPS),
              reads=["ss"], writes=["rr"])
            A("act", lambda e: e.activation(out=rr[:n, :], in_=rr[:n, :], func=AF.Exp, scale=-0.5),
              reads=["rr"], writes=["rr"])
            A("dve", lambda e: e.scalar_tensor_tensor(out=xn[:n, :], in0=xt[:n, :], scalar=rr[:n, 0:1], in1=g_bc[:n, :],
                                                      op0=ALU.mult, op1=ALU.mult),
              reads=["xt", "rr", "g_bc"], writes=["xn"])
            for c in range(8):
                A("pe", lambda e, c=c: e.transpose(out=pbf(0)[:, c * 128:c * 128 + n], in_=xn[:n, c * 128:(c + 1) * 128],
                                                   identity=identb[:n, :n]),
                  reads=["xn", "identb"], writes=[Rp[0]])
            A("act", lambda e: e.activation(out=xTb[:, :, 0:n], in_=pbf(0).rearrange("p (c t) -> p c t", c=8)[:, :, 0:n], func=AF.Copy),
              reads=[Rp[0]], writes=[xTr])

        def proj(bank, col0, width, n, xTb=None, xTr="xT"):
            xTb = xT if xTb is None else xTb
            for c in range(8):
                A("pe", lambda e, c=c: e.matmul(pf[bank][:n, 0:width], lhsT=xTb[:, c, 0:n], rhs=Wb[:, c, col0:col0 + width],
                                                start=(c == 0), stop=(c == 7)),
                  reads=[xTr, "Wb"], writes=[Rp[bank]])

        def headnorm(bank, n, nh, hd, gb, out_ap, out_res):
            A("act", lambda e: e.activation(out=sq[:n, :], in_=pf[bank][:n, :], func=AF.Square), reads=[Rp[bank]], writes=["sq"])
            A("dve", lambda e: e.tensor_reduce(out=ssq[:n, 0:nh], in_=sq[:n, :].rearrange("p (h d) -> p h d", h=nh),
                                               axis=AX.X, op=ALU.add), reads=["sq"], writes=["ssq"])
            A("act", lambda e: e.activation(out=rk[:n, 0:nh], in_=ssq[:n, 0:nh], func=AF.Ln, scale=1.0 / hd, bias=EPS),
              reads=["ssq"], writes=["rk"])
            A("act", lambda e: e.activation(out=rk[:n, 0:nh], in_=rk[:n, 0:nh], func=AF.Exp, scale=-0.5), reads=["rk"], writes=["rk"])
            A("dve", lambda e: e.tensor_tensor(out=kn[:n, :].rearrange("p (h d) -> p h d", h=nh),
                                               in0=pf[bank][:n, :].rearrange("p (h d) -> p h d", h=nh),
                                               in1=rk[:n, 0:nh].unsqueeze(2).to_broadcast([n, nh, hd]), op=ALU.mult),
              reads=[Rp[bank], "rk"], writes=["kn"])
            if hd == 64:
                in1 = gb[:n, :].unsqueeze(1).to_broadcast([n, nh, hd])
                A("dve", lambda e: e.tensor_tensor(out=out_ap.rearrange("p (h d) -> p h d", h=nh),
                                                   in0=kn[:n, :].rearrange("p (h d) -> p h d", h=nh), in1=in1, op=ALU.mult),
                  reads=["kn"], writes=[out_res])
            else:
                A("dve", lambda e: e.tensor_tensor(out=out_ap, in0=kn[:n, :], in1=gb[:n, :], op=ALU.mult),
                  reads=["kn"], writes=[out_res])

        def logf_of(bank, n):
            A("dve", lambda e: e.tensor_tensor(out=zf[:n, :], in0=pf[bank][:n, 0:8], in1=bf_bc[:n, :], op=ALU.add),
              reads=[Rp[bank], "bf_bc"], writes=["zf"])
            A("act", lambda e: e.activation(out=zf[:n, :], in_=zf[:n, :], func=AF.Exp, scale=-1.0), reads=["zf"], writes=["zf"])
            A("act", lambda e: e.activation(out=zf[:n, :], in_=zf[:n, :], func=AF.Ln, bias=1.0), reads=["zf"], writes=["zf"])
            A("dve", lambda e: e.tensor_scalar(out=lf[:n, :], in0=zf[:n, :], scalar1=-1.0, scalar2=None, op0=ALU.mult),
              reads=["zf"], writes=["lf"])

        def silu_from(bank, n, out_ap, out_res):
            A("act", lambda e: e.activation(out=et[:n, :], in_=pf[bank][:n, :], func=AF.Exp, scale=-1.0), reads=[Rp[bank]], writes=["et"])
            A("dve", lambda e: e.tensor_scalar(out=et[:n, :], in0=et[:n, :], scalar1=1.0, scalar2=None, op0=ALU.add),
              reads=["et"], writes=["et"])
            A("dve", lambda e: e.reciprocal(out=et[:n, :], in_=et[:n, :]), reads=["et"], writes=["et"])
            A("dve", lambda e: e.tensor_tensor(out=out_ap, in0=pf[bank][:n, :], in1=et[:n, :], op=ALU.mult),
              reads=[Rp[bank], "et"], writes=[out_res])

        def rest_proj(n, ga_out, ga_res, m_out, m_res, sample):
            proj(3, C_ZA, 512, n)
            silu_from(3, n, ga_out, ga_res)
            proj(4, C_U, 512, n)
            A("act", lambda e: e.activation(out=ub[:n, :], in_=pf[4][:n, :], func=AF.Copy), reads=[Rp[4]], writes=["ub"])
            proj(3, C_GV, 512, n)
            if sample:
                headnorm(3, n, 4, 128, gv_bc, gvnf[:n, :], "gvnf")
                A("sp", lambda e: e.dma_start(out=gv_s, in_=gvnf[:n, :]), reads=["gvnf"], dma=True)
            else:
                headnorm(3, n, 4, 128, gv_bc, gvn[:n, :], "gvn")
                for g in range(4):
                    A("pe", lambda e, g=g: e.matmul(pf[5][:n, g * 128:(g + 1) * 128], lhsT=wsT[:, g, :],
                                                    rhs=gvn[:, g * 128:(g + 1) * 128], start=True, stop=True),
                      reads=["wsT", "gvn"], writes=[Rp[5]])
            proj(4, C_ZC, 512, n)
            silu_from(4, n, gc[:n, :], "gc")
            for g in range(4):
                sl = slice(g * 128, (g + 1) * 128)
                if sample:
                    A("dve", lambda e, g=g, sl=sl: e.tensor_scalar(out=tmpm[:n, sl], in0=gvnf[:n, sl], scalar1=ws00[:n, g:g + 1],
                                                                   scalar2=bs0[:n, g:g + 1], op0=ALU.mult, op1=ALU.add),
                      reads=["gvnf", "ws00", "bs0"], writes=["tmpm"])
                    A("dve", lambda e, sl=sl: e.tensor_tensor(out=tmpm[:n, sl], in0=tmpm[:n, sl], in1=ub[:n, sl], op=ALU.mult),
                      reads=["tmpm", "ub"], writes=["tmpm"])
                else:
                    A("dve", lambda e, g=g, sl=sl: e.scalar_tensor_tensor(out=tmpm[:n, sl], in0=pf[5][:n, sl], scalar=bs_t[:n, g:g + 1],
                                                                          in1=ub[:n, sl], op0=ALU.add, op1=ALU.mult),
                      reads=[Rp[5], "bs_t", "ub"], writes=["tmpm"])
            A("dve", lambda e: e.tensor_tensor(out=m_out, in0=tmpm[:n, :], in1=gc[:n, :], op=ALU.mult),
              reads=["tmpm", "gc"], writes=[m_res])

        n = NS
        norm_T(x_s, n)
        proj(1, C_Q, 512, n)
        headnorm(1, n, 8, 64, gq_bc, qn_s[:n, :], "qn_s")
        A("sp", lambda e: e.dma_start(out=q_scr, in_=qn_s[:n, :]), reads=["qn_s"], writes=["q_scr"], dma=True)
        proj(1, C_K, 512, n)
        headnorm(1, n, 8, 64, gk_bc, kn_s[:n, :], "kn_s")
        A("sp", lambda e: e.dma_start(out=k_s, in_=kn_s[:n, :]), reads=["kn_s"], dma=True)
        proj(2, C_V, 512, n)
        A("act", lambda e: e.activation(out=vf_s[:n, :], in_=pf[2][:n, :], func=AF.Copy), reads=[Rp[2]], writes=["vf_s"])
        A("sp", lambda e: e.dma_start(out=v_s, in_=vf_s[:n, :]), reads=["vf_s"], dma=True)
        proj(2, C_F, 8, n)
        logf_of(2, n)
        A("dve", lambda e: e.tensor_copy(out=lfs[:n, :], in_=lf[:n, :]), reads=["lf"], writes=["lfs"])
        A("sp", lambda e: e.dma_start(out=l_s, in_=lfs[:n, :]), reads=["lfs"], dma=True)
        rest_proj(n, ga_s[:n, :], "ga_s", m_s[:n, :], "m_s", True)

        ck2 = ck.rearrange("(r t) n -> r (t n)", t=8)
        cv2 = cv.rearrange("(r t) n -> r (t n)", t=8)
        cl2 = cl.rearrange("(r t) n -> r (t n)", t=8)
        A("sp", lambda e: e.dma_start(out=ptb, in_=pt), writes=["ptb"], dma=True)
        A("dve", lambda e: e.tensor_scalar(out=idx, in0=ptb, scalar1=16.0, scalar2=iota, op0=ALU.mult, op1=ALU.add),
          reads=["ptb", "cst"], writes=["idx"])
        A("dve", lambda e: e.memset(arQ[:, 16384:20992].bitcast(F32), 0.0), writes=["Lall"])
        for col in range(2 * NS):
            A("pool", lambda e, col=col: e.indirect_dma_start(
                out=arQ[:, 16384:20992].bitcast(F32)[:, col * 72:col * 72 + 64], out_offset=None, in_=cl2,
                in_offset=bass.IndirectOffsetOnAxis(ap=idx[:, col:col + 1], axis=0)),
              reads=["idx"], writes=["Lall"], dma=True)

        def compute_E():
            A("dve", lambda e: e.tensor_reduce(out=tot_t[:, :].rearrange("p (c h) -> p c h", h=8),
                                               in_=Lh[:, :, 0:8, :].rearrange("p c t h -> p c h t"), axis=AX.X, op=ALU.add),
              reads=["Lall"], writes=["tot_t"])
            A("pe", lambda e: e.matmul(pf[3][:, 0:256], lhsT=ltri, rhs=tot_t[:, :], start=True, stop=True),
              reads=["tot_t", "cst"], writes=[Rp[3]])
            A("pe", lambda e: e.matmul(pf[4][:, 0:128], lhsT=ones,
                                       rhs=tot_t[:, :].rearrange("p (b f h) -> p b f h", f=2, h=8)[:, :, 1, :], start=True, stop=True),
              reads=["tot_t", "cst"], writes=[Rp[4]])
            A("dve", lambda e: e.tensor_copy(out=tsum[:, :], in_=pf[3][:, 0:256]), reads=[Rp[3]], writes=["tsum"])
            A("dve", lambda e: e.tensor_tensor(out=tsum[:, :].rearrange("p (b f h) -> p b f h", f=2, h=8)[:, :, 0, :],
                                               in0=tsum[:, :].rearrange("p (b f h) -> p b f h", f=2, h=8)[:, :, 0, :],
                                               in1=pf[4][:, 0:128].rearrange("p (b h) -> p b h", h=8), op=ALU.add),
              reads=[Rp[4], "tsum"], writes=["tsum"])
            for t in range(6, -1, -1):
                A("dve", lambda e, t=t: e.tensor_tensor(out=Lh[:, :, t, :], in0=Lh[:, :, t, :], in1=Lh[:, :, t + 1, :], op=ALU.add),
                  reads=["Lall"], writes=["Lall"])
            for t in range(1, 9):
                A("dve", lambda e, t=t: e.tensor_tensor(out=Lh[:, :, t, :], in0=Lh[:, :, t, :],
                                                        in1=tsum[:, :].rearrange("p (c h) -> p c h", h=8), op=ALU.add),
                  reads=["Lall", "tsum"], writes=["Lall"])

        def gather_half(b, hf):
            col = b * 2 + hf
            A("pool", lambda e: e.indirect_dma_start(
                out=Kb.rearrange("p g n -> p (g n)"), out_offset=None, in_=ck2,
                in_offset=bass.IndirectOffsetOnAxis(ap=idx[:, col:col + 1], axis=0)),
              reads=["idx"], writes=["Kb"], dma=True)
            A("pool", lambda e: e.indirect_dma_start(
                out=Vb.rearrange("p g n -> p (g n)"), out_offset=None, in_=cv2,
                in_offset=bass.IndirectOffsetOnAxis(ap=idx[:, col:col + 1], axis=0)),
              reads=["idx"], writes=["Vb"], dma=True)

        def sample_half(b, hf):
            hs = slice(hf * 64, hf * 64 + 64)
            if hf == 0:
                A("sp", lambda e: e.dma_start(out=qb[:, :], in_=q_scr[b].partition_broadcast(128)), reads=["q_scr"], writes=["qb"], dma=True)
            A("dve", lambda e: e.tensor_tensor(out=Kb, in0=Kb, in1=qb[:, :].unsqueeze(1).to_broadcast([128, 8, 512]), op=ALU.mult),
              reads=["Kb", "qb"], writes=["Kb"])
            A("dve", lambda e: e.tensor_reduce(out=s_t[:, hs], in_=Kb.rearrange("p g (h d) -> p (g h) d", h=8), axis=AX.X, op=ALU.add),
              reads=["Kb"], writes=["s_t"])
            A("dve", lambda e: e.scalar_tensor_tensor(out=s_t[:, hs].rearrange("p (g h) -> p g h", g=8), in0=s_t[:, hs].rearrange("p (g h) -> p g h", g=8),
                                                      scalar=SCALE, in1=Lh[:, b * 2 + hf, 1:9, :], op0=ALU.mult, op1=ALU.add),
              reads=["s_t", "Lall"], writes=["s_t"])
            A("act", lambda e: e.activation(out=p_t[:, hs], in_=s_t[:, hs], func=AF.Exp), reads=["s_t"], writes=["p_t"])
            A("act", lambda e: e.activation(out=Vbb, in_=Vb, func=AF.Copy), reads=["Vb"], writes=["Vbb"])
            for g in range(8):
                gg = hf * 8 + g
                A("pe", lambda e, g=g, gg=gg: e.matmul(pf[5][0:8, :], lhsT=p_t[:, gg * 8:(gg + 1) * 8], rhs=Vbb[:, g, :],
                                                       start=(gg == 0), stop=(gg == NPG - 1)),
                  reads=["p_t", "Vbb"], writes=[Rp[5]])
            if hf == 0:
                return
            A("dve", lambda e: e.tensor_reduce(out=psm[:, :], in_=p_t[:, :].rearrange("p (g h) -> p h g", g=NPG), axis=AX.X, op=ALU.add),
              reads=["p_t"], writes=["psm"])
            A("dve", lambda e: e.tensor_tensor(out=mo[:, :], in0=pf[5][0:8, :], in1=cst[0:8, K_BM:K_BM + 512], op=ALU.mult),
              reads=[Rp[5], "cst"], writes=["mo"])
            A("pe", lambda e: e.matmul(pf[3][0:NS, :], lhsT=cst[0:8, K_EB + b * 16:K_EB + (b + 1) * 16], rhs=mo[:, :],
                                       start=(b == 0), stop=(b == NS - 1)),
              reads=["mo", "cst"], writes=[Rp[3]])
            A("pe", lambda e: e.matmul(pf[4][0:NS, 0:8], lhsT=cst[:, K_CS + b * 16:K_CS + (b + 1) * 16], rhs=psm[:, :],
                                       start=(b == 0), stop=(b == NS - 1)),
              reads=["psm", "cst"], writes=[Rp[4]])

        xTs = ((xT, "xT"), (xT2, "xT2"))

        def pA_h1(I):
            xb, xr = xTs[I % 2]
            norm_T(x_all[I * 128:(I + 1) * 128, :], 128, xb, xr, "act")

        def pA_k(I):
            xb, xr = xTs[I % 2]
            proj(1, C_K, 512, 128, xb, xr)
            headnorm(1, 128, 8, 64, gk_bc, knb[:, :], "knb")

        def pA_kb(I):
            for hp in range(4):
                A("pe", lambda e, hp=hp: e.transpose(out=pbf(6)[:, hp * 128:(hp + 1) * 128], in_=knb[:, hp * 128:(hp + 1) * 128],
                                                     identity=identb[:, :]),
                  reads=["knb", "identb"], writes=[Rp[6]])
            r4 = I % 4
            sl = r4 // 2
            A("act", lambda e: e.activation(out=kT_stage[:, :, r4 * 128:(r4 + 1) * 128],
                                            in_=pbf(6)[:, 0:512].rearrange("p (h t) -> p h t", h=4), func=AF.Copy),
              reads=[Rp[6]], writes=["kT_stage%d" % sl])
            if r4 % 2 == 1:
                t0 = (I - 1) * 128
                A("sp", lambda e: e.dma_start(out=kT_scr[:, :, t0:t0 + 256].rearrange("h p t -> p h t"),
                                              in_=kT_stage[:, :, sl * 256:(sl + 1) * 256]),
                  reads=["kT_stage%d" % sl], writes=["kT_scr"], dma=True)

        def pA_vf(I):
            xb, xr = xTs[I % 2]
            r4 = I % 4
            proj(2, C_V, 512, 128, xb, xr)
            sl = r4 // 2
            A("act", lambda e: e.activation(out=v_stage[:, sl, :, r4 % 2, :, 0:64],
                                            in_=pf[2][:, :].rearrange("p (q h d) -> p q h d", q=4, h=2), func=AF.Copy),
              reads=[Rp[2]], writes=["v_stage%d" % sl])
            proj(7, C_F, 8, 128, xb, xr)
            logf_of(7, 128)
            if r4 % 2 == 1:
                A("sp", lambda e: e.dma_start(
                    out=v_scr[:, :, I - 1:I + 1, :].rearrange("q p a e -> p q (a e)"),
                    in_=v_stage[:, sl, :, :, :, :].rearrange("p q a h d -> p q (a h d)")),
                  reads=["v_stage%d" % sl], writes=["v_scr"], dma=True)

        def pA_vfb(I):
            A("pe", lambda e: e.matmul(pf[7][:, 8:16], lhsT=utri, rhs=lf[:, :], start=True, stop=True), reads=["lf", "cst"], writes=[Rp[7]])
            A("pe", lambda e: e.matmul(pf[7][:, 16:24], lhsT=ones, rhs=lf[:, :], start=True, stop=True), reads=["lf", "cst"], writes=[Rp[7]])
            A("dve", lambda e: e.tensor_copy(out=carry[:, I, :], in_=runc[:, :]), reads=["runc"], writes=["carry"])
            A("dve", lambda e: e.scalar_tensor_tensor(out=negc[:, I, :], in0=pf[7][:, 8:16], scalar=-1.0, in1=runc[:, :],
                                                      op0=ALU.mult, op1=ALU.subtract),
              reads=[Rp[7], "runc"], writes=["negc"])
            A("dve", lambda e: e.tensor_tensor(out=runc[:, :], in0=runc[:, :], in1=pf[7][:, 16:24], op=ALU.add),
              reads=[Rp[7], "runc"], writes=["runc"])

        cw = sb("cw", [128, 32], F32)
        cwo = sb("cwo", [128, 8], F32)
        fz_t = sb("fz_t", [128, 4], F32)

        def fence(names):
            A("dve", lambda e: e.memset(fz_t[:, :], 0.0), writes=names)

        fence(["et", "ub", "gc", "tmpm", "kT_stage0", "kT_stage1", "v_stage0", "v_stage1"])
        A("dve", lambda e: e.memset(arT[:, 1024:2064].bitcast(BF16), 1.0), writes=["v_stage0", "v_stage1"])
        sched = {}
        for hs_ in range(2 * NS):
            sched.setdefault(15 + (hs_ * 3) // 2, []).append(hs_)
        gather_half(0, 0)
        pA_h1(0)
        for I in range(NT):
            chains = [record(pA_k, I), record(pA_vf, I)]
            if I > 0:
                chains.append(record(pA_kb, I - 1))
                chains.append(record(pA_vfb, I - 1))
            if I + 1 < NT:
                chains.insert(0, record(pA_h1, I + 1))
            for hs_ in sched.get(I, []):
                chains.append(record(sample_half, hs_ // 2, hs_ % 2))
            interleave(*chains)
            for hs_ in sched.get(I, []):
                if hs_ + 1 < 2 * NS:
                    gather_half((hs_ + 1) // 2, (hs_ + 1) % 2)
            if I == 13:
                compute_E()
        pA_kb(NT - 1)
        pA_vfb(NT - 1)

        n = NS
        A("sp", lambda e: e.dma_start(out=xt[:n, :], in_=x_s), writes=["xt"], dma=True)
        A("dve", lambda e: e.tensor_tensor(out=sm1[:, :], in0=qn_s[:, :], in1=kn_s[:, :], op=ALU.mult), reads=["qn_s", "kn_s"], writes=["sm1"])
        A("dve", lambda e: e.tensor_reduce(out=sm2[:, :], in_=sm1[:, :].rearrange("p (h d) -> p h d", h=8), axis=AX.X, op=ALU.add),
          reads=["sm1"], writes=["sm2"])
        A("dve", lambda e: e.scalar_tensor_tensor(out=sm2[:, :], in0=sm2[:, :], scalar=SCALE, in1=lfs[:n, :], op0=ALU.mult, op1=ALU.subtract),
          reads=["sm2", "lfs"], writes=["sm2"])
        A("act", lambda e: e.activation(out=sm2[:, :], in_=sm2[:, :], func=AF.Exp), reads=["sm2"], writes=["sm2"])
        A("dve", lambda e: e.tensor_tensor(out=sm1[:, :].rearrange("p (h d) -> p h d", h=8), in0=vf_s[:, :].rearrange("p (h d) -> p h d", h=8),
                                           in1=sm2[:, :].unsqueeze(2).to_broadcast([n, 8, 64]), op=ALU.mult),
          reads=["vf_s", "sm2"], writes=["sm1"])
        A("dve", lambda e: e.tensor_tensor(out=sm1[:, :], in0=sm1[:, :], in1=pf[3][0:n, :], op=ALU.add), reads=["sm1", Rp[3]], writes=["sm1"])
        A("dve", lambda e: e.tensor_tensor(out=sm3[:, :], in0=pf[4][0:n, 0:8], in1=sm2[:, :], op=ALU.add), reads=[Rp[4], "sm2"], writes=["sm3"])
        A("dve", lambda e: e.reciprocal(out=sm3[:, :], in_=sm3[:, :]), reads=["sm3"], writes=["sm3"])
        A("dve", lambda e: e.tensor_tensor(out=sm1[:, :].rearrange("p (h d) -> p h d", h=8), in0=sm1[:, :].rearrange("p (h d) -> p h d", h=8),
                                           in1=sm3[:, :].unsqueeze(2).to_broadcast([n, 8, 64]), op=ALU.mult),
          reads=["sm1", "sm3"], writes=["sm1"])
        A("dve", lambda e: e.tensor_tensor(out=mrg_s[:, 0:512], in0=sm1[:, :], in1=ga_s[:, :], op=ALU.mult), reads=["sm1", "ga_s"], writes=["mrg_s"])
        A("dve", lambda e: e.tensor_copy(out=mrg_s[:, 512:1024], in_=m_s[:, :]), reads=["m_s"], writes=["mrg_s"])
        for c in range(8):
            A("pe", lambda e, c=c: e.transpose(out=pbf(0)[:, c * 128:c * 128 + n], in_=mrg_s[:n, c * 128:(c + 1) * 128], identity=identb[:n, :n]),
              reads=["mrg_s", "identb"], writes=[Rp[0]])
        A("act", lambda e: e.activation(out=mT_s[:, :, :], in_=pbf(0).rearrange("p (c t) -> p c t", c=8)[:, :, 0:n], func=AF.Copy),
          reads=[Rp[0]], writes=["mT_s"])
        for half in range(2):
            for c in range(8):
                A("pe", lambda e, c=c, half=half: e.matmul(pf[1 + half][:n, :], lhsT=mT_s[:, c, :], rhs=wout[:, c, half * 512:(half + 1) * 512],
                                                           start=(c == 0), stop=(c == 7)),
                  reads=["mT_s", "wout"], writes=[Rp[1 + half]])
            A("dve", lambda e, half=half: e.tensor_tensor(out=xt[:n, half * 512:(half + 1) * 512], in0=pf[1 + half][:n, :],
                                                          in1=xt[:n, half * 512:(half + 1) * 512], op=ALU.add),
              reads=[Rp[1 + half], "xt"], writes=["xt"])
        A("sp", lambda e: e.dma_start(out=y_s, in_=xt[:n, :]), reads=["xt"], dma=True)

        fence(["Kb", "Vb", "Vbb", "Lall", "ptb", "idx", "qT_st", "ga_st", "m_st", "et", "ub", "gc", "tmpm", "kT_stage0", "kT_stage1", "v_stage0", "v_stage1", "xT", "xT2"])
        def pB_h1(i):
            xb, xr = xTs[i % 2]
            norm_T(x_own[i * 128:(i + 1) * 128, :], 128, xb, xr, "act")

        def pB_X(i):
            xb, xr = xTs[i % 2]
            rows = slice(i * 128, (i + 1) * 128)
            proj(1, C_Q, 512, 128, xb, xr)
            headnorm(1, 128, 8, 64, gq_bc, knb[:, :], "knb")
            for hp in range(4):
                A("pe", lambda e, hp=hp: e.transpose(out=pbf(6)[:, hp * 128:(hp + 1) * 128], in_=knb[:, hp * 128:(hp + 1) * 128],
                                                     identity=identb[:, :]),
                  reads=["knb", "identb"], writes=[Rp[6]])
            A("act", lambda e: e.activation(out=qT_st[:, :, i * 128:(i + 1) * 128],
                                            in_=pbf(6)[:, 0:512].rearrange("p (h t) -> p h t", h=4), func=AF.Copy),
              reads=[Rp[6]], writes=["qT_st"])
            proj(1, C_K, 512, 128, xb, xr)
            headnorm(1, 128, 8, 64, gk_bc, kf[:, :], "kf")
            A("sp", lambda e: e.dma_start(out=k_own[rows, :], in_=kf[:, :]), reads=["kf"], dma=True)
            proj(3, C_GV, 512, 128, xb, xr)
            headnorm(3, 128, 4, 128, gv_bc, gvn[:, :], "gvn")
            for g in range(4):
                A("pe", lambda e, g=g: e.matmul(pf[5][:, g * 128:(g + 1) * 128], lhsT=wsT[:, g, :],
                                                rhs=gvn[:, g * 128:(g + 1) * 128], start=True, stop=True),
                  reads=["wsT", "gvn"], writes=[Rp[5]])

        def pB_Y(i):
            xb, xr = xTs[i % 2]
            rows = slice(i * 128, (i + 1) * 128)
            proj(2, C_V, 512, 128, xb, xr)
            A("act", lambda e: e.activation(out=vf[:, :], in_=pf[2][:, :], func=AF.Copy), reads=[Rp[2]], writes=["vf"])
            A("sp", lambda e: e.dma_start(out=v_own[rows, :], in_=vf[:, :]), reads=["vf"], dma=True)
            proj(7, C_F, 8, 128, xb, xr)
            logf_of(7, 128)
            A("sp", lambda e: e.dma_start(out=l_own[rows, :], in_=lf[:, :]), reads=["lf"], dma=True)
            A("pe", lambda e: e.matmul(pf[7][:, 8:16], lhsT=utri, rhs=lf[:, :], start=True, stop=True), reads=["lf", "cst"], writes=[Rp[7]])
            A("dve", lambda e: e.tensor_tensor(out=cw[:, :].rearrange("p (h w) -> p h w", h=8),
                                               in0=carry[:, 4 * i:4 * i + 4, :].rearrange("p w h -> p h w"),
                                               in1=oh4.unsqueeze(1).to_broadcast([128, 8, 4]), op=ALU.mult),
              reads=["carry", "cst"], writes=["cw"])
            A("dve", lambda e: e.tensor_reduce(out=cwo[:, :], in_=cw[:, :].rearrange("p (h w) -> p h w", h=8), axis=AX.X, op=ALU.add),
              reads=["cw"], writes=["cwo"])
            A("dve", lambda e: e.tensor_tensor(out=c_own[:, i, :], in0=pf[7][:, 8:16], in1=cwo[:, :], op=ALU.add),
              reads=[Rp[7], "cwo"], writes=["c_own"])

        def pB_Z(i):
            xb, xr = xTs[i % 2]
            proj(4, C_ZA, 512, 128, xb, xr)
            silu_from(4, 128, ga_st[:, i, :], "ga_st")
            proj(4, C_U, 512, 128, xb, xr)
            A("act", lambda e: e.activation(out=ub[:, :], in_=pf[4][:, :], func=AF.Copy), reads=[Rp[4]], writes=["ub"])
            proj(4, C_ZC, 512, 128, xb, xr)
            silu_from(4, 128, gc[:, :], "gc")

        def pB_tail(i):
            for g in range(4):
                sl = slice(g * 128, (g + 1) * 128)
                A("dve", lambda e, g=g, sl=sl: e.scalar_tensor_tensor(out=tmpm[:, sl], in0=pf[5][:, sl], scalar=bs_t[:, g:g + 1],
                                                                      in1=ub[:, sl], op0=ALU.add, op1=ALU.mult),
                  reads=[Rp[5], "bs_t", "ub"], writes=["tmpm"])
            A("dve", lambda e: e.tensor_tensor(out=m_st[:, i, :], in0=tmpm[:, :], in1=gc[:, :], op=ALU.mult),
              reads=["tmpm", "gc"], writes=["m_st"])

        pB_h1(0)
        for i in range(NOWN):
            chains = [record(pB_X, i), record(pB_Y, i), record(pB_Z, i)]
            if i + 1 < NOWN:
                chains.insert(0, record(pB_h1, i + 1))
            interleave(*chains)
            pB_tail(i)

        fence(["Wb", "kTp", "vP", "pTb0", "pTb1", "pTb2", "pTb3", "o_sb", "a_rows", "rcp_t", "mT", "mT2", "yt", "qTh"])
        A("dve", lambda e: e.memset(kTp[64:65, :], 1.0), writes=["kTp"])
        for i0 in range(0, NOWN, 4):
            for k in range(4):
                A("pe", lambda e, i0=i0, k=k: e.transpose(out=pf[7][0:8, k * 128:(k + 1) * 128], in_=c_own[:, i0 + k, :], identity=identf),
                  reads=["c_own", "cst"], writes=[Rp[7]])
            A("act", lambda e, i0=i0: e.activation(out=a_rows[0:8, i0 * 128:i0 * 128 + 512], in_=pf[7][0:8, :], func=AF.Copy, scale=1.0 / SCALE),
              reads=[Rp[7]], writes=["a_rows"])
        SB = (0, 1, 7)
        LA = 2
        its = []
        for h in range(8):
            for qg in range(4):
                nJ = 16 * qg + 16
                for J in range(nJ):
                    its.append((h, qg, J, nJ))

        def emit_qk(n):
            h, qg, J, nJ = its[n]
            hp, half = h // 2, h % 2
            rws = slice(half * 64, half * 64 + 64)
            if qg == 0 and J == 0:
                A("sp", lambda e: e.dma_start(out=kTp[0:64, :], in_=kT_scr[hp, half * 64:(half + 1) * 64, :]),
                  reads=["kT_scr"], writes=["kTp"], dma=True)
                A("sp", lambda e: e.dma_start(out=qTh[0:64, :], in_=qT_st[half * 64:(half + 1) * 64, hp, :]),
                  reads=["qT_st"], writes=["qTh"], dma=True)
                A("sp", lambda e: e.dma_start(out=qTh[64:65, :], in_=a_rows[h:h + 1, :]),
                  reads=["a_rows"], writes=["qTh"], dma=True)
                if half == 0:
                    A("sp", lambda e, hp=hp: e.dma_start(out=vP, in_=v_scr[hp]), reads=["v_scr"], writes=["vP"], dma=True)
            kmin = max(0, (J - 16 * qg) // 4) if J >= 16 * qg else 0
            c0 = kmin * 128
            q0 = qg * 512 + c0
            q1 = qg * 512 + 512
            sbk = SB[n % 3]
            pb = n % 4
            win = J >= 16 * qg
            A("pe", lambda e: e.matmul(pf[sbk][:, c0:512], lhsT=kTp[0:65, J * 128:(J + 1) * 128], rhs=qTh[0:65, q0:q1],
                                       start=True, stop=(not win)),
              reads=["kTp", "qTh"], writes=[Rp[sbk]])
            if win:
                w = (J - 16 * qg) % 4
                A("pe", lambda e: e.matmul(pf[sbk][:, c0:c0 + 128], lhsT=identb[:, :], rhs=maskb[:, w, :], start=False, stop=True),
                  reads=["identb", "maskb"], writes=[Rp[sbk]])
            A("act", lambda e: e.activation(out=pTb[pb][:, c0:512], in_=pf[sbk][:, c0:512], func=AF.Exp, scale=SCALE,
                                            bias=negc[:, J, h:h + 1]),
              reads=[Rp[sbk], "negc"], writes=["pTb%d" % pb])

        def emit_pv(n):
            h, qg, J, nJ = its[n]
            half = h % 2
            kmin = max(0, (J - 16 * qg) // 4) if J >= 16 * qg else 0
            c0 = kmin * 128
            pb = n % 4
            A("pe", lambda e: e.matmul(pf[2 + qg][0:65, c0:512], lhsT=vP[:, J, half * 65:(half + 1) * 65], rhs=pTb[pb][:, c0:512],
                                       start=(J == 0), stop=(J == nJ - 1)),
              reads=["vP", "pTb%d" % pb], writes=[Rp[2 + qg]])
            if J == nJ - 1:
                A("act", lambda e: e.activation(out=o_sb[0:65, :], in_=pf[2 + qg][0:65, :], func=AF.Copy),
                  reads=[Rp[2 + qg]], writes=["o_sb"])
                for k in range(4):
                    A("pe", lambda e, k=k: e.transpose(out=pf[6][:, k * 65:k * 65 + 65], in_=o_sb[0:65, k * 128:(k + 1) * 128],
                                                       identity=identf[0:65, 0:65]),
                      reads=["o_sb", "cst"], writes=[Rp[6]])
                A("dve", lambda e: e.reciprocal(out=rcp_t[:, 0:4], in_=pf[6][:, 0:260].rearrange("p (k e) -> p k e", k=4)[:, :, 64]),
                  reads=[Rp[6]], writes=["rcp_t"])
                for k in range(4):
                    A("dve", lambda e, k=k: e.scalar_tensor_tensor(
                        out=ga_st[:, 4 * qg + k, h * 64:(h + 1) * 64], in0=pf[6][:, k * 65:k * 65 + 64], scalar=rcp_t[:, k:k + 1],
                        in1=ga_st[:, 4 * qg + k, h * 64:(h + 1) * 64], op0=ALU.mult, op1=ALU.mult),
                      reads=[Rp[6], "rcp_t", "ga_st"], writes=["ga_st"])

        pend = []
        for nn in range(len(its)):
            h_, qg_, J_, _ = its[nn]
            if h_ % 2 == 0 and qg_ == 0 and J_ == 0:
                for m in pend:
                    emit_pv(m)
                pend = []
            emit_qk(nn)
            pend.append(nn)
            if len(pend) > LA:
                emit_pv(pend.pop(0))
        for m in pend:
            emit_pv(m)

        mTs = ((mT, "mT"), (mT2, "mT2"))

        def pD_T(i):
            mb, mr = mTs[i % 2]
            for c in range(8):
                src = ga_st[:, i, c * 128:(c + 1) * 128] if c < 4 else m_st[:, i, (c - 4) * 128:(c - 3) * 128]
                A("pe", lambda e, c=c, src=src: e.transpose(out=pbf(6)[:, c * 128:(c + 1) * 128], in_=src, identity=identb[:, :]),
                  reads=["ga_st", "m_st", "identb"], writes=[Rp[6]])
            A("act", lambda e: e.activation(out=mb[:, :, :], in_=pbf(6).rearrange("p (c t) -> p c t", c=8), func=AF.Copy),
              reads=[Rp[6]], writes=[mr])

        def pD_M(i):
            mb, mr = mTs[i % 2]
            rows = slice(i * 128, (i + 1) * 128)
            A("sp", lambda e: e.dma_start(out=xt[:, :], in_=x_own[rows, :]), writes=["xt"], dma=True)
            for half in range(2):
                for c in range(8):
                    A("pe", lambda e, c=c, half=half: e.matmul(pf[half][:, :], lhsT=mb[:, c, :], rhs=wout[:, c, half * 512:(half + 1) * 512],
                                                               start=(c == 0), stop=(c == 7)),
                      reads=[mr, "wout"], writes=[Rp[half]])
                A("dve", lambda e, half=half: e.tensor_tensor(out=yt[:, half * 512:(half + 1) * 512], in0=pf[half][:, :],
                                                              in1=xt[:, half * 512:(half + 1) * 512], op=ALU.add),
                  reads=[Rp[half], "xt"], writes=["yt"])
            A("sp", lambda e: e.dma_start(out=y_own[rows, :], in_=yt[:, :]), reads=["yt"], dma=True)

        pD_T(0)
        for i in range(NOWN):
            chains = [record(pD_M, i)]
            if i + 1 < NOWN:
                chains.append(record(pD_T, i + 1))
            interleave(*chains)

        P.emit(nc)
    return nc


def _consts(j):
    c = np.zeros((128, K_END), np.float32)
    p = np.arange(128)
    c[:, K_ID:K_ID + 128] = np.eye(128)
    c[:, K_U:K_U + 128] = (p[:, None] <= p[None, :])
    c[:, K_ONE:K_ONE + 128] = 1.0
    c[:, K_L:K_L + 128] = (p[:, None] > p[None, :])
    for w in range(4):
        if w < j:
            m = np.ones((128, 128))
        elif w == j:
            m = (p[:, None] <= p[None, :])
        else:
            m = np.zeros((128, 128))
        c[:, K_MW + w * 128:K_MW + (w + 1) * 128] = m
    c[:, K_IOTA] = p % 16
    c[:, K_OH4 + j] = 1.0
    for b in range(16):
        c[0:8, K_EB + b * 16 + b] = 1.0
        c[:, K_CS + b * 16 + b] = 1.0
    for h in range(8):
        c[h, K_BM + h * 64:K_BM + (h + 1) * 64] = 1.0
        c[h, K_ID8 + h] = 1.0
        c[h, K_OHT + h * 128:K_OHT + (h + 1) * 128] = 1.0
    return c


def kernel(x_prompt, x_sample, cache_k, cache_v, cache_logf, page_table, g_norm, w_in,
           b_f, g_q, g_k, g_v, w_s, b_s, w_out):
    f = lambda a: np.ascontiguousarray(np.asarray(a))
    x_prompt = f(x_prompt); x_sample = f(x_sample)
    ckr = f(cache_k).reshape(2560 * 128, 512)
    cvr = f(cache_v).reshape(2560 * 128, 512)
    clr = f(cache_logf).reshape(2560 * 128, 8)
    ptab = f(page_table).astype(np.int32)
    nc = build_nc()
    in_maps = []
    for c in range(8):
        b, j = c // 4, c % 4
        xa = x_prompt[b]
        xo = np.ascontiguousarray(xa.reshape(16, 4, 128, D)[:, j].reshape(NOWN * 128, D))
        in_maps.append(dict(
            x_all=xa, x_own=xo, x_s=np.ascontiguousarray(x_sample[16 * c:16 * c + 16, 0, :]),
            ck=ckr, cv=cvr, cl=clr,
            pt=np.ascontiguousarray(ptab[16 * c:16 * c + 16].reshape(16, 2, 8)[:, :, np.arange(128) // 16].transpose(2, 0, 1).reshape(128, 32)),
            g_norm=f(g_norm)[0], w_in=f(w_in)[0], b_f=f(b_f)[0], g_q=f(g_q)[0], g_k=f(g_k)[0], g_v=f(g_v)[0],
            w_s=f(w_s)[0], b_s=f(b_s)[0], w_out=f(w_out)[0], consts=_consts(j)))
    res = run_bass_kernel_spmd(nc, in_maps, core_ids=list(range(8)))
    R = res.results
    yp = np.zeros((2, S, D), np.float32)
    kp = np.zeros((1, 2, S, 8, 64), np.float32)
    vp = np.zeros((1, 2, S, 8, 64), np.float32)
    lp = np.zeros((1, 2, S, 8), np.float32)
    ys = np.zeros((128, 1, D), np.float32)
    ks = np.zeros((1, 128, 1, 8, 64), np.float32)
    vs = np.zeros((1, 128, 1, 8, 64), np.float32)
    ls = np.zeros((1, 128, 1, 8), np.float32)
    gs = np.zeros((1, 128, 1, 4, 128), np.float32)
    for c in range(8):
        b, j = c // 4, c % 4
        r = R[c]
        yp[b].reshape(16, 4, 128, D)[:, j] = r["y_own"].reshape(16, 128, D)
        kp[0, b].reshape(16, 4, 128, 512)[:, j] = r["k_own"].reshape(16, 128, 512)
        vp[0, b].reshape(16, 4, 128, 512)[:, j] = r["v_own"].reshape(16, 128, 512)
        lp[0, b].reshape(16, 4, 128, 8)[:, j] = r["l_own"].reshape(16, 128, 8)
        sl = slice(16 * c, 16 * c + 16)
        ys[sl, 0] = r["y_s"]
        ks[0, sl, 0] = r["k_s"].reshape(16, 8, 64)
        vs[0, sl, 0] = r["v_s"].reshape(16, 8, 64)
        ls[0, sl, 0] = r["l_s"]
        gs[0, sl, 0] = r["gv_s"].reshape(16, 4, 128)
    return (yp, ys, kp, vp, lp, ks, vs, ls, gs)
```

```python
import contextlib
import numpy as np
import concourse.bass as bass
import concourse.mybir as mybir
from concourse.bass_utils import run_bass_kernel_spmd

F32 = mybir.dt.float32
BF16 = mybir.dt.bfloat16
I32 = mybir.dt.int32
ALU = mybir.AluOpType
AF = mybir.ActivationFunctionType
AX = mybir.AxisListType

D = 1024
DIN = 3592
S = 8192
NT = 64
NOWN = 16
NS = 16
NPG = 16
EPS = 1e-6
SCALE = 0.125
C_Q, C_K, C_V, C_F, C_ZA, C_U, C_GV, C_ZC = 0, 512, 1024, 1536, 1544, 2056, 2568, 3080

K_ID, K_U, K_ONE, K_L, K_MW, K_IOTA, K_OH4, K_EB, K_BM, K_ID8, K_OHT, K_CS, K_END = (
    0, 128, 256, 384, 512, 1024, 1025, 1029, 1285, 1797, 1805, 2829, 3085)

ENGS = ("pe", "act", "dve", "pool", "sp")


class Res:
    __slots__ = ("w", "r")

    def __init__(self):
        self.w = None
        self.r = []


class Op:
    __slots__ = ("eng", "fn", "deps", "signaled", "count", "dma", "sem", "prev_on_sem")

    def __init__(self, eng, fn, dma):
        self.eng = eng
        self.fn = fn
        self.dma = dma
        self.deps = []
        self.signaled = dma
        self.count = None
        self.sem = None
        self.prev_on_sem = None


class Prog:
    def __init__(self, n_dma_sems=16):
        self.q = {e: [] for e in ENGS}
        self.n_dma_sems = n_dma_sems
        self.all_dma = []

    def add(self, eng, fn, reads=(), writes=(), dma=False):
        op = Op(eng, fn, dma)
        deps = {}
        for r in reads:
            if r.w is not None:
                deps[id(r.w)] = (r.w, "raw")
        for w in writes:
            if w.w is not None and id(w.w) not in deps:
                deps[id(w.w)] = (w.w, "waw")
            for rr in w.r:
                if id(rr) not in deps:
                    deps[id(rr)] = (rr, "war")
        for d, kind in deps.values():
            if d is op:
                continue
            if not d.dma and not dma and d.eng == eng:
                if eng == "pe":
                    continue
                if kind != "raw":
                    continue
            d.signaled = True
            op.deps.append(d)
        for r in reads:
            r.r.append(op)
        for w in writes:
            w.w = op
            w.r = []
        self.q[eng].append(op)
        if dma:
            self.all_dma.append(op)
        return op

    def emit(self, nc):
        stack = contextlib.ExitStack()
        with stack:
            esem = {e: stack.enter_context(nc.semaphore("s_" + e)) for e in ENGS}
            dsem = {e: [stack.enter_context(nc.semaphore("d_%s%d" % (e, i))) for i in range(self.n_dma_sems)]
                    for e in ("sp", "act", "pool")}
            for e in ENGS:
                c = 0
                dcount = [0] * self.n_dma_sems
                dlast = [None] * self.n_dma_sems
                k = 0
                for op in self.q[e]:
                    if op.dma:
                        s = k % self.n_dma_sems
                        k += 1
                        dcount[s] += 16
                        op.sem = dsem[e][s]
                        op.count = dcount[s]
                        op.prev_on_sem = dlast[s]
                        dlast[s] = op
                    elif op.signaled:
                        c += 1
                        op.sem = esem[e]
                        op.count = c
            block = stack.enter_context(nc.Block())

            def run(e):
                def body(eng):
                    waited = {}

                    def wait_for(d):
                        key = id(d.sem)
                        if waited.get(key, 0) >= d.count:
                            return
                        eng.wait_ge(d.sem, d.count)
                        waited[key] = d.count

                    for op in self.q[e]:
                        for d in op.deps:
                            wait_for(d)
                        if op.dma and op.prev_on_sem is not None:
                            wait_for(op.prev_on_sem)
                        ins = op.fn(eng)
                        if op.dma:
                            ins.then_inc(op.sem, 16)
                        elif op.signaled:
                            ins.then_inc(op.sem, 1)
                    if e == "sp":
                        last = {}
                        for d in self.all_dma:
                            last[id(d.sem)] = d
                        for d in last.values():
                            wait_for(d)
                return body

            block.tensor(run("pe"))
            block.scalar(run("act"))
            block.vector(run("dve"))
            block.gpsimd(run("pool"))
            block.sync(run("sp"))


def build_nc():
    nc = bass.Bass("TRN2", target_bir_lowering=False)

    def din(name, shape, dt=F32):
        return nc.dram_tensor(name, shape, dt, kind="ExternalInput").ap()

    def dout(name, shape, dt=F32):
        return nc.dram_tensor(name, shape, dt, kind="ExternalOutput").ap()

    x_all = din("x_all", [S, D])
    x_own = din("x_own", [NOWN * 128, D])
    x_s = din("x_s", [NS, D])
    ck = din("ck", [2560 * 128, 512])
    cv = din("cv", [2560 * 128, 512])
    cl = din("cl", [2560 * 128, 8])
    pt = din("pt", [128, 2 * NS], I32)
    g_norm = din("g_norm", [D])
    w_in = din("w_in", [D, DIN])
    b_f = din("b_f", [8])
    g_q = din("g_q", [64])
    g_k = din("g_k", [64])
    g_v = din("g_v", [512])
    w_s = din("w_s", [4, 128, 128])
    b_s = din("b_s", [4, 128])
    w_out = din("w_out", [D, D])
    consts_d = din("consts", [128, K_END])

    y_own = dout("y_own", [NOWN * 128, D])
    k_own = dout("k_own", [NOWN * 128, 512])
    v_own = dout("v_own", [NOWN * 128, 512])
    l_own = dout("l_own", [NOWN * 128, 8])
    y_s = dout("y_s", [NS, D])
    k_s = dout("k_s", [NS, 512])
    v_s = dout("v_s", [NS, 512])
    l_s = dout("l_s", [NS, 8])
    gv_s = dout("gv_s", [NS, 512])

    kT_scr = nc.dram_tensor("kT_scr", [4, 128, S], BF16, kind="Internal").ap()
    v_scr = nc.dram_tensor("v_scr", [4, 128, NT, 130], BF16, kind="Internal").ap()
    q_scr = nc.dram_tensor("q_scr", [NS, 512], F32, kind="Internal").ap()

    P = Prog()
    st = contextlib.ExitStack()
    with st:
        def sb(name, shape, dt):
            return st.enter_context(nc.sbuf_tensor(name, shape, dt))

        cst = sb("cst", [128, K_END], F32)
        identf = cst[:, K_ID:K_ID + 128]
        utri = cst[:, K_U:K_U + 128]
        ones = cst[:, K_ONE:K_ONE + 128]
        ltri = cst[:, K_L:K_L + 128]
        iota = cst[:, K_IOTA:K_IOTA + 1]
        oh4 = cst[:, K_OH4:K_OH4 + 4]
        identb = sb("identb", [128, 128], BF16)
        maskb = sb("maskb", [128, 4, 128], BF16)
        ohTb = sb("ohTb", [8, 8, 128], BF16)
        wout = sb("wout", [128, 8, D], BF16)
        g_bc = sb("g_bc", [128, D], F32)
        gv_bc = sb("gv_bc", [128, 512], F32)
        gq_bc = sb("gq_bc", [128, 64], F32)
        gk_bc = sb("gk_bc", [128, 64], F32)
        bf_bc = sb("bf_bc", [128, 8], F32)
        bs_t = sb("bs_t", [128, 4], F32)
        ws00 = sb("ws00", [128, 4], F32)
        bs0 = sb("bs0", [128, 4], F32)
        wsT = sb("wsT", [128, 4, 128], BF16)
        negc = sb("negc", [128, NT, 8], F32)
        carry = sb("carry", [128, NT, 8], F32)
        runc = sb("runc", [128, 8], F32)
        c_own = sb("c_own", [128, NOWN, 8], F32)
        xt = sb("xt", [128, D], F32)
        xn = sb("xn", [128, D], BF16)
        junk = xn
        xT = sb("xT", [128, 8, 128], BF16)
        xT2 = sb("xT2", [128, 8, 128], BF16)
        ss = sb("ss", [128, 1], F32)
        rr = sb("rr", [128, 1], F32)
        sq = sb("sq", [128, 512], F32)
        ssq = sb("ssq", [128, 8], F32)
        rk = sb("rk", [128, 8], F32)
        kn = sb("kn", [128, 512], F32)
        kf = sb("kf", [128, 512], F32)
        knb = sb("knb", [128, 512], BF16)
        vf = sb("vf", [128, 512], F32)
        zf = sb("zf", [128, 8], F32)
        lf = sb("lf", [128, 8], F32)
        lfs = sb("lfs", [128, 8], F32)
        arT = sb("arT", [128, 2080], F32)
        kT_stage = arT[:, 0:1024].bitcast(BF16).rearrange("p (h t) -> p h t", h=4)
        v_stage = arT[:, 1024:2064].bitcast(BF16).rearrange("p (s q a h d) -> p s q a h d", s=2, q=4, a=2, h=2)
        et = arT[:, 0:512]
        ub = arT[:, 512:1024]
        gvn = sb("gvn", [128, 512], BF16)
        gvnf = sb("gvnf", [128, 512], F32)
        gc = arT[:, 1024:1536]
        tmpm = arT[:, 1536:2048]
        arW = sb("arW", [128, 8 * DIN], BF16)
        arQ = sb("arQ", [128, 25600], BF16)

        Wb = arW[:, :].rearrange("p (c n) -> p c n", c=8)
        kTp = arW[:, 0:8192]
        vP = arW[:, 8192:8192 + 64 * 130].rearrange("p (t e) -> p t e", t=64)
        o = 8192 + 64 * 130
        pTb = [arW[:, o + i * 512:o + (i + 1) * 512] for i in range(4)]
        o += 4 * 512
        o_sb = arW[:, o:o + 1024].bitcast(F32)
        o += 1024
        a_rows = arW[:, o:o + 2048]
        o += 2048
        rcp_t = arW[:, o:o + 8].bitcast(F32)
        o += 8
        mT = arW[:, o:o + 1024].rearrange("p (c n) -> p c n", c=8)
        o += 1024
        yt = arW[:, o:o + 2048].bitcast(F32)
        o += 2048
        qTh = arW[:, o:o + 2048]
        o += 2048
        assert o <= 8 * DIN
        qT_st = arQ[:, 0:8192].rearrange("p (h n) -> p h n", h=4)
        ga_st = arQ[:, 8192:16384].rearrange("p (i n) -> p i n", i=NOWN)
        m_st = arQ[:, 16384:24576].rearrange("p (i n) -> p i n", i=NOWN)
        Kb = arQ[:, 0:8192].bitcast(F32).rearrange("p (g n) -> p g n", g=8)
        Vb = arQ[:, 8192:16384].bitcast(F32).rearrange("p (g n) -> p g n", g=8)
        Wstage = arQ[:, 0:2 * DIN].bitcast(F32)
        Lh = arQ[:, 16384:20992].bitcast(F32).rearrange("p (c t h) -> p c t h", c=2 * NS, t=9)
        Vbb = arQ[:, 20992:25088].rearrange("p (g n) -> p g n", g=8)
        ptb = arQ[:, 25088:25152].bitcast(I32)
        idx = arQ[:, 25152:25216].bitcast(I32)
        tot_t = sb("tot_t", [128, 256], F32)
        tsum = sb("tsum", [128, 256], F32)
        qb = sb("qb", [128, 512], F32)
        s_t = sb("s_t", [128, 128], F32)
        p_t = sb("p_t", [128, 128], BF16)
        psm = sb("psm", [128, 8], F32)
        mo = sb("mo", [8, 512], F32)
        md = sb("md", [8, 8], F32)
        qn_s = sb("qn_s", [NS, 512], F32)
        kn_s = sb("kn_s", [NS, 512], F32)
        vf_s = sb("vf_s", [NS, 512], F32)
        ga_s = sb("ga_s", [NS, 512], BF16)
        m_s = sb("m_s", [NS, 512], BF16)
        sm1 = sb("sm1", [NS, 512], F32)
        sm2 = sb("sm2", [NS, 8], F32)
        sm3 = sb("sm3", [NS, 8], F32)
        mrg_s = sb("mrg_s", [NS, D], BF16)
        mT_s = sb("mT_s", [128, 8, NS], BF16)

        pf = [st.enter_context(nc.psum_tensor("pf%d" % i, [128, 512], F32)) for i in range(8)]
        Rp = [Res() for _ in range(8)]

        def pbf(i):
            return pf[i][:, :].bitcast(BF16)

        R = {}

        def res(name):
            if name not in R:
                R[name] = Res()
            return R[name]

        cap = [None]

        def A(eng, fn, reads=(), writes=(), dma=False):
            if cap[0] is not None:
                cap[0].append((eng, fn, reads, writes, dma))
                return None
            rl = [res(r) if isinstance(r, str) else r for r in reads]
            wl = [res(w) if isinstance(w, str) else w for w in writes]
            return P.add(eng, fn, rl, wl, dma)

        def record(f, *args):
            cap[0] = []
            f(*args)
            lst = cap[0]
            cap[0] = None
            return lst

        def interleave(*lists):
            lists = [l for l in lists if l]
            pos = [0] * len(lists)
            left = sum(len(l) for l in lists)
            while left:
                for i, l in enumerate(lists):
                    if pos[i] < len(l):
                        A(*l[pos[i]])
                        pos[i] += 1
                        left -= 1

        A("sp", lambda e: e.dma_start(out=cst[:, :], in_=consts_d), writes=["cst"], dma=True)
        A("sp", lambda e: e.dma_start(out=g_bc[:, :], in_=g_norm.partition_broadcast(128)), writes=["g_bc"], dma=True)
        A("sp", lambda e: e.dma_start(out=gv_bc[:, :], in_=g_v.partition_broadcast(128)), writes=["gv_bc"], dma=True)
        A("sp", lambda e: e.dma_start(out=gq_bc[:, :], in_=g_q.partition_broadcast(128)), writes=["gq_bc"], dma=True)
        A("sp", lambda e: e.dma_start(out=gk_bc[:, :], in_=g_k.partition_broadcast(128)), writes=["gk_bc"], dma=True)
        A("sp", lambda e: e.dma_start(out=bf_bc[:, :], in_=b_f.partition_broadcast(128)), writes=["bf_bc"], dma=True)
        for g in range(4):
            A("sp", lambda e, g=g: e.dma_start(out=bs_t[:, g:g + 1], in_=b_s[g].rearrange("(t o) -> t o", o=1)),
              writes=["bs_t"], dma=True)
            A("sp", lambda e, g=g: e.dma_start(out=ws00[:, g:g + 1], in_=w_s[g, 0, 0:1].partition_broadcast(128)),
              writes=["ws00"], dma=True)
            A("sp", lambda e, g=g: e.dma_start(out=bs0[:, g:g + 1], in_=b_s[g, 0:1].partition_broadcast(128)),
              writes=["bs0"], dma=True)
        w_in_v = w_in.rearrange("(c p) n -> p c n", p=128)
        for c in range(8):
            A("sp", lambda e, c=c: e.dma_start(out=Wstage, in_=w_in[c * 128:(c + 1) * 128, :]), writes=["Kb"], dma=True)
            A("act", lambda e, c=c: e.activation(out=Wb[:, c, :], in_=Wstage, func=AF.Copy), reads=["Kb"], writes=["Wb"])
        A("pool", lambda e: e.dma_start(out=wout[:, :, :], in_=w_out.rearrange("(c p) n -> p c n", p=128)),
          writes=["wout"], dma=True)
        A("dve", lambda e: e.tensor_copy(out=identb[:, :], in_=identf), reads=["cst"], writes=["identb"])
        A("dve", lambda e: e.tensor_scalar(out=maskb[:, :, :].rearrange("p w t -> p (w t)"), in0=cst[:, K_MW:K_MW + 512],
                                           scalar1=1.0e4, scalar2=-1.0e4, op0=ALU.mult, op1=ALU.add),
          reads=["cst"], writes=["maskb"])
        A("dve", lambda e: e.tensor_copy(out=ohTb[:, :, :].rearrange("p h s -> p (h s)"), in_=cst[0:8, K_OHT:K_OHT + 1024]),
          reads=["cst"], writes=["ohTb"])
        A("dve", lambda e: e.memset(runc[:, :], 0.0), writes=["runc"])
        ws_t = tmpm.rearrange("p (g s) -> p g s", g=4)
        A("sp", lambda e: e.dma_start(out=ws_t, in_=w_s.rearrange("g t s -> t g s")), writes=["tmpm"], dma=True)
        for g in range(4):
            A("pe", lambda e, g=g: e.transpose(out=pf[0][:, g * 128:(g + 1) * 128], in_=ws_t[:, g, :], identity=identf),
              reads=["tmpm", "cst"], writes=[Rp[0]])
        for g in range(4):
            A("dve", lambda e, g=g: e.tensor_tensor(out=wsT[:, g, :], in0=pf[0][:, g * 128:(g + 1) * 128], in1=utri, op=ALU.mult),
              reads=[Rp[0], "cst"], writes=["wsT"])

        def rsqrt_act(dst, src, n, scale):
            A("act", lambda e: e.activation(out=dst, in_=src, func=AF.Ln, scale=scale, bias=EPS), reads=["tmp_r_in"], writes=["tmp_r"])
            A("act", lambda e: e.activation(out=dst, in_=dst, func=AF.Exp, scale=-0.5), reads=["tmp_r"], writes=["tmp_r"])

        def norm_T(src, n, xTb=None, xTr="xT", ldq="sp"):
            xTb = xT if xTb is None else xTb
            A(ldq, lambda e: e.dma_start(out=xt[:n, :], in_=src), writes=["xt"], dma=True)
            A("dve", lambda e: e.memset(ss[:n, :], 0.0), writes=["ss"])
            A("act", lambda e: e.activation(out=junk[:n, :], in_=xt[:n, :], func=AF.Square, accum_out=ss[:n, :]),
              reads=["xt", "ss"], writes=["xn", "ss"])
            A("act", lambda e: e.activation(out=rr[:n, :], in_=ss[:n, :], func=AF.Ln, scale=1.0 / D, bias=EPS),
              reads=["ss"], writes=["rr"])
            A("act", lambda e: e.activation(out=rr[:n, :], in_=rr[:n, :], func=AF.Exp, scale=-0.5),
              reads=["rr"], writes=["rr"])
            A("dve", lambda e: e.scalar_tensor_tensor(out=xn[:n, :], in0=xt[:n, :], scalar=rr[:n, 0:1], in1=g_bc[:n, :],
                                                      op0=ALU.mult, op1=ALU.mult),
              reads=["xt", "rr", "g_bc"], writes=["xn"])
            for c in range(8):
                A("pe", lambda e, c=c: e.transpose(out=pbf(0)[:, c * 128:c * 128 + n], in_=xn[:n, c * 128:(c + 1) * 128],
                                                   identity=identb[:n, :n]),
                  reads=["xn", "identb"], writes=[Rp[0]])
            A("act", lambda e: e.activation(out=xTb[:, :, 0:n], in_=pbf(0).rearrange("p (c t) -> p c t", c=8)[:, :, 0:n], func=AF.Copy),
              reads=[Rp[0]], writes=[xTr])

        def proj(bank, col0, width, n, xTb=None, xTr="xT"):
            xTb = xT if xTb is None else xTb
            for c in range(8):
                A("pe", lambda e, c=c: e.matmul(pf[bank][:n, 0:width], lhsT=xTb[:, c, 0:n], rhs=Wb[:, c, col0:col0 + width],
                                                start=(c == 0), stop=(c == 7)),
                  reads=[xTr, "Wb"], writes=[Rp[bank]])

        def headnorm(bank, n, nh, hd, gb, out_ap, out_res):
            A("act", lambda e: e.activation(out=sq[:n, :], in_=pf[bank][:n, :], func=AF.Square), reads=[Rp[bank]], writes=["sq"])
            A("dve", lambda e: e.tensor_reduce(out=ssq[:n, 0:nh], in_=sq[:n, :].rearrange("p (h d) -> p h d", h=nh),
                                               axis=AX.X, op=ALU.add), reads=["sq"], writes=["ssq"])
            A("act", lambda e: e.activation(out=rk[:n, 0:nh], in_=ssq[:n, 0:nh], func=AF.Ln, scale=1.0 / hd, bias=EPS),
              reads=["ssq"], writes=["rk"])
            A("act", lambda e: e.activation(out=rk[:n, 0:nh], in_=rk[:n, 0:nh], func=AF.Exp, scale=-0.5), reads=["rk"], writes=["rk"])
            A("dve", lambda e: e.tensor_tensor(out=kn[:n, :].rearrange("p (h d) -> p h d", h=nh),
                                               in0=pf[bank][:n, :].rearrange("p (h d) -> p h d", h=nh),
                                               in1=rk[:n, 0:nh].unsqueeze(2).to_broadcast([n, nh, hd]), op=ALU.mult),
              reads=[Rp[bank], "rk"], writes=["kn"])
            if hd == 64:
                in1 = gb[:n, :].unsqueeze(1).to_broadcast([n, nh, hd])
                A("dve", lambda e: e.tensor_tensor(out=out_ap.rearrange("p (h d) -> p h d", h=nh),
                                                   in0=kn[:n, :].rearrange("p (h d) -> p h d", h=nh), in1=in1, op=ALU.mult),
                  reads=["kn"], writes=[out_res])
            else:
                A("dve", lambda e: e.tensor_tensor(out=out_ap, in0=kn[:n, :], in1=gb[:n, :], op=ALU.mult),
                  reads=["kn"], writes=[out_res])

        def logf_of(bank, n):
            A("dve", lambda e: e.tensor_tensor(out=zf[:n, :], in0=pf[bank][:n, 0:8], in1=bf_bc[:n, :], op=ALU.add),
              reads=[Rp[bank], "bf_bc"], writes=["zf"])
            A("act", lambda e: e.activation(out=zf[:n, :], in_=zf[:n, :], func=AF.Exp, scale=-1.0), reads=["zf"], writes=["zf"])
            A("act", lambda e: e.activation(out=zf[:n, :], in_=zf[:n, :], func=AF.Ln, bias=1.0), reads=["zf"], writes=["zf"])
            A("dve", lambda e: e.tensor_scalar(out=lf[:n, :], in0=zf[:n, :], scalar1=-1.0, scalar2=None, op0=ALU.mult),
              reads=["zf"], writes=["lf"])

        def silu_from(bank, n, out_ap, out_res):
            A("act", lambda e: e.activation(out=et[:n, :], in_=pf[bank][:n, :], func=AF.Exp, scale=-1.0), reads=[Rp[bank]], writes=["et"])
            A("dve", lambda e: e.tensor_scalar(out=et[:n, :], in0=et[:n, :], scalar1=1.0, scalar2=None, op0=ALU.add),
              reads=["et"], writes=["et"])
            A("dve", lambda e: e.reciprocal(out=et[:n, :], in_=et[:n, :]), reads=["et"], writes=["et"])
            A("dve", lambda e: e.tensor_tensor(out=out_ap, in0=pf[bank][:n, :], in1=et[:n, :], op=ALU.mult),
              reads=[Rp[bank], "et"], writes=[out_res])

        def rest_proj(n, ga_out, ga_res, m_out, m_res, sample):
            proj(3, C_ZA, 512, n)
            silu_from(3, n, ga_out, ga_res)
            proj(4, C_U, 512, n)
            A("act", lambda e: e.activation(out=ub[:n, :], in_=pf[4][:n, :], func=AF.Copy), reads=[Rp[4]], writes=["ub"])
            proj(3, C_GV, 512, n)
            if sample:
                headnorm(3, n, 4, 128, gv_bc, gvnf[:n, :], "gvnf")
                A("sp", lambda e: e.dma_start(out=gv_s, in_=gvnf[:n, :]), reads=["gvnf"], dma=True)
            else:
                headnorm(3, n, 4, 128, gv_bc, gvn[:n, :], "gvn")
                for g in range(4):
                    A("pe", lambda e, g=g: e.matmul(pf[5][:n, g * 128:(g + 1) * 128], lhsT=wsT[:, g, :],
                                                    rhs=gvn[:, g * 128:(g + 1) * 128], start=True, stop=True),
                      reads=["wsT", "gvn"], writes=[Rp[5]])
            proj(4, C_ZC, 512, n)
            silu_from(4, n, gc[:n, :], "gc")
            for g in range(4):
                sl = slice(g * 128, (g + 1) * 128)
                if sample:
                    A("dve", lambda e, g=g, sl=sl: e.tensor_scalar(out=tmpm[:n, sl], in0=gvnf[:n, sl], scalar1=ws00[:n, g:g + 1],
                                                                   scalar2=bs0[:n, g:g + 1], op0=ALU.mult, op1=ALU.add),
                      reads=["gvnf", "ws00", "bs0"], writes=["tmpm"])
                    A("dve", lambda e, sl=sl: e.tensor_tensor(out=tmpm[:n, sl], in0=tmpm[:n, sl], in1=ub[:n, sl], op=ALU.mult),
                      reads=["tmpm", "ub"], writes=["tmpm"])
                else:
                    A("dve", lambda e, g=g, sl=sl: e.scalar_tensor_tensor(out=tmpm[:n, sl], in0=pf[5][:n, sl], scalar=bs_t[:n, g:g + 1],
                                                                          in1=ub[:n, sl], op0=ALU.add, op1=ALU.mult),
                      reads=[Rp[5], "bs_t", "ub"], writes=["tmpm"])
            A("dve", lambda e: e.tensor_tensor(out=m_out, in0=tmpm[:n, :], in1=gc[:n, :], op=ALU.mult),
              reads=["tmpm", "gc"], writes=[m_res])

        n = NS
        norm_T(x_s, n)
        proj(1, C_Q, 512, n)
        headnorm(1, n, 8, 64, gq_bc, qn_s[:n, :], "qn_s")
        A("sp", lambda e: e.dma_start(out=q_scr, in_=qn_s[:n, :]), reads=["qn_s"], writes=["q_scr"], dma=True)
        proj(1, C_K, 512, n)
        headnorm(1, n, 8, 64, gk_bc, kn_s[:n, :], "kn_s")
        A("sp", lambda e: e.dma_start(out=k_s, in_=kn_s[:n, :]), reads=["kn_s"], dma=True)
        proj(2, C_V, 512, n)
        A("act", lambda e: e.activation(out=vf_s[:n, :], in_=pf[2][:n, :], func=AF.Copy), reads=[Rp[2]], writes=["vf_s"])
        A("sp", lambda e: e.dma_start(out=v_s, in_=vf_s[:n, :]), reads=["vf_s"], dma=True)
        proj(2, C_F, 8, n)
        logf_of(2, n)
        A("dve", lambda e: e.tensor_copy(out=lfs[:n, :], in_=lf[:n, :]), reads=["lf"], writes=["lfs"])
        A("sp", lambda e: e.dma_start(out=l_s, in_=lfs[:n, :]), reads=["lfs"], dma=True)
        rest_proj(n, ga_s[:n, :], "ga_s", m_s[:n, :], "m_s", True)

        ck2 = ck.rearrange("(r t) n -> r (t n)", t=8)
        cv2 = cv.rearrange("(r t) n -> r (t n)", t=8)
        cl2 = cl.rearrange("(r t) n -> r (t n)", t=8)
        A("sp", lambda e: e.dma_start(out=ptb, in_=pt), writes=["ptb"], dma=True)
        A("dve", lambda e: e.tensor_scalar(out=idx, in0=ptb, scalar1=16.0, scalar2=iota, op0=ALU.mult, op1=ALU.add),
          reads=["ptb", "cst"], writes=["idx"])
        A("dve", lambda e: e.memset(arQ[:, 16384:20992].bitcast(F32), 0.0), writes=["Lall"])
        for col in range(2 * NS):
            A("pool", lambda e, col=col: e.indirect_dma_start(
                out=arQ[:, 16384:20992].bitcast(F32)[:, col * 72:col * 72 + 64], out_offset=None, in_=cl2,
                in_offset=bass.IndirectOffsetOnAxis(ap=idx[:, col:col + 1], axis=0)),
              reads=["idx"], writes=["Lall"], dma=True)

        def compute_E():
            A("dve", lambda e: e.tensor_reduce(out=tot_t[:, :].rearrange("p (c h) -> p c h", h=8),
                                               in_=Lh[:, :, 0:8, :].rearrange("p c t h -> p c h t"), axis=AX.X, op=ALU.add),
              reads=["Lall"], writes=["tot_t"])
            A("pe", lambda e: e.matmul(pf[3][:, 0:256], lhsT=ltri, rhs=tot_t[:, :], start=True, stop=True),
              reads=["tot_t", "cst"], writes=[Rp[3]])
            A("pe", lambda e: e.matmul(pf[4][:, 0:128], lhsT=ones,
                                       rhs=tot_t[:, :].rearrange("p (b f h) -> p b f h", f=2, h=8)[:, :, 1, :], start=True, stop=True),
              reads=["tot_t", "cst"], writes=[Rp[4]])
            A("dve", lambda e: e.tensor_copy(out=tsum[:, :], in_=pf[3][:, 0:256]), reads=[Rp[3]], writes=["tsum"])
            A("dve", lambda e: e.tensor_tensor(out=tsum[:, :].rearrange("p (b f h) -> p b f h", f=2, h=8)[:, :, 0, :],
                                               in0=tsum[:, :].rearrange("p (b f h) -> p b f h", f=2, h=8)[:, :, 0, :],
                                               in1=pf[4][:, 0:128].rearrange("p (b h) -> p b h", h=8), op=ALU.add),
              reads=[Rp[4], "tsum"], writes=["tsum"])
            for t in range(6, -1, -1):
                A("dve", lambda e, t=t: e.tensor_tensor(out=Lh[:, :, t, :], in0=Lh[:, :, t, :], in1=Lh[:, :, t + 1, :], op=ALU.add),
                  reads=["Lall"], writes=["Lall"])
            for t in range(1, 9):
                A("dve", lambda e, t=t: e.tensor_tensor(out=Lh[:, :, t, :], in0=Lh[:, :, t, :],
                                                        in1=tsum[:, :].rearrange("p (c h) -> p c h", h=8), op=ALU.add),
                  reads=["Lall", "tsum"], writes=["Lall"])

        def gather_half(b, hf):
            col = b * 2 + hf
            A("pool", lambda e: e.indirect_dma_start(
                out=Kb.rearrange("p g n -> p (g n)"), out_offset=None, in_=ck2,
                in_offset=bass.IndirectOffsetOnAxis(ap=idx[:, col:col + 1], axis=0)),
              reads=["idx"], writes=["Kb"], dma=True)
            A("pool", lambda e: e.indirect_dma_start(
                out=Vb.rearrange("p g n -> p (g n)"), out_offset=None, in_=cv2,
                in_offset=bass.IndirectOffsetOnAxis(ap=idx[:, col:col + 1], axis=0)),
              reads=["idx"], writes=["Vb"], dma=True)

        def sample_half(b, hf):
            hs = slice(hf * 64, hf * 64 + 64)
            if hf == 0:
                A("sp", lambda e: e.dma_start(out=qb[:, :], in_=q_scr[b].partition_broadcast(128)), reads=["q_scr"], writes=["qb"], dma=True)
            A("dve", lambda e: e.tensor_tensor(out=Kb, in0=Kb, in1=qb[:, :].unsqueeze(1).to_broadcast([128, 8, 512]), op=ALU.mult),
              reads=["Kb", "qb"], writes=["Kb"])
            A("dve", lambda e: e.tensor_reduce(out=s_t[:, hs], in_=Kb.rearrange("p g (h d) -> p (g h) d", h=8), axis=AX.X, op=ALU.add),
              reads=["Kb"], writes=["s_t"])
            A("dve", lambda e: e.scalar_tensor_tensor(out=s_t[:, hs].rearrange("p (g h) -> p g h", g=8), in0=s_t[:, hs].rearrange("p (g h) -> p g h", g=8),
                                                      scalar=SCALE, in1=Lh[:, b * 2 + hf, 1:9, :], op0=ALU.mult, op1=ALU.add),
              reads=["s_t", "Lall"], writes=["s_t"])
            A("act", lambda e: e.activation(out=p_t[:, hs], in_=s_t[:, hs], func=AF.Exp), reads=["s_t"], writes=["p_t"])
            A("act", lambda e: e.activation(out=Vbb, in_=Vb, func=AF.Copy), reads=["Vb"], writes=["Vbb"])
            for g in range(8):
                gg = hf * 8 + g
                A("pe", lambda e, g=g, gg=gg: e.matmul(pf[5][0:8, :], lhsT=p_t[:, gg * 8:(gg + 1) * 8], rhs=Vbb[:, g, :],
                                                       start=(gg == 0), stop=(gg == NPG - 1)),
                  reads=["p_t", "Vbb"], writes=[Rp[5]])
            if hf == 0:
                return
            A("dve", lambda e: e.tensor_reduce(out=psm[:, :], in_=p_t[:, :].rearrange("p (g h) -> p h g", g=NPG), axis=AX.X, op=ALU.add),
              reads=["p_t"], writes=["psm"])
            A("dve", lambda e: e.tensor_tensor(out=mo[:, :], in0=pf[5][0:8, :], in1=cst[0:8, K_BM:K_BM + 512], op=ALU.mult),
              reads=[Rp[5], "cst"], writes=["mo"])
            A("pe", lambda e: e.matmul(pf[3][0:NS, :], lhsT=cst[0:8, K_EB + b * 16:K_EB + (b + 1) * 16], rhs=mo[:, :],
                                       start=(b == 0), stop=(b == NS - 1)),
              reads=["mo", "cst"], writes=[Rp[3]])
            A("pe", lambda e: e.matmul(pf[4][0:NS, 0:8], lhsT=cst[:, K_CS + b * 16:K_CS + (b + 1) * 16], rhs=psm[:, :],
                                       start=(b == 0), stop=(b == NS - 1)),
              reads=["psm", "cst"], writes=[Rp[4]])

        xTs = ((xT, "xT"), (xT2, "xT2"))

        def pA_h1(I):
            xb, xr = xTs[I % 2]
            norm_T(x_all[I * 128:(I + 1) * 128, :], 128, xb, xr, "act")

        def pA_k(I):
            xb, xr = xTs[I % 2]
            proj(1, C_K, 512, 128, xb, xr)
            headnorm(1, 128, 8, 64, gk_bc, knb[:, :], "knb")

        def pA_kb(I):
            for hp in range(4):
                A("pe", lambda e, hp=hp: e.transpose(out=pbf(6)[:, hp * 128:(hp + 1) * 128], in_=knb[:, hp * 128:(hp + 1) * 128],
                                                     identity=identb[:, :]),
                  reads=["knb", "identb"], writes=[Rp[6]])
            r4 = I % 4
            sl = r4 // 2
            A("act", lambda e: e.activation(out=kT_stage[:, :, r4 * 128:(r4 + 1) * 128],
                                            in_=pbf(6)[:, 0:512].rearrange("p (h t) -> p h t", h=4), func=AF.Copy),
              reads=[Rp[6]], writes=["kT_stage%d" % sl])
            if r4 % 2 == 1:
                t0 = (I - 1) * 128
                A("sp", lambda e: e.dma_start(out=kT_scr[:, :, t0:t0 + 256].rearrange("h p t -> p h t"),
                                              in_=kT_stage[:, :, sl * 256:(sl + 1) * 256]),
                  reads=["kT_stage%d" % sl], writes=["kT_scr"], dma=True)

        def pA_vf(I):
            xb, xr = xTs[I % 2]
            r4 = I % 4
            proj(2, C_V, 512, 128, xb, xr)
            sl = r4 // 2
            A("act", lambda e: e.activation(out=v_stage[:, sl, :, r4 % 2, :, 0:64],
                                            in_=pf[2][:, :].rearrange("p (q h d) -> p q h d", q=4, h=2), func=AF.Copy),
              reads=[Rp[2]], writes=["v_stage%d" % sl])
            proj(7, C_F, 8, 128, xb, xr)
            logf_of(7, 128)
            if r4 % 2 == 1:
                A("sp", lambda e: e.dma_start(
                    out=v_scr[:, :, I - 1:I + 1, :].rearrange("q p a e -> p q (a e)"),
                    in_=v_stage[:, sl, :, :, :, :].rearrange("p q a h d -> p q (a h d)")),
                  reads=["v_stage%d" % sl], writes=["v_scr"], dma=True)

        def pA_vfb(I):
            A("pe", lambda e: e.matmul(pf[7][:, 8:16], lhsT=utri, rhs=lf[:, :], start=True, stop=True), reads=["lf", "cst"], writes=[Rp[7]])
            A("pe", lambda e: e.matmul(pf[7][:, 16:24], lhsT=ones, rhs=lf[:, :], start=True, stop=True), reads=["lf", "cst"], writes=[Rp[7]])
            A("dve", lambda e: e.tensor_copy(out=carry[:, I, :], in_=runc[:, :]), reads=["runc"], writes=["carry"])
            A("dve", lambda e: e.scalar_tensor_tensor(out=negc[:, I, :], in0=pf[7][:, 8:16], scalar=-1.0, in1=runc[:, :],
                                                      op0=ALU.mult, op1=ALU.subtract),
              reads=[Rp[7], "runc"], writes=["negc"])
            A("dve", lambda e: e.tensor_tensor(out=runc[:, :], in0=runc[:, :], in1=pf[7][:, 16:24], op=ALU.add),
              reads=[Rp[7], "runc"], writes=["runc"])

        cw = sb("cw", [128, 32], F32)
        cwo = sb("cwo", [128, 8], F32)
        fz_t = sb("fz_t", [128, 4], F32)

        def fence(names):
            A("dve", lambda e: e.memset(fz_t[:, :], 0.0), writes=names)

        fence(["et", "ub", "gc", "tmpm", "kT_stage0", "kT_stage1", "v_stage0", "v_stage1"])
        A("dve", lambda e: e.memset(arT[:, 1024:2064].bitcast(BF16), 1.0), writes=["v_stage0", "v_stage1"])
        sched = {}
        for hs_ in range(2 * NS):
            sched.setdefault(15 + (hs_ * 3) // 2, []).append(hs_)
        gather_half(0, 0)
        pA_h1(0)
        for I in range(NT):
            chains = [record(pA_k, I), record(pA_vf, I)]
            if I > 0:
                chains.append(record(pA_kb, I - 1))
                chains.append(record(pA_vfb, I - 1))
            if I + 1 < NT:
                chains.insert(0, record(pA_h1, I + 1))
            for hs_ in sched.get(I, []):
                chains.append(record(sample_half, hs_ // 2, hs_ % 2))
            interleave(*chains)
            for hs_ in sched.get(I, []):
                if hs_ + 1 < 2 * NS:
                    gather_half((hs_ + 1) // 2, (hs_ + 1) % 2)
            if I == 13:
                compute_E()
        pA_kb(NT - 1)
        pA_vfb(NT - 1)

        n = NS
        A("sp", lambda e: e.dma_start(out=xt[:n, :], in_=x_s), writes=["xt"], dma=True)
        A("dve", lambda e: e.tensor_tensor(out=sm1[:, :], in0=qn_s[:, :], in1=kn_s[:, :], op=ALU.mult), reads=["qn_s", "kn_s"], writes=["sm1"])
        A("dve", lambda e: e.tensor_reduce(out=sm2[:, :], in_=sm1[:, :].rearrange("p (h d) -> p h d", h=8), axis=AX.X, op=ALU.add),
          reads=["sm1"], writes=["sm2"])
        A("dve", lambda e: e.scalar_tensor_tensor(out=sm2[:, :], in0=sm2[:, :], scalar=SCALE, in1=lfs[:n, :], op0=ALU.mult, op1=ALU.subtract),
          reads=["sm2", "lfs"], writes=["sm2"])
        A("act", lambda e: e.activation(out=sm2[:, :], in_=sm2[:, :], func=AF.Exp), reads=["sm2"], writes=["sm2"])
        A("dve", lambda e: e.tensor_tensor(out=sm1[:, :].rearrange("p (h d) -> p h d", h=8), in0=vf_s[:, :].rearrange("p (h d) -> p h d", h=8),
                                           in1=sm2[:, :].unsqueeze(2).to_broadcast([n, 8, 64]), op=ALU.mult),
          reads=["vf_s", "sm2"], writes=["sm1"])
        A("dve", lambda e: e.tensor_tensor(out=sm1[:, :], in0=sm1[:, :], in1=pf[3][0:n, :], op=ALU.add), reads=["sm1", Rp[3]], writes=["sm1"])
        A("dve", lambda e: e.tensor_tensor(out=sm3[:, :], in0=pf[4][0:n, 0:8], in1=sm2[:, :], op=ALU.add), reads=[Rp[4], "sm2"], writes=["sm3"])
        A("dve", lambda e: e.reciprocal(out=sm3[:, :], in_=sm3[:, :]), reads=["sm3"], writes=["sm3"])
        A("dve", lambda e: e.tensor_tensor(out=sm1[:, :].rearrange("p (h d) -> p h d", h=8), in0=sm1[:, :].rearrange("p (h d) -> p h d", h=8),
                                           in1=sm3[:, :].unsqueeze(2).to_broadcast([n, 8, 64]), op=ALU.mult),
          reads=["sm1", "sm3"], writes=["sm1"])
        A("dve", lambda e: e.tensor_tensor(out=mrg_s[:, 0:512], in0=sm1[:, :], in1=ga_s[:, :], op=ALU.mult), reads=["sm1", "ga_s"], writes=["mrg_s"])
        A("dve", lambda e: e.tensor_copy(out=mrg_s[:, 512:1024], in_=m_s[:, :]), reads=["m_s"], writes=["mrg_s"])
        for c in range(8):
            A("pe", lambda e, c=c: e.transpose(out=pbf(0)[:, c * 128:c * 128 + n], in_=mrg_s[:n, c * 128:(c + 1) * 128], identity=identb[:n, :n]),
              reads=["mrg_s", "identb"], writes=[Rp[0]])
        A("act", lambda e: e.activation(out=mT_s[:, :, :], in_=pbf(0).rearrange("p (c t) -> p c t", c=8)[:, :, 0:n], func=AF.Copy),
          reads=[Rp[0]], writes=["mT_s"])
        for half in range(2):
            for c in range(8):
                A("pe", lambda e, c=c, half=half: e.matmul(pf[1 + half][:n, :], lhsT=mT_s[:, c, :], rhs=wout[:, c, half * 512:(half + 1) * 512],
                                                           start=(c == 0), stop=(c == 7)),
                  reads=["mT_s", "wout"], writes=[Rp[1 + half]])
            A("dve", lambda e, half=half: e.tensor_tensor(out=xt[:n, half * 512:(half + 1) * 512], in0=pf[1 + half][:n, :],
                                                          in1=xt[:n, half * 512:(half + 1) * 512], op=ALU.add),
              reads=[Rp[1 + half], "xt"], writes=["xt"])
        A("sp", lambda e: e.dma_start(out=y_s, in_=xt[:n, :]), reads=["xt"], dma=True)

        fence(["Kb", "Vb", "Vbb", "Lall", "ptb", "idx", "qT_st", "ga_st", "m_st", "et", "ub", "gc", "tmpm", "kT_stage0", "kT_stage1", "v_stage0", "v_stage1", "xT", "xT2"])
        def pB_h1(i):
            xb, xr = xTs[i % 2]
            norm_T(x_own[i * 128:(i + 1) * 128, :], 128, xb, xr, "act")

        def pB_X(i):
            xb, xr = xTs[i % 2]
            rows = slice(i * 128, (i + 1) * 128)
            proj(1, C_Q, 512, 128, xb, xr)
            headnorm(1, 128, 8, 64, gq_bc, knb[:, :], "knb")
            for hp in range(4):
                A("pe", lambda e, hp=hp: e.transpose(out=pbf(6)[:, hp * 128:(hp + 1) * 128], in_=knb[:, hp * 128:(hp + 1) * 128],
                                                     identity=identb[:, :]),
                  reads=["knb", "identb"], writes=[Rp[6]])
            A("act", lambda e: e.activation(out=qT_st[:, :, i * 128:(i + 1) * 128],
                                            in_=pbf(6)[:, 0:512].rearrange("p (h t) -> p h t", h=4), func=AF.Copy),
              reads=[Rp[6]], writes=["qT_st"])
            proj(1, C_K, 512, 128, xb, xr)
            headnorm(1, 128, 8, 64, gk_bc, kf[:, :], "kf")
            A("sp", lambda e: e.dma_start(out=k_own[rows, :], in_=kf[:, :]), reads=["kf"], dma=True)
            proj(3, C_GV, 512, 128, xb, xr)
            headnorm(3, 128, 4, 128, gv_bc, gvn[:, :], "gvn")
            for g in range(4):
                A("pe", lambda e, g=g: e.matmul(pf[5][:, g * 128:(g + 1) * 128], lhsT=wsT[:, g, :],
                                                rhs=gvn[:, g * 128:(g + 1) * 128], start=True, stop=True),
                  reads=["wsT", "gvn"], writes=[Rp[5]])

        def pB_Y(i):
            xb, xr = xTs[i % 2]
            rows = slice(i * 128, (i + 1) * 128)
            proj(2, C_V, 512, 128, xb, xr)
            A("act", lambda e: e.activation(out=vf[:, :], in_=pf[2][:, :], func=AF.Copy), reads=[Rp[2]], writes=["vf"])
            A("sp", lambda e: e.dma_start(out=v_own[rows, :], in_=vf[:, :]), reads=["vf"], dma=True)
            proj(7, C_F, 8, 128, xb, xr)
            logf_of(7, 128)
            A("sp", lambda e: e.dma_start(out=l_own[rows, :], in_=lf[:, :]), reads=["lf"], dma=True)
            A("pe", lambda e: e.matmul(pf[7][:, 8:16], lhsT=utri, rhs=lf[:, :], start=True, stop=True), reads=["lf", "cst"], writes=[Rp[7]])
            A("dve", lambda e: e.tensor_tensor(out=cw[:, :].rearrange("p (h w) -> p h w", h=8),
                                               in0=carry[:, 4 * i:4 * i + 4, :].rearrange("p w h -> p h w"),
                                               in1=oh4.unsqueeze(1).to_broadcast([128, 8, 4]), op=ALU.mult),
              reads=["carry", "cst"], writes=["cw"])
            A("dve", lambda e: e.tensor_reduce(out=cwo[:, :], in_=cw[:, :].rearrange("p (h w) -> p h w", h=8), axis=AX.X, op=ALU.add),
              reads=["cw"], writes=["cwo"])
            A("dve", lambda e: e.tensor_tensor(out=c_own[:, i, :], in0=pf[7][:, 8:16], in1=cwo[:, :], op=ALU.add),
              reads=[Rp[7], "cwo"], writes=["c_own"])

        def pB_Z(i):
            xb, xr = xTs[i % 2]
            proj(4, C_ZA, 512, 128, xb, xr)
            silu_from(4, 128, ga_st[:, i, :], "ga_st")
            proj(4, C_U, 512, 128, xb, xr)
            A("act", lambda e: e.activation(out=ub[:, :], in_=pf[4][:, :], func=AF.Copy), reads=[Rp[4]], writes=["ub"])
            proj(4, C_ZC, 512, 128, xb, xr)
            silu_from(4, 128, gc[:, :], "gc")

        def pB_tail(i):
            for g in range(4):
                sl = slice(g * 128, (g + 1) * 128)
                A("dve", lambda e, g=g, sl=sl: e.scalar_tensor_tensor(out=tmpm[:, sl], in0=pf[5][:, sl], scalar=bs_t[:, g:g + 1],
                                                                      in1=ub[:, sl], op0=ALU.add, op1=ALU.mult),
                  reads=[Rp[5], "bs_t", "ub"], writes=["tmpm"])
            A("dve", lambda e: e.tensor_tensor(out=m_st[:, i, :], in0=tmpm[:, :], in1=gc[:, :], op=ALU.mult),
              reads=["tmpm", "gc"], writes=["m_st"])

        pB_h1(0)
        for i in range(NOWN):
            chains = [record(pB_X, i), record(pB_Y, i), record(pB_Z, i)]
            if i + 1 < NOWN:
                chains.insert(0, record(pB_h1, i + 1))
            interleave(*chains)
            pB_tail(i)

        fence(["Wb", "kTp", "vP", "pTb0", "pTb1", "pTb2", "pTb3", "o_sb", "a_rows", "rcp_t", "mT", "yt", "qTh"])
        A("dve", lambda e: e.memset(kTp[64:65, :], 1.0), writes=["kTp"])
        for i0 in range(0, NOWN, 4):
            for k in range(4):
                A("pe", lambda e, i0=i0, k=k: e.transpose(out=pf[7][0:8, k * 128:(k + 1) * 128], in_=c_own[:, i0 + k, :], identity=identf),
                  reads=["c_own", "cst"], writes=[Rp[7]])
            A("act", lambda e, i0=i0: e.activation(out=a_rows[0:8, i0 * 128:i0 * 128 + 512], in_=pf[7][0:8, :], func=AF.Copy, scale=1.0 / SCALE),
              reads=[Rp[7]], writes=["a_rows"])
        SB = (0, 1, 7)
        LA = 3
        its = []
        for h in range(8):
            for qg in range(4):
                nJ = 16 * qg + 16
                for J in range(nJ):
                    its.append((h, qg, J, nJ))

        def emit_qk(n):
            h, qg, J, nJ = its[n]
            hp, half = h // 2, h % 2
            rws = slice(half * 64, half * 64 + 64)
            if qg == 0 and J == 0:
                A("sp", lambda e: e.dma_start(out=kTp[0:64, :], in_=kT_scr[hp, half * 64:(half + 1) * 64, :]),
                  reads=["kT_scr"], writes=["kTp"], dma=True)
                A("sp", lambda e: e.dma_start(out=qTh[0:64, :], in_=qT_st[half * 64:(half + 1) * 64, hp, :]),
                  reads=["qT_st"], writes=["qTh"], dma=True)
                A("sp", lambda e: e.dma_start(out=qTh[64:65, :], in_=a_rows[h:h + 1, :]),
                  reads=["a_rows"], writes=["qTh"], dma=True)
                if half == 0:
                    A("sp", lambda e, hp=hp: e.dma_start(out=vP, in_=v_scr[hp]), reads=["v_scr"], writes=["vP"], dma=True)
            kmin = max(0, (J - 16 * qg) // 4) if J >= 16 * qg else 0
            c0 = kmin * 128
            q0 = qg * 512 + c0
            q1 = qg * 512 + 512
            sbk = SB[n % 3]
            pb = n % 4
            win = J >= 16 * qg
            A("pe", lambda e: e.matmul(pf[sbk][:, c0:512], lhsT=kTp[0:65, J * 128:(J + 1) * 128], rhs=qTh[0:65, q0:q1],
                                       start=True, stop=(not win)),
              reads=["kTp", "qTh"], writes=[Rp[sbk]])
            if win:
                w = (J - 16 * qg) % 4
                A("pe", lambda e: e.matmul(pf[sbk][:, c0:c0 + 128], lhsT=identb[:, :], rhs=maskb[:, w, :], start=False, stop=True),
                  reads=["identb", "maskb"], writes=[Rp[sbk]])
            A("act", lambda e: e.activation(out=pTb[pb][:, c0:512], in_=pf[sbk][:, c0:512], func=AF.Exp, scale=SCALE,
                                            bias=negc[:, J, h:h + 1]),
              reads=[Rp[sbk], "negc"], writes=["pTb%d" % pb])

        def emit_pv(n):
            h, qg, J, nJ = its[n]
            half = h % 2
            kmin = max(0, (J - 16 * qg) // 4) if J >= 16 * qg else 0
            c0 = kmin * 128
            pb = n % 4
            A("pe", lambda e: e.matmul(pf[2 + qg][0:65, c0:512], lhsT=vP[:, J, half * 65:(half + 1) * 65], rhs=pTb[pb][:, c0:512],
                                       start=(J == 0), stop=(J == nJ - 1)),
              reads=["vP", "pTb%d" % pb], writes=[Rp[2 + qg]])
            if J == nJ - 1:
                A("act", lambda e: e.activation(out=o_sb[0:65, :], in_=pf[2 + qg][0:65, :], func=AF.Copy),
                  reads=[Rp[2 + qg]], writes=["o_sb"])
                for k in range(4):
                    A("pe", lambda e, k=k: e.transpose(out=pf[6][:, k * 65:k * 65 + 65], in_=o_sb[0:65, k * 128:(k + 1) * 128],
                                                       identity=identf[0:65, 0:65]),
                      reads=["o_sb", "cst"], writes=[Rp[6]])
                A("dve", lambda e: e.reciprocal(out=rcp_t[:, 0:4], in_=pf[6][:, 0:260].rearrange("p (k e) -> p k e", k=4)[:, :, 64]),
                  reads=[Rp[6]], writes=["rcp_t"])
                for k in range(4):
                    A("dve", lambda e, k=k: e.scalar_tensor_tensor(
                        out=ga_st[:, 4 * qg + k, h * 64:(h + 1) * 64], in0=pf[6][:, k * 65:k * 65 + 64], scalar=rcp_t[:, k:k + 1],
                        in1=ga_st[:, 4 * qg + k, h * 64:(h + 1) * 64], op0=ALU.mult, op1=ALU.mult),
                      reads=[Rp[6], "rcp_t", "ga_st"], writes=["ga_st"])

        pend = []
        for nn in range(len(its)):
            h_, qg_, J_, _ = its[nn]
            if h_ % 2 == 0 and qg_ == 0 and J_ == 0:
                for m in pend:
                    emit_pv(m)
                pend = []
            emit_qk(nn)
            pend.append(nn)
            if len(pend) > LA:
                emit_pv(pend.pop(0))
        for m in pend:
            emit_pv(m)

        for i in range(NOWN):
            rows = slice(i * 128, (i + 1) * 128)
            for c in range(8):
                src = ga_st[:, i, c * 128:(c + 1) * 128] if c < 4 else m_st[:, i, (c - 4) * 128:(c - 3) * 128]
                A("pe", lambda e, c=c, src=src: e.transpose(out=pbf(6)[:, c * 128:(c + 1) * 128], in_=src, identity=identb[:, :]),
                  reads=["ga_st", "m_st", "identb"], writes=[Rp[6]])
            A("act", lambda e: e.activation(out=mT[:, :, :], in_=pbf(6).rearrange("p (c t) -> p c t", c=8), func=AF.Copy),
              reads=[Rp[6]], writes=["mT"])
            A("sp", lambda e, rows=rows: e.dma_start(out=xt[:, :], in_=x_own[rows, :]), writes=["xt"], dma=True)
            for half in range(2):
                for c in range(8):
                    A("pe", lambda e, c=c, half=half: e.matmul(pf[half][:, :], lhsT=mT[:, c, :], rhs=wout[:, c, half * 512:(half + 1) * 512],
                                                               start=(c == 0), stop=(c == 7)),
                      reads=["mT", "wout"], writes=[Rp[half]])
                A("dve", lambda e, half=half: e.tensor_tensor(out=yt[:, half * 512:(half + 1) * 512], in0=pf[half][:, :],
                                                              in1=xt[:, half * 512:(half + 1) * 512], op=ALU.add),
                  reads=[Rp[half], "xt"], writes=["yt"])
            A("sp", lambda e, rows=rows: e.dma_start(out=y_own[rows, :], in_=yt[:, :]), reads=["yt"], dma=True)

        P.emit(nc)
    return nc


def _consts(j):
    c = np.zeros((128, K_END), np.float32)
    p = np.arange(128)
    c[:, K_ID:K_ID + 128] = np.eye(128)
    c[:, K_U:K_U + 128] = (p[:, None] <= p[None, :])
    c[:, K_ONE:K_ONE + 128] = 1.0
    c[:, K_L:K_L + 128] = (p[:, None] > p[None, :])
    for w in range(4):
        if w < j:
            m = np.ones((128, 128))
        elif w == j:
            m = (p[:, None] <= p[None, :])
        else:
            m = np.zeros((128, 128))
        c[:, K_MW + w * 128:K_MW + (w + 1) * 128] = m
    c[:, K_IOTA] = p % 16
    c[:, K_OH4 + j] = 1.0
    for b in range(16):
        c[0:8, K_EB + b * 16 + b] = 1.0
        c[:, K_CS + b * 16 + b] = 1.0
    for h in range(8):
        c[h, K_BM + h * 64:K_BM + (h + 1) * 64] = 1.0
        c[h, K_ID8 + h] = 1.0
        c[h, K_OHT + h * 128:K_OHT + (h + 1) * 128] = 1.0
    return c


def kernel(x_prompt, x_sample, cache_k, cache_v, cache_logf, page_table, g_norm, w_in,
           b_f, g_q, g_k, g_v, w_s, b_s, w_out):
    f = lambda a: np.ascontiguousarray(np.asarray(a))
    x_prompt = f(x_prompt); x_sample = f(x_sample)
    ckr = f(cache_k).reshape(2560 * 128, 512)
    cvr = f(cache_v).reshape(2560 * 128, 512)
    clr = f(cache_logf).reshape(2560 * 128, 8)
    ptab = f(page_table).astype(np.int32)
    nc = build_nc()
    in_maps = []
    for c in range(8):
        b, j = c // 4, c % 4
        xa = x_prompt[b]
        xo = np.ascontiguousarray(xa.reshape(16, 4, 128, D)[:, j].reshape(NOWN * 128, D))
        in_maps.append(dict(
            x_all=xa, x_own=xo, x_s=np.ascontiguousarray(x_sample[16 * c:16 * c + 16, 0, :]),
            ck=ckr, cv=cvr, cl=clr,
            pt=np.ascontiguousarray(ptab[16 * c:16 * c + 16].reshape(16, 2, 8)[:, :, np.arange(128) // 16].transpose(2, 0, 1).reshape(128, 32)),
            g_norm=f(g_norm)[0], w_in=f(w_in)[0], b_f=f(b_f)[0], g_q=f(g_q)[0], g_k=f(g_k)[0], g_v=f(g_v)[0],
            w_s=f(w_s)[0], b_s=f(b_s)[0], w_out=f(w_out)[0], consts=_consts(j)))
    res = run_bass_kernel_spmd(nc, in_maps, core_ids=list(range(8)))
    R = res.results
    yp = np.zeros((2, S, D), np.float32)
    kp = np.zeros((1, 2, S, 8, 64), np.float32)
    vp = np.zeros((1, 2, S, 8, 64), np.float32)
    lp = np.zeros((1, 2, S, 8), np.float32)
    ys = np.zeros((128, 1, D), np.float32)
    ks = np.zeros((1, 128, 1, 8, 64), np.float32)
    vs = np.zeros((1, 128, 1, 8, 64), np.float32)
    ls = np.zeros((1, 128, 1, 8), np.float32)
    gs = np.zeros((1, 128, 1, 4, 128), np.float32)
    for c in range(8):
        b, j = c // 4, c % 4
        r = R[c]
        yp[b].reshape(16, 4, 128, D)[:, j] = r["y_own"].reshape(16, 128, D)
        kp[0, b].reshape(16, 4, 128, 512)[:, j] = r["k_own"].reshape(16, 128, 512)
        vp[0, b].reshape(16, 4, 128, 512)[:, j] = r["v_own"].reshape(16, 128, 512)
        lp[0, b].reshape(16, 4, 128, 8)[:, j] = r["l_own"].reshape(16, 128, 8)
        sl = slice(16 * c, 16 * c + 16)
        ys[sl, 0] = r["y_s"]
        ks[0, sl, 0] = r["k_s"].reshape(16, 8, 64)
        vs[0, sl, 0] = r["v_s"].reshape(16, 8, 64)
        ls[0, sl, 0] = r["l_s"]
        gs[0, sl, 0] = r["gv_s"].reshape(16, 4, 128)
    return (yp, ys, kp, vp, lp, ks, vs, ls, gs)
```
